# Optimizing a Trainium2 kernel written in Bass

```python
import math
import jax, jax.numpy as jnp
from jax import lax
import numpy as np

D_MODEL = 1024
BATCH = 8
SEQ = 4096
DEPTH = 1

D_MIX = D_MODEL
D_SSM = D_MIX // 2
SSM_GROUP = 16
N_SSM_GROUPS = D_SSM // SSM_GROUP
SSM_STATE = 64
D_ATTN = D_MIX - D_SSM
HEAD_DIM = 64
N_HEADS = D_ATTN // HEAD_DIM
N_KV_HEADS = 2
KV_REP = N_HEADS // N_KV_HEADS
D_KV = N_KV_HEADS * HEAD_DIM
D_IN = D_SSM + D_ATTN + 2 * D_KV
WINDOW = 128
BLOCK = WINDOW
ROPE_THETA = 500000.0
ROPE_DIM = HEAD_DIM // 4
D_FF = ((8 * D_MODEL // 3 + 255) // 256) * 256
RES_HALF = 0.5
EPS = 1e-6
NEG_INF = -1e30
DT_MIN = 1e-3
DT_MAX = 1e-1

kernel_name = "hymba_s5_swa_sink_macaron"

F32 = jnp.float32


def rms_norm(x, g):
    xf = x.astype(F32)
    y = xf * lax.rsqrt(jnp.mean(xf * xf, axis=-1, keepdims=True) + EPS)
    return (y * g.astype(F32)).astype(x.dtype)


def swiglu(h, w_gate, w_up, w_down):
    return (jax.nn.silu(h @ w_gate) * (h @ w_up)) @ w_down


def rope_tables(L):
    half = ROPE_DIM // 2
    inv_freq = ROPE_THETA ** (-jnp.arange(half, dtype=F32) * 2.0 / ROPE_DIM)
    ang = jnp.arange(L, dtype=F32)[:, None] * inv_freq[None, :]
    return jnp.cos(ang), jnp.sin(ang)


def partial_rope(t, cos, sin):
    half = ROPE_DIM // 2
    c = cos[None, :, None, :]
    s = sin[None, :, None, :]
    t1 = t[..., :half].astype(F32)
    t2 = t[..., half:ROPE_DIM].astype(F32)
    rot = jnp.concatenate([t1 * c - t2 * s, t2 * c + t1 * s], axis=-1).astype(t.dtype)
    return jnp.concatenate([rot, t[..., ROPE_DIM:]], axis=-1)


def s5_mixer(u, A_re, A_im, log_dt, B_re, B_im, C_re, C_im, D, w_glu, b_glu):
    Bsz, L, _ = u.shape
    uf = u.astype(F32).reshape(Bsz, L, N_SSM_GROUPS, SSM_GROUP)
    lam = lax.complex(A_re.astype(F32), A_im.astype(F32))
    dt = jnp.exp(log_dt.astype(F32))[:, None]
    lam_bar = jnp.exp(lam * dt)
    zoh = (lam_bar - 1.0) / lam
    B_bar = zoh[..., None] * lax.complex(B_re.astype(F32), B_im.astype(F32))
    bu = lax.complex(jnp.einsum('blgc,gpc->blgp', uf, B_bar.real),
                     jnp.einsum('blgc,gpc->blgp', uf, B_bar.imag))
    a = jnp.broadcast_to(lam_bar, (1, L) + lam_bar.shape)

    def combine(left, right):
        a_l, b_l = left
        a_r, b_r = right
        return a_l * a_r, a_r * b_l + b_r

    _, states = lax.associative_scan(combine, (a, bu), axis=1)
    y = (jnp.einsum('blgp,gcp->blgc', states.real, C_re.astype(F32))
         - jnp.einsum('blgp,gcp->blgc', states.imag, C_im.astype(F32)))
    y = y + D.astype(F32).reshape(N_SSM_GROUPS, SSM_GROUP) * uf
    y = jax.nn.gelu(y.reshape(Bsz, L, D_SSM))
    y = y * jax.nn.sigmoid(y @ w_glu.astype(F32) + b_glu.astype(F32))
    return y.astype(u.dtype)


def swa_sink_attention(q, k, v, sinks):
    Bsz, L = q.shape[:2]
    nb = L // BLOCK
    qb = q.reshape(Bsz, nb, BLOCK, N_KV_HEADS, KV_REP, HEAD_DIM)

    def band(t):
        tb = t.reshape(Bsz, nb, BLOCK, N_KV_HEADS, HEAD_DIM)
        prev = jnp.pad(tb[:, :-1], ((0, 0), (1, 0), (0, 0), (0, 0), (0, 0)))
        return jnp.concatenate([prev, tb], axis=2)

    kw, vw = band(k), band(v)
    scale = 1.0 / math.sqrt(HEAD_DIM)
    scores = jnp.einsum('bnqkrd,bnskd->bnkrqs', qb, kw).astype(F32) * scale
    qi = jnp.arange(BLOCK)[:, None]
    sj = jnp.arange(2 * BLOCK)[None, :]
    diff = qi + BLOCK - sj
    in_band = (diff >= 0) & (diff < WINDOW)
    blk = jnp.arange(nb)[:, None, None]
    k_valid = (blk * BLOCK - BLOCK + sj[None]) >= 0
    mask = in_band[None] & k_valid
    scores = jnp.where(mask[None, :, None, None], scores, NEG_INF)
    sink = sinks.astype(F32).reshape(N_KV_HEADS, KV_REP)[None, None, :, :, None, None]
    sink = jnp.broadcast_to(sink, scores.shape[:-1] + (1,))
    probs = jax.nn.softmax(jnp.concatenate([scores, sink], axis=-1), axis=-1)[..., :-1]
    out = jnp.einsum('bnkrqs,bnskd->bnqkrd', probs.astype(v.dtype), vw)
    return out.reshape(Bsz, L, N_HEADS * HEAD_DIM)


def setup_inputs(seed: int = 0) -> dict:
    key = jax.random.key(seed)
    ks = jax.random.split(key, 32)
    nrm = lambda k, shape, s: jax.random.normal(k, shape, F32) * s
    gain = lambda k, n: 1.0 + 0.02 * jax.random.normal(k, (DEPTH, n), F32)
    P, G, C = SSM_STATE, N_SSM_GROUPS, SSM_GROUP
    a_im = math.pi * jnp.broadcast_to(jnp.arange(P, dtype=F32), (DEPTH, G, P))
    return {
        "x": jax.random.normal(ks[0], (BATCH, SEQ, D_MODEL), F32),
        "ffn1_norm": gain(ks[1], D_MODEL),
        "ffn1_w_gate": nrm(ks[2], (DEPTH, D_MODEL, D_FF), D_MODEL ** -0.5),
        "ffn1_w_up": nrm(ks[3], (DEPTH, D_MODEL, D_FF), D_MODEL ** -0.5),
        "ffn1_w_down": nrm(ks[4], (DEPTH, D_FF, D_MODEL), D_FF ** -0.5),
        "mix_norm": gain(ks[5], D_MODEL),
        "w_in": nrm(ks[6], (DEPTH, D_MODEL, D_IN), D_MODEL ** -0.5),
        "ssm_A_re": -0.5 + 0.01 * jax.random.normal(ks[7], (DEPTH, G, P), F32),
        "ssm_A_im": a_im + 0.01 * jax.random.normal(ks[8], (DEPTH, G, P), F32),
        "ssm_log_dt": jax.random.uniform(ks[9], (DEPTH, G), F32, math.log(DT_MIN), math.log(DT_MAX)),
        "ssm_B_re": nrm(ks[10], (DEPTH, G, P, C), (2 * C) ** -0.5),
        "ssm_B_im": nrm(ks[11], (DEPTH, G, P, C), (2 * C) ** -0.5),
        "ssm_C_re": nrm(ks[12], (DEPTH, G, C, P), (2 * P) ** -0.5),
        "ssm_C_im": nrm(ks[13], (DEPTH, G, C, P), (2 * P) ** -0.5),
        "ssm_D": nrm(ks[14], (DEPTH, D_SSM), 1.0),
        "ssm_w_glu": nrm(ks[15], (DEPTH, D_SSM, D_SSM), D_SSM ** -0.5),
        "ssm_b_glu": nrm(ks[16], (DEPTH, D_SSM), 0.01),
        "attn_sinks": nrm(ks[17], (DEPTH, N_HEADS), 1.0),
        "ssm_out_norm": gain(ks[18], D_SSM),
        "attn_out_norm": gain(ks[19], D_ATTN),
        "w_out": nrm(ks[20], (DEPTH, D_MIX, D_MODEL), D_MIX ** -0.5),
        "ffn2_norm": gain(ks[21], D_MODEL),
        "ffn2_w_gate": nrm(ks[22], (DEPTH, D_MODEL, D_FF), D_MODEL ** -0.5),
        "ffn2_w_up": nrm(ks[23], (DEPTH, D_MODEL, D_FF), D_MODEL ** -0.5),
        "ffn2_w_down": nrm(ks[24], (DEPTH, D_FF, D_MODEL), D_FF ** -0.5),
        "final_norm": 1.0 + 0.02 * jax.random.normal(ks[25], (D_MODEL,), F32),
    }


def reference(x, ffn1_norm, ffn1_w_gate, ffn1_w_up, ffn1_w_down, mix_norm, w_in,
              ssm_A_re, ssm_A_im, ssm_log_dt, ssm_B_re, ssm_B_im, ssm_C_re, ssm_C_im,
              ssm_D, ssm_w_glu, ssm_b_glu, attn_sinks, ssm_out_norm, attn_out_norm,
              w_out, ffn2_norm, ffn2_w_gate, ffn2_w_up, ffn2_w_down, final_norm):
    Bsz, L, _ = x.shape
    cos, sin = rope_tables(L)
    for l in range(DEPTH):
        h = rms_norm(x, ffn1_norm[l])
        x = x + RES_HALF * swiglu(h, ffn1_w_gate[l], ffn1_w_up[l], ffn1_w_down[l])
        h = rms_norm(x, mix_norm[l])
        proj = h @ w_in[l]
        u, q, k, v = jnp.split(proj, [D_SSM, D_SSM + D_ATTN, D_SSM + D_ATTN + D_KV], axis=-1)
        q = partial_rope(q.reshape(Bsz, L, N_HEADS, HEAD_DIM), cos, sin)
        k = partial_rope(k.reshape(Bsz, L, N_KV_HEADS, HEAD_DIM), cos, sin)
        v = v.reshape(Bsz, L, N_KV_HEADS, HEAD_DIM)
        y_ssm = s5_mixer(u, ssm_A_re[l], ssm_A_im[l], ssm_log_dt[l], ssm_B_re[l], ssm_B_im[l],
                         ssm_C_re[l], ssm_C_im[l], ssm_D[l], ssm_w_glu[l], ssm_b_glu[l])
        y_attn = swa_sink_attention(q, k, v, attn_sinks[l])
        y = jnp.concatenate([rms_norm(y_ssm, ssm_out_norm[l]),
                             rms_norm(y_attn, attn_out_norm[l])], axis=-1)
        x = x + y @ w_out[l]
        h = rms_norm(x, ffn2_norm[l])
        x = x + RES_HALF * swiglu(h, ffn2_w_gate[l], ffn2_w_up[l], ffn2_w_down[l])
    return rms_norm(x, final_norm)
```

```python
import math
import os
from contextlib import ExitStack

import numpy as np
import ml_dtypes

import concourse.bass as bass
import concourse.mybir as mybir
from concourse.bass_utils import run_bass_kernel_spmd

F32 = mybir.dt.float32
BF16 = mybir.dt.bfloat16
AF = mybir.ActivationFunctionType
ALU = mybir.AluOpType

D = 1024
KT = 8
FF = 2816
FT = 22
SEQ = 4096
TS = 512
NTILE = SEQ // TS
NB = TS // 8
EPS = 1e-6
NEG = -30000.0
NWIN = 1408

DEBUG = bool(int(os.environ.get("KDBG", "0")))
NT_RUN = int(os.environ.get("KNT", str(NTILE)))


class Prog:
    ENGS = ("pe", "act", "dve", "pool", "sp")

    def __init__(self):
        self.ops = []
        self.lastw = {}
        self.readers = {}
        self.tag = ""
        self.tile = -1

    def add(self, eng, fn, reads=(), writes=(), dma=None):
        i = len(self.ops)
        deps = set()
        for k in reads:
            w = self.lastw.get(k)
            if w is not None:
                deps.add(w)
        for k in writes:
            w = self.lastw.get(k)
            if w is not None:
                deps.add(w)
            for r in self.readers.get(k, ()):
                deps.add(r)
        for k in reads:
            lst = self.readers.setdefault(k, [])
            if dma is None:
                lst[:] = [r for r in lst if not (self.ops[r]["eng"] == eng and self.ops[r]["dma"] is None)]
            lst.append(i)
        for k in writes:
            self.lastw[k] = i
            self.readers[k] = []
        deps.discard(i)
        self.ops.append(dict(eng=eng, fn=fn, deps=deps, dma=dma, signal=False, count=0, tag=self.tag, tile=self.tile))
        return i

    def barrier(self):
        last = {}
        for i, op in enumerate(self.ops):
            if op["dma"] is not None:
                if str(op["dma"]).startswith("cv"):
                    continue
                last[("d", op["dma"])] = i
            elif op["fn"] is not None:
                last[("e", op["eng"])] = i
        deps = set(last.values())
        for e in self.ENGS:
            self.ops.append(dict(eng=e, fn=None, deps=set(deps), dma=None, signal=False, count=0, tag='barrier', tile=-1))

    def finalize(self, nc, stack):
        ops = self.ops
        for op in ops:
            for d in op["deps"]:
                dop = ops[d]
                if dop["dma"] is None:
                    if dop["eng"] == "pe" and op["eng"] == "pe" and op["dma"] is None:
                        continue
                    dop["signal"] = True
        esem = {e: stack.enter_context(nc.semaphore("sem_" + e)) for e in self.ENGS}
        dsem = {}
        ecnt = {e: 0 for e in self.ENGS}
        dcnt = {}
        waited = {e: {} for e in self.ENGS}
        streams = {e: [] for e in self.ENGS}
        for op in ops:
            e = op["eng"]
            waits = {}
            for d in op["deps"]:
                dop = ops[d]
                if dop["dma"] is not None:
                    key = ("d", dop["dma"])
                    val = dop["count"]
                    sem = dsem[dop["dma"]]
                else:
                    if dop["eng"] == "pe" and e == "pe" and op["dma"] is None:
                        continue
                    key = ("e", dop["eng"])
                    val = dop["count"]
                    sem = esem[dop["eng"]]
                if waited[e].get(key, 0) >= val:
                    continue
                if key not in waits or waits[key][1] < val:
                    waits[key] = (sem, val)
            for key, (sem, val) in waits.items():
                waited[e][key] = val
            if op["dma"] is not None:
                if op["dma"] not in dsem:
                    dsem[op["dma"]] = stack.enter_context(nc.semaphore("dsem%d" % len(dsem)))
                    dcnt[op["dma"]] = 0
                dcnt[op["dma"]] += 16
                op["count"] = dcnt[op["dma"]]
                inc = (dsem[op["dma"]], 16)
            elif op["signal"]:
                ecnt[e] += 1
                op["count"] = ecnt[e]
                inc = (esem[e], 1)
            else:
                inc = None
            streams[e].append((list(waits.values()), op["fn"], inc))
        self.streams = streams
        self.nsem = len(dsem) + len(esem)

    def emit(self, eng_name, e):
        for waits, fn, inc in self.streams[eng_name]:
            for sem, val in waits:
                e.wait_ge(sem, val)
            if fn is None:
                continue
            ins = fn(e)
            if inc is not None:
                ins.then_inc(inc[0], inc[1])


def _bf(a):
    return np.ascontiguousarray(a.astype(ml_dtypes.bfloat16))


def host_consts():
    c = {}
    c["identf"] = np.eye(128, dtype=np.float32)
    c["identb"] = _bf(np.eye(128, dtype=np.float32))
    c["onesb"] = _bf(np.ones((128, 128), np.float32))
    perm = np.zeros((128, 128), np.float32)
    for h in range(2):
        for d in range(16):
            pd = d + 8 if d < 8 else d - 8
            perm[h * 64 + pd, h * 64 + d] = 1.0
    c["permb"] = _bf(perm)
    kj = np.arange(128)[:, None]
    qi = np.arange(128)[None, :]
    prev = np.where(kj > qi, 0.0, NEG).astype(np.float32)
    same = np.where(kj <= qi, 0.0, NEG).astype(np.float32)
    full = np.full((128, 128), NEG, np.float32)
    c["mask"] = _bf(np.concatenate([prev, same, prev, same], axis=1))
    c["maskf"] = _bf(np.concatenate([full, same, prev, same], axis=1))
    sel = np.zeros((128, 8, 240), np.float32)
    for g in range(8):
        for cc in range(16):
            sel[16 * g + cc, g, 112 + cc] = 1.0
    c["sel"] = _bf(sel)
    half = 8
    inv_freq = (500000.0 ** (-np.arange(half, dtype=np.float32) * 2.0 / 16)).astype(np.float32)
    ang = np.arange(SEQ, dtype=np.float32)[:, None] * inv_freq[None, :]
    cos = np.cos(ang).astype(np.float32).T
    sin = np.sin(ang).astype(np.float32).T
    C = np.ones((128, SEQ), np.float32)
    S = np.zeros((128, SEQ), np.float32)
    for h in range(2):
        C[h * 64 + 0:h * 64 + 8] = cos
        C[h * 64 + 8:h * 64 + 16] = cos
        S[h * 64 + 0:h * 64 + 8] = -sin
        S[h * 64 + 8:h * 64 + 16] = sin
    c["ropec"] = C
    c["ropes"] = S
    return c


def host_layout(inp):
    o = {}
    f = lambda a: np.ascontiguousarray(np.asarray(a, dtype=np.float32))
    o["wg1"] = f(inp["ffn1_w_gate"][0]); o["wu1"] = f(inp["ffn1_w_up"][0]); o["wd1"] = f(inp["ffn1_w_down"][0])
    o["wg2"] = f(inp["ffn2_w_gate"][0]); o["wu2"] = f(inp["ffn2_w_up"][0]); o["wd2"] = f(inp["ffn2_w_down"][0])
    w_in = f(inp["w_in"][0])
    u = w_in[:, 0:512]; q = w_in[:, 512:1024]; k = w_in[:, 1024:1152]; v = w_in[:, 1152:1280]
    o["win"] = np.ascontiguousarray(np.concatenate([u, q, k[:, 0:64], k[:, 0:64], k[:, 64:128], k[:, 64:128], v], axis=1))
    o["wout"] = f(inp["w_out"][0])
    o["wglu"] = f(inp["ssm_w_glu"][0])
    fm = lambda vec: np.ascontiguousarray(f(vec).reshape(-1, 128).T)
    gains = np.concatenate([fm(inp["ffn1_norm"][0]), fm(inp["mix_norm"][0]), fm(inp["ffn2_norm"][0]),
                            fm(inp["final_norm"]), fm(inp["ssm_out_norm"][0]), fm(inp["attn_out_norm"][0]),
                            fm(inp["ssm_b_glu"][0])], axis=1)
    o["gains"] = np.ascontiguousarray(gains)
    Dv = f(inp["ssm_D"][0]).reshape(32, 16)
    o["dblk"] = np.ascontiguousarray(np.tile(Dv.T, (8, 1)))
    sk = f(inp["attn_sinks"][0])
    o["sinkrow"] = np.ascontiguousarray(np.repeat(sk.reshape(4, 2), 64, axis=1).T)
    pl = lambda a: np.ascontiguousarray(f(a).reshape(16, 2, 64).transpose(1, 2, 0).reshape(128, 16))
    o["are"] = pl(inp["ssm_A_re"][0]); o["aim"] = pl(inp["ssm_A_im"][0])
    ldt = f(inp["ssm_log_dt"][0])
    o["ldt"] = pl(np.repeat(ldt[:, None], 64, axis=1))
    pb = lambda a: np.ascontiguousarray(f(a).reshape(16, 2, 64, 16).transpose(1, 2, 0, 3).reshape(128, 16, 16))
    o["bre"] = pb(inp["ssm_B_re"][0]); o["bim"] = pb(inp["ssm_B_im"][0])
    pc = lambda a: np.ascontiguousarray(f(a).transpose(0, 2, 1).reshape(16, 2, 64, 16).transpose(1, 2, 0, 3).reshape(128, 16, 16))
    o["cre"] = pc(inp["ssm_C_re"][0]); o["cim"] = pc(inp["ssm_C_im"][0])
    return o


def build_program():
    nc = bass.Bass("TRN2", target_bir_lowering=False)
    P = Prog()
    stack = ExitStack()
    dr = {}

    def din(name, shape, dt=F32):
        dr[name] = nc.dram_tensor(name, list(shape), dt, kind="ExternalInput").ap()
        return dr[name]

    x_d = din("x", [D, SEQ])
    for n in ("wg1", "wu1", "wg2", "wu2"):
        din(n, [D, FF])
    for n in ("wd1", "wd2"):
        din(n, [FF, D])
    din("win", [D, NWIN]); din("wout", [D, D]); din("wglu", [512, 512])
    din("gains", [128, 44]); din("dblk", [128, 32]); din("sinkrow", [128, 4])
    for n in ("are", "aim", "ldt"):
        din(n, [128, 16])
    for n in ("bre", "bim", "cre", "cim"):
        din(n, [128, 16, 16])
    din("identf", [128, 128]); din("identb", [128, 128], BF16); din("onesb", [128, 128], BF16)
    din("permb", [128, 128], BF16); din("mask", [128, 512], BF16); din("maskf", [128, 512], BF16)
    din("sel", [128, 8, 240], BF16); din("ropec", [128, SEQ]); din("ropes", [128, SEQ])
    out_d = nc.dram_tensor("out", [D, SEQ], F32, kind="ExternalOutput").ap()
    dbg_d = {}

    def sb_main(name, shape, dt):
        return stack.enter_context(nc.sbuf_tensor("s_" + name, list(shape), dt))

    sb = sb_main

    wgu = sb("wgu", [128, 4, 2, 4, 128], BF16)
    wd = sb("wd", [128, 4, 2, 512], BF16)
    wis = sb("wis", [128, 4, KT, 128], BF16)
    wo = sb("wo", [128, 4, KT, 128], BF16)
    w_glu = sb("w_glu", [128, 4, 512], BF16)
    sgs = sb("sgs", [128, 2, TS], F32)
    sqF = sb("sqF", [128, 2, TS], BF16)
    sqM = sb("sqM", [128, 1, TS], BF16)
    rsF = sb("rsF", [128, 2, TS], F32)
    rsM = sb("rsM", [128, 2, TS], F32)
    tabR = sb("tabR", [128, 16, 128], BF16)
    tabI = sb("tabI", [128, 16, 128], BF16)
    Wm = sb("Wm", [128, 32, 128], BF16)
    Mm = sb("Mm", [128, 32, 128], BF16)
    cosT = sb("cosT", [128, 16, NB], F32)
    sinT = sb("sinT", [128, 16, NB], F32)
    r8 = sb("r8", [128, 16], F32)
    sel = sb("sel", [128, 8, 240], BF16)
    dblk = sb("dblk", [128, 32], F32)
    gains = sb("gains", [128, 44], F32)
    esink = sb("esink", [128, 4], F32)
    cst = sb("cst", [128, 4], F32)
    identf = sb("identf", [128, 128], F32)
    identb = sb("identb", [128, 128], BF16)
    onesb = sb("onesb", [128, 128], BF16)
    permb = sb("permb", [128, 128], BF16)
    mask = sb("mask", [128, 512], BF16)
    maskf = sb("maskf", [128, 512], BF16)
    kTd = sb("kTd", [128, 2, 5 * 128], BF16)
    vsb = sb("vsb", [128, 5, 128], BF16)
    PT = sb("PT", [128, 2, 512], BF16)
    ropec = sb("ropec", [128, TS], F32)
    ropes = sb("ropes", [128, TS], F32)
    qpre = sb("qpre", [128, 1, TS], BF16)
    tmpf = sb("tmpf", [128, 3, TS], F32)
    Gin = sb("Gin", [128, 2, 4, NB], F32)
    Gs = sb("Gs", [128, 2, 4, NB], F32)
    Hb = sb("Hb", [128, 16, 2, NB + 1], BF16)
    Hc = sb("Hc", [128, 16, 2], F32)

    ps = [stack.enter_context(nc.psum_tensor("ps%d" % i, [128, 512], F32)) for i in range(8)]

    rrM = [0]
    rrF = [0]

    def bankM():
        b = rrM[0] % 3
        rrM[0] += 1
        return b

    def bankF():
        b = 4 + rrF[0] % 4
        rrF[0] += 1
        return b

    bankA = bankM
    BATT = 3

    def mm(out, lhsT, rhs, start, stop, reads, writes):
        P.add("pe", lambda e: e.matmul(out, lhsT=lhsT, rhs=rhs, start=start, stop=stop), reads, writes)

    def tr(out, in_, reads, writes):
        P.add("pe", lambda e: e.transpose(out, in_, identf[:]), reads + ["identf"], writes)

    def actf(out, in_, func, reads, writes, bias=None, scale=None):
        kw = {}
        if bias is not None:
            kw["bias"] = bias
        if scale is not None:
            kw["scale"] = scale
        P.add("act", lambda e: e.activation(out=out, in_=in_, func=func, **kw), reads, writes)

    def tt(out, in0, in1, op, reads, writes, eng="dve"):
        P.add(eng, lambda e: e.tensor_tensor(out=out, in0=in0, in1=in1, op=op), reads, writes)

    def ts1(out, in0, s1, op0, reads, writes, eng="dve"):
        P.add(eng, lambda e: e.tensor_single_scalar(out=out, in_=in0, scalar=s1, op=op0), reads, writes)

    def stt(out, in0, scalar, in1, op0, op1, reads, writes):
        P.add("dve", lambda e: e.scalar_tensor_tensor(out=out, in0=in0, scalar=scalar, in1=in1, op0=op0, op1=op1), reads, writes)

    def cp(out, in_, reads, writes, eng="dve"):
        if eng == "act":
            P.add("act", lambda e: e.activation(out=out, in_=in_, func=AF.Copy), reads, writes)
        else:
            P.add(eng, lambda e: e.tensor_copy(out=out, in_=in_), reads, writes)

    def recip(out, in_, reads, writes):
        P.add("dve", lambda e: e.reciprocal(out=out, in_=in_), reads, writes)

    def memset(ap, val, writes, eng="dve"):
        P.add(eng, lambda e: e.memset(ap, val), [], writes)

    def dma(q, out, in_, reads, writes, key):
        P.add(q, lambda e: e.dma_start(out=out, in_=in_), reads, writes, dma=key)

    def dump(name, ap, reads, shape, dt=F32):
        if not DEBUG:
            return
        if name not in dbg_d:
            dbg_d[name] = nc.dram_tensor("dbg_" + name, list(shape), dt, kind="ExternalOutput").ap()
        dma("sp", dbg_d[name], ap, reads, [("dbg", name)], "dbg_" + name)

    def ld(q, t, src, key):
        dma(q, t[:], src, [], [key], "ld_" + key)

    ld("sp", identf, dr["identf"], "identf"); ld("sp", identb, dr["identb"], "identb")
    ld("sp", onesb, dr["onesb"], "onesb"); ld("sp", permb, dr["permb"], "permb")
    ld("sp", mask, dr["mask"], "mask"); ld("sp", maskf, dr["maskf"], "maskf")
    ld("sp", sel, dr["sel"], "sel"); ld("sp", gains, dr["gains"], "gains")
    ld("sp", dblk, dr["dblk"], "dblk"); ld("sp", esink, dr["sinkrow"], "esink")
    memset(cst[:, 0:1], EPS, ["cst"])
    memset(cst[:, 1:2], math.pi / 2, ["cst"])
    memset(cst[:, 2:3], 4.0 * EPS, ["cst"])
    memset(kTd[:], 0.0, ["kTd"])
    memset(vsb[:], 0.0, ["vsb"])
    memset(Hc[:], 0.0, ["Hc"])
    memset(Hb[:], 0.0, ["Hb"])
    actf(esink[:], esink[:], AF.Exp, ["esink"], ["esink"])
    hbg = sb("hbg", [128, 4], F32)
    ts1(hbg[:], gains[:, 40:44], 0.5, ALU.mult, ["gains"], ["hbg"])

    scr_gu = {fid: nc.dram_tensor("scr_gu%d" % fid, [2 * FT, 128, 2 * 4 * 128], BF16).ap() for fid in (1, 2)}
    scr_d = {fid: nc.dram_tensor("scr_d%d" % fid, [22, 128, 2 * 512], BF16).ap() for fid in (1, 2)}
    scr_wo = nc.dram_tensor("scr_wo", [8, 128, KT * 128], BF16).ap()
    scr_wi = nc.dram_tensor("scr_wi", [11, 128, KT * 128], BF16).ap()

    ffn_order = [1]
    for s in range(NT_RUN - 1):
        ffn_order += [1, 2]
    ffn_order.append(2)
    ffn_w = {1: ("wg1", "wu1", "wd1"), 2: ("wg2", "wu2", "wd2")}
    wgu_loads = [(fid, f, kh) for fid in ffn_order for f in range(FT) for kh in range(2)]
    wd_loads = [(fid, half, jc) for fid in ffn_order for half in range(2) for jc in range(11)]
    wgu_ptr = [0]
    wd_ptr = [0]
    seen_gu = set()
    seen_d = set()
    multi = NT_RUN > 1

    NCV = 16
    cvi = [0]

    def conv(out_ap, in_ap, key):
        i = cvi[0] % NCV
        cvi[0] += 1
        dma("pool", out_ap, in_ap, [], [key, ("cvslot", i)], "cv%d" % i)

    def conv_gu(fid):
        gname, uname, _ = ffn_w[fid]
        for f in range(FT):
            for kh in range(2):
                scr = scr_gu[fid][2 * f + kh].rearrange("p (g k c) -> p g k c", g=2, k=4)
                for gi, nm in enumerate((gname, uname)):
                    src = dr[nm].rearrange("(k p) f -> p k f", p=128)[:, 4 * kh:4 * kh + 4, f * 128:(f + 1) * 128]
                    conv(scr[:, gi], src, ("scr_gu", fid, f, kh, gi))

    def conv_d(fid):
        for half in range(2):
            for jc in range(11):
                scr = scr_d[fid][half * 11 + jc].rearrange("p (f d) -> p f d", f=2)
                src = dr[ffn_w[fid][2]].rearrange("(f p) d -> p f d", p=128)[:, 2 * jc:2 * jc + 2, half * 512:(half + 1) * 512]
                conv(scr, src, ("scr_d", fid, half, jc))

    P.tag = 'prep'
    dma("pool", w_glu[:], dr["wglu"].rearrange("(k p) f -> p k f", p=128), [], ["w_glu"], "ld_w_glu")
    conv_gu(1)
    conv_d(1)
    for cg in range(11):
        conv(scr_wi[cg].rearrange("p (k c) -> p k c", k=KT), dr["win"].rearrange("(k p) f -> p k f", p=128)[:, :, cg * 128:(cg + 1) * 128], ("scr_wi", cg))
    for o in range(8):
        conv(scr_wo[o].rearrange("p (k c) -> p k c", k=KT), dr["wout"].rearrange("(k p) d -> p k d", p=128)[:, :, o * 128:(o + 1) * 128], ("scr_wo", o))
    conv_gu(2)
    conv_d(2)

    def wgu_ensure(upto):
        while wgu_ptr[0] <= upto and wgu_ptr[0] < len(wgu_loads):
            L = wgu_ptr[0]
            fid, f, kh = wgu_loads[L]
            slot = L % 4
            scr = scr_gu[fid][2 * f + kh].rearrange("p (g k c) -> p g k c", g=2, k=4)
            dma("sp", wgu[:, slot], scr, [("scr_gu", fid, f, kh, 0), ("scr_gu", fid, f, kh, 1)], [("wgu", slot, 0), ("wgu", slot, 1)], "wgu%d" % slot)
            wgu_ptr[0] += 1

    def wd_ensure(upto):
        while wd_ptr[0] <= upto and wd_ptr[0] < len(wd_loads):
            L = wd_ptr[0]
            fid, half, jc = wd_loads[L]
            slot = L % 4
            scr = scr_d[fid][half * 11 + jc].rearrange("p (f d) -> p f d", f=2)
            dma("sp", wd[:, slot], scr, [("scr_d", fid, half, jc)], [("wd", slot)], "wd%d" % slot)
            wd_ptr[0] += 1

    wgu_use = [0]
    wd_use = [0]

    wi_ptr = [0]
    wi_use = [0]
    wo_ptr = [0]
    wo_use = [0]

    def wi_ensure(upto):
        while wi_ptr[0] <= upto and wi_ptr[0] < 11 * NT_RUN:
            L = wi_ptr[0]
            cg = L % 11
            slot = L % 4
            dma("sp", wis[:, slot], scr_wi[cg].rearrange("p (k c) -> p k c", k=KT), [("scr_wi", cg)], [("wis", slot)], "wis%d" % slot)
            wi_ptr[0] += 1

    def wo_ensure(upto):
        while wo_ptr[0] <= upto and wo_ptr[0] < 8 * NT_RUN:
            L = wo_ptr[0]
            o = L % 8
            slot = L % 4
            dma("sp", wo[:, slot], scr_wo[o].rearrange("p (k c) -> p k c", k=KT), [("scr_wo", o)], [("wo", slot)], "wo%d" % slot)
            wo_ptr[0] += 1

    def ssm_precompute():
        P.tag = 'pre'
        are = sb("p_are", [128, 16], F32); aim = sb("p_aim", [128, 16], F32); ldt = sb("p_ldt", [128, 16], F32)
        bre = sb("p_bre", [128, 16, 16], F32); bim = sb("p_bim", [128, 16, 16], F32)
        cre = sb("p_cre", [128, 16, 16], F32); cim = sb("p_cim", [128, 16, 16], F32)
        sm = sb("p_sm", [128, 24, 16], F32)
        pw = sb("p_pw", [128, 2, 16, 9], F32)
        bb = sb("p_bb", [128, 2, 16, 16], F32)
        bbp = sb("p_bbp", [128, 2, 2, 240], BF16)
        wtt = sb("p_wt", [128, 2, 2, 8, 16], F32)
        big = sb("p_big", [128, 2, 16, 64], F32)
        for nm, t in (("are", are), ("aim", aim), ("ldt", ldt), ("bre", bre), ("bim", bim), ("cre", cre), ("cim", cim)):
            dma("sp", t[:], dr[nm], [], ["p_" + nm], "ld_p_" + nm)
        S = lambda i: sm[:, i, :]
        K = "p_sm"
        rk = ["p_are", "p_aim", "p_ldt", K, "cst"]
        DT, AR, TH, MAG, C0, S0, CC, SS, CS, LBR, LBI, DEN, NRE, T1, T2, ZR, ZI, C8, S8, T3 = range(20)
        actf(S(DT), ldt[:], AF.Exp, rk, [K])
        tt(S(AR), are[:], S(DT), ALU.mult, rk, [K])
        tt(S(TH), aim[:], S(DT), ALU.mult, rk, [K])
        actf(S(MAG), S(AR), AF.Exp, rk, [K])
        actf(r8[:], S(AR), AF.Exp, rk, ["r8"], scale=8.0)
        actf(S(S0), S(TH), AF.Sin, rk, [K], scale=1.0 / 16)
        actf(S(C0), S(TH), AF.Sin, rk, [K], scale=1.0 / 16, bias=cst[:, 1:2])
        yield

        def dbl():
            tt(S(CC), S(C0), S(C0), ALU.mult, rk, [K])
            tt(S(SS), S(S0), S(S0), ALU.mult, rk, [K])
            tt(S(CS), S(C0), S(S0), ALU.mult, rk, [K])
            tt(S(C0), S(CC), S(SS), ALU.subtract, rk, [K])
            ts1(S(S0), S(CS), 2.0, ALU.mult, rk, [K])
        for _ in range(4):
            dbl()
            yield
        tt(S(LBR), S(MAG), S(C0), ALU.mult, rk, [K])
        tt(S(LBI), S(MAG), S(S0), ALU.mult, rk, [K])
        for _ in range(3):
            dbl()
            yield
        cp(S(C8), S(C0), rk, [K]); cp(S(S8), S(S0), rk, [K])
        tt(S(T1), are[:], are[:], ALU.mult, rk, [K])
        tt(S(T2), aim[:], aim[:], ALU.mult, rk, [K])
        tt(S(DEN), S(T1), S(T2), ALU.add, rk, [K])
        recip(S(DEN), S(DEN), rk, [K])
        ts1(S(NRE), S(LBR), -1.0, ALU.add, rk, [K])
        tt(S(T1), S(NRE), are[:], ALU.mult, rk, [K])
        tt(S(T2), S(LBI), aim[:], ALU.mult, rk, [K])
        tt(S(T1), S(T1), S(T2), ALU.add, rk, [K])
        tt(S(ZR), S(T1), S(DEN), ALU.mult, rk, [K])
        tt(S(T1), S(LBI), are[:], ALU.mult, rk, [K])
        tt(S(T2), S(NRE), aim[:], ALU.mult, rk, [K])
        tt(S(T1), S(T1), S(T2), ALU.subtract, rk, [K])
        tt(S(ZI), S(T1), S(DEN), ALU.mult, rk, [K])
        yield
        kp = ["p_pw", K]
        memset(pw[:, 0, :, 0:1], 1.0, ["p_pw"]); memset(pw[:, 1, :, 0:1], 0.0, ["p_pw"])
        for k in range(1, 9):
            pr, pi_ = pw[:, 0, :, k - 1], pw[:, 1, :, k - 1]
            tt(S(T1), pr, S(LBR), ALU.mult, kp, [K]); tt(S(T2), pi_, S(LBI), ALU.mult, kp, [K])
            tt(pw[:, 0, :, k], S(T1), S(T2), ALU.subtract, kp, ["p_pw"])
            tt(S(T1), pr, S(LBI), ALU.mult, kp, [K]); tt(S(T2), pi_, S(LBR), ALU.mult, kp, [K])
            tt(pw[:, 1, :, k], S(T1), S(T2), ALU.add, kp, ["p_pw"])
            yield
        bc16 = lambda ap2: ap2.unsqueeze(2).to_broadcast([128, 16, 16])
        kb = ["p_bre", "p_bim", K, "p_bb", "p_big"]
        t1 = big[:, 0, :, 0:16]; t2 = big[:, 1, :, 0:16]
        tt(t1, bre[:], bc16(S(ZR)), ALU.mult, kb, ["p_big"]); tt(t2, bim[:], bc16(S(ZI)), ALU.mult, kb, ["p_big"])
        tt(bb[:, 0], t1, t2, ALU.subtract, kb, ["p_bb"])
        tt(t1, bim[:], bc16(S(ZR)), ALU.mult, kb, ["p_big"]); tt(t2, bre[:], bc16(S(ZI)), ALU.mult, kb, ["p_big"])
        tt(bb[:, 1], t1, t2, ALU.add, kb, ["p_bb"])
        memset(bbp[:], 0.0, [("p_bbp", 0), ("p_bbp", 1)])
        yield
        kc = ["p_cre", "p_cim", "p_pw", "p_big", "p_wt"]
        tabRe = sb("p_tabRe", [128, 16, 256], BF16); tabIm = sb("p_tabIm", [128, 16, 256], BF16)
        memset(tabRe[:], 0.0, ["tabRe"]); memset(tabIm[:], 0.0, ["tabIm"])
        for tau in range(9):
            pr = pw[:, 0, :, tau:tau + 1].to_broadcast([128, 16, 16])
            pi_ = pw[:, 1, :, tau:tau + 1].to_broadcast([128, 16, 16])
            o_re = tabRe[:, :, 112 + tau * 16:112 + (tau + 1) * 16]; o_im = tabIm[:, :, 112 + tau * 16:112 + (tau + 1) * 16]
            ta = wtt[:, 0].rearrange("p r i c -> p (r i) c")
            tb = wtt[:, 1].rearrange("p r i c -> p (r i) c")
            tt(ta, cre[:], pr, ALU.mult, kc, ["p_wt"]); tt(tb, cim[:], pi_, ALU.mult, kc, ["p_wt"])
            tt(o_re, ta, tb, ALU.subtract, kc, ["tabRe"])
            tt(ta, cre[:], pi_, ALU.mult, kc, ["p_wt"]); tt(tb, cim[:], pr, ALU.mult, kc, ["p_wt"])
            tt(tb, ta, tb, ALU.add, kc, ["p_wt"])
            ts1(o_im, tb, -1.0, ALU.mult, kc, ["tabIm"])
            yield
        yield "PE_PART"
        pwr = sb("p_pwr", [128, 3, 16, 8], F32)
        for i in range(8):
            cp(pwr[:, 0, :, i:i + 1], pw[:, 0, :, 7 - i:8 - i], ["p_pw"], ["p_pwr"])
            cp(pwr[:, 1, :, i:i + 1], pw[:, 1, :, 7 - i:8 - i], ["p_pw"], ["p_pwr"])
        ts1(pwr[:, 2], pwr[:, 1], -1.0, ALU.mult, ["p_pwr"], ["p_pwr"])
        kw_ = ["p_pwr", "p_bb", "p_wt", "p_big"]
        for t in range(16):
            buf = t % 2
            wre = wtt[:, buf, 0]; wim = wtt[:, buf, 1]
            pr = pwr[:, 0, t, :].unsqueeze(2).to_broadcast([128, 8, 16])
            pi_ = pwr[:, 1, t, :].unsqueeze(2).to_broadcast([128, 8, 16])
            br = bb[:, 0, t, :].unsqueeze(1).to_broadcast([128, 8, 16])
            bi = bb[:, 1, t, :].unsqueeze(1).to_broadcast([128, 8, 16])
            x1 = big[:, 0, 0:8, 0:16]; x2 = big[:, 1, 0:8, 0:16]
            tt(x1, pr, br, ALU.mult, kw_, ["p_big"]); tt(x2, pi_, bi, ALU.mult, kw_, ["p_big"])
            tt(wre, x1, x2, ALU.subtract, kw_, ["p_wt"])
            tt(x1, pr, bi, ALU.mult, kw_, ["p_big"]); tt(x2, pi_, br, ALU.mult, kw_, ["p_big"])
            tt(wim, x1, x2, ALU.add, kw_, ["p_wt"])
            b = bankA()
            tr(ps[b][:, 0:128], wre.rearrange("p i c -> p (i c)"), ["p_wt"], [("ps", b)])
            tr(ps[b][:, 128:256], wim.rearrange("p i c -> p (i c)"), ["p_wt"], [("ps", b)])
            src = ps[b][:, 0:256].rearrange("p (r g q) -> p g r q", r=2, g=2)
            dst = Wm[:, 2 * t:2 * t + 2, :].rearrange("p g (r q) -> p g r q", r=2)
            cp(dst, src, [("ps", b)], ["Wm"])
            yield
        for g in range(32):
            t, g2 = g // 2, g % 2
            buf = t % 2
            if g2 == 0:
                cp(bbp[:, buf, 0, 112:128], bb[:, 0, t, :], ["p_bb"], [("p_bbp", buf)])
                cp(bbp[:, buf, 1, 112:128], bb[:, 1, t, :], ["p_bb"], [("p_bbp", buf)])
            rows = slice(g2 * 64, (g2 + 1) * 64)
            b = bankA()
            n = 0
            for i in range(8):
                off = (7 - i) * 16
                for ri, tab in ((0, tabRe), (1, tabIm)):
                    mm(ps[b][:, 0:128], bbp[rows, buf, ri, off:off + 128], tab[rows, t, off:off + 128],
                       n == 0, n == 15, [("p_bbp", buf), "tabRe", "tabIm"], [("ps", b)])
                    n += 1
            cp(Mm[:, g, :], ps[b][:, 0:128], [("ps", b)], ["Mm"], eng="act")
            yield
        cp(tabR[:], tabRe[:, :, 128:256], ["tabRe"], ["tabR"])
        cp(tabI[:], tabIm[:, :, 128:256], ["tabIm"], ["tabI"])
        kt = ["cosT", "sinT", K, "p_wt", "p_big"]
        cp(cosT[:, :, 0:1], S(C8).unsqueeze(2), kt, ["cosT"]); cp(sinT[:, :, 0:1], S(S8).unsqueeze(2), kt, ["sinT"])
        m = 1
        while m < NB:
            cr = cosT[:, :, m - 1:m].to_broadcast([128, 16, m]); sr = sinT[:, :, m - 1:m].to_broadcast([128, 16, m])
            a1 = big[:, 0, :, 0:m]; a2 = big[:, 1, :, 0:m]; a3 = big[:, 0, :, 32:32 + m]; a4 = big[:, 1, :, 32:32 + m]
            tt(a1, cosT[:, :, 0:m], cr, ALU.mult, kt, ["p_big"]); tt(a2, sinT[:, :, 0:m], sr, ALU.mult, kt, ["p_big"])
            tt(a3, cosT[:, :, 0:m], sr, ALU.mult, kt, ["p_big"]); tt(a4, sinT[:, :, 0:m], cr, ALU.mult, kt, ["p_big"])
            tt(cosT[:, :, m:2 * m], a1, a2, ALU.subtract, kt, ["cosT"])
            tt(sinT[:, :, m:2 * m], a3, a4, ALU.add, kt, ["sinT"])
            m *= 2
            yield

    pstack = ExitStack()

    def sbp(name, shape, dt):
        return pstack.enter_context(nc.sbuf_tensor("s_" + name, list(shape), dt))

    xTa = sb("xTa", [128, KT, TS], F32)
    hTF = sb("hTF", [128, KT, TS], BF16)
    act = sb("act", [128, FT, TS], BF16)
    xsel = lambda par: xTa if par == 0 else xTb

    def xT(par, k):
        return xsel(par)[:, k, :]

    def xk(par, k):
        return ("xT", par, k)

    def rstd_from(b, n, dst, dkey, ec=0):
        actf(dst, ps[b][:, :], AF.Sqrt, [("ps", b), "cst"], [dkey], bias=cst[:, ec:ec + 1], scale=1.0 / n)
        recip(dst, dst, [dkey], [dkey])

    def norm_to(par, goff, hT, hkey, sqs, rdst, rkey, b):
        for k in range(KT):
            sap, skey = sqs[k % len(sqs)]
            actf(sap, xT(par, k), AF.Square, [xk(par, k)], [skey])
            mm(ps[b][:, :], onesb[:], sap, k == 0, k == KT - 1, [skey, "onesb"], [("ps", b)])
        rstd_from(b, D, rdst, rkey)
        for k in range(KT):
            stt(hT[:, k, :], xT(par, k), gains[:, goff + k:goff + k + 1], rdst, ALU.mult, ALU.mult,
                [xk(par, k), "gains", rkey], [(hkey, k)])

    def ffn_norm_gen(goff, par):
        P.tag = 'ffn.norm'
        norm_to(par, goff, hTF, "hTF", [(sqF[:, 0, :], ("sqF", 0)), (sqF[:, 1, :], ("sqF", 1))], rsF[:, 0, :], ("rsF", 0), 4)
        yield

    def ffn_gen(fid, goff, par, do_norm=True):
        if do_norm:
            P.tag = 'ffn.norm'
            norm_to(par, goff, hTF, "hTF", [(sqF[:, 0, :], ("sqF", 0)), (sqF[:, 1, :], ("sqF", 1))], rsF[:, 0, :], ("rsF", 0), 4)
            yield
        for f in range(FT):
            P.tag = 'ffn.gu'
            bg, bu = (4, 5) if f % 2 == 0 else (6, 7)
            for kh in range(2):
                L = wgu_use[0]
                wgu_use[0] += 1
                slot = L % 4
                wgu_ensure(L + 3)
                for k4 in range(4):
                    k = 4 * kh + k4
                    mm(ps[bg][:, :], wgu[:, slot, 0, k4, :], hTF[:, k, :], k == 0, k == KT - 1,
                       [("wgu", slot, 0), ("hTF", k)], [("ps", bg)])
                for k4 in range(4):
                    k = 4 * kh + k4
                    mm(ps[bu][:, :], wgu[:, slot, 1, k4, :], hTF[:, k, :], k == 0, k == KT - 1,
                       [("wgu", slot, 1), ("hTF", k)], [("ps", bu)])
            s_ = f % 2
            actf(sgs[:, s_, :], ps[bg][:, :], AF.Tanh, [("ps", bg)], [("sgs", s_)], scale=0.5)
            stt(sgs[:, s_, :], sgs[:, s_, :], 1.0, ps[bg][:, :], ALU.add, ALU.mult, [("sgs", s_), ("ps", bg)], [("sgs", s_)])
            tt(act[:, f, :], sgs[:, s_, :], ps[bu][:, :], ALU.mult, [("sgs", s_), ("ps", bu)], [("act", f)])
            yield
        if wd_ptr[0] == 0:
            wd_ensure(2)
        for half in range(2):
            bs = [4, 5, 6, 7]
            for jc in range(11):
                P.tag = 'ffn.down'
                L = wd_use[0]
                wd_use[0] += 1
                slot = L % 4
                wd_ensure(L + 3)
                for f2 in range(2):
                    f = 2 * jc + f2
                    for o in range(4):
                        mm(ps[bs[o]][:, :], wd[:, slot, f2, o * 128:(o + 1) * 128], act[:, f, :], f == 0, f == FT - 1,
                           [("wd", slot), ("act", f)], [("ps", bs[o])])
                if jc == 10:
                    for o in range(4):
                        k = half * 4 + o
                        stt(xT(par, k), ps[bs[o]][:, :], 0.25, xT(par, k), ALU.mult, ALU.add, [("ps", bs[o]), xk(par, k)], [xk(par, k)])
                yield

    def loadx_gen(s):
        par = s % 2
        P.tag = 'loadx'
        for k in range(KT):
            dma("sp", xsel(par)[:, k, :], x_d[k * 128:(k + 1) * 128, s * TS:(s + 1) * TS], [], [xk(par, k)], "xl%d" % k)
        yield

    def xn_tile(k):
        return act[:, 2 * k:2 * k + 2, :].rearrange("p a t -> p (a t)").bitcast(F32)

    def final_gen(s):
        P.tag = 'final'
        par = s % 2
        b = 4
        for k in range(KT):
            s_ = k % 2
            actf(sqF[:, s_, :], xT(par, k), AF.Square, [xk(par, k)], [("sqF", s_)])
            mm(ps[b][:, :], onesb[:], sqF[:, s_, :], k == 0, k == KT - 1, [("sqF", s_), "onesb"], [("ps", b)])
        rstd_from(b, D, rsF[:, 1, :], ("rsF", 1))
        yield
        for k in range(KT):
            P.tag = 'final'
            stt(xn_tile(k), xT(par, k), gains[:, 24 + k:25 + k], rsF[:, 1, :], ALU.mult, ALU.mult,
                [xk(par, k), "gains", ("rsF", 1)], [("act", 2 * k), ("act", 2 * k + 1)])
            dma("sp", out_d[k * 128:(k + 1) * 128, s * TS:(s + 1) * TS], xn_tile(k),
                [("act", 2 * k), ("act", 2 * k + 1)], [("out", s, k)], "out%d" % k)
        yield

    uTd = lambda m: am[:, m, :].rearrange("p (i n) -> p i n", i=8)
    qT = lambda m: am[:, 4 + m, :]
    Ugrp = lambda g: am[:, 8 + g // 8, (g % 8) * NB:(g % 8 + 1) * NB]
    Y2grp = lambda g: am[:, g // 8, (g % 8) * NB:(g % 8 + 1) * NB]
    y2T = lambda m: am[:, 12 + m, :]
    yg = lambda m: (am[:, 4 + m, :], ("am", 4 + m))
    yat = lambda m: (am[:, 16 + m, :], ("am", 16 + m))

    def rope_finish(qs, out_ap, out_key):
        qap, qkey = qs
        b2 = bankM()
        mm(ps[b2][:, :], permb[:], qap, True, True, ["permb", qkey], [("ps", b2)])
        tt(tmpf[:, 0, :], qap, ropec[:], ALU.mult, [qkey, "ropec"], [("tmpf", 0), ("tmpf", "0b")])
        tt(tmpf[:, 1, :], ps[b2][:, :], ropes[:], ALU.mult, [("ps", b2), "ropes"], [("tmpf", 1)])
        tt(out_ap, tmpf[:, 0, :], tmpf[:, 1, :], ALU.add, [("tmpf", 0), ("tmpf", 1)], [out_key])

    def mixer_gen(s):
        par = s % 2
        P.tag = 'mix.norm'
        if wi_ptr[0] == 0:
            wi_ensure(2)
            wo_ensure(2)
        dma("sp", ropec[:], dr["ropec"][:, s * TS:(s + 1) * TS], [], ["ropec"], "ropec")
        dma("sp", ropes[:], dr["ropes"][:, s * TS:(s + 1) * TS], [], ["ropes"], "ropes")
        norm_to(par, 8, hTM, "hTM", [(sqM[:, 0, :], ("sqM", 0)), (qpre[:, 0, :], ("qpre", 0))], rsM[:, 0, :], ("rsM", 0), bankM())
        yield

        def proj_group():
            L = wi_use[0]
            wi_use[0] += 1
            slot = L % 4
            wi_ensure(L + 3)
            b = bankM()
            for k in range(KT):
                mm(ps[b][:, :], wis[:, slot, k, :], hTM[:, k, :], k == 0, k == KT - 1, [("wis", slot), ("hTM", k)], [("ps", b)])
            return b

        for m in range(4):
            P.tag = 'mix.proj'
            b = proj_group()
            actf(uTd(m), ps[b][:, :].rearrange("p (n i) -> p i n", i=8), AF.Copy, [("ps", b)], [("am", m)])
            yield
        jobs = [(qT(m), ("am", 4 + m)) for m in range(4)]
        jobs += [(kTd[:, kk, 128:640], ("kTd", kk)) for kk in range(2)]
        QS = [(qpre[:, 0, :], ("qpre", 0)), (sqM[:, 0, :], ("sqM", 0))]
        pending = None
        for idx, (out_ap, out_key) in enumerate(jobs):
            P.tag = 'mix.proj'
            b = proj_group()
            qs = QS[idx % 2]
            actf(qs[0], ps[b][:, :], AF.Copy, [("ps", b)], [qs[1]])
            if pending is not None:
                rope_finish(*pending)
            pending = (qs, out_ap, out_key)
            yield
        P.tag = 'mix.proj'
        L = wi_use[0]
        wi_use[0] += 1
        slot = L % 4
        wi_ensure(L + 3)
        b = bankM()
        for blk in range(4):
            for k in range(KT):
                mm(ps[b][:, blk * 128:(blk + 1) * 128], hTM[:, k, blk * 128:(blk + 1) * 128], wis[:, slot, k, :],
                   k == 0, k == KT - 1, [("wis", slot), ("hTM", k)], [("ps", b)])
        rope_finish(*pending)
        actf(vsb[:, 1:5, :], ps[b][:, :].rearrange("p (a d) -> p a d", a=4), AF.Copy, [("ps", b)], ["vsb"])
        yield

        def attn_gen():
            ptc = 0
            units = [(m, pair) for m in range(4) for pair in range(2)]

            def a1(m, pair, hh, pi_):
                kv = m // 2
                rows = slice(hh * 64, (hh + 1) * 64)
                bs_ = bankM()
                first = (s == 0 and pair == 0)
                mk, mkk = (maskf, "maskf") if first else (mask, "mask")
                mm(ps[bs_][:, :], identb[:], mk[:], True, False, ["identb", mkk], [("ps", bs_)])
                n = 0
                for qb in range(2):
                    blkq = 2 * pair + qb
                    for piece in range(2):
                        slot = blkq + piece
                        mm(ps[bs_][:, (qb * 2 + piece) * 128:(qb * 2 + piece + 1) * 128],
                           kTd[rows, kv, slot * 128:(slot + 1) * 128], qT(m)[rows, blkq * 128:(blkq + 1) * 128],
                           False, n == 3, [("kTd", kv), ("am", 4 + m)], [("ps", bs_)])
                        n += 1
                actf(PT[:, pi_, :], ps[bs_][:, :], AF.Exp, [("ps", bs_)], [("PT", pi_)], scale=0.125)

            def pv(m, pair, hh, pi_):
                kv = m // 2
                rows = slice(hh * 64, (hh + 1) * 64)
                for qb in range(2):
                    blkq = 2 * pair + qb
                    ncol = slice(qb * 128, (qb + 1) * 128)
                    dcol = slice(256 + qb * 128, 256 + (qb + 1) * 128)
                    for piece in range(2):
                        slot = blkq + piece
                        mm(ps[BATT][rows, ncol], vsb[:, slot, kv * 64:(kv + 1) * 64], PT[:, pi_, (qb * 2 + piece) * 128:(qb * 2 + piece + 1) * 128],
                           piece == 0, piece == 1, ["vsb", ("PT", pi_)], [("ps", BATT)])
                    for piece in range(2):
                        mm(ps[BATT][rows, dcol], onesb[:, 0:64], PT[:, pi_, (qb * 2 + piece) * 128:(qb * 2 + piece + 1) * 128],
                           piece == 0, piece == 1, ["onesb", ("PT", pi_)], [("ps", BATT)])

            def norm_unit(m, pair):
                den = tmpf[:, 2, 0:256]
                ts1(den, ps[BATT][:, 256:512], esink[:, m:m + 1], ALU.add, [("ps", BATT), "esink"], [("tmpf", 2)])
                recip(den, den, [("tmpf", 2)], [("tmpf", 2)])
                ya, yak = yat(m)
                tt(ya[:, pair * 256:(pair + 1) * 256], ps[BATT][:, 0:256], den, ALU.mult, [("ps", BATT), ("tmpf", 2)], [yak])

            a1(0, 0, 0, 0); yield
            a1(0, 0, 1, 1); yield
            for ui, (m, pair) in enumerate(units):
                nxt = units[ui + 1] if ui + 1 < len(units) else None
                pv(m, pair, 0, 0); yield
                if nxt:
                    a1(nxt[0], nxt[1], 0, 0); yield
                pv(m, pair, 1, 1); yield
                norm_unit(m, pair)
                if nxt:
                    a1(nxt[0], nxt[1], 1, 1)
                yield
            b = bankM()
            for m in range(4):
                ya, yak = yat(m)
                actf(sqM[:, 0, :], ya, AF.Square, [yak], [("sqM", 0)])
                mm(ps[b][:, :], onesb[:], sqM[:, 0, :], m == 0, m == 3, [("sqM", 0), "onesb"], [("ps", b)])
            rstd_from(b, 512, rsM[:, 1, :], ("rsM", 1))
            yield

        def ssm_gen():
            cp(Hb[:, :, :, 0:1], Hc[:].unsqueeze(3), ["Hc"], ["Hb"])
            vbank = {}

            def U_unit(m):
                b = bankM()
                for gg in range(8):
                    for i in range(8):
                        mm(ps[b][:, gg * NB:(gg + 1) * NB], sel[:, gg, (7 - i) * 16:(7 - i) * 16 + 128], uTd(m)[:, i, :],
                           i == 0, i == 7, ["sel", ("am", m)], [("ps", b)])
                actf(am[:, 8 + m, :], ps[b][:, :], AF.Copy, [("ps", b)], [("am", 8 + m)])

            def V_unit(q4):
                b = bankM()
                vbank[q4] = b
                for tt_ in range(4):
                    t = 4 * q4 + tt_
                    for g2 in range(2):
                        g = 2 * t + g2
                        for ri in range(2):
                            c0 = (tt_ * 2 + ri) * NB
                            mm(ps[b][g2 * 64:(g2 + 1) * 64, c0:c0 + NB], Wm[:, g, ri * 64:(ri + 1) * 64], Ugrp(g), True, True,
                               ["Wm", ("am", 8 + g // 8)], [("ps", b)])

            def D_unit(q4):
                b = vbank[q4]
                V = ps[b][:, :].rearrange("p (t r n) -> p t r n", t=4, r=2)
                Vre, Vim = V[:, :, 0, :], V[:, :, 1, :]
                cs_ = cosT[:, 4 * q4:4 * q4 + 4, :]
                sn_ = sinT[:, 4 * q4:4 * q4 + 4, :]
                tA = tmpf[:, 0, 0:256].rearrange("p (t n) -> p t n", t=4)
                tB = tmpf[:, 0, 256:512].rearrange("p (t n) -> p t n", t=4)
                kA, kB = ("tmpf", 0), ("tmpf", "0b")
                gk = "Gin"
                tt(tA, Vre, cs_, ALU.mult, [("ps", b), "cosT"], [kA]); tt(tB, Vim, sn_, ALU.mult, [("ps", b), "sinT"], [kB])
                tt(Gin[:, 0], tA, tB, ALU.add, [kA, kB], [gk])
                tt(tA, Vim, cs_, ALU.mult, [("ps", b), "cosT"], [kA]); tt(tB, Vre, sn_, ALU.mult, [("ps", b), "sinT"], [kB])
                tt(Gin[:, 1], tA, tB, ALU.subtract, [kA, kB, gk], [gk])
                sk_ = "Gs"
                for tt_ in range(4):
                    t = 4 * q4 + tt_
                    for ri in range(2):
                        out_ap = Gs[:, ri, tt_, :]
                        d0 = r8[:, t:t + 1].to_broadcast([128, NB])
                        d1 = Gin[:, ri, tt_, :]
                        init = Hc[:, t, ri:ri + 1]
                        P.add("dve", (lambda o_=out_ap, a_=d0, b_=d1, i_=init: (lambda e: e.tensor_tensor_scan(
                            out=o_, data0=a_, data1=b_, initial=i_, op0=ALU.mult, op1=ALU.add)))(),
                            [gk, "r8", "Hc", sk_], [sk_])
                Gre, Gim = Gs[:, 0], Gs[:, 1]
                tt(tA, Gre, cs_, ALU.mult, [sk_, "cosT"], [kA]); tt(tB, Gim, sn_, ALU.mult, [sk_, "sinT"], [kB])
                tt(Gin[:, 0], tA, tB, ALU.subtract, [kA, kB, gk], [gk])
                tt(tA, Gre, sn_, ALU.mult, [sk_, "sinT"], [kA]); tt(tB, Gim, cs_, ALU.mult, [sk_, "cosT"], [kB])
                tt(Gin[:, 1], tA, tB, ALU.add, [kA, kB, gk], [gk])
                for ri in range(2):
                    actf(Hb[:, 4 * q4:4 * q4 + 4, ri, 1:NB + 1], Gin[:, ri], AF.Copy, [gk], [("Hb", q4)])
                    cp(Hc[:, 4 * q4:4 * q4 + 4, ri:ri + 1], Gin[:, ri, :, NB - 1:NB], [gk], ["Hc"])

            def Y_unit(m):
                b = bankM()
                for gg in range(8):
                    g = 8 * m + gg
                    t, g2 = g // 2, g % 2
                    rows = slice(g2 * 64, (g2 + 1) * 64)
                    o_ = ps[b][:, gg * NB:(gg + 1) * NB]
                    mm(o_, Mm[:, g, :], Ugrp(g), True, False, ["Mm", ("am", 8 + m)], [("ps", b)])
                    mm(o_, tabR[rows, t, :], Hb[rows, t, 0, 0:NB], False, False, ["tabR", "Hb", ("Hb", m)], [("ps", b)])
                    mm(o_, tabI[rows, t, :], Hb[rows, t, 1, 0:NB], False, True, ["tabI", "Hb", ("Hb", m)], [("ps", b)])
                U3 = am[:, 8 + m, :].rearrange("p (g n) -> p g n", g=8)
                Db = dblk[:, 8 * m:8 * m + 8].unsqueeze(2).to_broadcast([128, 8, NB])
                y1 = tmpf[:, 1, :]
                tt(y1.rearrange("p (g n) -> p g n", g=8), U3, Db, ALU.mult, [("am", 8 + m), "dblk"], [("tmpf", 1)])
                tt(y1, y1, ps[b][:, :], ALU.add, [("tmpf", 1), ("ps", b)], [("tmpf", 1)])
                actf(am[:, m, :], y1, AF.Gelu_apprx_tanh, [("tmpf", 1)], [("am", m)])

            def I_unit(m):
                b = bankM()
                for j in range(8):
                    for gg in range(8):
                        mm(ps[b][:, j * NB:(j + 1) * NB], sel[:, j, (7 - gg) * 16:(7 - gg) * 16 + 128], Y2grp(8 * m + gg),
                           gg == 0, gg == 7, ["sel", ("am", m)], [("ps", b)])
                actf(y2T(m).rearrange("p (n j) -> p j n", j=8), ps[b][:, :].rearrange("p (j n) -> p j n", j=8), AF.Copy,
                     [("ps", b)], [("am", 12 + m)])

            order = [(U_unit, 0), (U_unit, 1), (V_unit, 0), (D_unit, 0), (U_unit, 2), (V_unit, 1), (D_unit, 1), (U_unit, 3),
                     (V_unit, 2), (D_unit, 2), (Y_unit, 0), (V_unit, 3), (D_unit, 3), (Y_unit, 1), (I_unit, 0), (Y_unit, 2),
                     (I_unit, 1), (Y_unit, 3), (I_unit, 2), (I_unit, 3)]
            for fn_, arg in order:
                fn_(arg)
                yield
            yield "need_attn_done"
            for mo in range(4):
                b = bankM()
                for k in range(4):
                    mm(ps[b][:, :], w_glu[:, k, mo * 128:(mo + 1) * 128], y2T(k), k == 0, k == 3, ["w_glu", ("am", 12 + k)], [("ps", b)])
                sg = tmpf[:, 1, :]
                actf(sg, ps[b][:, :], AF.Tanh, [("ps", b), "hbg"], [("tmpf", 1)], bias=hbg[:, mo:mo + 1], scale=0.5)
                ygm, ygk = yg(mo)
                stt(ygm, sg, 1.0, y2T(mo), ALU.add, ALU.mult, [("am", 12 + mo), ("tmpf", 1)], [ygk])
                yield
            b = bankM()
            for mo in range(4):
                ygm, ygk = yg(mo)
                actf(sqM[:, 0, :], ygm, AF.Square, [ygk], [("sqM", 0)])
                mm(ps[b][:, :], onesb[:], sqM[:, 0, :], mo == 0, mo == 3, [("sqM", 0), "onesb"], [("ps", b)])
            rstd_from(b, 512, rsM[:, 0, :], ("rsM", 0), ec=2)
            yield

        ga, gs = attn_gen(), ssm_gen()
        a_done = s_done = s_wait = False
        while not (a_done and s_done):
            if not a_done:
                P.tag = 'mix.attn'
                try:
                    next(ga)
                except StopIteration:
                    a_done = True
                yield
            if not s_done and not (s_wait and not a_done):
                P.tag = 'mix.ssm'
                try:
                    if next(gs) == "need_attn_done":
                        s_wait = True
                except StopIteration:
                    s_done = True
                yield
        if s == 0:
            dump("yattn", am[:, 16:20, :], [("am", 16 + m) for m in range(4)], [128, 4, TS], BF16)
            dump("y2T", am[:, 12:16, :], [("am", 12 + m) for m in range(4)], [128, 4, TS], BF16)
        P.tag = 'mix.onorm'
        for k in range(4):
            ygm, ygk = yg(k)
            stt(hTM[:, k, :], ygm, gains[:, 32 + k:33 + k], rsM[:, 0, :], ALU.mult, ALU.mult, [ygk, "gains", ("rsM", 0)], [("hTM", k)])
        for k in range(4):
            ya, yak = yat(k)
            stt(hTM[:, 4 + k, :], ya, gains[:, 36 + k:37 + k], rsM[:, 1, :], ALU.mult, ALU.mult, [yak, "gains", ("rsM", 1)], [("hTM", 4 + k)])
        yield
        for o in range(8):
            P.tag = 'mix.wout'
            L = wo_use[0]
            wo_use[0] += 1
            slot = L % 4
            wo_ensure(L + 3)
            b = bankM()
            for k in range(KT):
                mm(ps[b][:, :], wo[:, slot, k, :], hTM[:, k, :], k == 0, k == KT - 1, [("wo", slot), ("hTM", k)], [("ps", b)])
            tt(xT(par, o), ps[b][:, :], xT(par, o), ALU.add, [("ps", b), xk(par, o)], [xk(par, o)])
            yield
        cp(kTd[:, :, 0:128], kTd[:, :, 512:640], [("kTd", 0), ("kTd", 1)], [("kTd", 0), ("kTd", 1)], eng="act")
        cp(vsb[:, 0, :], vsb[:, 4, :], ["vsb"], ["vsb"], eng="act")
        yield

    def drain(g):
        for _ in g:
            pass

    def chain(*gens):
        for g in gens:
            for _ in g:
                yield

    def interleave(ga, na, gb, nb):
        ca = cb = 0
        a_done = b_done = False
        while not (a_done and b_done):
            pick_a = (not a_done) and (b_done or ca * nb <= cb * na)
            if pick_a:
                try:
                    next(ga)
                    ca += 1
                except StopIteration:
                    a_done = True
            else:
                try:
                    next(gb)
                    cb += 1
                except StopIteration:
                    b_done = True

    P.tile = 0
    sb = sbp
    pg = ssm_precompute()
    next(pg)
    wgu_ensure(2)
    drain(chain(loadx_gen(0), ffn_norm_gen(0, 0)))
    fg = ffn_gen(1, 0, 0, do_norm=False)
    p_part1 = True
    f_live = True
    while p_part1 or f_live:
        if p_part1:
            P.tag = 'pre'
            if next(pg) == "PE_PART":
                p_part1 = False
        if f_live:
            try:
                next(fg)
            except StopIteration:
                f_live = False
    P.tag = 'pre'
    drain(pg)
    P.barrier()
    pstack.close()
    sb = sb_main
    xTb = sb("xTb", [128, KT, TS], F32)
    hTM = sb("hTM", [128, KT, TS], BF16)
    am = sb("am", [128, 20, TS], BF16)
    dump("x1", xTa[:], [xk(0, k) for k in range(KT)], [128, KT, TS])
    for s in range(NT_RUN):
        P.tile = s
        bparts = []
        nb = 0
        if s >= 1:
            bparts += [ffn_gen(2, 16, (s - 1) % 2), final_gen(s - 1)]
            nb += 50
        if s + 1 < NT_RUN:
            bparts += [loadx_gen(s + 1), ffn_norm_gen(0, (s + 1) % 2)]
            nb += 5
        if bparts:
            interleave(mixer_gen(s), 95, chain(*bparts), nb)
        else:
            drain(mixer_gen(s))
        if s == 0:
            dump("x2", xTa[:], [xk(0, k) for k in range(KT)], [128, KT, TS])
        if s + 1 < NT_RUN:
            drain(ffn_gen(1, 0, (s + 1) % 2, do_norm=False))
    drain(chain(ffn_gen(2, 16, (NT_RUN - 1) % 2), final_gen(NT_RUN - 1)))
    P.add("sp", None, [("out", s, k) for s in range(NT_RUN) for k in range(KT)] + [("dbg", n) for n in dbg_d], [])

    P.finalize(nc, stack)
    with nc.Block() as block:
        @block.sync
        def _(e):
            P.emit("sp", e)

        @block.tensor
        def _(e):
            P.emit("pe", e)

        @block.scalar
        def _(e):
            P.emit("act", e)

        @block.vector
        def _(e):
            P.emit("dve", e)

        @block.gpsimd
        def _(e):
            P.emit("pool", e)
    stack.close()
    nc._prog = P
    return nc, list(dbg_d.keys())


_CACHE = {}


def kernel(**inputs):
    x = np.ascontiguousarray(np.asarray(inputs["x"], dtype=np.float32))
    B = x.shape[0]
    if "nc" not in _CACHE:
        _CACHE["nc"] = build_program()
    nc, dbg = _CACHE["nc"]
    shared = host_layout(inputs)
    shared.update(host_consts())
    in_maps = []
    for b in range(B):
        m = dict(shared)
        m["x"] = np.ascontiguousarray(x[b].T)
        in_maps.append(m)
    res = run_bass_kernel_spmd(nc, in_maps, core_ids=list(range(B)))
    out = np.stack([np.ascontiguousarray(np.asarray(r["out"], dtype=np.float32).T) for r in res.results], axis=0)
    if DEBUG:
        _CACHE["dbg"] = {n: np.asarray(res.results[0]["dbg_" + n]) for n in dbg}
    return out
```

```python
import math
import os
from contextlib import ExitStack

import numpy as np
import ml_dtypes

import concourse.bass as bass
import concourse.mybir as mybir
from concourse.bass_utils import run_bass_kernel_spmd

F32 = mybir.dt.float32
BF16 = mybir.dt.bfloat16
AF = mybir.ActivationFunctionType
ALU = mybir.AluOpType

D = 1024
KT = 8
FF = 2816
FT = 22
SEQ = 4096
TS = 512
NTILE = SEQ // TS
NB = TS // 8
EPS = 1e-6
NEG = -30000.0
NWIN = 1408

DEBUG = bool(int(os.environ.get("KDBG", "0")))
NT_RUN = int(os.environ.get("KNT", str(NTILE)))


class Prog:
    ENGS = ("pe", "act", "dve", "pool", "sp")

    def __init__(self):
        self.ops = []
        self.lastw = {}
        self.readers = {}
        self.tag = ""
        self.tile = -1

    def add(self, eng, fn, reads=(), writes=(), dma=None):
        i = len(self.ops)
        deps = set()
        for k in reads:
            w = self.lastw.get(k)
            if w is not None:
                deps.add(w)
        for k in writes:
            w = self.lastw.get(k)
            if w is not None:
                deps.add(w)
            for r in self.readers.get(k, ()):
                deps.add(r)
        for k in reads:
            lst = self.readers.setdefault(k, [])
            if dma is None:
                lst[:] = [r for r in lst if not (self.ops[r]["eng"] == eng and self.ops[r]["dma"] is None)]
            lst.append(i)
        for k in writes:
            self.lastw[k] = i
            self.readers[k] = []
        deps.discard(i)
        self.ops.append(dict(eng=eng, fn=fn, deps=deps, dma=dma, signal=False, count=0, tag=self.tag, tile=self.tile))
        return i

    def barrier(self):
        last = {}
        for i, op in enumerate(self.ops):
            if op["dma"] is not None:
                if str(op["dma"]).startswith("cv"):
                    continue
                last[("d", op["dma"])] = i
            elif op["fn"] is not None:
                last[("e", op["eng"])] = i
        deps = set(last.values())
        for e in self.ENGS:
            self.ops.append(dict(eng=e, fn=None, deps=set(deps), dma=None, signal=False, count=0, tag='barrier', tile=-1))

    def finalize(self, nc, stack):
        ops = self.ops
        for op in ops:
            for d in op["deps"]:
                dop = ops[d]
                if dop["dma"] is None:
                    if dop["eng"] == "pe" and op["eng"] == "pe" and op["dma"] is None:
                        continue
                    dop["signal"] = True
        esem = {e: stack.enter_context(nc.semaphore("sem_" + e)) for e in self.ENGS}
        dsem = {}
        ecnt = {e: 0 for e in self.ENGS}
        dcnt = {}
        waited = {e: {} for e in self.ENGS}
        streams = {e: [] for e in self.ENGS}
        for op in ops:
            e = op["eng"]
            waits = {}
            for d in op["deps"]:
                dop = ops[d]
                if dop["dma"] is not None:
                    key = ("d", dop["dma"])
                    val = dop["count"]
                    sem = dsem[dop["dma"]]
                else:
                    if dop["eng"] == "pe" and e == "pe" and op["dma"] is None:
                        continue
                    key = ("e", dop["eng"])
                    val = dop["count"]
                    sem = esem[dop["eng"]]
                if waited[e].get(key, 0) >= val:
                    continue
                if key not in waits or waits[key][1] < val:
                    waits[key] = (sem, val)
            for key, (sem, val) in waits.items():
                waited[e][key] = val
            if op["dma"] is not None:
                if op["dma"] not in dsem:
                    dsem[op["dma"]] = stack.enter_context(nc.semaphore("dsem%d" % len(dsem)))
                    dcnt[op["dma"]] = 0
                dcnt[op["dma"]] += 16
                op["count"] = dcnt[op["dma"]]
                inc = (dsem[op["dma"]], 16)
            elif op["signal"]:
                ecnt[e] += 1
                op["count"] = ecnt[e]
                inc = (esem[e], 1)
            else:
                inc = None
            streams[e].append((list(waits.values()), op["fn"], inc))
        self.streams = streams
        self.nsem = len(dsem) + len(esem)

    def emit(self, eng_name, e):
        for waits, fn, inc in self.streams[eng_name]:
            for sem, val in waits:
                e.wait_ge(sem, val)
            if fn is None:
                continue
            ins = fn(e)
            if inc is not None:
                ins.then_inc(inc[0], inc[1])


def _bf(a):
    return np.ascontiguousarray(a.astype(ml_dtypes.bfloat16))


def host_consts():
    c = {}
    c["identf"] = np.eye(128, dtype=np.float32)
    c["identb"] = _bf(np.eye(128, dtype=np.float32))
    c["onesb"] = _bf(np.ones((128, 128), np.float32))
    perm = np.zeros((128, 128), np.float32)
    for h in range(2):
        for d in range(16):
            pd = d + 8 if d < 8 else d - 8
            perm[h * 64 + pd, h * 64 + d] = 1.0
    c["permb"] = _bf(perm)
    kj = np.arange(128)[:, None]
    qi = np.arange(128)[None, :]
    prev = np.where(kj > qi, 0.0, NEG).astype(np.float32)
    same = np.where(kj <= qi, 0.0, NEG).astype(np.float32)
    full = np.full((128, 128), NEG, np.float32)
    c["mask"] = _bf(np.concatenate([prev, same, prev, same], axis=1))
    c["maskf"] = _bf(np.concatenate([full, same, prev, same], axis=1))
    sel = np.zeros((128, 8, 240), np.float32)
    for g in range(8):
        for cc in range(16):
            sel[16 * g + cc, g, 112 + cc] = 1.0
    c["sel"] = _bf(sel)
    half = 8
    inv_freq = (500000.0 ** (-np.arange(half, dtype=np.float32) * 2.0 / 16)).astype(np.float32)
    ang = np.arange(SEQ, dtype=np.float32)[:, None] * inv_freq[None, :]
    cos = np.cos(ang).astype(np.float32).T
    sin = np.sin(ang).astype(np.float32).T
    C = np.ones((128, SEQ), np.float32)
    S = np.zeros((128, SEQ), np.float32)
    for h in range(2):
        C[h * 64 + 0:h * 64 + 8] = cos
        C[h * 64 + 8:h * 64 + 16] = cos
        S[h * 64 + 0:h * 64 + 8] = -sin
        S[h * 64 + 8:h * 64 + 16] = sin
    c["ropec"] = C
    c["ropes"] = S
    return c


def host_layout(inp):
    o = {}
    f = lambda a: np.ascontiguousarray(np.asarray(a, dtype=np.float32))
    o["wg1"] = f(inp["ffn1_w_gate"][0]); o["wu1"] = f(inp["ffn1_w_up"][0]); o["wd1"] = f(inp["ffn1_w_down"][0])
    o["wg2"] = f(inp["ffn2_w_gate"][0]); o["wu2"] = f(inp["ffn2_w_up"][0]); o["wd2"] = f(inp["ffn2_w_down"][0])
    w_in = f(inp["w_in"][0])
    u = w_in[:, 0:512]; q = w_in[:, 512:1024]; k = w_in[:, 1024:1152]; v = w_in[:, 1152:1280]
    o["win"] = np.ascontiguousarray(np.concatenate([u, q, k[:, 0:64], k[:, 0:64], k[:, 64:128], k[:, 64:128], v], axis=1))
    o["wout"] = f(inp["w_out"][0])
    o["wglu"] = f(inp["ssm_w_glu"][0])
    fm = lambda vec: np.ascontiguousarray(f(vec).reshape(-1, 128).T)
    gains = np.concatenate([fm(inp["ffn1_norm"][0]), fm(inp["mix_norm"][0]), fm(inp["ffn2_norm"][0]),
                            fm(inp["final_norm"]), fm(inp["ssm_out_norm"][0]), fm(inp["attn_out_norm"][0]),
                            fm(inp["ssm_b_glu"][0])], axis=1)
    o["gains"] = np.ascontiguousarray(gains)
    Dv = f(inp["ssm_D"][0]).reshape(32, 16)
    o["dblk"] = np.ascontiguousarray(np.tile(Dv.T, (8, 1)))
    sk = f(inp["attn_sinks"][0])
    o["sinkrow"] = np.ascontiguousarray(np.repeat(sk.reshape(4, 2), 64, axis=1).T)
    pl = lambda a: np.ascontiguousarray(f(a).reshape(16, 2, 64).transpose(1, 2, 0).reshape(128, 16))
    o["are"] = pl(inp["ssm_A_re"][0]); o["aim"] = pl(inp["ssm_A_im"][0])
    ldt = f(inp["ssm_log_dt"][0])
    o["ldt"] = pl(np.repeat(ldt[:, None], 64, axis=1))
    pb = lambda a: np.ascontiguousarray(f(a).reshape(16, 2, 64, 16).transpose(1, 2, 0, 3).reshape(128, 16, 16))
    o["bre"] = pb(inp["ssm_B_re"][0]); o["bim"] = pb(inp["ssm_B_im"][0])
    pc = lambda a: np.ascontiguousarray(f(a).transpose(0, 2, 1).reshape(16, 2, 64, 16).transpose(1, 2, 0, 3).reshape(128, 16, 16))
    o["cre"] = pc(inp["ssm_C_re"][0]); o["cim"] = pc(inp["ssm_C_im"][0])
    return o


def build_program():
    nc = bass.Bass("TRN2", target_bir_lowering=False)
    P = Prog()
    stack = ExitStack()
    dr = {}

    def din(name, shape, dt=F32):
        dr[name] = nc.dram_tensor(name, list(shape), dt, kind="ExternalInput").ap()
        return dr[name]

    x_d = din("x", [D, SEQ])
    for n in ("wg1", "wu1", "wg2", "wu2"):
        din(n, [D, FF])
    for n in ("wd1", "wd2"):
        din(n, [FF, D])
    din("win", [D, NWIN]); din("wout", [D, D]); din("wglu", [512, 512])
    din("gains", [128, 44]); din("dblk", [128, 32]); din("sinkrow", [128, 4])
    for n in ("are", "aim", "ldt"):
        din(n, [128, 16])
    for n in ("bre", "bim", "cre", "cim"):
        din(n, [128, 16, 16])
    din("identf", [128, 128]); din("identb", [128, 128], BF16); din("onesb", [128, 128], BF16)
    din("permb", [128, 128], BF16); din("mask", [128, 512], BF16); din("maskf", [128, 512], BF16)
    din("sel", [128, 8, 240], BF16); din("ropec", [128, SEQ]); din("ropes", [128, SEQ])
    out_d = nc.dram_tensor("out", [D, SEQ], F32, kind="ExternalOutput").ap()
    dbg_d = {}

    def sb_main(name, shape, dt):
        return stack.enter_context(nc.sbuf_tensor("s_" + name, list(shape), dt))

    sb = sb_main

    wgu = sb("wgu", [128, 4, 2, 4, 128], BF16)
    wd = sb("wd", [128, 3, 2, 512], BF16)
    wis = sb("wis", [128, 4, KT, 128], BF16)
    wo = sb("wo", [128, 2, KT, 128], BF16)
    w_glu = sb("w_glu", [128, 4, 512], BF16)
    sgs = sb("sgs", [128, 2, TS], F32)
    sqF = sb("sqF", [128, 2, TS], BF16)
    sqM = sb("sqM", [128, 1, TS], BF16)
    rsF = sb("rsF", [128, 2, TS], F32)
    rsM = sb("rsM", [128, 2, TS], F32)
    tabR = sb("tabR", [128, 16, 128], BF16)
    tabI = sb("tabI", [128, 16, 128], BF16)
    Wm = sb("Wm", [128, 32, 128], BF16)
    Mm = sb("Mm", [128, 32, 128], BF16)
    cosT = sb("cosT", [128, 16, NB], F32)
    sinT = sb("sinT", [128, 16, NB], F32)
    r8 = sb("r8", [128, 16], F32)
    sel = sb("sel", [128, 8, 240], BF16)
    dblk = sb("dblk", [128, 32], F32)
    gains = sb("gains", [128, 44], F32)
    esink = sb("esink", [128, 4], F32)
    cst = sb("cst", [128, 4], F32)
    identf = sb("identf", [128, 128], F32)
    identb = sb("identb", [128, 128], BF16)
    onesb = sb("onesb", [128, 128], BF16)
    permb = sb("permb", [128, 128], BF16)
    mask = sb("mask", [128, 512], BF16)
    maskf = sb("maskf", [128, 512], BF16)
    kTd = sb("kTd", [128, 2, 5 * 128], BF16)
    vsb = sb("vsb", [128, 5, 128], BF16)
    PT = sb("PT", [128, 2, 512], BF16)
    ropec = sb("ropec", [128, TS], F32)
    ropes = sb("ropes", [128, TS], F32)
    qpre = sb("qpre", [128, 1, TS], BF16)
    tmpf = sb("tmpf", [128, 3, TS], F32)
    Gin = sb("Gin", [128, 2, 4, NB], F32)
    Gs = sb("Gs", [128, 2, 4, NB], F32)
    Hb = sb("Hb", [128, 16, 2, NB + 1], BF16)
    Hc = sb("Hc", [128, 16, 2], F32)

    ps = [stack.enter_context(nc.psum_tensor("ps%d" % i, [128, 512], F32)) for i in range(8)]

    rrM = [0]
    rrF = [0]

    def bankM():
        b = rrM[0] % 3
        rrM[0] += 1
        return b

    def bankF():
        b = 4 + rrF[0] % 4
        rrF[0] += 1
        return b

    bankA = bankM
    BATT = 3

    def mm(out, lhsT, rhs, start, stop, reads, writes):
        P.add("pe", lambda e: e.matmul(out, lhsT=lhsT, rhs=rhs, start=start, stop=stop), reads, writes)

    def tr(out, in_, reads, writes):
        P.add("pe", lambda e: e.transpose(out, in_, identf[:]), reads + ["identf"], writes)

    def actf(out, in_, func, reads, writes, bias=None, scale=None):
        kw = {}
        if bias is not None:
            kw["bias"] = bias
        if scale is not None:
            kw["scale"] = scale
        P.add("act", lambda e: e.activation(out=out, in_=in_, func=func, **kw), reads, writes)

    def tt(out, in0, in1, op, reads, writes, eng="dve"):
        P.add(eng, lambda e: e.tensor_tensor(out=out, in0=in0, in1=in1, op=op), reads, writes)

    def ts1(out, in0, s1, op0, reads, writes, eng="dve"):
        P.add(eng, lambda e: e.tensor_single_scalar(out=out, in_=in0, scalar=s1, op=op0), reads, writes)

    def stt(out, in0, scalar, in1, op0, op1, reads, writes):
        P.add("dve", lambda e: e.scalar_tensor_tensor(out=out, in0=in0, scalar=scalar, in1=in1, op0=op0, op1=op1), reads, writes)

    def cp(out, in_, reads, writes, eng="dve"):
        if eng == "act":
            P.add("act", lambda e: e.activation(out=out, in_=in_, func=AF.Copy), reads, writes)
        else:
            P.add(eng, lambda e: e.tensor_copy(out=out, in_=in_), reads, writes)

    def recip(out, in_, reads, writes):
        P.add("dve", lambda e: e.reciprocal(out=out, in_=in_), reads, writes)

    def memset(ap, val, writes, eng="dve"):
        P.add(eng, lambda e: e.memset(ap, val), [], writes)

    def dma(q, out, in_, reads, writes, key):
        P.add(q, lambda e: e.dma_start(out=out, in_=in_), reads, writes, dma=key)

    def dump(name, ap, reads, shape, dt=F32):
        if not DEBUG:
            return
        if name not in dbg_d:
            dbg_d[name] = nc.dram_tensor("dbg_" + name, list(shape), dt, kind="ExternalOutput").ap()
        dma("sp", dbg_d[name], ap, reads, [("dbg", name)], "dbg_" + name)

    def ld(q, t, src, key):
        dma(q, t[:], src, [], [key], "ld_" + key)

    ld("sp", identf, dr["identf"], "identf"); ld("sp", identb, dr["identb"], "identb")
    ld("sp", onesb, dr["onesb"], "onesb"); ld("sp", permb, dr["permb"], "permb")
    ld("sp", mask, dr["mask"], "mask"); ld("sp", maskf, dr["maskf"], "maskf")
    ld("sp", sel, dr["sel"], "sel"); ld("sp", gains, dr["gains"], "gains")
    ld("sp", dblk, dr["dblk"], "dblk"); ld("sp", esink, dr["sinkrow"], "esink")
    memset(cst[:, 0:1], EPS, ["cst"])
    memset(cst[:, 1:2], math.pi / 2, ["cst"])
    memset(cst[:, 2:3], 4.0 * EPS, ["cst"])
    memset(kTd[:], 0.0, ["kTd"])
    memset(vsb[:], 0.0, ["vsb"])
    memset(Hc[:], 0.0, ["Hc"])
    memset(Hb[:], 0.0, ["Hb"])
    actf(esink[:], esink[:], AF.Exp, ["esink"], ["esink"])
    hbg = sb("hbg", [128, 4], F32)
    ts1(hbg[:], gains[:, 40:44], 0.5, ALU.mult, ["gains"], ["hbg"])

    scr_gu = {fid: nc.dram_tensor("scr_gu%d" % fid, [2 * FT, 128, 2 * 4 * 128], BF16).ap() for fid in (1, 2)}
    scr_d = {fid: nc.dram_tensor("scr_d%d" % fid, [22, 128, 2 * 512], BF16).ap() for fid in (1, 2)}
    scr_wo = nc.dram_tensor("scr_wo", [8, 128, KT * 128], BF16).ap()
    scr_wi = nc.dram_tensor("scr_wi", [11, 128, KT * 128], BF16).ap()

    ffn_order = [1]
    for s in range(NT_RUN - 1):
        ffn_order += [1, 2]
    ffn_order.append(2)
    ffn_w = {1: ("wg1", "wu1", "wd1"), 2: ("wg2", "wu2", "wd2")}
    wgu_loads = [(fid, f, kh) for fid in ffn_order for f in range(FT) for kh in range(2)]
    wd_loads = [(fid, half, jc) for fid in ffn_order for half in range(2) for jc in range(11)]
    wgu_ptr = [0]
    wd_ptr = [0]
    seen_gu = set()
    seen_d = set()
    multi = NT_RUN > 1

    NCV = 16
    cvi = [0]

    def conv(out_ap, in_ap, key):
        i = cvi[0] % NCV
        cvi[0] += 1
        dma("pool", out_ap, in_ap, [], [key, ("cvslot", i)], "cv%d" % i)

    def conv_gu(fid):
        gname, uname, _ = ffn_w[fid]
        for f in range(FT):
            for kh in range(2):
                scr = scr_gu[fid][2 * f + kh].rearrange("p (g k c) -> p g k c", g=2, k=4)
                for gi, nm in enumerate((gname, uname)):
                    src = dr[nm].rearrange("(k p) f -> p k f", p=128)[:, 4 * kh:4 * kh + 4, f * 128:(f + 1) * 128]
                    conv(scr[:, gi], src, ("scr_gu", fid, f, kh, gi))

    def conv_d(fid):
        for half in range(2):
            for jc in range(11):
                scr = scr_d[fid][half * 11 + jc].rearrange("p (f d) -> p f d", f=2)
                src = dr[ffn_w[fid][2]].rearrange("(f p) d -> p f d", p=128)[:, 2 * jc:2 * jc + 2, half * 512:(half + 1) * 512]
                conv(scr, src, ("scr_d", fid, half, jc))

    P.tag = 'prep'
    dma("pool", w_glu[:], dr["wglu"].rearrange("(k p) f -> p k f", p=128), [], ["w_glu"], "ld_w_glu")
    conv_gu(1)
    conv_d(1)
    for cg in range(11):
        conv(scr_wi[cg].rearrange("p (k c) -> p k c", k=KT), dr["win"].rearrange("(k p) f -> p k f", p=128)[:, :, cg * 128:(cg + 1) * 128], ("scr_wi", cg))
    for o in range(8):
        conv(scr_wo[o].rearrange("p (k c) -> p k c", k=KT), dr["wout"].rearrange("(k p) d -> p k d", p=128)[:, :, o * 128:(o + 1) * 128], ("scr_wo", o))
    conv_gu(2)
    conv_d(2)

    def wgu_ensure(upto):
        while wgu_ptr[0] <= upto and wgu_ptr[0] < len(wgu_loads):
            L = wgu_ptr[0]
            fid, f, kh = wgu_loads[L]
            slot = L % 4
            scr = scr_gu[fid][2 * f + kh].rearrange("p (g k c) -> p g k c", g=2, k=4)
            dma("sp", wgu[:, slot], scr, [("scr_gu", fid, f, kh, 0), ("scr_gu", fid, f, kh, 1)], [("wgu", slot, 0), ("wgu", slot, 1)], "wgu%d" % slot)
            wgu_ptr[0] += 1

    def wd_ensure(upto):
        while wd_ptr[0] <= upto and wd_ptr[0] < len(wd_loads):
            L = wd_ptr[0]
            fid, half, jc = wd_loads[L]
            slot = L % 3
            scr = scr_d[fid][half * 11 + jc].rearrange("p (f d) -> p f d", f=2)
            dma("sp", wd[:, slot], scr, [("scr_d", fid, half, jc)], [("wd", slot)], "wd%d" % slot)
            wd_ptr[0] += 1

    wgu_use = [0]
    wd_use = [0]

    wi_ptr = [0]
    wi_use = [0]
    wo_ptr = [0]
    wo_use = [0]

    def mq(tile):
        return ("sp", "") if tile == 0 else ("pool", "p")

    def wi_ensure(upto):
        while wi_ptr[0] <= upto and wi_ptr[0] < 11 * NT_RUN:
            L = wi_ptr[0]
            cg = L % 11
            slot = L % 4
            q, sfx = mq(L // 11)
            dma(q, wis[:, slot], scr_wi[cg].rearrange("p (k c) -> p k c", k=KT), [("scr_wi", cg)], [("wis", slot)], "wis%d%s" % (slot, sfx))
            wi_ptr[0] += 1

    def wo_ensure(upto):
        while wo_ptr[0] <= upto and wo_ptr[0] < 8 * NT_RUN:
            L = wo_ptr[0]
            o = L % 8
            slot = L % 2
            q, sfx = mq(L // 8)
            dma(q, wo[:, slot], scr_wo[o].rearrange("p (k c) -> p k c", k=KT), [("scr_wo", o)], [("wo", slot)], "wo%d%s" % (slot, sfx))
            wo_ptr[0] += 1

    def ssm_precompute():
        P.tag = 'pre'
        are = sb("p_are", [128, 16], F32); aim = sb("p_aim", [128, 16], F32); ldt = sb("p_ldt", [128, 16], F32)
        bre = sb("p_bre", [128, 16, 16], F32); bim = sb("p_bim", [128, 16, 16], F32)
        cre = sb("p_cre", [128, 16, 16], F32); cim = sb("p_cim", [128, 16, 16], F32)
        sm = sb("p_sm", [128, 24, 16], F32)
        pw = sb("p_pw", [128, 2, 16, 9], F32)
        bb = sb("p_bb", [128, 2, 16, 16], F32)
        bbp = sb("p_bbp", [128, 2, 2, 240], BF16)
        wtt = sb("p_wt", [128, 2, 2, 8, 16], F32)
        big = sb("p_big", [128, 2, 16, 64], F32)
        for nm, t in (("are", are), ("aim", aim), ("ldt", ldt), ("bre", bre), ("bim", bim), ("cre", cre), ("cim", cim)):
            dma("sp", t[:], dr[nm], [], ["p_" + nm], "ld_p_" + nm)
        S = lambda i: sm[:, i, :]
        K = "p_sm"
        rk = ["p_are", "p_aim", "p_ldt", K, "cst"]
        DT, AR, TH, MAG, C0, S0, CC, SS, CS, LBR, LBI, DEN, NRE, T1, T2, ZR, ZI, C8, S8, T3 = range(20)
        actf(S(DT), ldt[:], AF.Exp, rk, [K])
        tt(S(AR), are[:], S(DT), ALU.mult, rk, [K])
        tt(S(TH), aim[:], S(DT), ALU.mult, rk, [K])
        actf(S(MAG), S(AR), AF.Exp, rk, [K])
        actf(r8[:], S(AR), AF.Exp, rk, ["r8"], scale=8.0)
        actf(S(S0), S(TH), AF.Sin, rk, [K], scale=1.0 / 16)
        actf(S(C0), S(TH), AF.Sin, rk, [K], scale=1.0 / 16, bias=cst[:, 1:2])
        yield

        def dbl():
            tt(S(CC), S(C0), S(C0), ALU.mult, rk, [K])
            tt(S(SS), S(S0), S(S0), ALU.mult, rk, [K])
            tt(S(CS), S(C0), S(S0), ALU.mult, rk, [K])
            tt(S(C0), S(CC), S(SS), ALU.subtract, rk, [K])
            ts1(S(S0), S(CS), 2.0, ALU.mult, rk, [K])
        for _ in range(4):
            dbl()
            yield
        tt(S(LBR), S(MAG), S(C0), ALU.mult, rk, [K])
        tt(S(LBI), S(MAG), S(S0), ALU.mult, rk, [K])
        for _ in range(3):
            dbl()
            yield
        cp(S(C8), S(C0), rk, [K]); cp(S(S8), S(S0), rk, [K])
        tt(S(T1), are[:], are[:], ALU.mult, rk, [K])
        tt(S(T2), aim[:], aim[:], ALU.mult, rk, [K])
        tt(S(DEN), S(T1), S(T2), ALU.add, rk, [K])
        recip(S(DEN), S(DEN), rk, [K])
        ts1(S(NRE), S(LBR), -1.0, ALU.add, rk, [K])
        tt(S(T1), S(NRE), are[:], ALU.mult, rk, [K])
        tt(S(T2), S(LBI), aim[:], ALU.mult, rk, [K])
        tt(S(T1), S(T1), S(T2), ALU.add, rk, [K])
        tt(S(ZR), S(T1), S(DEN), ALU.mult, rk, [K])
        tt(S(T1), S(LBI), are[:], ALU.mult, rk, [K])
        tt(S(T2), S(NRE), aim[:], ALU.mult, rk, [K])
        tt(S(T1), S(T1), S(T2), ALU.subtract, rk, [K])
        tt(S(ZI), S(T1), S(DEN), ALU.mult, rk, [K])
        yield
        kp = ["p_pw", K]
        memset(pw[:, 0, :, 0:1], 1.0, ["p_pw"]); memset(pw[:, 1, :, 0:1], 0.0, ["p_pw"])
        for k in range(1, 9):
            pr, pi_ = pw[:, 0, :, k - 1], pw[:, 1, :, k - 1]
            tt(S(T1), pr, S(LBR), ALU.mult, kp, [K]); tt(S(T2), pi_, S(LBI), ALU.mult, kp, [K])
            tt(pw[:, 0, :, k], S(T1), S(T2), ALU.subtract, kp, ["p_pw"])
            tt(S(T1), pr, S(LBI), ALU.mult, kp, [K]); tt(S(T2), pi_, S(LBR), ALU.mult, kp, [K])
            tt(pw[:, 1, :, k], S(T1), S(T2), ALU.add, kp, ["p_pw"])
            yield
        bc16 = lambda ap2: ap2.unsqueeze(2).to_broadcast([128, 16, 16])
        kb = ["p_bre", "p_bim", K, "p_bb", "p_big"]
        t1 = big[:, 0, :, 0:16]; t2 = big[:, 1, :, 0:16]
        tt(t1, bre[:], bc16(S(ZR)), ALU.mult, kb, ["p_big"]); tt(t2, bim[:], bc16(S(ZI)), ALU.mult, kb, ["p_big"])
        tt(bb[:, 0], t1, t2, ALU.subtract, kb, ["p_bb"])
        tt(t1, bim[:], bc16(S(ZR)), ALU.mult, kb, ["p_big"]); tt(t2, bre[:], bc16(S(ZI)), ALU.mult, kb, ["p_big"])
        tt(bb[:, 1], t1, t2, ALU.add, kb, ["p_bb"])
        memset(bbp[:], 0.0, [("p_bbp", 0), ("p_bbp", 1)])
        yield
        kc = ["p_cre", "p_cim", "p_pw", "p_big", "p_wt"]
        tabRe = sb("p_tabRe", [128, 16, 256], BF16); tabIm = sb("p_tabIm", [128, 16, 256], BF16)
        memset(tabRe[:], 0.0, ["tabRe"]); memset(tabIm[:], 0.0, ["tabIm"])
        for tau in range(9):
            pr = pw[:, 0, :, tau:tau + 1].to_broadcast([128, 16, 16])
            pi_ = pw[:, 1, :, tau:tau + 1].to_broadcast([128, 16, 16])
            o_re = tabRe[:, :, 112 + tau * 16:112 + (tau + 1) * 16]; o_im = tabIm[:, :, 112 + tau * 16:112 + (tau + 1) * 16]
            ta = wtt[:, 0].rearrange("p r i c -> p (r i) c")
            tb = wtt[:, 1].rearrange("p r i c -> p (r i) c")
            tt(ta, cre[:], pr, ALU.mult, kc, ["p_wt"]); tt(tb, cim[:], pi_, ALU.mult, kc, ["p_wt"])
            tt(o_re, ta, tb, ALU.subtract, kc, ["tabRe"])
            tt(ta, cre[:], pi_, ALU.mult, kc, ["p_wt"]); tt(tb, cim[:], pr, ALU.mult, kc, ["p_wt"])
            tt(tb, ta, tb, ALU.add, kc, ["p_wt"])
            ts1(o_im, tb, -1.0, ALU.mult, kc, ["tabIm"])
            yield
        yield "PE_PART"
        pwr = sb("p_pwr", [128, 3, 16, 8], F32)
        for i in range(8):
            cp(pwr[:, 0, :, i:i + 1], pw[:, 0, :, 7 - i:8 - i], ["p_pw"], ["p_pwr"])
            cp(pwr[:, 1, :, i:i + 1], pw[:, 1, :, 7 - i:8 - i], ["p_pw"], ["p_pwr"])
        ts1(pwr[:, 2], pwr[:, 1], -1.0, ALU.mult, ["p_pwr"], ["p_pwr"])
        kw_ = ["p_pwr", "p_bb", "p_wt", "p_big"]
        for t in range(16):
            buf = t % 2
            wre = wtt[:, buf, 0]; wim = wtt[:, buf, 1]
            pr = pwr[:, 0, t, :].unsqueeze(2).to_broadcast([128, 8, 16])
            pi_ = pwr[:, 1, t, :].unsqueeze(2).to_broadcast([128, 8, 16])
            br = bb[:, 0, t, :].unsqueeze(1).to_broadcast([128, 8, 16])
            bi = bb[:, 1, t, :].unsqueeze(1).to_broadcast([128, 8, 16])
            x1 = big[:, 0, 0:8, 0:16]; x2 = big[:, 1, 0:8, 0:16]
            tt(x1, pr, br, ALU.mult, kw_, ["p_big"]); tt(x2, pi_, bi, ALU.mult, kw_, ["p_big"])
            tt(wre, x1, x2, ALU.subtract, kw_, ["p_wt"])
            tt(x1, pr, bi, ALU.mult, kw_, ["p_big"]); tt(x2, pi_, br, ALU.mult, kw_, ["p_big"])
            tt(wim, x1, x2, ALU.add, kw_, ["p_wt"])
            b = bankA()
            tr(ps[b][:, 0:128], wre.rearrange("p i c -> p (i c)"), ["p_wt"], [("ps", b)])
            tr(ps[b][:, 128:256], wim.rearrange("p i c -> p (i c)"), ["p_wt"], [("ps", b)])
            src = ps[b][:, 0:256].rearrange("p (r g q) -> p g r q", r=2, g=2)
            dst = Wm[:, 2 * t:2 * t + 2, :].rearrange("p g (r q) -> p g r q", r=2)
            cp(dst, src, [("ps", b)], ["Wm"])
            yield
        for g in range(32):
            t, g2 = g // 2, g % 2
            buf = t % 2
            if g2 == 0:
                cp(bbp[:, buf, 0, 112:128], bb[:, 0, t, :], ["p_bb"], [("p_bbp", buf)])
                cp(bbp[:, buf, 1, 112:128], bb[:, 1, t, :], ["p_bb"], [("p_bbp", buf)])
            rows = slice(g2 * 64, (g2 + 1) * 64)
            b = bankA()
            n = 0
            for i in range(8):
                off = (7 - i) * 16
                for ri, tab in ((0, tabRe), (1, tabIm)):
                    mm(ps[b][:, 0:128], bbp[rows, buf, ri, off:off + 128], tab[rows, t, off:off + 128],
                       n == 0, n == 15, [("p_bbp", buf), "tabRe", "tabIm"], [("ps", b)])
                    n += 1
            cp(Mm[:, g, :], ps[b][:, 0:128], [("ps", b)], ["Mm"], eng="act")
            yield
        cp(tabR[:], tabRe[:, :, 128:256], ["tabRe"], ["tabR"])
        cp(tabI[:], tabIm[:, :, 128:256], ["tabIm"], ["tabI"])
        kt = ["cosT", "sinT", K, "p_wt", "p_big"]
        cp(cosT[:, :, 0:1], S(C8).unsqueeze(2), kt, ["cosT"]); cp(sinT[:, :, 0:1], S(S8).unsqueeze(2), kt, ["sinT"])
        m = 1
        while m < NB:
            cr = cosT[:, :, m - 1:m].to_broadcast([128, 16, m]); sr = sinT[:, :, m - 1:m].to_broadcast([128, 16, m])
            a1 = big[:, 0, :, 0:m]; a2 = big[:, 1, :, 0:m]; a3 = big[:, 0, :, 32:32 + m]; a4 = big[:, 1, :, 32:32 + m]
            tt(a1, cosT[:, :, 0:m], cr, ALU.mult, kt, ["p_big"]); tt(a2, sinT[:, :, 0:m], sr, ALU.mult, kt, ["p_big"])
            tt(a3, cosT[:, :, 0:m], sr, ALU.mult, kt, ["p_big"]); tt(a4, sinT[:, :, 0:m], cr, ALU.mult, kt, ["p_big"])
            tt(cosT[:, :, m:2 * m], a1, a2, ALU.subtract, kt, ["cosT"])
            tt(sinT[:, :, m:2 * m], a3, a4, ALU.add, kt, ["sinT"])
            m *= 2
            yield

    pstack = ExitStack()

    def sbp(name, shape, dt):
        return pstack.enter_context(nc.sbuf_tensor("s_" + name, list(shape), dt))

    xTa = sb("xTa", [128, KT, TS], F32)
    hTF = sb("hTF", [128, KT, TS], BF16)
    act = sb("act", [128, FT, TS], BF16)
    xsel = lambda par: xTa if par == 0 else xTb

    def xT(par, k):
        return xsel(par)[:, k, :]

    def xk(par, k):
        return ("xT", par, k)

    def rstd_from(b, n, dst, dkey, ec=0):
        actf(dst, ps[b][:, :], AF.Sqrt, [("ps", b), "cst"], [dkey], bias=cst[:, ec:ec + 1], scale=1.0 / n)
        recip(dst, dst, [dkey], [dkey])

    def norm_to(par, goff, hT, hkey, sqs, rdst, rkey, b):
        for k in range(KT):
            sap, skey = sqs[k % len(sqs)]
            actf(sap, xT(par, k), AF.Square, [xk(par, k)], [skey])
            mm(ps[b][:, :], onesb[:], sap, k == 0, k == KT - 1, [skey, "onesb"], [("ps", b)])
        rstd_from(b, D, rdst, rkey)
        for k in range(KT):
            stt(hT[:, k, :], xT(par, k), gains[:, goff + k:goff + k + 1], rdst, ALU.mult, ALU.mult,
                [xk(par, k), "gains", rkey], [(hkey, k)])

    def ffn_norm_gen(goff, par):
        P.tag = 'ffn.norm'
        norm_to(par, goff, hTF, "hTF", [(sqF[:, 0, :], ("sqF", 0)), (sqF[:, 1, :], ("sqF", 1))], rsF[:, 0, :], ("rsF", 0), 4)
        yield

    def ffn_gen(fid, goff, par, do_norm=True):
        if do_norm:
            P.tag = 'ffn.norm'
            norm_to(par, goff, hTF, "hTF", [(sqF[:, 0, :], ("sqF", 0)), (sqF[:, 1, :], ("sqF", 1))], rsF[:, 0, :], ("rsF", 0), 4)
            yield
        for f in range(FT):
            P.tag = 'ffn.gu'
            bg, bu = (4, 5) if f % 2 == 0 else (6, 7)
            for kh in range(2):
                L = wgu_use[0]
                wgu_use[0] += 1
                slot = L % 4
                wgu_ensure(L + 3)
                for k4 in range(4):
                    k = 4 * kh + k4
                    mm(ps[bg][:, :], wgu[:, slot, 0, k4, :], hTF[:, k, :], k == 0, k == KT - 1,
                       [("wgu", slot, 0), ("hTF", k)], [("ps", bg)])
                for k4 in range(4):
                    k = 4 * kh + k4
                    mm(ps[bu][:, :], wgu[:, slot, 1, k4, :], hTF[:, k, :], k == 0, k == KT - 1,
                       [("wgu", slot, 1), ("hTF", k)], [("ps", bu)])
            s_ = f % 2
            actf(sgs[:, s_, :], ps[bg][:, :], AF.Tanh, [("ps", bg)], [("sgs", s_)], scale=0.5)
            stt(sgs[:, s_, :], sgs[:, s_, :], 1.0, ps[bg][:, :], ALU.add, ALU.mult, [("sgs", s_), ("ps", bg)], [("sgs", s_)])
            tt(act[:, f, :], sgs[:, s_, :], ps[bu][:, :], ALU.mult, [("sgs", s_), ("ps", bu)], [("act", f)])
            yield
        if wd_ptr[0] == 0:
            wd_ensure(1)
        for half in range(2):
            bs = [4, 5, 6, 7]
            for jc in range(11):
                P.tag = 'ffn.down'
                L = wd_use[0]
                wd_use[0] += 1
                slot = L % 3
                wd_ensure(L + 2)
                for f2 in range(2):
                    f = 2 * jc + f2
                    for o in range(4):
                        mm(ps[bs[o]][:, :], wd[:, slot, f2, o * 128:(o + 1) * 128], act[:, f, :], f == 0, f == FT - 1,
                           [("wd", slot), ("act", f)], [("ps", bs[o])])
                if jc == 10:
                    for o in range(4):
                        k = half * 4 + o
                        stt(xT(par, k), ps[bs[o]][:, :], 0.25, xT(par, k), ALU.mult, ALU.add, [("ps", bs[o]), xk(par, k)], [xk(par, k)])
                yield

    def loadx_gen(s):
        par = s % 2
        P.tag = 'loadx'
        for k in range(KT):
            q, sfx = mq(0 if s <= 1 else s)
            dma(q, xsel(par)[:, k, :], x_d[k * 128:(k + 1) * 128, s * TS:(s + 1) * TS], [], [xk(par, k)], "xl%d%s" % (k, sfx))
        yield

    def xn_tile(k):
        return act[:, 2 * k:2 * k + 2, :].rearrange("p a t -> p (a t)").bitcast(F32)

    def final_gen(s):
        P.tag = 'final'
        par = s % 2
        b = 4
        for k in range(KT):
            s_ = k % 2
            actf(sqF[:, s_, :], xT(par, k), AF.Square, [xk(par, k)], [("sqF", s_)])
            mm(ps[b][:, :], onesb[:], sqF[:, s_, :], k == 0, k == KT - 1, [("sqF", s_), "onesb"], [("ps", b)])
        rstd_from(b, D, rsF[:, 1, :], ("rsF", 1))
        yield
        for k in range(KT):
            P.tag = 'final'
            stt(xn_tile(k), xT(par, k), gains[:, 24 + k:25 + k], rsF[:, 1, :], ALU.mult, ALU.mult,
                [xk(par, k), "gains", ("rsF", 1)], [("act", 2 * k), ("act", 2 * k + 1)])
            q, sfx = mq(s + 1)
            dma(q, out_d[k * 128:(k + 1) * 128, s * TS:(s + 1) * TS], xn_tile(k),
                [("act", 2 * k), ("act", 2 * k + 1)], [("out", s, k)], "out%d%s" % (k, sfx))
        yield

    uTd = lambda m: am[:, m, :].rearrange("p (i n) -> p i n", i=8)
    qT = lambda m: am[:, 4 + m, :]
    Ugrp = lambda g: am[:, 8 + g // 8, (g % 8) * NB:(g % 8 + 1) * NB]
    Y2grp = lambda g: am[:, g // 8, (g % 8) * NB:(g % 8 + 1) * NB]
    y2T = lambda m: am[:, 12 + m, :]
    yg = lambda m: (am[:, 4 + m, :], ("am", 4 + m))
    yat = lambda m: (am[:, 16 + m, :], ("am", 16 + m))

    def rope_finish(qs, out_ap, out_key):
        qap, qkey = qs
        b2 = bankM()
        mm(ps[b2][:, :], permb[:], qap, True, True, ["permb", qkey], [("ps", b2)])
        tt(tmpf[:, 0, :], qap, ropec[:], ALU.mult, [qkey, "ropec"], [("tmpf", 0), ("tmpf", "0b")])
        tt(tmpf[:, 1, :], ps[b2][:, :], ropes[:], ALU.mult, [("ps", b2), "ropes"], [("tmpf", 1)])
        tt(out_ap, tmpf[:, 0, :], tmpf[:, 1, :], ALU.add, [("tmpf", 0), ("tmpf", 1)], [out_key])

    def mixer_gen(s):
        par = s % 2
        P.tag = 'mix.norm'
        if wi_ptr[0] == 0:
            wi_ensure(2)
            wo_ensure(0)
        q, sfx = mq(s)
        dma(q, ropec[:], dr["ropec"][:, s * TS:(s + 1) * TS], [], ["ropec"], "ropec" + sfx)
        dma(q, ropes[:], dr["ropes"][:, s * TS:(s + 1) * TS], [], ["ropes"], "ropes" + sfx)
        norm_to(par, 8, hTM, "hTM", [(sqM[:, 0, :], ("sqM", 0)), (qpre[:, 0, :], ("qpre", 0))], rsM[:, 0, :], ("rsM", 0), bankM())
        yield

        def proj_group():
            L = wi_use[0]
            wi_use[0] += 1
            slot = L % 4
            wi_ensure(L + 3)
            b = bankM()
            for k in range(KT):
                mm(ps[b][:, :], wis[:, slot, k, :], hTM[:, k, :], k == 0, k == KT - 1, [("wis", slot), ("hTM", k)], [("ps", b)])
            return b

        for m in range(4):
            P.tag = 'mix.proj'
            b = proj_group()
            actf(uTd(m), ps[b][:, :].rearrange("p (n i) -> p i n", i=8), AF.Copy, [("ps", b)], [("am", m)])
            yield
        jobs = [(qT(m), ("am", 4 + m)) for m in range(4)]
        jobs += [(kTd[:, kk, 128:640], ("kTd", kk)) for kk in range(2)]
        QS = [(qpre[:, 0, :], ("qpre", 0)), (sqM[:, 0, :], ("sqM", 0))]
        pending = None
        for idx, (out_ap, out_key) in enumerate(jobs):
            P.tag = 'mix.proj'
            b = proj_group()
            qs = QS[idx % 2]
            actf(qs[0], ps[b][:, :], AF.Copy, [("ps", b)], [qs[1]])
            if pending is not None:
                rope_finish(*pending)
            pending = (qs, out_ap, out_key)
            yield
        P.tag = 'mix.proj'
        L = wi_use[0]
        wi_use[0] += 1
        slot = L % 4
        wi_ensure(L + 3)
        b = bankM()
        for blk in range(4):
            for k in range(KT):
                mm(ps[b][:, blk * 128:(blk + 1) * 128], hTM[:, k, blk * 128:(blk + 1) * 128], wis[:, slot, k, :],
                   k == 0, k == KT - 1, [("wis", slot), ("hTM", k)], [("ps", b)])
        rope_finish(*pending)
        actf(vsb[:, 1:5, :], ps[b][:, :].rearrange("p (a d) -> p a d", a=4), AF.Copy, [("ps", b)], ["vsb"])
        yield

        def attn_gen():
            ptc = 0
            units = [(m, pair) for m in range(4) for pair in range(2)]

            def a1(m, pair, hh, pi_):
                kv = m // 2
                rows = slice(hh * 64, (hh + 1) * 64)
                bs_ = bankM()
                first = (s == 0 and pair == 0)
                mk, mkk = (maskf, "maskf") if first else (mask, "mask")
                mm(ps[bs_][:, :], identb[:], mk[:], True, False, ["identb", mkk], [("ps", bs_)])
                n = 0
                for qb in range(2):
                    blkq = 2 * pair + qb
                    for piece in range(2):
                        slot = blkq + piece
                        mm(ps[bs_][:, (qb * 2 + piece) * 128:(qb * 2 + piece + 1) * 128],
                           kTd[rows, kv, slot * 128:(slot + 1) * 128], qT(m)[rows, blkq * 128:(blkq + 1) * 128],
                           False, n == 3, [("kTd", kv), ("am", 4 + m)], [("ps", bs_)])
                        n += 1
                actf(PT[:, pi_, :], ps[bs_][:, :], AF.Exp, [("ps", bs_)], [("PT", pi_)], scale=0.125)

            def pv(m, pair, hh, pi_):
                kv = m // 2
                rows = slice(hh * 64, (hh + 1) * 64)
                for qb in range(2):
                    blkq = 2 * pair + qb
                    ncol = slice(qb * 128, (qb + 1) * 128)
                    dcol = slice(256 + qb * 128, 256 + (qb + 1) * 128)
                    for piece in range(2):
                        slot = blkq + piece
                        mm(ps[BATT][rows, ncol], vsb[:, slot, kv * 64:(kv + 1) * 64], PT[:, pi_, (qb * 2 + piece) * 128:(qb * 2 + piece + 1) * 128],
                           piece == 0, piece == 1, ["vsb", ("PT", pi_)], [("ps", BATT)])
                    for piece in range(2):
                        mm(ps[BATT][rows, dcol], onesb[:, 0:64], PT[:, pi_, (qb * 2 + piece) * 128:(qb * 2 + piece + 1) * 128],
                           piece == 0, piece == 1, ["onesb", ("PT", pi_)], [("ps", BATT)])

            def norm_unit(m, pair):
                den = tmpf[:, 2, 0:256]
                ts1(den, ps[BATT][:, 256:512], esink[:, m:m + 1], ALU.add, [("ps", BATT), "esink"], [("tmpf", 2)])
                recip(den, den, [("tmpf", 2)], [("tmpf", 2)])
                ya, yak = yat(m)
                tt(ya[:, pair * 256:(pair + 1) * 256], ps[BATT][:, 0:256], den, ALU.mult, [("ps", BATT), ("tmpf", 2)], [yak])

            a1(0, 0, 0, 0); yield
            a1(0, 0, 1, 1); yield
            for ui, (m, pair) in enumerate(units):
                nxt = units[ui + 1] if ui + 1 < len(units) else None
                pv(m, pair, 0, 0); yield
                if nxt:
                    a1(nxt[0], nxt[1], 0, 0); yield
                pv(m, pair, 1, 1); yield
                norm_unit(m, pair)
                if nxt:
                    a1(nxt[0], nxt[1], 1, 1)
                yield
            b = bankM()
            for m in range(4):
                ya, yak = yat(m)
                actf(sqM[:, 0, :], ya, AF.Square, [yak], [("sqM", 0)])
                mm(ps[b][:, :], onesb[:], sqM[:, 0, :], m == 0, m == 3, [("sqM", 0), "onesb"], [("ps", b)])
            rstd_from(b, 512, rsM[:, 1, :], ("rsM", 1))
            yield

        def ssm_gen():
            cp(Hb[:, :, :, 0:1], Hc[:].unsqueeze(3), ["Hc"], ["Hb"])
            vbank = {}

            def U_unit(m):
                b = bankM()
                for gg in range(8):
                    for i in range(8):
                        mm(ps[b][:, gg * NB:(gg + 1) * NB], sel[:, gg, (7 - i) * 16:(7 - i) * 16 + 128], uTd(m)[:, i, :],
                           i == 0, i == 7, ["sel", ("am", m)], [("ps", b)])
                actf(am[:, 8 + m, :], ps[b][:, :], AF.Copy, [("ps", b)], [("am", 8 + m)])

            def V_unit(q4):
                b = bankM()
                vbank[q4] = b
                for tt_ in range(4):
                    t = 4 * q4 + tt_
                    for g2 in range(2):
                        g = 2 * t + g2
                        for ri in range(2):
                            c0 = (tt_ * 2 + ri) * NB
                            mm(ps[b][g2 * 64:(g2 + 1) * 64, c0:c0 + NB], Wm[:, g, ri * 64:(ri + 1) * 64], Ugrp(g), True, True,
                               ["Wm", ("am", 8 + g // 8)], [("ps", b)])

            def D_unit(q4):
                b = vbank[q4]
                V = ps[b][:, :].rearrange("p (t r n) -> p t r n", t=4, r=2)
                Vre, Vim = V[:, :, 0, :], V[:, :, 1, :]
                cs_ = cosT[:, 4 * q4:4 * q4 + 4, :]
                sn_ = sinT[:, 4 * q4:4 * q4 + 4, :]
                tA = tmpf[:, 0, 0:256].rearrange("p (t n) -> p t n", t=4)
                tB = tmpf[:, 0, 256:512].rearrange("p (t n) -> p t n", t=4)
                kA, kB = ("tmpf", 0), ("tmpf", "0b")
                gk = "Gin"
                tt(tA, Vre, cs_, ALU.mult, [("ps", b), "cosT"], [kA]); tt(tB, Vim, sn_, ALU.mult, [("ps", b), "sinT"], [kB])
                tt(Gin[:, 0], tA, tB, ALU.add, [kA, kB], [gk])
                tt(tA, Vim, cs_, ALU.mult, [("ps", b), "cosT"], [kA]); tt(tB, Vre, sn_, ALU.mult, [("ps", b), "sinT"], [kB])
                tt(Gin[:, 1], tA, tB, ALU.subtract, [kA, kB, gk], [gk])
                sk_ = "Gs"
                for tt_ in range(4):
                    t = 4 * q4 + tt_
                    for ri in range(2):
                        out_ap = Gs[:, ri, tt_, :]
                        d0 = r8[:, t:t + 1].to_broadcast([128, NB])
                        d1 = Gin[:, ri, tt_, :]
                        init = Hc[:, t, ri:ri + 1]
                        P.add("dve", (lambda o_=out_ap, a_=d0, b_=d1, i_=init: (lambda e: e.tensor_tensor_scan(
                            out=o_, data0=a_, data1=b_, initial=i_, op0=ALU.mult, op1=ALU.add)))(),
                            [gk, "r8", "Hc", sk_], [sk_])
                Gre, Gim = Gs[:, 0], Gs[:, 1]
                tt(tA, Gre, cs_, ALU.mult, [sk_, "cosT"], [kA]); tt(tB, Gim, sn_, ALU.mult, [sk_, "sinT"], [kB])
                tt(Gin[:, 0], tA, tB, ALU.subtract, [kA, kB, gk], [gk])
                tt(tA, Gre, sn_, ALU.mult, [sk_, "sinT"], [kA]); tt(tB, Gim, cs_, ALU.mult, [sk_, "cosT"], [kB])
                tt(Gin[:, 1], tA, tB, ALU.add, [kA, kB, gk], [gk])
                for ri in range(2):
                    actf(Hb[:, 4 * q4:4 * q4 + 4, ri, 1:NB + 1], Gin[:, ri], AF.Copy, [gk], [("Hb", q4)])
                    cp(Hc[:, 4 * q4:4 * q4 + 4, ri:ri + 1], Gin[:, ri, :, NB - 1:NB], [gk], ["Hc"])

            def Y_unit(m):
                b = bankM()
                for gg in range(8):
                    g = 8 * m + gg
                    t, g2 = g // 2, g % 2
                    rows = slice(g2 * 64, (g2 + 1) * 64)
                    o_ = ps[b][:, gg * NB:(gg + 1) * NB]
                    mm(o_, Mm[:, g, :], Ugrp(g), True, False, ["Mm", ("am", 8 + m)], [("ps", b)])
                    mm(o_, tabR[rows, t, :], Hb[rows, t, 0, 0:NB], False, False, ["tabR", "Hb", ("Hb", m)], [("ps", b)])
                    mm(o_, tabI[rows, t, :], Hb[rows, t, 1, 0:NB], False, True, ["tabI", "Hb", ("Hb", m)], [("ps", b)])
                U3 = am[:, 8 + m, :].rearrange("p (g n) -> p g n", g=8)
                Db = dblk[:, 8 * m:8 * m + 8].unsqueeze(2).to_broadcast([128, 8, NB])
                y1 = tmpf[:, 1, :]
                tt(y1.rearrange("p (g n) -> p g n", g=8), U3, Db, ALU.mult, [("am", 8 + m), "dblk"], [("tmpf", 1)])
                tt(y1, y1, ps[b][:, :], ALU.add, [("tmpf", 1), ("ps", b)], [("tmpf", 1)])
                actf(am[:, m, :], y1, AF.Gelu_apprx_tanh, [("tmpf", 1)], [("am", m)])

            def I_unit(m):
                b = bankM()
                for j in range(8):
                    for gg in range(8):
                        mm(ps[b][:, j * NB:(j + 1) * NB], sel[:, j, (7 - gg) * 16:(7 - gg) * 16 + 128], Y2grp(8 * m + gg),
                           gg == 0, gg == 7, ["sel", ("am", m)], [("ps", b)])
                actf(y2T(m).rearrange("p (n j) -> p j n", j=8), ps[b][:, :].rearrange("p (j n) -> p j n", j=8), AF.Copy,
                     [("ps", b)], [("am", 12 + m)])

            order = [(U_unit, 0), (U_unit, 1), (V_unit, 0), (D_unit, 0), (U_unit, 2), (V_unit, 1), (D_unit, 1), (U_unit, 3),
                     (V_unit, 2), (D_unit, 2), (Y_unit, 0), (V_unit, 3), (D_unit, 3), (Y_unit, 1), (I_unit, 0), (Y_unit, 2),
                     (I_unit, 1), (Y_unit, 3), (I_unit, 2), (I_unit, 3)]
            for fn_, arg in order:
                fn_(arg)
                yield
            yield "need_attn_done"
            for mo in range(4):
                b = bankM()
                for k in range(4):
                    mm(ps[b][:, :], w_glu[:, k, mo * 128:(mo + 1) * 128], y2T(k), k == 0, k == 3, ["w_glu", ("am", 12 + k)], [("ps", b)])
                sg = tmpf[:, 1, :]
                actf(sg, ps[b][:, :], AF.Tanh, [("ps", b), "hbg"], [("tmpf", 1)], bias=hbg[:, mo:mo + 1], scale=0.5)
                ygm, ygk = yg(mo)
                stt(ygm, sg, 1.0, y2T(mo), ALU.add, ALU.mult, [("am", 12 + mo), ("tmpf", 1)], [ygk])
                yield
            b = bankM()
            for mo in range(4):
                ygm, ygk = yg(mo)
                actf(sqM[:, 0, :], ygm, AF.Square, [ygk], [("sqM", 0)])
                mm(ps[b][:, :], onesb[:], sqM[:, 0, :], mo == 0, mo == 3, [("sqM", 0), "onesb"], [("ps", b)])
            rstd_from(b, 512, rsM[:, 0, :], ("rsM", 0), ec=2)
            yield

        ga, gs = attn_gen(), ssm_gen()
        a_done = s_done = s_wait = False
        while not (a_done and s_done):
            if not a_done:
                P.tag = 'mix.attn'
                try:
                    next(ga)
                except StopIteration:
                    a_done = True
                yield
            if not s_done and not (s_wait and not a_done):
                P.tag = 'mix.ssm'
                try:
                    if next(gs) == "need_attn_done":
                        s_wait = True
                except StopIteration:
                    s_done = True
                yield
        if s == 0:
            dump("yattn", am[:, 16:20, :], [("am", 16 + m) for m in range(4)], [128, 4, TS], BF16)
            dump("y2T", am[:, 12:16, :], [("am", 12 + m) for m in range(4)], [128, 4, TS], BF16)
        P.tag = 'mix.onorm'
        for k in range(4):
            ygm, ygk = yg(k)
            stt(hTM[:, k, :], ygm, gains[:, 32 + k:33 + k], rsM[:, 0, :], ALU.mult, ALU.mult, [ygk, "gains", ("rsM", 0)], [("hTM", k)])
        for k in range(4):
            ya, yak = yat(k)
            stt(hTM[:, 4 + k, :], ya, gains[:, 36 + k:37 + k], rsM[:, 1, :], ALU.mult, ALU.mult, [yak, "gains", ("rsM", 1)], [("hTM", 4 + k)])
        yield
        for o in range(8):
            P.tag = 'mix.wout'
            L = wo_use[0]
            wo_use[0] += 1
            slot = L % 2
            wo_ensure(L + 1)
            b = bankM()
            for k in range(KT):
                mm(ps[b][:, :], wo[:, slot, k, :], hTM[:, k, :], k == 0, k == KT - 1, [("wo", slot), ("hTM", k)], [("ps", b)])
            tt(xT(par, o), ps[b][:, :], xT(par, o), ALU.add, [("ps", b), xk(par, o)], [xk(par, o)])
            yield
        cp(kTd[:, :, 0:128], kTd[:, :, 512:640], [("kTd", 0), ("kTd", 1)], [("kTd", 0), ("kTd", 1)], eng="act")
        cp(vsb[:, 0, :], vsb[:, 4, :], ["vsb"], ["vsb"], eng="act")
        yield

    def drain(g):
        for _ in g:
            pass

    def chain(*gens):
        for g in gens:
            for _ in g:
                yield

    def interleave(ga, na, gb, nb):
        ca = cb = 0
        a_done = b_done = False
        while not (a_done and b_done):
            pick_a = (not a_done) and (b_done or ca * nb <= cb * na)
            if pick_a:
                try:
                    next(ga)
                    ca += 1
                except StopIteration:
                    a_done = True
            else:
                try:
                    next(gb)
                    cb += 1
                except StopIteration:
                    b_done = True

    P.tile = 0
    sb = sbp
    pg = ssm_precompute()
    next(pg)
    wgu_ensure(2)
    drain(chain(loadx_gen(0), ffn_norm_gen(0, 0)))
    fg = ffn_gen(1, 0, 0, do_norm=False)
    p_part1 = True
    f_live = True
    while p_part1 or f_live:
        if p_part1:
            P.tag = 'pre'
            if next(pg) == "PE_PART":
                p_part1 = False
        if f_live:
            try:
                next(fg)
            except StopIteration:
                f_live = False
    P.tag = 'pre'
    drain(pg)
    P.barrier()
    pstack.close()
    sb = sb_main
    xTb = sb("xTb", [128, KT, TS], F32)
    hTM = sb("hTM", [128, KT, TS], BF16)
    am = sb("am", [128, 20, TS], BF16)
    dump("x1", xTa[:], [xk(0, k) for k in range(KT)], [128, KT, TS])
    for s in range(NT_RUN):
        P.tile = s
        bparts = []
        nb = 0
        if s >= 1:
            bparts += [ffn_gen(2, 16, (s - 1) % 2), final_gen(s - 1)]
            nb += 50
        if s + 1 < NT_RUN:
            bparts += [loadx_gen(s + 1), ffn_norm_gen(0, (s + 1) % 2)]
            nb += 5
        if bparts:
            interleave(mixer_gen(s), 95, chain(*bparts), nb)
        else:
            drain(mixer_gen(s))
        if s == 0:
            dump("x2", xTa[:], [xk(0, k) for k in range(KT)], [128, KT, TS])
        if s + 1 < NT_RUN:
            drain(ffn_gen(1, 0, (s + 1) % 2, do_norm=False))
    drain(chain(ffn_gen(2, 16, (NT_RUN - 1) % 2), final_gen(NT_RUN - 1)))
    P.add("sp", None, [("out", s, k) for s in range(NT_RUN) for k in range(KT)] + [("dbg", n) for n in dbg_d], [])

    P.finalize(nc, stack)
    with nc.Block() as block:
        @block.sync
        def _(e):
            P.emit("sp", e)

        @block.tensor
        def _(e):
            P.emit("pe", e)

        @block.scalar
        def _(e):
            P.emit("act", e)

        @block.vector
        def _(e):
            P.emit("dve", e)

        @block.gpsimd
        def _(e):
            P.emit("pool", e)
    stack.close()
    nc._prog = P
    return nc, list(dbg_d.keys())


_CACHE = {}


def kernel(**inputs):
    x = np.ascontiguousarray(np.asarray(inputs["x"], dtype=np.float32))
    B = x.shape[0]
    if "nc" not in _CACHE:
        _CACHE["nc"] = build_program()
    nc, dbg = _CACHE["nc"]
    shared = host_layout(inputs)
    shared.update(host_consts())
    in_maps = []
    for b in range(B):
        m = dict(shared)
        m["x"] = np.ascontiguousarray(x[b].T)
        in_maps.append(m)
    res = run_bass_kernel_spmd(nc, in_maps, core_ids=list(range(B)))
    out = np.stack([np.ascontiguousarray(np.asarray(r["out"], dtype=np.float32).T) for r in res.results], axis=0)
    if DEBUG:
        _CACHE["dbg"] = {n: np.asarray(res.results[0]["dbg_" + n]) for n in dbg}
    return out
```

```python
import math
import os
from contextlib import ExitStack

import numpy as np
import ml_dtypes

import concourse.bass as bass
import concourse.mybir as mybir
from concourse.bass_utils import run_bass_kernel_spmd

F32 = mybir.dt.float32
BF16 = mybir.dt.bfloat16
AF = mybir.ActivationFunctionType
ALU = mybir.AluOpType

D = 1024
KT = 8
FF = 2816
FT = 22
SEQ = 4096
TS = 512
NTILE = SEQ // TS
NB = TS // 8
EPS = 1e-6
NEG = -30000.0
NWIN = 1408

DEBUG = bool(int(os.environ.get("KDBG", "0")))
NT_RUN = int(os.environ.get("KNT", str(NTILE)))


class Prog:
    ENGS = ("pe", "act", "dve", "pool", "sp")

    def __init__(self):
        self.ops = []
        self.lastw = {}
        self.readers = {}
        self.tag = ""
        self.tile = -1

    def add(self, eng, fn, reads=(), writes=(), dma=None):
        i = len(self.ops)
        deps = set()
        for k in reads:
            w = self.lastw.get(k)
            if w is not None:
                deps.add(w)
        for k in writes:
            w = self.lastw.get(k)
            if w is not None:
                deps.add(w)
            for r in self.readers.get(k, ()):
                deps.add(r)
        for k in reads:
            lst = self.readers.setdefault(k, [])
            if dma is None:
                lst[:] = [r for r in lst if not (self.ops[r]["eng"] == eng and self.ops[r]["dma"] is None)]
            lst.append(i)
        for k in writes:
            self.lastw[k] = i
            self.readers[k] = []
        deps.discard(i)
        self.ops.append(dict(eng=eng, fn=fn, deps=deps, dma=dma, signal=False, count=0, tag=self.tag, tile=self.tile))
        return i

    def barrier(self):
        last = {}
        for i, op in enumerate(self.ops):
            if op["dma"] is not None:
                if str(op["dma"]).startswith("cv"):
                    continue
                last[("d", op["dma"])] = i
            elif op["fn"] is not None:
                last[("e", op["eng"])] = i
        deps = set(last.values())
        for e in self.ENGS:
            self.ops.append(dict(eng=e, fn=None, deps=set(deps), dma=None, signal=False, count=0, tag='barrier', tile=-1))

    def finalize(self, nc, stack):
        ops = self.ops
        for op in ops:
            for d in op["deps"]:
                dop = ops[d]
                if dop["dma"] is None:
                    if dop["eng"] == "pe" and op["eng"] == "pe" and op["dma"] is None:
                        continue
                    dop["signal"] = True
        esem = {e: stack.enter_context(nc.semaphore("sem_" + e)) for e in self.ENGS}
        dsem = {}
        ecnt = {e: 0 for e in self.ENGS}
        dcnt = {}
        waited = {e: {} for e in self.ENGS}
        streams = {e: [] for e in self.ENGS}
        for op in ops:
            e = op["eng"]
            waits = {}
            for d in op["deps"]:
                dop = ops[d]
                if dop["dma"] is not None:
                    key = ("d", dop["dma"])
                    val = dop["count"]
                    sem = dsem[dop["dma"]]
                else:
                    if dop["eng"] == "pe" and e == "pe" and op["dma"] is None:
                        continue
                    key = ("e", dop["eng"])
                    val = dop["count"]
                    sem = esem[dop["eng"]]
                if waited[e].get(key, 0) >= val:
                    continue
                if key not in waits or waits[key][1] < val:
                    waits[key] = (sem, val)
            for key, (sem, val) in waits.items():
                waited[e][key] = val
            if op["dma"] is not None:
                if op["dma"] not in dsem:
                    dsem[op["dma"]] = stack.enter_context(nc.semaphore("dsem%d" % len(dsem)))
                    dcnt[op["dma"]] = 0
                dcnt[op["dma"]] += 16
                op["count"] = dcnt[op["dma"]]
                inc = (dsem[op["dma"]], 16)
            elif op["signal"]:
                ecnt[e] += 1
                op["count"] = ecnt[e]
                inc = (esem[e], 1)
            else:
                inc = None
            streams[e].append((list(waits.values()), op["fn"], inc))
        self.streams = streams
        self.nsem = len(dsem) + len(esem)

    def emit(self, eng_name, e):
        for waits, fn, inc in self.streams[eng_name]:
            for sem, val in waits:
                e.wait_ge(sem, val)
            if fn is None:
                continue
            ins = fn(e)
            if inc is not None:
                ins.then_inc(inc[0], inc[1])


def _bf(a):
    return np.ascontiguousarray(a.astype(ml_dtypes.bfloat16))


def host_consts():
    c = {}
    c["identf"] = np.eye(128, dtype=np.float32)
    c["identb"] = _bf(np.eye(128, dtype=np.float32))
    c["onesb"] = _bf(np.ones((128, 128), np.float32))
    perm = np.zeros((128, 128), np.float32)
    for h in range(2):
        for d in range(16):
            pd = d + 8 if d < 8 else d - 8
            perm[h * 64 + pd, h * 64 + d] = 1.0
    c["permb"] = _bf(perm)
    kj = np.arange(128)[:, None]
    qi = np.arange(128)[None, :]
    prev = np.where(kj > qi, 0.0, NEG).astype(np.float32)
    same = np.where(kj <= qi, 0.0, NEG).astype(np.float32)
    full = np.full((128, 128), NEG, np.float32)
    c["mask"] = _bf(np.concatenate([prev, same, prev, same], axis=1))
    c["maskf"] = _bf(np.concatenate([full, same, prev, same], axis=1))
    sel = np.zeros((128, 8, 240), np.float32)
    for g in range(8):
        for cc in range(16):
            sel[16 * g + cc, g, 112 + cc] = 1.0
    c["sel"] = _bf(sel)
    half = 8
    inv_freq = (500000.0 ** (-np.arange(half, dtype=np.float32) * 2.0 / 16)).astype(np.float32)
    ang = np.arange(SEQ, dtype=np.float32)[:, None] * inv_freq[None, :]
    cos = np.cos(ang).astype(np.float32).T
    sin = np.sin(ang).astype(np.float32).T
    C = np.ones((128, SEQ), np.float32)
    S = np.zeros((128, SEQ), np.float32)
    for h in range(2):
        C[h * 64 + 0:h * 64 + 8] = cos
        C[h * 64 + 8:h * 64 + 16] = cos
        S[h * 64 + 0:h * 64 + 8] = -sin
        S[h * 64 + 8:h * 64 + 16] = sin
    c["ropec"] = C
    c["ropes"] = S
    return c


def host_layout(inp):
    o = {}
    f = lambda a: np.ascontiguousarray(np.asarray(a, dtype=np.float32))
    o["wg1"] = f(inp["ffn1_w_gate"][0]); o["wu1"] = f(inp["ffn1_w_up"][0]); o["wd1"] = f(inp["ffn1_w_down"][0])
    o["wg2"] = f(inp["ffn2_w_gate"][0]); o["wu2"] = f(inp["ffn2_w_up"][0]); o["wd2"] = f(inp["ffn2_w_down"][0])
    w_in = f(inp["w_in"][0])
    u = w_in[:, 0:512]; q = w_in[:, 512:1024]; k = w_in[:, 1024:1152]; v = w_in[:, 1152:1280]
    o["win"] = np.ascontiguousarray(np.concatenate([u, q, k[:, 0:64], k[:, 0:64], k[:, 64:128], k[:, 64:128], v], axis=1))
    o["wout"] = f(inp["w_out"][0])
    o["wglu"] = f(inp["ssm_w_glu"][0])
    fm = lambda vec: np.ascontiguousarray(f(vec).reshape(-1, 128).T)
    gains = np.concatenate([fm(inp["ffn1_norm"][0]), fm(inp["mix_norm"][0]), fm(inp["ffn2_norm"][0]),
                            fm(inp["final_norm"]), fm(inp["ssm_out_norm"][0]), fm(inp["attn_out_norm"][0]),
                            fm(inp["ssm_b_glu"][0])], axis=1)
    o["gains"] = np.ascontiguousarray(gains)
    Dv = f(inp["ssm_D"][0]).reshape(32, 16)
    o["dblk"] = np.ascontiguousarray(np.tile(Dv.T, (8, 1)))
    sk = f(inp["attn_sinks"][0])
    o["sinkrow"] = np.ascontiguousarray(np.repeat(sk.reshape(4, 2), 64, axis=1).T)
    pl = lambda a: np.ascontiguousarray(f(a).reshape(16, 2, 64).transpose(1, 2, 0).reshape(128, 16))
    o["are"] = pl(inp["ssm_A_re"][0]); o["aim"] = pl(inp["ssm_A_im"][0])
    ldt = f(inp["ssm_log_dt"][0])
    o["ldt"] = pl(np.repeat(ldt[:, None], 64, axis=1))
    pb = lambda a: np.ascontiguousarray(f(a).reshape(16, 2, 64, 16).transpose(1, 2, 0, 3).reshape(128, 16, 16))
    o["bre"] = pb(inp["ssm_B_re"][0]); o["bim"] = pb(inp["ssm_B_im"][0])
    pc = lambda a: np.ascontiguousarray(f(a).transpose(0, 2, 1).reshape(16, 2, 64, 16).transpose(1, 2, 0, 3).reshape(128, 16, 16))
    o["cre"] = pc(inp["ssm_C_re"][0]); o["cim"] = pc(inp["ssm_C_im"][0])
    return o


def build_program():
    nc = bass.Bass("TRN2", target_bir_lowering=False)
    P = Prog()
    stack = ExitStack()
    dr = {}

    def din(name, shape, dt=F32):
        dr[name] = nc.dram_tensor(name, list(shape), dt, kind="ExternalInput").ap()
        return dr[name]

    x_d = din("x", [D, SEQ])
    for n in ("wg1", "wu1", "wg2", "wu2"):
        din(n, [D, FF])
    for n in ("wd1", "wd2"):
        din(n, [FF, D])
    din("win", [D, NWIN]); din("wout", [D, D]); din("wglu", [512, 512])
    din("gains", [128, 44]); din("dblk", [128, 32]); din("sinkrow", [128, 4])
    for n in ("are", "aim", "ldt"):
        din(n, [128, 16])
    for n in ("bre", "bim", "cre", "cim"):
        din(n, [128, 16, 16])
    din("identf", [128, 128]); din("identb", [128, 128], BF16); din("onesb", [128, 128], BF16)
    din("permb", [128, 128], BF16); din("mask", [128, 512], BF16); din("maskf", [128, 512], BF16)
    din("sel", [128, 8, 240], BF16); din("ropec", [128, SEQ]); din("ropes", [128, SEQ])
    out_d = nc.dram_tensor("out", [D, SEQ], F32, kind="ExternalOutput").ap()
    dbg_d = {}

    def sb_main(name, shape, dt):
        return stack.enter_context(nc.sbuf_tensor("s_" + name, list(shape), dt))

    sb = sb_main

    wgu = sb("wgu", [128, 4, 2, 4, 128], BF16)
    wd = sb("wd", [128, 3, 2, 512], BF16)
    wis = sb("wis", [128, 4, KT, 128], BF16)
    wo = sb("wo", [128, 2, KT, 128], BF16)
    w_glu = sb("w_glu", [128, 4, 512], BF16)
    sgs = sb("sgs", [128, 2, TS], F32)
    sqF = sb("sqF", [128, 2, TS], BF16)
    sqM = sb("sqM", [128, 1, TS], BF16)
    rsF = sb("rsF", [128, 2, TS], F32)
    rsM = sb("rsM", [128, 2, TS], F32)
    tabR = sb("tabR", [128, 16, 128], BF16)
    tabI = sb("tabI", [128, 16, 128], BF16)
    Wm = sb("Wm", [128, 32, 128], BF16)
    Mm = sb("Mm", [128, 32, 128], BF16)
    cosT = sb("cosT", [128, 16, NB], F32)
    sinT = sb("sinT", [128, 16, NB], F32)
    r8 = sb("r8", [128, 16], F32)
    sel = sb("sel", [128, 8, 240], BF16)
    dblk = sb("dblk", [128, 32], F32)
    gains = sb("gains", [128, 44], F32)
    esink = sb("esink", [128, 4], F32)
    cst = sb("cst", [128, 4], F32)
    identf = sb("identf", [128, 128], F32)
    identb = sb("identb", [128, 128], BF16)
    onesb = sb("onesb", [128, 128], BF16)
    permb = sb("permb", [128, 128], BF16)
    mask = sb("mask", [128, 512], BF16)
    maskf = sb("maskf", [128, 512], BF16)
    kTd = sb("kTd", [128, 2, 5 * 128], BF16)
    vsb = sb("vsb", [128, 5, 128], BF16)
    PT = sb("PT", [128, 2, 512], BF16)
    ropec = sb("ropec", [128, TS], F32)
    ropes = sb("ropes", [128, TS], F32)
    qpre = sb("qpre", [128, 1, TS], BF16)
    tmpf = sb("tmpf", [128, 3, TS], F32)
    Gin = sb("Gin", [128, 2, 4, NB], F32)
    Gs = sb("Gs", [128, 2, 4, NB], F32)
    Hb = sb("Hb", [128, 16, 2, NB + 1], BF16)
    Hc = sb("Hc", [128, 16, 2], F32)

    ps = [stack.enter_context(nc.psum_tensor("ps%d" % i, [128, 512], F32)) for i in range(8)]

    rrM = [0]
    rrF = [0]

    def bankM():
        b = rrM[0] % 3
        rrM[0] += 1
        return b

    def bankF():
        b = 4 + rrF[0] % 4
        rrF[0] += 1
        return b

    bankA = bankM
    BATT = 3

    def mm(out, lhsT, rhs, start, stop, reads, writes):
        P.add("pe", lambda e: e.matmul(out, lhsT=lhsT, rhs=rhs, start=start, stop=stop), reads, writes)

    def tr(out, in_, reads, writes):
        P.add("pe", lambda e: e.transpose(out, in_, identf[:]), reads + ["identf"], writes)

    def actf(out, in_, func, reads, writes, bias=None, scale=None):
        kw = {}
        if bias is not None:
            kw["bias"] = bias
        if scale is not None:
            kw["scale"] = scale
        P.add("act", lambda e: e.activation(out=out, in_=in_, func=func, **kw), reads, writes)

    def tt(out, in0, in1, op, reads, writes, eng="dve"):
        P.add(eng, lambda e: e.tensor_tensor(out=out, in0=in0, in1=in1, op=op), reads, writes)

    def ts1(out, in0, s1, op0, reads, writes, eng="dve"):
        P.add(eng, lambda e: e.tensor_single_scalar(out=out, in_=in0, scalar=s1, op=op0), reads, writes)

    def stt(out, in0, scalar, in1, op0, op1, reads, writes):
        P.add("dve", lambda e: e.scalar_tensor_tensor(out=out, in0=in0, scalar=scalar, in1=in1, op0=op0, op1=op1), reads, writes)

    def cp(out, in_, reads, writes, eng="dve"):
        if eng == "act":
            P.add("act", lambda e: e.activation(out=out, in_=in_, func=AF.Copy), reads, writes)
        else:
            P.add(eng, lambda e: e.tensor_copy(out=out, in_=in_), reads, writes)

    def recip(out, in_, reads, writes):
        P.add("dve", lambda e: e.reciprocal(out=out, in_=in_), reads, writes)

    def memset(ap, val, writes, eng="dve"):
        P.add(eng, lambda e: e.memset(ap, val), [], writes)

    def dma(q, out, in_, reads, writes, key):
        P.add(q, lambda e: e.dma_start(out=out, in_=in_), reads, writes, dma=key)

    def dump(name, ap, reads, shape, dt=F32):
        if not DEBUG:
            return
        if name not in dbg_d:
            dbg_d[name] = nc.dram_tensor("dbg_" + name, list(shape), dt, kind="ExternalOutput").ap()
        dma("sp", dbg_d[name], ap, reads, [("dbg", name)], "dbg_" + name)

    def ld(q, t, src, key):
        dma(q, t[:], src, [], [key], "ld_" + key)

    ld("sp", identf, dr["identf"], "identf"); ld("sp", identb, dr["identb"], "identb")
    ld("sp", onesb, dr["onesb"], "onesb"); ld("sp", permb, dr["permb"], "permb")
    ld("sp", mask, dr["mask"], "mask"); ld("sp", maskf, dr["maskf"], "maskf")
    ld("sp", sel, dr["sel"], "sel"); ld("sp", gains, dr["gains"], "gains")
    ld("sp", dblk, dr["dblk"], "dblk"); ld("sp", esink, dr["sinkrow"], "esink")
    memset(cst[:, 0:1], EPS, ["cst"])
    memset(cst[:, 1:2], math.pi / 2, ["cst"])
    memset(cst[:, 2:3], 4.0 * EPS, ["cst"])
    memset(kTd[:], 0.0, ["kTd"])
    memset(vsb[:], 0.0, ["vsb"])
    memset(Hc[:], 0.0, ["Hc"])
    memset(Hb[:], 0.0, ["Hb"])
    actf(esink[:], esink[:], AF.Exp, ["esink"], ["esink"])
    hbg = sb("hbg", [128, 4], F32)
    ts1(hbg[:], gains[:, 40:44], 0.5, ALU.mult, ["gains"], ["hbg"])

    scr_gu = {fid: nc.dram_tensor("scr_gu%d" % fid, [2 * FT, 128, 2 * 4 * 128], BF16).ap() for fid in (1, 2)}
    scr_d = {fid: nc.dram_tensor("scr_d%d" % fid, [22, 128, 2 * 512], BF16).ap() for fid in (1, 2)}
    scr_wo = nc.dram_tensor("scr_wo", [8, 128, KT * 128], BF16).ap()
    scr_wi = nc.dram_tensor("scr_wi", [11, 128, KT * 128], BF16).ap()

    ffn_order = [1]
    for s in range(NT_RUN - 1):
        ffn_order += [1, 2]
    ffn_order.append(2)
    ffn_w = {1: ("wg1", "wu1", "wd1"), 2: ("wg2", "wu2", "wd2")}
    wgu_loads = [(fid, f, kh) for fid in ffn_order for f in range(FT) for kh in range(2)]
    wd_loads = [(fid, half, jc) for fid in ffn_order for half in range(2) for jc in range(11)]
    wgu_ptr = [0]
    wd_ptr = [0]
    seen_gu = set()
    seen_d = set()
    multi = NT_RUN > 1

    NCV = 16
    cvi = [0]

    def conv(out_ap, in_ap, key):
        i = cvi[0] % NCV
        cvi[0] += 1
        dma("pool", out_ap, in_ap, [], [key, ("cvslot", i)], "cv%d" % i)

    def conv_gu(fid):
        gname, uname, _ = ffn_w[fid]
        for f in range(FT):
            for kh in range(2):
                scr = scr_gu[fid][2 * f + kh].rearrange("p (g k c) -> p g k c", g=2, k=4)
                for gi, nm in enumerate((gname, uname)):
                    src = dr[nm].rearrange("(k p) f -> p k f", p=128)[:, 4 * kh:4 * kh + 4, f * 128:(f + 1) * 128]
                    conv(scr[:, gi], src, ("scr_gu", fid, f, kh, gi))

    def conv_d(fid):
        for half in range(2):
            for jc in range(11):
                scr = scr_d[fid][half * 11 + jc].rearrange("p (f d) -> p f d", f=2)
                src = dr[ffn_w[fid][2]].rearrange("(f p) d -> p f d", p=128)[:, 2 * jc:2 * jc + 2, half * 512:(half + 1) * 512]
                conv(scr, src, ("scr_d", fid, half, jc))

    P.tag = 'prep'
    dma("pool", w_glu[:], dr["wglu"].rearrange("(k p) f -> p k f", p=128), [], ["w_glu"], "ld_w_glu")
    conv_gu(1)
    conv_d(1)
    for cg in range(11):
        conv(scr_wi[cg].rearrange("p (k c) -> p k c", k=KT), dr["win"].rearrange("(k p) f -> p k f", p=128)[:, :, cg * 128:(cg + 1) * 128], ("scr_wi", cg))
    for o in range(8):
        conv(scr_wo[o].rearrange("p (k c) -> p k c", k=KT), dr["wout"].rearrange("(k p) d -> p k d", p=128)[:, :, o * 128:(o + 1) * 128], ("scr_wo", o))
    conv_gu(2)
    conv_d(2)

    def wgu_ensure(upto):
        while wgu_ptr[0] <= upto and wgu_ptr[0] < len(wgu_loads):
            L = wgu_ptr[0]
            fid, f, kh = wgu_loads[L]
            slot = L % 4
            scr = scr_gu[fid][2 * f + kh].rearrange("p (g k c) -> p g k c", g=2, k=4)
            dma("sp", wgu[:, slot], scr, [("scr_gu", fid, f, kh, 0), ("scr_gu", fid, f, kh, 1)], [("wgu", slot, 0), ("wgu", slot, 1)], "wgu%d" % slot)
            wgu_ptr[0] += 1

    def wd_ensure(upto):
        while wd_ptr[0] <= upto and wd_ptr[0] < len(wd_loads):
            L = wd_ptr[0]
            fid, half, jc = wd_loads[L]
            slot = L % 3
            scr = scr_d[fid][half * 11 + jc].rearrange("p (f d) -> p f d", f=2)
            dma("sp", wd[:, slot], scr, [("scr_d", fid, half, jc)], [("wd", slot)], "wd%d" % slot)
            wd_ptr[0] += 1

    wgu_use = [0]
    wd_use = [0]

    wi_ptr = [0]
    wi_use = [0]
    wo_ptr = [0]
    wo_use = [0]

    def wi_ensure(upto):
        while wi_ptr[0] <= upto and wi_ptr[0] < 11 * NT_RUN:
            L = wi_ptr[0]
            cg = L % 11
            slot = L % 4
            dma("sp", wis[:, slot], scr_wi[cg].rearrange("p (k c) -> p k c", k=KT), [("scr_wi", cg)], [("wis", slot)], "wis%d" % slot)
            wi_ptr[0] += 1

    def wo_ensure(upto):
        while wo_ptr[0] <= upto and wo_ptr[0] < 8 * NT_RUN:
            L = wo_ptr[0]
            o = L % 8
            slot = L % 2
            dma("sp", wo[:, slot], scr_wo[o].rearrange("p (k c) -> p k c", k=KT), [("scr_wo", o)], [("wo", slot)], "wo%d" % slot)
            wo_ptr[0] += 1

    def ssm_precompute():
        P.tag = 'pre'
        are = sb("p_are", [128, 16], F32); aim = sb("p_aim", [128, 16], F32); ldt = sb("p_ldt", [128, 16], F32)
        bre = sb("p_bre", [128, 16, 16], F32); bim = sb("p_bim", [128, 16, 16], F32)
        cre = sb("p_cre", [128, 16, 16], F32); cim = sb("p_cim", [128, 16, 16], F32)
        sm = sb("p_sm", [128, 24, 16], F32)
        pw = sb("p_pw", [128, 2, 16, 9], F32)
        bb = sb("p_bb", [128, 2, 16, 16], F32)
        bbp = sb("p_bbp", [128, 2, 2, 240], BF16)
        wtt = sb("p_wt", [128, 2, 2, 8, 16], F32)
        big = sb("p_big", [128, 2, 16, 64], F32)
        for nm, t in (("are", are), ("aim", aim), ("ldt", ldt), ("bre", bre), ("bim", bim), ("cre", cre), ("cim", cim)):
            dma("sp", t[:], dr[nm], [], ["p_" + nm], "ld_p_" + nm)
        S = lambda i: sm[:, i, :]
        K = "p_sm"
        rk = ["p_are", "p_aim", "p_ldt", K, "cst"]
        DT, AR, TH, MAG, C0, S0, CC, SS, CS, LBR, LBI, DEN, NRE, T1, T2, ZR, ZI, C8, S8, T3 = range(20)
        actf(S(DT), ldt[:], AF.Exp, rk, [K])
        tt(S(AR), are[:], S(DT), ALU.mult, rk, [K])
        tt(S(TH), aim[:], S(DT), ALU.mult, rk, [K])
        actf(S(MAG), S(AR), AF.Exp, rk, [K])
        actf(r8[:], S(AR), AF.Exp, rk, ["r8"], scale=8.0)
        actf(S(S0), S(TH), AF.Sin, rk, [K], scale=1.0 / 16)
        actf(S(C0), S(TH), AF.Sin, rk, [K], scale=1.0 / 16, bias=cst[:, 1:2])
        yield

        def dbl():
            tt(S(CC), S(C0), S(C0), ALU.mult, rk, [K])
            tt(S(SS), S(S0), S(S0), ALU.mult, rk, [K])
            tt(S(CS), S(C0), S(S0), ALU.mult, rk, [K])
            tt(S(C0), S(CC), S(SS), ALU.subtract, rk, [K])
            ts1(S(S0), S(CS), 2.0, ALU.mult, rk, [K])
        for _ in range(4):
            dbl()
            yield
        tt(S(LBR), S(MAG), S(C0), ALU.mult, rk, [K])
        tt(S(LBI), S(MAG), S(S0), ALU.mult, rk, [K])
        for _ in range(3):
            dbl()
            yield
        cp(S(C8), S(C0), rk, [K]); cp(S(S8), S(S0), rk, [K])
        tt(S(T1), are[:], are[:], ALU.mult, rk, [K])
        tt(S(T2), aim[:], aim[:], ALU.mult, rk, [K])
        tt(S(DEN), S(T1), S(T2), ALU.add, rk, [K])
        recip(S(DEN), S(DEN), rk, [K])
        ts1(S(NRE), S(LBR), -1.0, ALU.add, rk, [K])
        tt(S(T1), S(NRE), are[:], ALU.mult, rk, [K])
        tt(S(T2), S(LBI), aim[:], ALU.mult, rk, [K])
        tt(S(T1), S(T1), S(T2), ALU.add, rk, [K])
        tt(S(ZR), S(T1), S(DEN), ALU.mult, rk, [K])
        tt(S(T1), S(LBI), are[:], ALU.mult, rk, [K])
        tt(S(T2), S(NRE), aim[:], ALU.mult, rk, [K])
        tt(S(T1), S(T1), S(T2), ALU.subtract, rk, [K])
        tt(S(ZI), S(T1), S(DEN), ALU.mult, rk, [K])
        yield
        kp = ["p_pw", K]
        memset(pw[:, 0, :, 0:1], 1.0, ["p_pw"]); memset(pw[:, 1, :, 0:1], 0.0, ["p_pw"])
        for k in range(1, 9):
            pr, pi_ = pw[:, 0, :, k - 1], pw[:, 1, :, k - 1]
            tt(S(T1), pr, S(LBR), ALU.mult, kp, [K]); tt(S(T2), pi_, S(LBI), ALU.mult, kp, [K])
            tt(pw[:, 0, :, k], S(T1), S(T2), ALU.subtract, kp, ["p_pw"])
            tt(S(T1), pr, S(LBI), ALU.mult, kp, [K]); tt(S(T2), pi_, S(LBR), ALU.mult, kp, [K])
            tt(pw[:, 1, :, k], S(T1), S(T2), ALU.add, kp, ["p_pw"])
            yield
        bc16 = lambda ap2: ap2.unsqueeze(2).to_broadcast([128, 16, 16])
        kb = ["p_bre", "p_bim", K, "p_bb", "p_big"]
        t1 = big[:, 0, :, 0:16]; t2 = big[:, 1, :, 0:16]
        tt(t1, bre[:], bc16(S(ZR)), ALU.mult, kb, ["p_big"]); tt(t2, bim[:], bc16(S(ZI)), ALU.mult, kb, ["p_big"])
        tt(bb[:, 0], t1, t2, ALU.subtract, kb, ["p_bb"])
        tt(t1, bim[:], bc16(S(ZR)), ALU.mult, kb, ["p_big"]); tt(t2, bre[:], bc16(S(ZI)), ALU.mult, kb, ["p_big"])
        tt(bb[:, 1], t1, t2, ALU.add, kb, ["p_bb"])
        memset(bbp[:], 0.0, [("p_bbp", 0), ("p_bbp", 1)])
        yield
        kc = ["p_cre", "p_cim", "p_pw", "p_big", "p_wt"]
        tabRe = sb("p_tabRe", [128, 16, 256], BF16); tabIm = sb("p_tabIm", [128, 16, 256], BF16)
        memset(tabRe[:], 0.0, ["tabRe"]); memset(tabIm[:], 0.0, ["tabIm"])
        for tau in range(9):
            pr = pw[:, 0, :, tau:tau + 1].to_broadcast([128, 16, 16])
            pi_ = pw[:, 1, :, tau:tau + 1].to_broadcast([128, 16, 16])
            o_re = tabRe[:, :, 112 + tau * 16:112 + (tau + 1) * 16]; o_im = tabIm[:, :, 112 + tau * 16:112 + (tau + 1) * 16]
            ta = wtt[:, 0].rearrange("p r i c -> p (r i) c")
            tb = wtt[:, 1].rearrange("p r i c -> p (r i) c")
            tt(ta, cre[:], pr, ALU.mult, kc, ["p_wt"]); tt(tb, cim[:], pi_, ALU.mult, kc, ["p_wt"])
            tt(o_re, ta, tb, ALU.subtract, kc, ["tabRe"])
            tt(ta, cre[:], pi_, ALU.mult, kc, ["p_wt"]); tt(tb, cim[:], pr, ALU.mult, kc, ["p_wt"])
            tt(tb, ta, tb, ALU.add, kc, ["p_wt"])
            ts1(o_im, tb, -1.0, ALU.mult, kc, ["tabIm"])
            yield
        yield "PE_PART"
        pwr = sb("p_pwr", [128, 3, 16, 8], F32)
        for i in range(8):
            cp(pwr[:, 0, :, i:i + 1], pw[:, 0, :, 7 - i:8 - i], ["p_pw"], ["p_pwr"])
            cp(pwr[:, 1, :, i:i + 1], pw[:, 1, :, 7 - i:8 - i], ["p_pw"], ["p_pwr"])
        ts1(pwr[:, 2], pwr[:, 1], -1.0, ALU.mult, ["p_pwr"], ["p_pwr"])
        kw_ = ["p_pwr", "p_bb", "p_wt", "p_big"]
        for t in range(16):
            buf = t % 2
            wre = wtt[:, buf, 0]; wim = wtt[:, buf, 1]
            pr = pwr[:, 0, t, :].unsqueeze(2).to_broadcast([128, 8, 16])
            pi_ = pwr[:, 1, t, :].unsqueeze(2).to_broadcast([128, 8, 16])
            br = bb[:, 0, t, :].unsqueeze(1).to_broadcast([128, 8, 16])
            bi = bb[:, 1, t, :].unsqueeze(1).to_broadcast([128, 8, 16])
            x1 = big[:, 0, 0:8, 0:16]; x2 = big[:, 1, 0:8, 0:16]
            tt(x1, pr, br, ALU.mult, kw_, ["p_big"]); tt(x2, pi_, bi, ALU.mult, kw_, ["p_big"])
            tt(wre, x1, x2, ALU.subtract, kw_, ["p_wt"])
            tt(x1, pr, bi, ALU.mult, kw_, ["p_big"]); tt(x2, pi_, br, ALU.mult, kw_, ["p_big"])
            tt(wim, x1, x2, ALU.add, kw_, ["p_wt"])
            b = bankA()
            tr(ps[b][:, 0:128], wre.rearrange("p i c -> p (i c)"), ["p_wt"], [("ps", b)])
            tr(ps[b][:, 128:256], wim.rearrange("p i c -> p (i c)"), ["p_wt"], [("ps", b)])
            src = ps[b][:, 0:256].rearrange("p (r g q) -> p g r q", r=2, g=2)
            dst = Wm[:, 2 * t:2 * t + 2, :].rearrange("p g (r q) -> p g r q", r=2)
            cp(dst, src, [("ps", b)], ["Wm"])
            yield
        for g in range(32):
            t, g2 = g // 2, g % 2
            buf = t % 2
            if g2 == 0:
                cp(bbp[:, buf, 0, 112:128], bb[:, 0, t, :], ["p_bb"], [("p_bbp", buf)])
                cp(bbp[:, buf, 1, 112:128], bb[:, 1, t, :], ["p_bb"], [("p_bbp", buf)])
            rows = slice(g2 * 64, (g2 + 1) * 64)
            b = bankA()
            n = 0
            for i in range(8):
                off = (7 - i) * 16
                for ri, tab in ((0, tabRe), (1, tabIm)):
                    mm(ps[b][:, 0:128], bbp[rows, buf, ri, off:off + 128], tab[rows, t, off:off + 128],
                       n == 0, n == 15, [("p_bbp", buf), "tabRe", "tabIm"], [("ps", b)])
                    n += 1
            cp(Mm[:, g, :], ps[b][:, 0:128], [("ps", b)], ["Mm"], eng="act")
            yield
        cp(tabR[:], tabRe[:, :, 128:256], ["tabRe"], ["tabR"])
        cp(tabI[:], tabIm[:, :, 128:256], ["tabIm"], ["tabI"])
        kt = ["cosT", "sinT", K, "p_wt", "p_big"]
        cp(cosT[:, :, 0:1], S(C8).unsqueeze(2), kt, ["cosT"]); cp(sinT[:, :, 0:1], S(S8).unsqueeze(2), kt, ["sinT"])
        m = 1
        while m < NB:
            cr = cosT[:, :, m - 1:m].to_broadcast([128, 16, m]); sr = sinT[:, :, m - 1:m].to_broadcast([128, 16, m])
            a1 = big[:, 0, :, 0:m]; a2 = big[:, 1, :, 0:m]; a3 = big[:, 0, :, 32:32 + m]; a4 = big[:, 1, :, 32:32 + m]
            tt(a1, cosT[:, :, 0:m], cr, ALU.mult, kt, ["p_big"]); tt(a2, sinT[:, :, 0:m], sr, ALU.mult, kt, ["p_big"])
            tt(a3, cosT[:, :, 0:m], sr, ALU.mult, kt, ["p_big"]); tt(a4, sinT[:, :, 0:m], cr, ALU.mult, kt, ["p_big"])
            tt(cosT[:, :, m:2 * m], a1, a2, ALU.subtract, kt, ["cosT"])
            tt(sinT[:, :, m:2 * m], a3, a4, ALU.add, kt, ["sinT"])
            m *= 2
            yield

    pstack = ExitStack()

    def sbp(name, shape, dt):
        return pstack.enter_context(nc.sbuf_tensor("s_" + name, list(shape), dt))

    xTa = sb("xTa", [128, KT, TS], F32)
    hTF = sb("hTF", [128, KT, TS], BF16)
    act = sb("act", [128, FT, TS], BF16)
    xsel = lambda par: xTa if par == 0 else xTb

    def xT(par, k):
        return xsel(par)[:, k, :]

    def xk(par, k):
        return ("xT", par, k)

    def rstd_from(b, n, dst, dkey, ec=0):
        actf(dst, ps[b][:, :], AF.Sqrt, [("ps", b), "cst"], [dkey], bias=cst[:, ec:ec + 1], scale=1.0 / n)
        recip(dst, dst, [dkey], [dkey])

    def norm_to(par, goff, hT, hkey, sqs, rdst, rkey, b):
        for k in range(KT):
            sap, skey = sqs[k % len(sqs)]
            actf(sap, xT(par, k), AF.Square, [xk(par, k)], [skey])
            mm(ps[b][:, :], onesb[:], sap, k == 0, k == KT - 1, [skey, "onesb"], [("ps", b)])
        rstd_from(b, D, rdst, rkey)
        for k in range(KT):
            stt(hT[:, k, :], xT(par, k), gains[:, goff + k:goff + k + 1], rdst, ALU.mult, ALU.mult,
                [xk(par, k), "gains", rkey], [(hkey, k)])

    def ffn_norm_gen(goff, par):
        P.tag = 'ffn.norm'
        norm_to(par, goff, hTF, "hTF", [(sqF[:, 0, :], ("sqF", 0)), (sqF[:, 1, :], ("sqF", 1))], rsF[:, 0, :], ("rsF", 0), 4)
        yield

    def ffn_gen(fid, goff, par, do_norm=True):
        if do_norm:
            P.tag = 'ffn.norm'
            norm_to(par, goff, hTF, "hTF", [(sqF[:, 0, :], ("sqF", 0)), (sqF[:, 1, :], ("sqF", 1))], rsF[:, 0, :], ("rsF", 0), 4)
            yield
        for f in range(FT):
            P.tag = 'ffn.gu'
            bg, bu = (4, 5) if f % 2 == 0 else (6, 7)
            for kh in range(2):
                L = wgu_use[0]
                wgu_use[0] += 1
                slot = L % 4
                wgu_ensure(L + 3)
                for k4 in range(4):
                    k = 4 * kh + k4
                    mm(ps[bg][:, :], wgu[:, slot, 0, k4, :], hTF[:, k, :], k == 0, k == KT - 1,
                       [("wgu", slot, 0), ("hTF", k)], [("ps", bg)])
                for k4 in range(4):
                    k = 4 * kh + k4
                    mm(ps[bu][:, :], wgu[:, slot, 1, k4, :], hTF[:, k, :], k == 0, k == KT - 1,
                       [("wgu", slot, 1), ("hTF", k)], [("ps", bu)])
            s_ = f % 2
            actf(sgs[:, s_, :], ps[bg][:, :], AF.Tanh, [("ps", bg)], [("sgs", s_)], scale=0.5)
            stt(sgs[:, s_, :], sgs[:, s_, :], 1.0, ps[bg][:, :], ALU.add, ALU.mult, [("sgs", s_), ("ps", bg)], [("sgs", s_)])
            tt(act[:, f, :], sgs[:, s_, :], ps[bu][:, :], ALU.mult, [("sgs", s_), ("ps", bu)], [("act", f)])
            yield
        if wd_ptr[0] == 0:
            wd_ensure(1)
        for half in range(2):
            bs = [4, 5, 6, 7]
            for jc in range(11):
                P.tag = 'ffn.down'
                L = wd_use[0]
                wd_use[0] += 1
                slot = L % 3
                wd_ensure(L + 2)
                for f2 in range(2):
                    f = 2 * jc + f2
                    for o in range(4):
                        mm(ps[bs[o]][:, :], wd[:, slot, f2, o * 128:(o + 1) * 128], act[:, f, :], f == 0, f == FT - 1,
                           [("wd", slot), ("act", f)], [("ps", bs[o])])
                if jc == 10:
                    for o in range(4):
                        k = half * 4 + o
                        stt(xT(par, k), ps[bs[o]][:, :], 0.25, xT(par, k), ALU.mult, ALU.add, [("ps", bs[o]), xk(par, k)], [xk(par, k)])
                yield

    def loadx_gen(s):
        par = s % 2
        P.tag = 'loadx'
        for k in range(KT):
            dma("sp", xsel(par)[:, k, :], x_d[k * 128:(k + 1) * 128, s * TS:(s + 1) * TS], [], [xk(par, k)], "xl%d" % k)
        yield

    def xn_tile(k):
        return act[:, 2 * k:2 * k + 2, :].rearrange("p a t -> p (a t)").bitcast(F32)

    def final_gen(s):
        P.tag = 'final'
        par = s % 2
        b = 4
        for k in range(KT):
            s_ = k % 2
            actf(sqF[:, s_, :], xT(par, k), AF.Square, [xk(par, k)], [("sqF", s_)])
            mm(ps[b][:, :], onesb[:], sqF[:, s_, :], k == 0, k == KT - 1, [("sqF", s_), "onesb"], [("ps", b)])
        rstd_from(b, D, rsF[:, 1, :], ("rsF", 1))
        yield
        for k in range(KT):
            P.tag = 'final'
            stt(xn_tile(k), xT(par, k), gains[:, 24 + k:25 + k], rsF[:, 1, :], ALU.mult, ALU.mult,
                [xk(par, k), "gains", ("rsF", 1)], [("act", 2 * k), ("act", 2 * k + 1)])
            dma("sp", out_d[k * 128:(k + 1) * 128, s * TS:(s + 1) * TS], xn_tile(k),
                [("act", 2 * k), ("act", 2 * k + 1)], [("out", s, k)], "out%d" % k)
        yield

    uTd = lambda m: am[:, m, :].rearrange("p (i n) -> p i n", i=8)
    qT = lambda m: am[:, 4 + m, :]
    Ugrp = lambda g: am[:, 8 + g // 8, (g % 8) * NB:(g % 8 + 1) * NB]
    Y2grp = lambda g: am[:, g // 8, (g % 8) * NB:(g % 8 + 1) * NB]
    y2T = lambda m: am[:, 12 + m, :]
    yg = lambda m: (am[:, 4 + m, :], ("am", 4 + m))
    yat = lambda m: (am[:, 16 + m, :], ("am", 16 + m))

    def rope_finish(qs, out_ap, out_key):
        qap, qkey = qs
        b2 = bankM()
        mm(ps[b2][:, :], permb[:], qap, True, True, ["permb", qkey], [("ps", b2)])
        tt(tmpf[:, 0, :], qap, ropec[:], ALU.mult, [qkey, "ropec"], [("tmpf", 0), ("tmpf", "0b")])
        tt(tmpf[:, 1, :], ps[b2][:, :], ropes[:], ALU.mult, [("ps", b2), "ropes"], [("tmpf", 1)])
        tt(out_ap, tmpf[:, 0, :], tmpf[:, 1, :], ALU.add, [("tmpf", 0), ("tmpf", 1)], [out_key])

    def mixer_gen(s):
        par = s % 2
        P.tag = 'mix.norm'
        if wi_ptr[0] == 0:
            wi_ensure(2)
            wo_ensure(0)
        dma("sp", ropec[:], dr["ropec"][:, s * TS:(s + 1) * TS], [], ["ropec"], "ropec")
        dma("sp", ropes[:], dr["ropes"][:, s * TS:(s + 1) * TS], [], ["ropes"], "ropes")
        norm_to(par, 8, hTM, "hTM", [(sqM[:, 0, :], ("sqM", 0)), (qpre[:, 0, :], ("qpre", 0))], rsM[:, 0, :], ("rsM", 0), bankM())
        yield

        def proj_group():
            L = wi_use[0]
            wi_use[0] += 1
            slot = L % 4
            wi_ensure(L + 3)
            b = bankM()
            for k in range(KT):
                mm(ps[b][:, :], wis[:, slot, k, :], hTM[:, k, :], k == 0, k == KT - 1, [("wis", slot), ("hTM", k)], [("ps", b)])
            return b

        for m in range(4):
            P.tag = 'mix.proj'
            b = proj_group()
            actf(uTd(m), ps[b][:, :].rearrange("p (n i) -> p i n", i=8), AF.Copy, [("ps", b)], [("am", m)])
            yield
        jobs = [(qT(m), ("am", 4 + m)) for m in range(4)]
        jobs += [(kTd[:, kk, 128:640], ("kTd", kk)) for kk in range(2)]
        QS = [(qpre[:, 0, :], ("qpre", 0)), (sqM[:, 0, :], ("sqM", 0))]
        pending = None
        for idx, (out_ap, out_key) in enumerate(jobs):
            P.tag = 'mix.proj'
            b = proj_group()
            qs = QS[idx % 2]
            actf(qs[0], ps[b][:, :], AF.Copy, [("ps", b)], [qs[1]])
            if pending is not None:
                rope_finish(*pending)
            pending = (qs, out_ap, out_key)
            yield
        P.tag = 'mix.proj'
        L = wi_use[0]
        wi_use[0] += 1
        slot = L % 4
        wi_ensure(L + 3)
        b = bankM()
        for blk in range(4):
            for k in range(KT):
                mm(ps[b][:, blk * 128:(blk + 1) * 128], hTM[:, k, blk * 128:(blk + 1) * 128], wis[:, slot, k, :],
                   k == 0, k == KT - 1, [("wis", slot), ("hTM", k)], [("ps", b)])
        rope_finish(*pending)
        actf(vsb[:, 1:5, :], ps[b][:, :].rearrange("p (a d) -> p a d", a=4), AF.Copy, [("ps", b)], ["vsb"])
        yield

        def attn_gen():
            ptc = 0
            units = [(m, pair) for m in range(4) for pair in range(2)]

            def a1(m, pair, hh, pi_):
                kv = m // 2
                rows = slice(hh * 64, (hh + 1) * 64)
                bs_ = bankM()
                first = (s == 0 and pair == 0)
                mk, mkk = (maskf, "maskf") if first else (mask, "mask")
                mm(ps[bs_][:, :], identb[:], mk[:], True, False, ["identb", mkk], [("ps", bs_)])
                n = 0
                for qb in range(2):
                    blkq = 2 * pair + qb
                    for piece in range(2):
                        slot = blkq + piece
                        mm(ps[bs_][:, (qb * 2 + piece) * 128:(qb * 2 + piece + 1) * 128],
                           kTd[rows, kv, slot * 128:(slot + 1) * 128], qT(m)[rows, blkq * 128:(blkq + 1) * 128],
                           False, n == 3, [("kTd", kv), ("am", 4 + m)], [("ps", bs_)])
                        n += 1
                actf(PT[:, pi_, :], ps[bs_][:, :], AF.Exp, [("ps", bs_)], [("PT", pi_)], scale=0.125)

            def pv(m, pair, hh, pi_):
                kv = m // 2
                rows = slice(hh * 64, (hh + 1) * 64)
                for qb in range(2):
                    blkq = 2 * pair + qb
                    ncol = slice(qb * 128, (qb + 1) * 128)
                    dcol = slice(256 + qb * 128, 256 + (qb + 1) * 128)
                    for piece in range(2):
                        slot = blkq + piece
                        mm(ps[BATT][rows, ncol], vsb[:, slot, kv * 64:(kv + 1) * 64], PT[:, pi_, (qb * 2 + piece) * 128:(qb * 2 + piece + 1) * 128],
                           piece == 0, piece == 1, ["vsb", ("PT", pi_)], [("ps", BATT)])
                    for piece in range(2):
                        mm(ps[BATT][rows, dcol], onesb[:, 0:64], PT[:, pi_, (qb * 2 + piece) * 128:(qb * 2 + piece + 1) * 128],
                           piece == 0, piece == 1, ["onesb", ("PT", pi_)], [("ps", BATT)])

            def norm_unit(m, pair):
                den = tmpf[:, 2, 0:256]
                ts1(den, ps[BATT][:, 256:512], esink[:, m:m + 1], ALU.add, [("ps", BATT), "esink"], [("tmpf", 2)])
                recip(den, den, [("tmpf", 2)], [("tmpf", 2)])
                ya, yak = yat(m)
                tt(ya[:, pair * 256:(pair + 1) * 256], ps[BATT][:, 0:256], den, ALU.mult, [("ps", BATT), ("tmpf", 2)], [yak])

            a1(0, 0, 0, 0); yield
            a1(0, 0, 1, 1); yield
            for ui, (m, pair) in enumerate(units):
                nxt = units[ui + 1] if ui + 1 < len(units) else None
                pv(m, pair, 0, 0); yield
                if nxt:
                    a1(nxt[0], nxt[1], 0, 0); yield
                pv(m, pair, 1, 1); yield
                norm_unit(m, pair)
                if nxt:
                    a1(nxt[0], nxt[1], 1, 1)
                yield
            b = bankM()
            for m in range(4):
                ya, yak = yat(m)
                actf(sqM[:, 0, :], ya, AF.Square, [yak], [("sqM", 0)])
                mm(ps[b][:, :], onesb[:], sqM[:, 0, :], m == 0, m == 3, [("sqM", 0), "onesb"], [("ps", b)])
            rstd_from(b, 512, rsM[:, 1, :], ("rsM", 1))
            yield

        def ssm_gen():
            cp(Hb[:, :, :, 0:1], Hc[:].unsqueeze(3), ["Hc"], ["Hb"])
            vbank = {}

            def U_unit(m):
                b = bankM()
                for gg in range(8):
                    for i in range(8):
                        mm(ps[b][:, gg * NB:(gg + 1) * NB], sel[:, gg, (7 - i) * 16:(7 - i) * 16 + 128], uTd(m)[:, i, :],
                           i == 0, i == 7, ["sel", ("am", m)], [("ps", b)])
                actf(am[:, 8 + m, :], ps[b][:, :], AF.Copy, [("ps", b)], [("am", 8 + m)])

            def V_unit(q4):
                b = bankM()
                vbank[q4] = b
                for tt_ in range(4):
                    t = 4 * q4 + tt_
                    for g2 in range(2):
                        g = 2 * t + g2
                        for ri in range(2):
                            c0 = (tt_ * 2 + ri) * NB
                            mm(ps[b][g2 * 64:(g2 + 1) * 64, c0:c0 + NB], Wm[:, g, ri * 64:(ri + 1) * 64], Ugrp(g), True, True,
                               ["Wm", ("am", 8 + g // 8)], [("ps", b)])

            def D_unit(q4):
                b = vbank[q4]
                V = ps[b][:, :].rearrange("p (t r n) -> p t r n", t=4, r=2)
                Vre, Vim = V[:, :, 0, :], V[:, :, 1, :]
                cs_ = cosT[:, 4 * q4:4 * q4 + 4, :]
                sn_ = sinT[:, 4 * q4:4 * q4 + 4, :]
                tA = tmpf[:, 0, 0:256].rearrange("p (t n) -> p t n", t=4)
                tB = tmpf[:, 0, 256:512].rearrange("p (t n) -> p t n", t=4)
                kA, kB = ("tmpf", 0), ("tmpf", "0b")
                gk = "Gin"
                tt(tA, Vre, cs_, ALU.mult, [("ps", b), "cosT"], [kA]); tt(tB, Vim, sn_, ALU.mult, [("ps", b), "sinT"], [kB])
                tt(Gin[:, 0], tA, tB, ALU.add, [kA, kB], [gk])
                tt(tA, Vim, cs_, ALU.mult, [("ps", b), "cosT"], [kA]); tt(tB, Vre, sn_, ALU.mult, [("ps", b), "sinT"], [kB])
                tt(Gin[:, 1], tA, tB, ALU.subtract, [kA, kB, gk], [gk])
                sk_ = "Gs"
                for tt_ in range(4):
                    t = 4 * q4 + tt_
                    for ri in range(2):
                        out_ap = Gs[:, ri, tt_, :]
                        d0 = r8[:, t:t + 1].to_broadcast([128, NB])
                        d1 = Gin[:, ri, tt_, :]
                        init = Hc[:, t, ri:ri + 1]
                        P.add("dve", (lambda o_=out_ap, a_=d0, b_=d1, i_=init: (lambda e: e.tensor_tensor_scan(
                            out=o_, data0=a_, data1=b_, initial=i_, op0=ALU.mult, op1=ALU.add)))(),
                            [gk, "r8", "Hc", sk_], [sk_])
                Gre, Gim = Gs[:, 0], Gs[:, 1]
                tt(tA, Gre, cs_, ALU.mult, [sk_, "cosT"], [kA]); tt(tB, Gim, sn_, ALU.mult, [sk_, "sinT"], [kB])
                tt(Gin[:, 0], tA, tB, ALU.subtract, [kA, kB, gk], [gk])
                tt(tA, Gre, sn_, ALU.mult, [sk_, "sinT"], [kA]); tt(tB, Gim, cs_, ALU.mult, [sk_, "cosT"], [kB])
                tt(Gin[:, 1], tA, tB, ALU.add, [kA, kB, gk], [gk])
                for ri in range(2):
                    actf(Hb[:, 4 * q4:4 * q4 + 4, ri, 1:NB + 1], Gin[:, ri], AF.Copy, [gk], [("Hb", q4)])
                    cp(Hc[:, 4 * q4:4 * q4 + 4, ri:ri + 1], Gin[:, ri, :, NB - 1:NB], [gk], ["Hc"])

            def Y_unit(m):
                b = bankM()
                for gg in range(8):
                    g = 8 * m + gg
                    t, g2 = g // 2, g % 2
                    rows = slice(g2 * 64, (g2 + 1) * 64)
                    o_ = ps[b][:, gg * NB:(gg + 1) * NB]
                    mm(o_, Mm[:, g, :], Ugrp(g), True, False, ["Mm", ("am", 8 + m)], [("ps", b)])
                    mm(o_, tabR[rows, t, :], Hb[rows, t, 0, 0:NB], False, False, ["tabR", "Hb", ("Hb", m)], [("ps", b)])
                    mm(o_, tabI[rows, t, :], Hb[rows, t, 1, 0:NB], False, True, ["tabI", "Hb", ("Hb", m)], [("ps", b)])
                U3 = am[:, 8 + m, :].rearrange("p (g n) -> p g n", g=8)
                Db = dblk[:, 8 * m:8 * m + 8].unsqueeze(2).to_broadcast([128, 8, NB])
                y1 = tmpf[:, 1, :]
                tt(y1.rearrange("p (g n) -> p g n", g=8), U3, Db, ALU.mult, [("am", 8 + m), "dblk"], [("tmpf", 1)])
                tt(y1, y1, ps[b][:, :], ALU.add, [("tmpf", 1), ("ps", b)], [("tmpf", 1)])
                actf(am[:, m, :], y1, AF.Gelu_apprx_tanh, [("tmpf", 1)], [("am", m)])

            def I_unit(m):
                b = bankM()
                for j in range(8):
                    for gg in range(8):
                        mm(ps[b][:, j * NB:(j + 1) * NB], sel[:, j, (7 - gg) * 16:(7 - gg) * 16 + 128], Y2grp(8 * m + gg),
                           gg == 0, gg == 7, ["sel", ("am", m)], [("ps", b)])
                actf(y2T(m).rearrange("p (n j) -> p j n", j=8), ps[b][:, :].rearrange("p (j n) -> p j n", j=8), AF.Copy,
                     [("ps", b)], [("am", 12 + m)])

            order = [(U_unit, 0), (U_unit, 1), (V_unit, 0), (D_unit, 0), (U_unit, 2), (V_unit, 1), (D_unit, 1), (U_unit, 3),
                     (V_unit, 2), (D_unit, 2), (Y_unit, 0), (V_unit, 3), (D_unit, 3), (Y_unit, 1), (I_unit, 0), (Y_unit, 2),
                     (I_unit, 1), (Y_unit, 3), (I_unit, 2), (I_unit, 3)]
            for fn_, arg in order:
                fn_(arg)
                yield
            yield "need_attn_done"
            for mo in range(4):
                b = bankM()
                for k in range(4):
                    mm(ps[b][:, :], w_glu[:, k, mo * 128:(mo + 1) * 128], y2T(k), k == 0, k == 3, ["w_glu", ("am", 12 + k)], [("ps", b)])
                sg = tmpf[:, 1, :]
                actf(sg, ps[b][:, :], AF.Tanh, [("ps", b), "hbg"], [("tmpf", 1)], bias=hbg[:, mo:mo + 1], scale=0.5)
                ygm, ygk = yg(mo)
                stt(ygm, sg, 1.0, y2T(mo), ALU.add, ALU.mult, [("am", 12 + mo), ("tmpf", 1)], [ygk])
                yield
            b = bankM()
            for mo in range(4):
                ygm, ygk = yg(mo)
                actf(sqM[:, 0, :], ygm, AF.Square, [ygk], [("sqM", 0)])
                mm(ps[b][:, :], onesb[:], sqM[:, 0, :], mo == 0, mo == 3, [("sqM", 0), "onesb"], [("ps", b)])
            rstd_from(b, 512, rsM[:, 0, :], ("rsM", 0), ec=2)
            yield

        ga, gs = attn_gen(), ssm_gen()
        a_done = s_done = s_wait = False
        while not (a_done and s_done):
            if not a_done:
                P.tag = 'mix.attn'
                try:
                    next(ga)
                except StopIteration:
                    a_done = True
                yield
            if not s_done and not (s_wait and not a_done):
                P.tag = 'mix.ssm'
                try:
                    if next(gs) == "need_attn_done":
                        s_wait = True
                except StopIteration:
                    s_done = True
                yield
        if s == 0:
            dump("yattn", am[:, 16:20, :], [("am", 16 + m) for m in range(4)], [128, 4, TS], BF16)
            dump("y2T", am[:, 12:16, :], [("am", 12 + m) for m in range(4)], [128, 4, TS], BF16)
        P.tag = 'mix.onorm'
        for k in range(4):
            ygm, ygk = yg(k)
            stt(hTM[:, k, :], ygm, gains[:, 32 + k:33 + k], rsM[:, 0, :], ALU.mult, ALU.mult, [ygk, "gains", ("rsM", 0)], [("hTM", k)])
        for k in range(4):
            ya, yak = yat(k)
            stt(hTM[:, 4 + k, :], ya, gains[:, 36 + k:37 + k], rsM[:, 1, :], ALU.mult, ALU.mult, [yak, "gains", ("rsM", 1)], [("hTM", 4 + k)])
        yield
        for o in range(8):
            P.tag = 'mix.wout'
            L = wo_use[0]
            wo_use[0] += 1
            slot = L % 2
            wo_ensure(L + 1)
            b = bankM()
            for k in range(KT):
                mm(ps[b][:, :], wo[:, slot, k, :], hTM[:, k, :], k == 0, k == KT - 1, [("wo", slot), ("hTM", k)], [("ps", b)])
            tt(xT(par, o), ps[b][:, :], xT(par, o), ALU.add, [("ps", b), xk(par, o)], [xk(par, o)])
            yield
        cp(kTd[:, :, 0:128], kTd[:, :, 512:640], [("kTd", 0), ("kTd", 1)], [("kTd", 0), ("kTd", 1)], eng="act")
        cp(vsb[:, 0, :], vsb[:, 4, :], ["vsb"], ["vsb"], eng="act")
        yield

    def drain(g):
        for _ in g:
            pass

    def chain(*gens):
        for g in gens:
            for _ in g:
                yield

    def interleave(ga, na, gb, nb):
        ca = cb = 0
        a_done = b_done = False
        while not (a_done and b_done):
            pick_a = (not a_done) and (b_done or ca * nb <= cb * na)
            if pick_a:
                try:
                    next(ga)
                    ca += 1
                except StopIteration:
                    a_done = True
            else:
                try:
                    next(gb)
                    cb += 1
                except StopIteration:
                    b_done = True

    P.tile = 0
    sb = sbp
    pg = ssm_precompute()
    next(pg)
    wgu_ensure(2)
    drain(chain(loadx_gen(0), ffn_norm_gen(0, 0)))
    fg = ffn_gen(1, 0, 0, do_norm=False)
    p_part1 = True
    f_live = True
    while p_part1 or f_live:
        if p_part1:
            P.tag = 'pre'
            if next(pg) == "PE_PART":
                p_part1 = False
        if f_live:
            try:
                next(fg)
            except StopIteration:
                f_live = False
    P.tag = 'pre'
    drain(pg)
    P.barrier()
    pstack.close()
    sb = sb_main
    xTb = sb("xTb", [128, KT, TS], F32)
    hTM = sb("hTM", [128, KT, TS], BF16)
    am = sb("am", [128, 20, TS], BF16)
    dump("x1", xTa[:], [xk(0, k) for k in range(KT)], [128, KT, TS])
    for s in range(NT_RUN):
        P.tile = s
        bparts = []
        nb = 0
        if s >= 1:
            bparts += [ffn_gen(2, 16, (s - 1) % 2), final_gen(s - 1)]
            nb += 50
        if s + 1 < NT_RUN:
            bparts += [loadx_gen(s + 1), ffn_norm_gen(0, (s + 1) % 2)]
            nb += 5
        if bparts:
            interleave(mixer_gen(s), 80, chain(*bparts), nb)
        else:
            drain(mixer_gen(s))
        if s == 0:
            dump("x2", xTa[:], [xk(0, k) for k in range(KT)], [128, KT, TS])
        if s + 1 < NT_RUN:
            drain(ffn_gen(1, 0, (s + 1) % 2, do_norm=False))
    drain(chain(ffn_gen(2, 16, (NT_RUN - 1) % 2), final_gen(NT_RUN - 1)))
    P.add("sp", None, [("out", s, k) for s in range(NT_RUN) for k in range(KT)] + [("dbg", n) for n in dbg_d], [])

    P.finalize(nc, stack)
    with nc.Block() as block:
        @block.sync
        def _(e):
            P.emit("sp", e)

        @block.tensor
        def _(e):
            P.emit("pe", e)

        @block.scalar
        def _(e):
            P.emit("act", e)

        @block.vector
        def _(e):
            P.emit("dve", e)

        @block.gpsimd
        def _(e):
            P.emit("pool", e)
    stack.close()
    nc._prog = P
    return nc, list(dbg_d.keys())


_CACHE = {}


def kernel(**inputs):
    x = np.ascontiguousarray(np.asarray(inputs["x"], dtype=np.float32))
    B = x.shape[0]
    if "nc" not in _CACHE:
        _CACHE["nc"] = build_program()
    nc, dbg = _CACHE["nc"]
    shared = host_layout(inputs)
    shared.update(host_consts())
    in_maps = []
    for b in range(B):
        m = dict(shared)
        m["x"] = np.ascontiguousarray(x[b].T)
        in_maps.append(m)
    res = run_bass_kernel_spmd(nc, in_maps, core_ids=list(range(B)))
    out = np.stack([np.ascontiguousarray(np.asarray(r["out"], dtype=np.float32).T) for r in res.results], axis=0)
    if DEBUG:
        _CACHE["dbg"] = {n: np.asarray(res.results[0]["dbg_" + n]) for n in dbg}
    return out
```

```python
import math
import os
from contextlib import ExitStack

import numpy as np
import ml_dtypes

import concourse.bass as bass
import concourse.mybir as mybir
from concourse.bass_utils import run_bass_kernel_spmd

F32 = mybir.dt.float32
BF16 = mybir.dt.bfloat16
AF = mybir.ActivationFunctionType
ALU = mybir.AluOpType

D = 1024
KT = 8
FF = 2816
FT = 22
SEQ = 4096
TS = 512
NTILE = SEQ // TS
NB = TS // 8
EPS = 1e-6
NEG = -30000.0
NWIN = 1408

DEBUG = bool(int(os.environ.get("KDBG", "0")))
NT_RUN = int(os.environ.get("KNT", str(NTILE)))


class Prog:
    ENGS = ("pe", "act", "dve", "pool", "sp")

    def __init__(self):
        self.ops = []
        self.lastw = {}
        self.readers = {}
        self.tag = ""
        self.tile = -1

    def add(self, eng, fn, reads=(), writes=(), dma=None):
        i = len(self.ops)
        deps = set()
        for k in reads:
            w = self.lastw.get(k)
            if w is not None:
                deps.add(w)
        for k in writes:
            w = self.lastw.get(k)
            if w is not None:
                deps.add(w)
            for r in self.readers.get(k, ()):
                deps.add(r)
        for k in reads:
            lst = self.readers.setdefault(k, [])
            if dma is None:
                lst[:] = [r for r in lst if not (self.ops[r]["eng"] == eng and self.ops[r]["dma"] is None)]
            lst.append(i)
        for k in writes:
            self.lastw[k] = i
            self.readers[k] = []
        deps.discard(i)
        self.ops.append(dict(eng=eng, fn=fn, deps=deps, dma=dma, signal=False, count=0, tag=self.tag, tile=self.tile))
        return i

    def barrier(self):
        last = {}
        for i, op in enumerate(self.ops):
            if op["dma"] is not None:
                if str(op["dma"]).startswith("cv"):
                    continue
                last[("d", op["dma"])] = i
            elif op["fn"] is not None:
                last[("e", op["eng"])] = i
        deps = set(last.values())
        for e in self.ENGS:
            self.ops.append(dict(eng=e, fn=None, deps=set(deps), dma=None, signal=False, count=0, tag='barrier', tile=-1))

    def finalize(self, nc, stack):
        ops = self.ops
        for op in ops:
            for d in op["deps"]:
                dop = ops[d]
                if dop["dma"] is None:
                    if dop["eng"] == "pe" and op["eng"] == "pe" and op["dma"] is None:
                        continue
                    dop["signal"] = True
        esem = {e: stack.enter_context(nc.semaphore("sem_" + e)) for e in self.ENGS}
        dsem = {}
        ecnt = {e: 0 for e in self.ENGS}
        dcnt = {}
        waited = {e: {} for e in self.ENGS}
        streams = {e: [] for e in self.ENGS}
        for op in ops:
            e = op["eng"]
            waits = {}
            for d in op["deps"]:
                dop = ops[d]
                if dop["dma"] is not None:
                    key = ("d", dop["dma"])
                    val = dop["count"]
                    sem = dsem[dop["dma"]]
                else:
                    if dop["eng"] == "pe" and e == "pe" and op["dma"] is None:
                        continue
                    key = ("e", dop["eng"])
                    val = dop["count"]
                    sem = esem[dop["eng"]]
                if waited[e].get(key, 0) >= val:
                    continue
                if key not in waits or waits[key][1] < val:
                    waits[key] = (sem, val)
            for key, (sem, val) in waits.items():
                waited[e][key] = val
            if op["dma"] is not None:
                if op["dma"] not in dsem:
                    dsem[op["dma"]] = stack.enter_context(nc.semaphore("dsem%d" % len(dsem)))
                    dcnt[op["dma"]] = 0
                dcnt[op["dma"]] += 16
                op["count"] = dcnt[op["dma"]]
                inc = (dsem[op["dma"]], 16)
            elif op["signal"]:
                ecnt[e] += 1
                op["count"] = ecnt[e]
                inc = (esem[e], 1)
            else:
                inc = None
            streams[e].append((list(waits.values()), op["fn"], inc))
        self.streams = streams
        self.nsem = len(dsem) + len(esem)

    def emit(self, eng_name, e):
        for waits, fn, inc in self.streams[eng_name]:
            for sem, val in waits:
                e.wait_ge(sem, val)
            if fn is None:
                continue
            ins = fn(e)
            if inc is not None:
                ins.then_inc(inc[0], inc[1])


def _bf(a):
    return np.ascontiguousarray(a.astype(ml_dtypes.bfloat16))


def host_consts():
    c = {}
    c["identf"] = np.eye(128, dtype=np.float32)
    c["identb"] = _bf(np.eye(128, dtype=np.float32))
    c["onesb"] = _bf(np.ones((128, 128), np.float32))
    perm = np.zeros((128, 128), np.float32)
    for h in range(2):
        for d in range(16):
            pd = d + 8 if d < 8 else d - 8
            perm[h * 64 + pd, h * 64 + d] = 1.0
    c["permb"] = _bf(perm)
    kj = np.arange(128)[:, None]
    qi = np.arange(128)[None, :]
    prev = np.where(kj > qi, 0.0, NEG).astype(np.float32)
    same = np.where(kj <= qi, 0.0, NEG).astype(np.float32)
    full = np.full((128, 128), NEG, np.float32)
    c["mask"] = _bf(np.concatenate([prev, same, prev, same], axis=1))
    c["maskf"] = _bf(np.concatenate([full, same, prev, same], axis=1))
    sel = np.zeros((128, 8, 240), np.float32)
    for g in range(8):
        for cc in range(16):
            sel[16 * g + cc, g, 112 + cc] = 1.0
    c["sel"] = _bf(sel)
    half = 8
    inv_freq = (500000.0 ** (-np.arange(half, dtype=np.float32) * 2.0 / 16)).astype(np.float32)
    ang = np.arange(SEQ, dtype=np.float32)[:, None] * inv_freq[None, :]
    cos = np.cos(ang).astype(np.float32).T
    sin = np.sin(ang).astype(np.float32).T
    C = np.ones((128, SEQ), np.float32)
    S = np.zeros((128, SEQ), np.float32)
    for h in range(2):
        C[h * 64 + 0:h * 64 + 8] = cos
        C[h * 64 + 8:h * 64 + 16] = cos
        S[h * 64 + 0:h * 64 + 8] = -sin
        S[h * 64 + 8:h * 64 + 16] = sin
    c["ropec"] = C
    c["ropes"] = S
    return c


def host_layout(inp):
    o = {}
    f = lambda a: np.ascontiguousarray(np.asarray(a, dtype=np.float32))
    o["wg1"] = f(inp["ffn1_w_gate"][0]); o["wu1"] = f(inp["ffn1_w_up"][0]); o["wd1"] = f(inp["ffn1_w_down"][0])
    o["wg2"] = f(inp["ffn2_w_gate"][0]); o["wu2"] = f(inp["ffn2_w_up"][0]); o["wd2"] = f(inp["ffn2_w_down"][0])
    w_in = f(inp["w_in"][0])
    u = w_in[:, 0:512]; q = w_in[:, 512:1024]; k = w_in[:, 1024:1152]; v = w_in[:, 1152:1280]
    o["win"] = np.ascontiguousarray(np.concatenate([u, q, k[:, 0:64], k[:, 0:64], k[:, 64:128], k[:, 64:128], v], axis=1))
    o["wout"] = f(inp["w_out"][0])
    o["wglu"] = f(inp["ssm_w_glu"][0])
    fm = lambda vec: np.ascontiguousarray(f(vec).reshape(-1, 128).T)
    gains = np.concatenate([fm(inp["ffn1_norm"][0]), fm(inp["mix_norm"][0]), fm(inp["ffn2_norm"][0]),
                            fm(inp["final_norm"]), fm(inp["ssm_out_norm"][0]), fm(inp["attn_out_norm"][0]),
                            fm(inp["ssm_b_glu"][0])], axis=1)
    o["gains"] = np.ascontiguousarray(gains)
    Dv = f(inp["ssm_D"][0]).reshape(32, 16)
    o["dblk"] = np.ascontiguousarray(np.tile(Dv.T, (8, 1)))
    sk = f(inp["attn_sinks"][0])
    o["sinkrow"] = np.ascontiguousarray(np.repeat(sk.reshape(4, 2), 64, axis=1).T)
    pl = lambda a: np.ascontiguousarray(f(a).reshape(16, 2, 64).transpose(1, 2, 0).reshape(128, 16))
    o["are"] = pl(inp["ssm_A_re"][0]); o["aim"] = pl(inp["ssm_A_im"][0])
    ldt = f(inp["ssm_log_dt"][0])
    o["ldt"] = pl(np.repeat(ldt[:, None], 64, axis=1))
    pb = lambda a: np.ascontiguousarray(f(a).reshape(16, 2, 64, 16).transpose(1, 2, 0, 3).reshape(128, 16, 16))
    o["bre"] = pb(inp["ssm_B_re"][0]); o["bim"] = pb(inp["ssm_B_im"][0])
    pc = lambda a: np.ascontiguousarray(f(a).transpose(0, 2, 1).reshape(16, 2, 64, 16).transpose(1, 2, 0, 3).reshape(128, 16, 16))
    o["cre"] = pc(inp["ssm_C_re"][0]); o["cim"] = pc(inp["ssm_C_im"][0])
    return o


def build_program():
    nc = bass.Bass("TRN2", target_bir_lowering=False)
    P = Prog()
    stack = ExitStack()
    dr = {}

    def din(name, shape, dt=F32):
        dr[name] = nc.dram_tensor(name, list(shape), dt, kind="ExternalInput").ap()
        return dr[name]

    x_d = din("x", [D, SEQ])
    for n in ("wg1", "wu1", "wg2", "wu2"):
        din(n, [D, FF])
    for n in ("wd1", "wd2"):
        din(n, [FF, D])
    din("win", [D, NWIN]); din("wout", [D, D]); din("wglu", [512, 512])
    din("gains", [128, 44]); din("dblk", [128, 32]); din("sinkrow", [128, 4])
    for n in ("are", "aim", "ldt"):
        din(n, [128, 16])
    for n in ("bre", "bim", "cre", "cim"):
        din(n, [128, 16, 16])
    din("identf", [128, 128]); din("identb", [128, 128], BF16); din("onesb", [128, 128], BF16)
    din("permb", [128, 128], BF16); din("mask", [128, 512], BF16); din("maskf", [128, 512], BF16)
    din("sel", [128, 8, 240], BF16); din("ropec", [128, SEQ]); din("ropes", [128, SEQ])
    out_d = nc.dram_tensor("out", [D, SEQ], F32, kind="ExternalOutput").ap()
    dbg_d = {}

    def sb_main(name, shape, dt):
        return stack.enter_context(nc.sbuf_tensor("s_" + name, list(shape), dt))

    sb = sb_main

    wgu = sb("wgu", [128, 4, 2, 4, 128], BF16)
    wd = sb("wd", [128, 3, 2, 512], BF16)
    wis = sb("wis", [128, 4, KT, 128], BF16)
    wo = sb("wo", [128, 2, KT, 128], BF16)
    w_glu = sb("w_glu", [128, 4, 512], BF16)
    sgs = sb("sgs", [128, 2, TS], F32)
    sqF = sb("sqF", [128, 2, TS], BF16)
    sqM = sb("sqM", [128, 1, TS], BF16)
    rsF = sb("rsF", [128, 2, TS], F32)
    rsM = sb("rsM", [128, 2, TS], F32)
    tabR = sb("tabR", [128, 16, 128], BF16)
    tabI = sb("tabI", [128, 16, 128], BF16)
    Wm = sb("Wm", [128, 32, 128], BF16)
    Mm = sb("Mm", [128, 32, 128], BF16)
    cosT = sb("cosT", [128, 16, NB], F32)
    sinT = sb("sinT", [128, 16, NB], F32)
    r8 = sb("r8", [128, 16], F32)
    sel = sb("sel", [128, 8, 240], BF16)
    dblk = sb("dblk", [128, 32], F32)
    gains = sb("gains", [128, 44], F32)
    esink = sb("esink", [128, 4], F32)
    cst = sb("cst", [128, 4], F32)
    identf = sb("identf", [128, 128], F32)
    identb = sb("identb", [128, 128], BF16)
    onesb = sb("onesb", [128, 128], BF16)
    permb = sb("permb", [128, 128], BF16)
    mask = sb("mask", [128, 512], BF16)
    maskf = sb("maskf", [128, 512], BF16)
    kTd = sb("kTd", [128, 2, 5 * 128], BF16)
    vsb = sb("vsb", [128, 5, 128], BF16)
    PT = sb("PT", [128, 2, 512], BF16)
    ropec = sb("ropec", [128, TS], F32)
    ropes = sb("ropes", [128, TS], F32)
    qpre = sb("qpre", [128, 1, TS], BF16)
    tmpf = sb("tmpf", [128, 3, TS], F32)
    Gin = sb("Gin", [128, 2, 4, NB], F32)
    Gs = sb("Gs", [128, 2, 4, NB], F32)
    Hb = sb("Hb", [128, 16, 2, NB + 1], BF16)
    Hc = sb("Hc", [128, 16, 2], F32)

    ps = [stack.enter_context(nc.psum_tensor("ps%d" % i, [128, 512], F32)) for i in range(8)]

    rrM = [0]
    rrF = [0]

    def bankM():
        b = rrM[0] % 3
        rrM[0] += 1
        return b

    def bankF():
        b = 4 + rrF[0] % 4
        rrF[0] += 1
        return b

    bankA = bankM
    BATT = 3

    def mm(out, lhsT, rhs, start, stop, reads, writes):
        P.add("pe", lambda e: e.matmul(out, lhsT=lhsT, rhs=rhs, start=start, stop=stop), reads, writes)

    def tr(out, in_, reads, writes):
        P.add("pe", lambda e: e.transpose(out, in_, identf[:]), reads + ["identf"], writes)

    def actf(out, in_, func, reads, writes, bias=None, scale=None):
        kw = {}
        if bias is not None:
            kw["bias"] = bias
        if scale is not None:
            kw["scale"] = scale
        P.add("act", lambda e: e.activation(out=out, in_=in_, func=func, **kw), reads, writes)

    def tt(out, in0, in1, op, reads, writes, eng="dve"):
        P.add(eng, lambda e: e.tensor_tensor(out=out, in0=in0, in1=in1, op=op), reads, writes)

    def ts1(out, in0, s1, op0, reads, writes, eng="dve"):
        P.add(eng, lambda e: e.tensor_single_scalar(out=out, in_=in0, scalar=s1, op=op0), reads, writes)

    def stt(out, in0, scalar, in1, op0, op1, reads, writes):
        P.add("dve", lambda e: e.scalar_tensor_tensor(out=out, in0=in0, scalar=scalar, in1=in1, op0=op0, op1=op1), reads, writes)

    def cp(out, in_, reads, writes, eng="dve"):
        if eng == "act":
            P.add("act", lambda e: e.activation(out=out, in_=in_, func=AF.Copy), reads, writes)
        else:
            P.add(eng, lambda e: e.tensor_copy(out=out, in_=in_), reads, writes)

    def recip(out, in_, reads, writes):
        P.add("dve", lambda e: e.reciprocal(out=out, in_=in_), reads, writes)

    def memset(ap, val, writes, eng="dve"):
        P.add(eng, lambda e: e.memset(ap, val), [], writes)

    def dma(q, out, in_, reads, writes, key):
        P.add(q, lambda e: e.dma_start(out=out, in_=in_), reads, writes, dma=key)

    def dump(name, ap, reads, shape, dt=F32):
        if not DEBUG:
            return
        if name not in dbg_d:
            dbg_d[name] = nc.dram_tensor("dbg_" + name, list(shape), dt, kind="ExternalOutput").ap()
        dma("sp", dbg_d[name], ap, reads, [("dbg", name)], "dbg_" + name)

    def ld(q, t, src, key):
        dma(q, t[:], src, [], [key], "ld_" + key)

    ld("sp", identf, dr["identf"], "identf"); ld("sp", identb, dr["identb"], "identb")
    ld("sp", onesb, dr["onesb"], "onesb"); ld("sp", permb, dr["permb"], "permb")
    ld("sp", mask, dr["mask"], "mask"); ld("sp", maskf, dr["maskf"], "maskf")
    ld("sp", sel, dr["sel"], "sel"); ld("sp", gains, dr["gains"], "gains")
    ld("sp", dblk, dr["dblk"], "dblk"); ld("sp", esink, dr["sinkrow"], "esink")
    memset(cst[:, 0:1], EPS, ["cst"])
    memset(cst[:, 1:2], math.pi / 2, ["cst"])
    memset(cst[:, 2:3], 4.0 * EPS, ["cst"])
    memset(kTd[:], 0.0, ["kTd"])
    memset(vsb[:], 0.0, ["vsb"])
    memset(Hc[:], 0.0, ["Hc"])
    memset(Hb[:], 0.0, ["Hb"])
    actf(esink[:], esink[:], AF.Exp, ["esink"], ["esink"])
    hbg = sb("hbg", [128, 4], F32)
    ts1(hbg[:], gains[:, 40:44], 0.5, ALU.mult, ["gains"], ["hbg"])

    scr_gu = {fid: nc.dram_tensor("scr_gu%d" % fid, [2 * FT, 128, 2 * 4 * 128], BF16).ap() for fid in (1, 2)}
    scr_d = {fid: nc.dram_tensor("scr_d%d" % fid, [22, 128, 2 * 512], BF16).ap() for fid in (1, 2)}
    scr_wo = nc.dram_tensor("scr_wo", [8, 128, KT * 128], BF16).ap()
    scr_wi = nc.dram_tensor("scr_wi", [11, 128, KT * 128], BF16).ap()

    ffn_order = [1]
    for s in range(NT_RUN - 1):
        ffn_order += [1, 2]
    ffn_order.append(2)
    ffn_w = {1: ("wg1", "wu1", "wd1"), 2: ("wg2", "wu2", "wd2")}
    wgu_loads = [(fid, f, kh) for fid in ffn_order for f in range(FT) for kh in range(2)]
    wd_loads = [(fid, half, jc) for fid in ffn_order for half in range(2) for jc in range(11)]
    wgu_ptr = [0]
    wd_ptr = [0]
    seen_gu = set()
    seen_d = set()
    multi = NT_RUN > 1

    NCV = 16
    cvi = [0]

    def conv(out_ap, in_ap, key):
        i = cvi[0] % NCV
        cvi[0] += 1
        dma("pool", out_ap, in_ap, [], [key, ("cvslot", i)], "cv%d" % i)

    def conv_gu(fid):
        gname, uname, _ = ffn_w[fid]
        for f in range(FT):
            for kh in range(2):
                scr = scr_gu[fid][2 * f + kh].rearrange("p (g k c) -> p g k c", g=2, k=4)
                for gi, nm in enumerate((gname, uname)):
                    src = dr[nm].rearrange("(k p) f -> p k f", p=128)[:, 4 * kh:4 * kh + 4, f * 128:(f + 1) * 128]
                    conv(scr[:, gi], src, ("scr_gu", fid, f, kh, gi))

    def conv_d(fid):
        for half in range(2):
            for jc in range(11):
                scr = scr_d[fid][half * 11 + jc].rearrange("p (f d) -> p f d", f=2)
                src = dr[ffn_w[fid][2]].rearrange("(f p) d -> p f d", p=128)[:, 2 * jc:2 * jc + 2, half * 512:(half + 1) * 512]
                conv(scr, src, ("scr_d", fid, half, jc))

    P.tag = 'prep'
    dma("pool", w_glu[:], dr["wglu"].rearrange("(k p) f -> p k f", p=128), [], ["w_glu"], "ld_w_glu")
    conv_gu(1)
    conv_d(1)
    for cg in range(11):
        conv(scr_wi[cg].rearrange("p (k c) -> p k c", k=KT), dr["win"].rearrange("(k p) f -> p k f", p=128)[:, :, cg * 128:(cg + 1) * 128], ("scr_wi", cg))
    for o in range(8):
        conv(scr_wo[o].rearrange("p (k c) -> p k c", k=KT), dr["wout"].rearrange("(k p) d -> p k d", p=128)[:, :, o * 128:(o + 1) * 128], ("scr_wo", o))
    conv_gu(2)
    conv_d(2)

    def wgu_ensure(upto):
        while wgu_ptr[0] <= upto and wgu_ptr[0] < len(wgu_loads):
            L = wgu_ptr[0]
            fid, f, kh = wgu_loads[L]
            slot = L % 4
            scr = scr_gu[fid][2 * f + kh].rearrange("p (g k c) -> p g k c", g=2, k=4)
            dma("sp", wgu[:, slot], scr, [("scr_gu", fid, f, kh, 0), ("scr_gu", fid, f, kh, 1)], [("wgu", slot, 0), ("wgu", slot, 1)], "wgu%d" % slot)
            wgu_ptr[0] += 1

    def wd_ensure(upto):
        while wd_ptr[0] <= upto and wd_ptr[0] < len(wd_loads):
            L = wd_ptr[0]
            fid, half, jc = wd_loads[L]
            slot = L % 3
            scr = scr_d[fid][half * 11 + jc].rearrange("p (f d) -> p f d", f=2)
            dma("sp", wd[:, slot], scr, [("scr_d", fid, half, jc)], [("wd", slot)], "wd%d" % slot)
            wd_ptr[0] += 1

    wgu_use = [0]
    wd_use = [0]

    wi_ptr = [0]
    wi_use = [0]
    wo_ptr = [0]
    wo_use = [0]

    def wi_ensure(upto):
        while wi_ptr[0] <= upto and wi_ptr[0] < 11 * NT_RUN:
            L = wi_ptr[0]
            cg = L % 11
            slot = L % 4
            dma("sp", wis[:, slot], scr_wi[cg].rearrange("p (k c) -> p k c", k=KT), [("scr_wi", cg)], [("wis", slot)], "wis%d" % slot)
            wi_ptr[0] += 1

    def wo_ensure(upto):
        while wo_ptr[0] <= upto and wo_ptr[0] < 8 * NT_RUN:
            L = wo_ptr[0]
            o = L % 8
            slot = L % 2
            dma("sp", wo[:, slot], scr_wo[o].rearrange("p (k c) -> p k c", k=KT), [("scr_wo", o)], [("wo", slot)], "wo%d" % slot)
            wo_ptr[0] += 1

    def ssm_precompute():
        P.tag = 'pre'
        are = sb("p_are", [128, 16], F32); aim = sb("p_aim", [128, 16], F32); ldt = sb("p_ldt", [128, 16], F32)
        bre = sb("p_bre", [128, 16, 16], F32); bim = sb("p_bim", [128, 16, 16], F32)
        cre = sb("p_cre", [128, 16, 16], F32); cim = sb("p_cim", [128, 16, 16], F32)
        sm = sb("p_sm", [128, 24, 16], F32)
        pw = sb("p_pw", [128, 2, 16, 9], F32)
        bb = sb("p_bb", [128, 2, 16, 16], F32)
        bbp = sb("p_bbp", [128, 2, 2, 240], BF16)
        wtt = sb("p_wt", [128, 2, 2, 8, 16], F32)
        big = sb("p_big", [128, 2, 16, 64], F32)
        for nm, t in (("are", are), ("aim", aim), ("ldt", ldt), ("bre", bre), ("bim", bim), ("cre", cre), ("cim", cim)):
            dma("sp", t[:], dr[nm], [], ["p_" + nm], "ld_p_" + nm)
        S = lambda i: sm[:, i, :]
        K = "p_sm"
        rk = ["p_are", "p_aim", "p_ldt", K, "cst"]
        DT, AR, TH, MAG, C0, S0, CC, SS, CS, LBR, LBI, DEN, NRE, T1, T2, ZR, ZI, C8, S8, T3 = range(20)
        actf(S(DT), ldt[:], AF.Exp, rk, [K])
        tt(S(AR), are[:], S(DT), ALU.mult, rk, [K])
        tt(S(TH), aim[:], S(DT), ALU.mult, rk, [K])
        actf(S(MAG), S(AR), AF.Exp, rk, [K])
        actf(r8[:], S(AR), AF.Exp, rk, ["r8"], scale=8.0)
        actf(S(S0), S(TH), AF.Sin, rk, [K], scale=1.0 / 16)
        actf(S(C0), S(TH), AF.Sin, rk, [K], scale=1.0 / 16, bias=cst[:, 1:2])
        yield

        def dbl():
            tt(S(CC), S(C0), S(C0), ALU.mult, rk, [K])
            tt(S(SS), S(S0), S(S0), ALU.mult, rk, [K])
            tt(S(CS), S(C0), S(S0), ALU.mult, rk, [K])
            tt(S(C0), S(CC), S(SS), ALU.subtract, rk, [K])
            ts1(S(S0), S(CS), 2.0, ALU.mult, rk, [K])
        for _ in range(4):
            dbl()
            yield
        tt(S(LBR), S(MAG), S(C0), ALU.mult, rk, [K])
        tt(S(LBI), S(MAG), S(S0), ALU.mult, rk, [K])
        for _ in range(3):
            dbl()
            yield
        cp(S(C8), S(C0), rk, [K]); cp(S(S8), S(S0), rk, [K])
        tt(S(T1), are[:], are[:], ALU.mult, rk, [K])
        tt(S(T2), aim[:], aim[:], ALU.mult, rk, [K])
        tt(S(DEN), S(T1), S(T2), ALU.add, rk, [K])
        recip(S(DEN), S(DEN), rk, [K])
        ts1(S(NRE), S(LBR), -1.0, ALU.add, rk, [K])
        tt(S(T1), S(NRE), are[:], ALU.mult, rk, [K])
        tt(S(T2), S(LBI), aim[:], ALU.mult, rk, [K])
        tt(S(T1), S(T1), S(T2), ALU.add, rk, [K])
        tt(S(ZR), S(T1), S(DEN), ALU.mult, rk, [K])
        tt(S(T1), S(LBI), are[:], ALU.mult, rk, [K])
        tt(S(T2), S(NRE), aim[:], ALU.mult, rk, [K])
        tt(S(T1), S(T1), S(T2), ALU.subtract, rk, [K])
        tt(S(ZI), S(T1), S(DEN), ALU.mult, rk, [K])
        yield
        kp = ["p_pw", K]
        memset(pw[:, 0, :, 0:1], 1.0, ["p_pw"]); memset(pw[:, 1, :, 0:1], 0.0, ["p_pw"])
        for k in range(1, 9):
            pr, pi_ = pw[:, 0, :, k - 1], pw[:, 1, :, k - 1]
            tt(S(T1), pr, S(LBR), ALU.mult, kp, [K]); tt(S(T2), pi_, S(LBI), ALU.mult, kp, [K])
            tt(pw[:, 0, :, k], S(T1), S(T2), ALU.subtract, kp, ["p_pw"])
            tt(S(T1), pr, S(LBI), ALU.mult, kp, [K]); tt(S(T2), pi_, S(LBR), ALU.mult, kp, [K])
            tt(pw[:, 1, :, k], S(T1), S(T2), ALU.add, kp, ["p_pw"])
            yield
        bc16 = lambda ap2: ap2.unsqueeze(2).to_broadcast([128, 16, 16])
        kb = ["p_bre", "p_bim", K, "p_bb", "p_big"]
        t1 = big[:, 0, :, 0:16]; t2 = big[:, 1, :, 0:16]
        tt(t1, bre[:], bc16(S(ZR)), ALU.mult, kb, ["p_big"]); tt(t2, bim[:], bc16(S(ZI)), ALU.mult, kb, ["p_big"])
        tt(bb[:, 0], t1, t2, ALU.subtract, kb, ["p_bb"])
        tt(t1, bim[:], bc16(S(ZR)), ALU.mult, kb, ["p_big"]); tt(t2, bre[:], bc16(S(ZI)), ALU.mult, kb, ["p_big"])
        tt(bb[:, 1], t1, t2, ALU.add, kb, ["p_bb"])
        memset(bbp[:], 0.0, [("p_bbp", 0), ("p_bbp", 1)])
        yield
        kc = ["p_cre", "p_cim", "p_pw", "p_big", "p_wt"]
        tabRe = sb("p_tabRe", [128, 16, 256], BF16); tabIm = sb("p_tabIm", [128, 16, 256], BF16)
        memset(tabRe[:], 0.0, ["tabRe"]); memset(tabIm[:], 0.0, ["tabIm"])
        for tau in range(9):
            pr = pw[:, 0, :, tau:tau + 1].to_broadcast([128, 16, 16])
            pi_ = pw[:, 1, :, tau:tau + 1].to_broadcast([128, 16, 16])
            o_re = tabRe[:, :, 112 + tau * 16:112 + (tau + 1) * 16]; o_im = tabIm[:, :, 112 + tau * 16:112 + (tau + 1) * 16]
            ta = wtt[:, 0].rearrange("p r i c -> p (r i) c")
            tb = wtt[:, 1].rearrange("p r i c -> p (r i) c")
            tt(ta, cre[:], pr, ALU.mult, kc, ["p_wt"]); tt(tb, cim[:], pi_, ALU.mult, kc, ["p_wt"])
            tt(o_re, ta, tb, ALU.subtract, kc, ["tabRe"])
            tt(ta, cre[:], pi_, ALU.mult, kc, ["p_wt"]); tt(tb, cim[:], pr, ALU.mult, kc, ["p_wt"])
            tt(tb, ta, tb, ALU.add, kc, ["p_wt"])
            ts1(o_im, tb, -1.0, ALU.mult, kc, ["tabIm"])
            yield
        yield "PE_PART"
        pwr = sb("p_pwr", [128, 3, 16, 8], F32)
        for i in range(8):
            cp(pwr[:, 0, :, i:i + 1], pw[:, 0, :, 7 - i:8 - i], ["p_pw"], ["p_pwr"])
            cp(pwr[:, 1, :, i:i + 1], pw[:, 1, :, 7 - i:8 - i], ["p_pw"], ["p_pwr"])
        ts1(pwr[:, 2], pwr[:, 1], -1.0, ALU.mult, ["p_pwr"], ["p_pwr"])
        kw_ = ["p_pwr", "p_bb", "p_wt", "p_big"]
        for t in range(16):
            buf = t % 2
            wre = wtt[:, buf, 0]; wim = wtt[:, buf, 1]
            pr = pwr[:, 0, t, :].unsqueeze(2).to_broadcast([128, 8, 16])
            pi_ = pwr[:, 1, t, :].unsqueeze(2).to_broadcast([128, 8, 16])
            br = bb[:, 0, t, :].unsqueeze(1).to_broadcast([128, 8, 16])
            bi = bb[:, 1, t, :].unsqueeze(1).to_broadcast([128, 8, 16])
            x1 = big[:, 0, 0:8, 0:16]; x2 = big[:, 1, 0:8, 0:16]
            tt(x1, pr, br, ALU.mult, kw_, ["p_big"]); tt(x2, pi_, bi, ALU.mult, kw_, ["p_big"])
            tt(wre, x1, x2, ALU.subtract, kw_, ["p_wt"])
            tt(x1, pr, bi, ALU.mult, kw_, ["p_big"]); tt(x2, pi_, br, ALU.mult, kw_, ["p_big"])
            tt(wim, x1, x2, ALU.add, kw_, ["p_wt"])
            b = bankA()
            tr(ps[b][:, 0:128], wre.rearrange("p i c -> p (i c)"), ["p_wt"], [("ps", b)])
            tr(ps[b][:, 128:256], wim.rearrange("p i c -> p (i c)"), ["p_wt"], [("ps", b)])
            src = ps[b][:, 0:256].rearrange("p (r g q) -> p g r q", r=2, g=2)
            dst = Wm[:, 2 * t:2 * t + 2, :].rearrange("p g (r q) -> p g r q", r=2)
            cp(dst, src, [("ps", b)], ["Wm"])
            yield
        for g in range(32):
            t, g2 = g // 2, g % 2
            buf = t % 2
            if g2 == 0:
                cp(bbp[:, buf, 0, 112:128], bb[:, 0, t, :], ["p_bb"], [("p_bbp", buf)])
                cp(bbp[:, buf, 1, 112:128], bb[:, 1, t, :], ["p_bb"], [("p_bbp", buf)])
            rows = slice(g2 * 64, (g2 + 1) * 64)
            b = bankA()
            n = 0
            for i in range(8):
                off = (7 - i) * 16
                for ri, tab in ((0, tabRe), (1, tabIm)):
                    mm(ps[b][:, 0:128], bbp[rows, buf, ri, off:off + 128], tab[rows, t, off:off + 128],
                       n == 0, n == 15, [("p_bbp", buf), "tabRe", "tabIm"], [("ps", b)])
                    n += 1
            cp(Mm[:, g, :], ps[b][:, 0:128], [("ps", b)], ["Mm"], eng="act")
            yield
        cp(tabR[:], tabRe[:, :, 128:256], ["tabRe"], ["tabR"])
        cp(tabI[:], tabIm[:, :, 128:256], ["tabIm"], ["tabI"])
        kt = ["cosT", "sinT", K, "p_wt", "p_big"]
        cp(cosT[:, :, 0:1], S(C8).unsqueeze(2), kt, ["cosT"]); cp(sinT[:, :, 0:1], S(S8).unsqueeze(2), kt, ["sinT"])
        m = 1
        while m < NB:
            cr = cosT[:, :, m - 1:m].to_broadcast([128, 16, m]); sr = sinT[:, :, m - 1:m].to_broadcast([128, 16, m])
            a1 = big[:, 0, :, 0:m]; a2 = big[:, 1, :, 0:m]; a3 = big[:, 0, :, 32:32 + m]; a4 = big[:, 1, :, 32:32 + m]
            tt(a1, cosT[:, :, 0:m], cr, ALU.mult, kt, ["p_big"]); tt(a2, sinT[:, :, 0:m], sr, ALU.mult, kt, ["p_big"])
            tt(a3, cosT[:, :, 0:m], sr, ALU.mult, kt, ["p_big"]); tt(a4, sinT[:, :, 0:m], cr, ALU.mult, kt, ["p_big"])
            tt(cosT[:, :, m:2 * m], a1, a2, ALU.subtract, kt, ["cosT"])
            tt(sinT[:, :, m:2 * m], a3, a4, ALU.add, kt, ["sinT"])
            m *= 2
            yield

    pstack = ExitStack()

    def sbp(name, shape, dt):
        return pstack.enter_context(nc.sbuf_tensor("s_" + name, list(shape), dt))

    xTa = sb("xTa", [128, KT, TS], F32)
    hTF = sb("hTF", [128, KT, TS], BF16)
    act = sb("act", [128, FT, TS], BF16)
    xsel = lambda par: xTa if par == 0 else xTb

    def xT(par, k):
        return xsel(par)[:, k, :]

    def xk(par, k):
        return ("xT", par, k)

    def rstd_from(b, n, dst, dkey, ec=0):
        actf(dst, ps[b][:, :], AF.Sqrt, [("ps", b), "cst"], [dkey], bias=cst[:, ec:ec + 1], scale=1.0 / n)
        recip(dst, dst, [dkey], [dkey])

    def norm_to(par, goff, hT, hkey, sqs, rdst, rkey, b):
        for k in range(KT):
            sap, skey = sqs[k % len(sqs)]
            actf(sap, xT(par, k), AF.Square, [xk(par, k)], [skey])
            mm(ps[b][:, :], onesb[:], sap, k == 0, k == KT - 1, [skey, "onesb"], [("ps", b)])
        rstd_from(b, D, rdst, rkey)
        for k in range(KT):
            stt(hT[:, k, :], xT(par, k), gains[:, goff + k:goff + k + 1], rdst, ALU.mult, ALU.mult,
                [xk(par, k), "gains", rkey], [(hkey, k)])

    def ffn_norm_gen(goff, par):
        P.tag = 'ffn.norm'
        norm_to(par, goff, hTF, "hTF", [(sqF[:, 0, :], ("sqF", 0)), (sqF[:, 1, :], ("sqF", 1))], rsF[:, 0, :], ("rsF", 0), 4)
        yield

    def ffn_gen(fid, goff, par, do_norm=True):
        if do_norm:
            P.tag = 'ffn.norm'
            norm_to(par, goff, hTF, "hTF", [(sqF[:, 0, :], ("sqF", 0)), (sqF[:, 1, :], ("sqF", 1))], rsF[:, 0, :], ("rsF", 0), 4)
            yield
        for f in range(FT):
            P.tag = 'ffn.gu'
            bg, bu = (4, 5) if f % 2 == 0 else (6, 7)
            for kh in range(2):
                L = wgu_use[0]
                wgu_use[0] += 1
                slot = L % 4
                wgu_ensure(L + 3)
                for k4 in range(4):
                    k = 4 * kh + k4
                    mm(ps[bg][:, :], wgu[:, slot, 0, k4, :], hTF[:, k, :], k == 0, k == KT - 1,
                       [("wgu", slot, 0), ("hTF", k)], [("ps", bg)])
                for k4 in range(4):
                    k = 4 * kh + k4
                    mm(ps[bu][:, :], wgu[:, slot, 1, k4, :], hTF[:, k, :], k == 0, k == KT - 1,
                       [("wgu", slot, 1), ("hTF", k)], [("ps", bu)])
            s_ = f % 2
            actf(sgs[:, s_, :], ps[bg][:, :], AF.Tanh, [("ps", bg)], [("sgs", s_)], scale=0.5)
            stt(sgs[:, s_, :], sgs[:, s_, :], 1.0, ps[bg][:, :], ALU.add, ALU.mult, [("sgs", s_), ("ps", bg)], [("sgs", s_)])
            tt(act[:, f, :], sgs[:, s_, :], ps[bu][:, :], ALU.mult, [("sgs", s_), ("ps", bu)], [("act", f)])
            yield
        if wd_ptr[0] == 0:
            wd_ensure(1)
        for half in range(2):
            bs = [4, 5, 6, 7]
            for jc in range(11):
                P.tag = 'ffn.down'
                L = wd_use[0]
                wd_use[0] += 1
                slot = L % 3
                wd_ensure(L + 2)
                for f2 in range(2):
                    f = 2 * jc + f2
                    for o in range(4):
                        mm(ps[bs[o]][:, :], wd[:, slot, f2, o * 128:(o + 1) * 128], act[:, f, :], f == 0, f == FT - 1,
                           [("wd", slot), ("act", f)], [("ps", bs[o])])
                if jc == 10:
                    for o in range(4):
                        k = half * 4 + o
                        stt(xT(par, k), ps[bs[o]][:, :], 0.25, xT(par, k), ALU.mult, ALU.add, [("ps", bs[o]), xk(par, k)], [xk(par, k)])
                yield

    def loadx_gen(s):
        par = s % 2
        P.tag = 'loadx'
        for k in range(KT):
            dma("sp", xsel(par)[:, k, :], x_d[k * 128:(k + 1) * 128, s * TS:(s + 1) * TS], [], [xk(par, k)], "xl%d" % k)
        yield

    def xn_tile(k):
        return act[:, 2 * k:2 * k + 2, :].rearrange("p a t -> p (a t)").bitcast(F32)

    def final_gen(s):
        P.tag = 'final'
        par = s % 2
        b = 4
        for k in range(KT):
            s_ = k % 2
            actf(sqF[:, s_, :], xT(par, k), AF.Square, [xk(par, k)], [("sqF", s_)])
            mm(ps[b][:, :], onesb[:], sqF[:, s_, :], k == 0, k == KT - 1, [("sqF", s_), "onesb"], [("ps", b)])
        rstd_from(b, D, rsF[:, 1, :], ("rsF", 1))
        yield
        for k in range(KT):
            P.tag = 'final'
            stt(xn_tile(k), xT(par, k), gains[:, 24 + k:25 + k], rsF[:, 1, :], ALU.mult, ALU.mult,
                [xk(par, k), "gains", ("rsF", 1)], [("act", 2 * k), ("act", 2 * k + 1)])
            dma("sp", out_d[k * 128:(k + 1) * 128, s * TS:(s + 1) * TS], xn_tile(k),
                [("act", 2 * k), ("act", 2 * k + 1)], [("out", s, k)], "out%d" % k)
        yield

    uTd = lambda m: am[:, m, :].rearrange("p (i n) -> p i n", i=8)
    qT = lambda m: am[:, 4 + m, :]
    Ugrp = lambda g: am[:, 8 + g // 8, (g % 8) * NB:(g % 8 + 1) * NB]
    Y2grp = lambda g: am[:, g // 8, (g % 8) * NB:(g % 8 + 1) * NB]
    y2T = lambda m: am[:, 12 + m, :]
    yg = lambda m: (am[:, 4 + m, :], ("am", 4 + m))
    yat = lambda m: (am[:, 16 + m, :], ("am", 16 + m))

    def rope_finish(qs, out_ap, out_key):
        qap, qkey = qs
        b2 = bankM()
        mm(ps[b2][:, :], permb[:], qap, True, True, ["permb", qkey], [("ps", b2)])
        tt(tmpf[:, 0, :], qap, ropec[:], ALU.mult, [qkey, "ropec"], [("tmpf", 0), ("tmpf", "0b")])
        tt(tmpf[:, 1, :], ps[b2][:, :], ropes[:], ALU.mult, [("ps", b2), "ropes"], [("tmpf", 1)])
        tt(out_ap, tmpf[:, 0, :], tmpf[:, 1, :], ALU.add, [("tmpf", 0), ("tmpf", 1)], [out_key])

    def mixer_gen(s):
        par = s % 2
        P.tag = 'mix.norm'
        if wi_ptr[0] == 0:
            wi_ensure(2)
            wo_ensure(0)
        dma("sp", ropec[:], dr["ropec"][:, s * TS:(s + 1) * TS], [], ["ropec"], "ropec")
        dma("sp", ropes[:], dr["ropes"][:, s * TS:(s + 1) * TS], [], ["ropes"], "ropes")
        norm_to(par, 8, hTM, "hTM", [(sqM[:, 0, :], ("sqM", 0)), (qpre[:, 0, :], ("qpre", 0))], rsM[:, 0, :], ("rsM", 0), bankM())
        yield

        def proj_group():
            L = wi_use[0]
            wi_use[0] += 1
            slot = L % 4
            wi_ensure(L + 3)
            b = bankM()
            for k in range(KT):
                mm(ps[b][:, :], wis[:, slot, k, :], hTM[:, k, :], k == 0, k == KT - 1, [("wis", slot), ("hTM", k)], [("ps", b)])
            return b

        for m in range(4):
            P.tag = 'mix.proj'
            b = proj_group()
            actf(uTd(m), ps[b][:, :].rearrange("p (n i) -> p i n", i=8), AF.Copy, [("ps", b)], [("am", m)])
            yield
        jobs = [(qT(m), ("am", 4 + m)) for m in range(4)]
        jobs += [(kTd[:, kk, 128:640], ("kTd", kk)) for kk in range(2)]
        QS = [(qpre[:, 0, :], ("qpre", 0)), (sqM[:, 0, :], ("sqM", 0))]
        pending = None
        for idx, (out_ap, out_key) in enumerate(jobs):
            P.tag = 'mix.proj'
            b = proj_group()
            qs = QS[idx % 2]
            actf(qs[0], ps[b][:, :], AF.Copy, [("ps", b)], [qs[1]])
            if pending is not None:
                rope_finish(*pending)
            pending = (qs, out_ap, out_key)
            yield
        P.tag = 'mix.proj'
        L = wi_use[0]
        wi_use[0] += 1
        slot = L % 4
        wi_ensure(L + 3)
        b = bankM()
        for blk in range(4):
            for k in range(KT):
                mm(ps[b][:, blk * 128:(blk + 1) * 128], hTM[:, k, blk * 128:(blk + 1) * 128], wis[:, slot, k, :],
                   k == 0, k == KT - 1, [("wis", slot), ("hTM", k)], [("ps", b)])
        rope_finish(*pending)
        actf(vsb[:, 1:5, :], ps[b][:, :].rearrange("p (a d) -> p a d", a=4), AF.Copy, [("ps", b)], ["vsb"])
        yield

        def attn_gen():
            ptc = 0
            units = [(m, pair) for m in range(4) for pair in range(2)]

            def a1(m, pair, hh, pi_):
                kv = m // 2
                rows = slice(hh * 64, (hh + 1) * 64)
                bs_ = bankM()
                first = (s == 0 and pair == 0)
                mk, mkk = (maskf, "maskf") if first else (mask, "mask")
                mm(ps[bs_][:, :], identb[:], mk[:], True, False, ["identb", mkk], [("ps", bs_)])
                n = 0
                for qb in range(2):
                    blkq = 2 * pair + qb
                    for piece in range(2):
                        slot = blkq + piece
                        mm(ps[bs_][:, (qb * 2 + piece) * 128:(qb * 2 + piece + 1) * 128],
                           kTd[rows, kv, slot * 128:(slot + 1) * 128], qT(m)[rows, blkq * 128:(blkq + 1) * 128],
                           False, n == 3, [("kTd", kv), ("am", 4 + m)], [("ps", bs_)])
                        n += 1
                actf(PT[:, pi_, :], ps[bs_][:, :], AF.Exp, [("ps", bs_)], [("PT", pi_)], scale=0.125)

            def pv(m, pair, hh, pi_):
                kv = m // 2
                rows = slice(hh * 64, (hh + 1) * 64)
                for qb in range(2):
                    blkq = 2 * pair + qb
                    ncol = slice(qb * 128, (qb + 1) * 128)
                    dcol = slice(256 + qb * 128, 256 + (qb + 1) * 128)
                    for piece in range(2):
                        slot = blkq + piece
                        mm(ps[BATT][rows, ncol], vsb[:, slot, kv * 64:(kv + 1) * 64], PT[:, pi_, (qb * 2 + piece) * 128:(qb * 2 + piece + 1) * 128],
                           piece == 0, piece == 1, ["vsb", ("PT", pi_)], [("ps", BATT)])
                    for piece in range(2):
                        mm(ps[BATT][rows, dcol], onesb[:, 0:64], PT[:, pi_, (qb * 2 + piece) * 128:(qb * 2 + piece + 1) * 128],
                           piece == 0, piece == 1, ["onesb", ("PT", pi_)], [("ps", BATT)])

            def norm_unit(m, pair):
                den = tmpf[:, 2, 0:256]
                ts1(den, ps[BATT][:, 256:512], esink[:, m:m + 1], ALU.add, [("ps", BATT), "esink"], [("tmpf", 2)])
                recip(den, den, [("tmpf", 2)], [("tmpf", 2)])
                ya, yak = yat(m)
                tt(ya[:, pair * 256:(pair + 1) * 256], ps[BATT][:, 0:256], den, ALU.mult, [("ps", BATT), ("tmpf", 2)], [yak])

            a1(0, 0, 0, 0); yield
            a1(0, 0, 1, 1); yield
            for ui, (m, pair) in enumerate(units):
                nxt = units[ui + 1] if ui + 1 < len(units) else None
                pv(m, pair, 0, 0); yield
                if nxt:
                    a1(nxt[0], nxt[1], 0, 0); yield
                pv(m, pair, 1, 1); yield
                norm_unit(m, pair)
                if nxt:
                    a1(nxt[0], nxt[1], 1, 1)
                yield
            b = bankM()
            for m in range(4):
                ya, yak = yat(m)
                actf(sqM[:, 0, :], ya, AF.Square, [yak], [("sqM", 0)])
                mm(ps[b][:, :], onesb[:], sqM[:, 0, :], m == 0, m == 3, [("sqM", 0), "onesb"], [("ps", b)])
            rstd_from(b, 512, rsM[:, 1, :], ("rsM", 1))
            yield

        def ssm_gen():
            cp(Hb[:, :, :, 0:1], Hc[:].unsqueeze(3), ["Hc"], ["Hb"])
            vbank = {}

            def U_unit(m):
                b = bankM()
                for gg in range(8):
                    for i in range(8):
                        mm(ps[b][:, gg * NB:(gg + 1) * NB], sel[:, gg, (7 - i) * 16:(7 - i) * 16 + 128], uTd(m)[:, i, :],
                           i == 0, i == 7, ["sel", ("am", m)], [("ps", b)])
                actf(am[:, 8 + m, :], ps[b][:, :], AF.Copy, [("ps", b)], [("am", 8 + m)])

            def V_unit(q4):
                b = bankM()
                vbank[q4] = b
                for tt_ in range(4):
                    t = 4 * q4 + tt_
                    for g2 in range(2):
                        g = 2 * t + g2
                        for ri in range(2):
                            c0 = (tt_ * 2 + ri) * NB
                            mm(ps[b][g2 * 64:(g2 + 1) * 64, c0:c0 + NB], Wm[:, g, ri * 64:(ri + 1) * 64], Ugrp(g), True, True,
                               ["Wm", ("am", 8 + g // 8)], [("ps", b)])

            def D_unit(q4):
                b = vbank[q4]
                V = ps[b][:, :].rearrange("p (t r n) -> p t r n", t=4, r=2)
                Vre, Vim = V[:, :, 0, :], V[:, :, 1, :]
                cs_ = cosT[:, 4 * q4:4 * q4 + 4, :]
                sn_ = sinT[:, 4 * q4:4 * q4 + 4, :]
                tA = tmpf[:, 0, 0:256].rearrange("p (t n) -> p t n", t=4)
                tB = tmpf[:, 0, 256:512].rearrange("p (t n) -> p t n", t=4)
                kA, kB = ("tmpf", 0), ("tmpf", "0b")
                gk = "Gin"
                tt(tA, Vre, cs_, ALU.mult, [("ps", b), "cosT"], [kA]); tt(tB, Vim, sn_, ALU.mult, [("ps", b), "sinT"], [kB])
                tt(Gin[:, 0], tA, tB, ALU.add, [kA, kB], [gk])
                tt(tA, Vim, cs_, ALU.mult, [("ps", b), "cosT"], [kA]); tt(tB, Vre, sn_, ALU.mult, [("ps", b), "sinT"], [kB])
                tt(Gin[:, 1], tA, tB, ALU.subtract, [kA, kB, gk], [gk])
                sk_ = "Gs"
                for tt_ in range(4):
                    t = 4 * q4 + tt_
                    for ri in range(2):
                        out_ap = Gs[:, ri, tt_, :]
                        d0 = r8[:, t:t + 1].to_broadcast([128, NB])
                        d1 = Gin[:, ri, tt_, :]
                        init = Hc[:, t, ri:ri + 1]
                        P.add("dve", (lambda o_=out_ap, a_=d0, b_=d1, i_=init: (lambda e: e.tensor_tensor_scan(
                            out=o_, data0=a_, data1=b_, initial=i_, op0=ALU.mult, op1=ALU.add)))(),
                            [gk, "r8", "Hc", sk_], [sk_])
                Gre, Gim = Gs[:, 0], Gs[:, 1]
                tt(tA, Gre, cs_, ALU.mult, [sk_, "cosT"], [kA]); tt(tB, Gim, sn_, ALU.mult, [sk_, "sinT"], [kB])
                tt(Gin[:, 0], tA, tB, ALU.subtract, [kA, kB, gk], [gk])
                tt(tA, Gre, sn_, ALU.mult, [sk_, "sinT"], [kA]); tt(tB, Gim, cs_, ALU.mult, [sk_, "cosT"], [kB])
                tt(Gin[:, 1], tA, tB, ALU.add, [kA, kB, gk], [gk])
                for ri in range(2):
                    actf(Hb[:, 4 * q4:4 * q4 + 4, ri, 1:NB + 1], Gin[:, ri], AF.Copy, [gk], [("Hb", q4)])
                    cp(Hc[:, 4 * q4:4 * q4 + 4, ri:ri + 1], Gin[:, ri, :, NB - 1:NB], [gk], ["Hc"])

            def Y_unit(m):
                b = bankM()
                for gg in range(8):
                    g = 8 * m + gg
                    t, g2 = g // 2, g % 2
                    rows = slice(g2 * 64, (g2 + 1) * 64)
                    o_ = ps[b][:, gg * NB:(gg + 1) * NB]
                    mm(o_, Mm[:, g, :], Ugrp(g), True, False, ["Mm", ("am", 8 + m)], [("ps", b)])
                    mm(o_, tabR[rows, t, :], Hb[rows, t, 0, 0:NB], False, False, ["tabR", "Hb", ("Hb", m)], [("ps", b)])
                    mm(o_, tabI[rows, t, :], Hb[rows, t, 1, 0:NB], False, True, ["tabI", "Hb", ("Hb", m)], [("ps", b)])
                U3 = am[:, 8 + m, :].rearrange("p (g n) -> p g n", g=8)
                Db = dblk[:, 8 * m:8 * m + 8].unsqueeze(2).to_broadcast([128, 8, NB])
                y1 = tmpf[:, 1, :]
                tt(y1.rearrange("p (g n) -> p g n", g=8), U3, Db, ALU.mult, [("am", 8 + m), "dblk"], [("tmpf", 1)])
                tt(y1, y1, ps[b][:, :], ALU.add, [("tmpf", 1), ("ps", b)], [("tmpf", 1)])
                actf(am[:, m, :], y1, AF.Gelu_apprx_tanh, [("tmpf", 1)], [("am", m)])

            def I_unit(m):
                b = bankM()
                for j in range(8):
                    for gg in range(8):
                        mm(ps[b][:, j * NB:(j + 1) * NB], sel[:, j, (7 - gg) * 16:(7 - gg) * 16 + 128], Y2grp(8 * m + gg),
                           gg == 0, gg == 7, ["sel", ("am", m)], [("ps", b)])
                actf(y2T(m).rearrange("p (n j) -> p j n", j=8), ps[b][:, :].rearrange("p (j n) -> p j n", j=8), AF.Copy,
                     [("ps", b)], [("am", 12 + m)])

            order = [(U_unit, 0), (U_unit, 1), (V_unit, 0), (D_unit, 0), (U_unit, 2), (V_unit, 1), (D_unit, 1), (U_unit, 3),
                     (V_unit, 2), (D_unit, 2), (Y_unit, 0), (V_unit, 3), (D_unit, 3), (Y_unit, 1), (I_unit, 0), (Y_unit, 2),
                     (I_unit, 1), (Y_unit, 3), (I_unit, 2), (I_unit, 3)]
            for fn_, arg in order:
                fn_(arg)
                yield
            yield "need_attn_done"
            for mo in range(4):
                b = bankM()
                for k in range(4):
                    mm(ps[b][:, :], w_glu[:, k, mo * 128:(mo + 1) * 128], y2T(k), k == 0, k == 3, ["w_glu", ("am", 12 + k)], [("ps", b)])
                sg = tmpf[:, 1, :]
                actf(sg, ps[b][:, :], AF.Tanh, [("ps", b), "hbg"], [("tmpf", 1)], bias=hbg[:, mo:mo + 1], scale=0.5)
                ygm, ygk = yg(mo)
                stt(ygm, sg, 1.0, y2T(mo), ALU.add, ALU.mult, [("am", 12 + mo), ("tmpf", 1)], [ygk])
                yield
            b = bankM()
            for mo in range(4):
                ygm, ygk = yg(mo)
                actf(sqM[:, 0, :], ygm, AF.Square, [ygk], [("sqM", 0)])
                mm(ps[b][:, :], onesb[:], sqM[:, 0, :], mo == 0, mo == 3, [("sqM", 0), "onesb"], [("ps", b)])
            rstd_from(b, 512, rsM[:, 0, :], ("rsM", 0), ec=2)
            yield

        ga, gs = attn_gen(), ssm_gen()
        a_done = s_done = s_wait = False
        while not (a_done and s_done):
            if not a_done:
                P.tag = 'mix.attn'
                try:
                    next(ga)
                except StopIteration:
                    a_done = True
                yield
            if not s_done and not (s_wait and not a_done):
                P.tag = 'mix.ssm'
                try:
                    if next(gs) == "need_attn_done":
                        s_wait = True
                except StopIteration:
                    s_done = True
                yield
        if s == 0:
            dump("yattn", am[:, 16:20, :], [("am", 16 + m) for m in range(4)], [128, 4, TS], BF16)
            dump("y2T", am[:, 12:16, :], [("am", 12 + m) for m in range(4)], [128, 4, TS], BF16)
        P.tag = 'mix.onorm'
        for k in range(4):
            ygm, ygk = yg(k)
            stt(hTM[:, k, :], ygm, gains[:, 32 + k:33 + k], rsM[:, 0, :], ALU.mult, ALU.mult, [ygk, "gains", ("rsM", 0)], [("hTM", k)])
        for k in range(4):
            ya, yak = yat(k)
            stt(hTM[:, 4 + k, :], ya, gains[:, 36 + k:37 + k], rsM[:, 1, :], ALU.mult, ALU.mult, [yak, "gains", ("rsM", 1)], [("hTM", 4 + k)])
        yield
        for o in range(8):
            P.tag = 'mix.wout'
            L = wo_use[0]
            wo_use[0] += 1
            slot = L % 2
            wo_ensure(L + 1)
            b = bankM()
            for k in range(KT):
                mm(ps[b][:, :], wo[:, slot, k, :], hTM[:, k, :], k == 0, k == KT - 1, [("wo", slot), ("hTM", k)], [("ps", b)])
            tt(xT(par, o), ps[b][:, :], xT(par, o), ALU.add, [("ps", b), xk(par, o)], [xk(par, o)])
            yield
        cp(kTd[:, :, 0:128], kTd[:, :, 512:640], [("kTd", 0), ("kTd", 1)], [("kTd", 0), ("kTd", 1)], eng="act")
        cp(vsb[:, 0, :], vsb[:, 4, :], ["vsb"], ["vsb"], eng="act")
        yield

    def drain(g):
        for _ in g:
            pass

    def chain(*gens):
        for g in gens:
            for _ in g:
                yield

    def interleave(ga, na, gb, nb):
        ca = cb = 0
        a_done = b_done = False
        while not (a_done and b_done):
            pick_a = (not a_done) and (b_done or ca * nb <= cb * na)
            if pick_a:
                try:
                    next(ga)
                    ca += 1
                except StopIteration:
                    a_done = True
            else:
                try:
                    next(gb)
                    cb += 1
                except StopIteration:
                    b_done = True

    P.tile = 0
    sb = sbp
    pg = ssm_precompute()
    next(pg)
    wgu_ensure(2)
    drain(chain(loadx_gen(0), ffn_norm_gen(0, 0)))
    fg = ffn_gen(1, 0, 0, do_norm=False)
    p_part1 = True
    f_live = True
    while p_part1 or f_live:
        if p_part1:
            P.tag = 'pre'
            if next(pg) == "PE_PART":
                p_part1 = False
        if f_live:
            try:
                next(fg)
            except StopIteration:
                f_live = False
    P.tag = 'pre'
    drain(pg)
    P.barrier()
    pstack.close()
    sb = sb_main
    xTb = sb("xTb", [128, KT, TS], F32)
    hTM = sb("hTM", [128, KT, TS], BF16)
    am = sb("am", [128, 20, TS], BF16)
    dump("x1", xTa[:], [xk(0, k) for k in range(KT)], [128, KT, TS])
    for s in range(NT_RUN):
        P.tile = s
        bparts = []
        nb = 0
        if s >= 1:
            bparts += [ffn_gen(2, 16, (s - 1) % 2), final_gen(s - 1)]
            nb += 50
        if s + 1 < NT_RUN:
            bparts += [loadx_gen(s + 1), ffn_norm_gen(0, (s + 1) % 2)]
            nb += 5
        if bparts:
            interleave(mixer_gen(s), 115, chain(*bparts), nb)
        else:
            drain(mixer_gen(s))
        if s == 0:
            dump("x2", xTa[:], [xk(0, k) for k in range(KT)], [128, KT, TS])
        if s + 1 < NT_RUN:
            drain(ffn_gen(1, 0, (s + 1) % 2, do_norm=False))
    drain(chain(ffn_gen(2, 16, (NT_RUN - 1) % 2), final_gen(NT_RUN - 1)))
    P.add("sp", None, [("out", s, k) for s in range(NT_RUN) for k in range(KT)] + [("dbg", n) for n in dbg_d], [])

    P.finalize(nc, stack)
    with nc.Block() as block:
        @block.sync
        def _(e):
            P.emit("sp", e)

        @block.tensor
        def _(e):
            P.emit("pe", e)

        @block.scalar
        def _(e):
            P.emit("act", e)

        @block.vector
        def _(e):
            P.emit("dve", e)

        @block.gpsimd
        def _(e):
            P.emit("pool", e)
    stack.close()
    nc._prog = P
    return nc, list(dbg_d.keys())


_CACHE = {}


def kernel(**inputs):
    x = np.ascontiguousarray(np.asarray(inputs["x"], dtype=np.float32))
    B = x.shape[0]
    if "nc" not in _CACHE:
        _CACHE["nc"] = build_program()
    nc, dbg = _CACHE["nc"]
    shared = host_layout(inputs)
    shared.update(host_consts())
    in_maps = []
    for b in range(B):
        m = dict(shared)
        m["x"] = np.ascontiguousarray(x[b].T)
        in_maps.append(m)
    res = run_bass_kernel_spmd(nc, in_maps, core_ids=list(range(B)))
    out = np.stack([np.ascontiguousarray(np.asarray(r["out"], dtype=np.float32).T) for r in res.results], axis=0)
    if DEBUG:
        _CACHE["dbg"] = {n: np.asarray(res.results[0]["dbg_" + n]) for n in dbg}
    return out
```

```python
import math
import os
from contextlib import ExitStack

import numpy as np
import ml_dtypes

import concourse.bass as bass
import concourse.mybir as mybir
from concourse.bass_utils import run_bass_kernel_spmd

F32 = mybir.dt.float32
BF16 = mybir.dt.bfloat16
AF = mybir.ActivationFunctionType
ALU = mybir.AluOpType

D = 1024
KT = 8
FF = 2816
FT = 22
SEQ = 4096
TS = 512
NTILE = SEQ // TS
NB = TS // 8
EPS = 1e-6
NEG = -30000.0
NWIN = 1408

DEBUG = bool(int(os.environ.get("KDBG", "0")))
NT_RUN = int(os.environ.get("KNT", str(NTILE)))


class Prog:
    ENGS = ("pe", "act", "dve", "pool", "sp")

    def __init__(self):
        self.ops = []
        self.lastw = {}
        self.readers = {}
        self.tag = ""
        self.tile = -1

    def add(self, eng, fn, reads=(), writes=(), dma=None):
        i = len(self.ops)
        deps = set()
        for k in reads:
            w = self.lastw.get(k)
            if w is not None:
                deps.add(w)
        for k in writes:
            w = self.lastw.get(k)
            if w is not None:
                deps.add(w)
            for r in self.readers.get(k, ()):
                deps.add(r)
        for k in reads:
            lst = self.readers.setdefault(k, [])
            if dma is None:
                lst[:] = [r for r in lst if not (self.ops[r]["eng"] == eng and self.ops[r]["dma"] is None)]
            lst.append(i)
        for k in writes:
            self.lastw[k] = i
            self.readers[k] = []
        deps.discard(i)
        self.ops.append(dict(eng=eng, fn=fn, deps=deps, dma=dma, signal=False, count=0, tag=self.tag, tile=self.tile))
        return i

    def barrier(self):
        last = {}
        for i, op in enumerate(self.ops):
            if op["dma"] is not None:
                if str(op["dma"]).startswith("cv"):
                    continue
                last[("d", op["dma"])] = i
            elif op["fn"] is not None:
                last[("e", op["eng"])] = i
        deps = set(last.values())
        for e in self.ENGS:
            self.ops.append(dict(eng=e, fn=None, deps=set(deps), dma=None, signal=False, count=0, tag='barrier', tile=-1))

    def finalize(self, nc, stack):
        ops = self.ops
        for op in ops:
            for d in op["deps"]:
                dop = ops[d]
                if dop["dma"] is None:
                    if dop["eng"] == "pe" and op["eng"] == "pe" and op["dma"] is None:
                        continue
                    dop["signal"] = True
        esem = {e: stack.enter_context(nc.semaphore("sem_" + e)) for e in self.ENGS}
        dsem = {}
        ecnt = {e: 0 for e in self.ENGS}
        dcnt = {}
        waited = {e: {} for e in self.ENGS}
        streams = {e: [] for e in self.ENGS}
        for op in ops:
            e = op["eng"]
            waits = {}
            for d in op["deps"]:
                dop = ops[d]
                if dop["dma"] is not None:
                    key = ("d", dop["dma"])
                    val = dop["count"]
                    sem = dsem[dop["dma"]]
                else:
                    if dop["eng"] == "pe" and e == "pe" and op["dma"] is None:
                        continue
                    key = ("e", dop["eng"])
                    val = dop["count"]
                    sem = esem[dop["eng"]]
                if waited[e].get(key, 0) >= val:
                    continue
                if key not in waits or waits[key][1] < val:
                    waits[key] = (sem, val)
            for key, (sem, val) in waits.items():
                waited[e][key] = val
            if op["dma"] is not None:
                if op["dma"] not in dsem:
                    dsem[op["dma"]] = stack.enter_context(nc.semaphore("dsem%d" % len(dsem)))
                    dcnt[op["dma"]] = 0
                dcnt[op["dma"]] += 16
                op["count"] = dcnt[op["dma"]]
                inc = (dsem[op["dma"]], 16)
            elif op["signal"]:
                ecnt[e] += 1
                op["count"] = ecnt[e]
                inc = (esem[e], 1)
            else:
                inc = None
            streams[e].append((list(waits.values()), op["fn"], inc))
        self.streams = streams
        self.nsem = len(dsem) + len(esem)

    def emit(self, eng_name, e):
        for waits, fn, inc in self.streams[eng_name]:
            for sem, val in waits:
                e.wait_ge(sem, val)
            if fn is None:
                continue
            ins = fn(e)
            if inc is not None:
                ins.then_inc(inc[0], inc[1])


def _bf(a):
    return np.ascontiguousarray(a.astype(ml_dtypes.bfloat16))


def host_consts():
    c = {}
    c["identf"] = np.eye(128, dtype=np.float32)
    c["identb"] = _bf(np.eye(128, dtype=np.float32))
    c["onesb"] = _bf(np.ones((128, 128), np.float32))
    perm = np.zeros((128, 128), np.float32)
    for h in range(2):
        for d in range(16):
            pd = d + 8 if d < 8 else d - 8
            perm[h * 64 + pd, h * 64 + d] = 1.0
    c["permb"] = _bf(perm)
    kj = np.arange(128)[:, None]
    qi = np.arange(128)[None, :]
    prev = np.where(kj > qi, 0.0, NEG).astype(np.float32)
    same = np.where(kj <= qi, 0.0, NEG).astype(np.float32)
    full = np.full((128, 128), NEG, np.float32)
    c["mask"] = _bf(np.concatenate([prev, same, prev, same], axis=1))
    c["maskf"] = _bf(np.concatenate([full, same, prev, same], axis=1))
    sel = np.zeros((128, 8, 240), np.float32)
    for g in range(8):
        for cc in range(16):
            sel[16 * g + cc, g, 112 + cc] = 1.0
    c["sel"] = _bf(sel)
    half = 8
    inv_freq = (500000.0 ** (-np.arange(half, dtype=np.float32) * 2.0 / 16)).astype(np.float32)
    ang = np.arange(SEQ, dtype=np.float32)[:, None] * inv_freq[None, :]
    cos = np.cos(ang).astype(np.float32).T
    sin = np.sin(ang).astype(np.float32).T
    C = np.ones((128, SEQ), np.float32)
    S = np.zeros((128, SEQ), np.float32)
    for h in range(2):
        C[h * 64 + 0:h * 64 + 8] = cos
        C[h * 64 + 8:h * 64 + 16] = cos
        S[h * 64 + 0:h * 64 + 8] = -sin
        S[h * 64 + 8:h * 64 + 16] = sin
    c["ropec"] = C
    c["ropes"] = S
    return c


def host_layout(inp):
    o = {}
    f = lambda a: np.ascontiguousarray(np.asarray(a, dtype=np.float32))
    o["wg1"] = f(inp["ffn1_w_gate"][0]); o["wu1"] = f(inp["ffn1_w_up"][0]); o["wd1"] = f(inp["ffn1_w_down"][0])
    o["wg2"] = f(inp["ffn2_w_gate"][0]); o["wu2"] = f(inp["ffn2_w_up"][0]); o["wd2"] = f(inp["ffn2_w_down"][0])
    w_in = f(inp["w_in"][0])
    u = w_in[:, 0:512]; q = w_in[:, 512:1024]; k = w_in[:, 1024:1152]; v = w_in[:, 1152:1280]
    o["win"] = np.ascontiguousarray(np.concatenate([u, q, k[:, 0:64], k[:, 0:64], k[:, 64:128], k[:, 64:128], v], axis=1))
    o["wout"] = f(inp["w_out"][0])
    o["wglu"] = f(inp["ssm_w_glu"][0])
    fm = lambda vec: np.ascontiguousarray(f(vec).reshape(-1, 128).T)
    gains = np.concatenate([fm(inp["ffn1_norm"][0]), fm(inp["mix_norm"][0]), fm(inp["ffn2_norm"][0]),
                            fm(inp["final_norm"]), fm(inp["ssm_out_norm"][0]), fm(inp["attn_out_norm"][0]),
                            fm(inp["ssm_b_glu"][0])], axis=1)
    o["gains"] = np.ascontiguousarray(gains)
    Dv = f(inp["ssm_D"][0]).reshape(32, 16)
    o["dblk"] = np.ascontiguousarray(np.tile(Dv.T, (8, 1)))
    sk = f(inp["attn_sinks"][0])
    o["sinkrow"] = np.ascontiguousarray(np.repeat(sk.reshape(4, 2), 64, axis=1).T)
    pl = lambda a: np.ascontiguousarray(f(a).reshape(16, 2, 64).transpose(1, 2, 0).reshape(128, 16))
    o["are"] = pl(inp["ssm_A_re"][0]); o["aim"] = pl(inp["ssm_A_im"][0])
    ldt = f(inp["ssm_log_dt"][0])
    o["ldt"] = pl(np.repeat(ldt[:, None], 64, axis=1))
    pb = lambda a: np.ascontiguousarray(f(a).reshape(16, 2, 64, 16).transpose(1, 2, 0, 3).reshape(128, 16, 16))
    o["bre"] = pb(inp["ssm_B_re"][0]); o["bim"] = pb(inp["ssm_B_im"][0])
    pc = lambda a: np.ascontiguousarray(f(a).transpose(0, 2, 1).reshape(16, 2, 64, 16).transpose(1, 2, 0, 3).reshape(128, 16, 16))
    o["cre"] = pc(inp["ssm_C_re"][0]); o["cim"] = pc(inp["ssm_C_im"][0])
    return o


def build_program():
    nc = bass.Bass("TRN2", target_bir_lowering=False)
    P = Prog()
    stack = ExitStack()
    dr = {}

    def din(name, shape, dt=F32):
        dr[name] = nc.dram_tensor(name, list(shape), dt, kind="ExternalInput").ap()
        return dr[name]

    x_d = din("x", [D, SEQ])
    for n in ("wg1", "wu1", "wg2", "wu2"):
        din(n, [D, FF])
    for n in ("wd1", "wd2"):
        din(n, [FF, D])
    din("win", [D, NWIN]); din("wout", [D, D]); din("wglu", [512, 512])
    din("gains", [128, 44]); din("dblk", [128, 32]); din("sinkrow", [128, 4])
    for n in ("are", "aim", "ldt"):
        din(n, [128, 16])
    for n in ("bre", "bim", "cre", "cim"):
        din(n, [128, 16, 16])
    din("identf", [128, 128]); din("identb", [128, 128], BF16); din("onesb", [128, 128], BF16)
    din("permb", [128, 128], BF16); din("mask", [128, 512], BF16); din("maskf", [128, 512], BF16)
    din("sel", [128, 8, 240], BF16); din("ropec", [128, SEQ]); din("ropes", [128, SEQ])
    out_d = nc.dram_tensor("out", [D, SEQ], F32, kind="ExternalOutput").ap()
    dbg_d = {}

    def sb_main(name, shape, dt):
        return stack.enter_context(nc.sbuf_tensor("s_" + name, list(shape), dt))

    sb = sb_main

    wgu = sb("wgu", [128, 4, 2, 4, 128], BF16)
    wd = sb("wd", [128, 3, 2, 512], BF16)
    wis = sb("wis", [128, 4, KT, 128], BF16)
    wo = sb("wo", [128, 2, KT, 128], BF16)
    w_glu = sb("w_glu", [128, 4, 512], BF16)
    sgs = sb("sgs", [128, 2, TS], F32)
    sqF = sb("sqF", [128, 2, TS], BF16)
    sqM = sb("sqM", [128, 1, TS], BF16)
    rsF = sb("rsF", [128, 2, TS], F32)
    rsM = sb("rsM", [128, 2, TS], F32)
    tabR = sb("tabR", [128, 16, 128], BF16)
    tabI = sb("tabI", [128, 16, 128], BF16)
    Wm = sb("Wm", [128, 32, 128], BF16)
    Mm = sb("Mm", [128, 32, 128], BF16)
    cosT = sb("cosT", [128, 16, NB], F32)
    sinT = sb("sinT", [128, 16, NB], F32)
    r8 = sb("r8", [128, 16], F32)
    sel = sb("sel", [128, 8, 240], BF16)
    dblk = sb("dblk", [128, 32], F32)
    gains = sb("gains", [128, 44], F32)
    esink = sb("esink", [128, 4], F32)
    cst = sb("cst", [128, 4], F32)
    identf = sb("identf", [128, 128], F32)
    identb = sb("identb", [128, 128], BF16)
    onesb = sb("onesb", [128, 128], BF16)
    permb = sb("permb", [128, 128], BF16)
    mask = sb("mask", [128, 512], BF16)
    maskf = sb("maskf", [128, 512], BF16)
    kTd = sb("kTd", [128, 2, 5 * 128], BF16)
    vsb = sb("vsb", [128, 5, 128], BF16)
    PT = sb("PT", [128, 2, 512], BF16)
    ropec = sb("ropec", [128, TS], F32)
    ropes = sb("ropes", [128, TS], F32)
    qpre = sb("qpre", [128, 1, TS], BF16)
    tmpf = sb("tmpf", [128, 3, TS], F32)
    Gin = sb("Gin", [128, 2, 4, NB], F32)
    Gs = sb("Gs", [128, 2, 4, NB], F32)
    Hb = sb("Hb", [128, 16, 2, NB + 1], BF16)
    Hc = sb("Hc", [128, 16, 2], F32)

    ps = [stack.enter_context(nc.psum_tensor("ps%d" % i, [128, 512], F32)) for i in range(8)]

    rrM = [0]
    rrF = [0]

    def bankM():
        b = rrM[0] % 3
        rrM[0] += 1
        return b

    def bankF():
        b = 4 + rrF[0] % 4
        rrF[0] += 1
        return b

    bankA = bankM
    BATT = 3

    def mm(out, lhsT, rhs, start, stop, reads, writes):
        P.add("pe", lambda e: e.matmul(out, lhsT=lhsT, rhs=rhs, start=start, stop=stop), reads, writes)

    def tr(out, in_, reads, writes):
        P.add("pe", lambda e: e.transpose(out, in_, identf[:]), reads + ["identf"], writes)

    def actf(out, in_, func, reads, writes, bias=None, scale=None):
        kw = {}
        if bias is not None:
            kw["bias"] = bias
        if scale is not None:
            kw["scale"] = scale
        P.add("act", lambda e: e.activation(out=out, in_=in_, func=func, **kw), reads, writes)

    def tt(out, in0, in1, op, reads, writes, eng="dve"):
        P.add(eng, lambda e: e.tensor_tensor(out=out, in0=in0, in1=in1, op=op), reads, writes)

    def ts1(out, in0, s1, op0, reads, writes, eng="dve"):
        P.add(eng, lambda e: e.tensor_single_scalar(out=out, in_=in0, scalar=s1, op=op0), reads, writes)

    def stt(out, in0, scalar, in1, op0, op1, reads, writes):
        P.add("dve", lambda e: e.scalar_tensor_tensor(out=out, in0=in0, scalar=scalar, in1=in1, op0=op0, op1=op1), reads, writes)

    def cp(out, in_, reads, writes, eng="dve"):
        if eng == "act":
            P.add("act", lambda e: e.activation(out=out, in_=in_, func=AF.Copy), reads, writes)
        else:
            P.add(eng, lambda e: e.tensor_copy(out=out, in_=in_), reads, writes)

    def recip(out, in_, reads, writes):
        P.add("dve", lambda e: e.reciprocal(out=out, in_=in_), reads, writes)

    def memset(ap, val, writes, eng="dve"):
        P.add(eng, lambda e: e.memset(ap, val), [], writes)

    def dma(q, out, in_, reads, writes, key):
        P.add(q, lambda e: e.dma_start(out=out, in_=in_), reads, writes, dma=key)

    def dump(name, ap, reads, shape, dt=F32):
        if not DEBUG:
            return
        if name not in dbg_d:
            dbg_d[name] = nc.dram_tensor("dbg_" + name, list(shape), dt, kind="ExternalOutput").ap()
        dma("sp", dbg_d[name], ap, reads, [("dbg", name)], "dbg_" + name)

    def ld(q, t, src, key):
        dma(q, t[:], src, [], [key], "ld_" + key)

    ld("sp", identf, dr["identf"], "identf"); ld("sp", identb, dr["identb"], "identb")
    ld("sp", onesb, dr["onesb"], "onesb"); ld("sp", permb, dr["permb"], "permb")
    ld("sp", mask, dr["mask"], "mask"); ld("sp", maskf, dr["maskf"], "maskf")
    ld("sp", sel, dr["sel"], "sel"); ld("sp", gains, dr["gains"], "gains")
    ld("sp", dblk, dr["dblk"], "dblk"); ld("sp", esink, dr["sinkrow"], "esink")
    memset(cst[:, 0:1], EPS, ["cst"])
    memset(cst[:, 1:2], math.pi / 2, ["cst"])
    memset(cst[:, 2:3], 4.0 * EPS, ["cst"])
    memset(kTd[:], 0.0, ["kTd"])
    memset(vsb[:], 0.0, ["vsb"])
    memset(Hc[:], 0.0, ["Hc"])
    memset(Hb[:], 0.0, ["Hb"])
    actf(esink[:], esink[:], AF.Exp, ["esink"], ["esink"])
    hbg = sb("hbg", [128, 4], F32)
    ts1(hbg[:], gains[:, 40:44], 0.5, ALU.mult, ["gains"], ["hbg"])

    scr_gu = {fid: nc.dram_tensor("scr_gu%d" % fid, [2 * FT, 128, 2 * 4 * 128], BF16).ap() for fid in (1, 2)}
    scr_d = {fid: nc.dram_tensor("scr_d%d" % fid, [22, 128, 2 * 512], BF16).ap() for fid in (1, 2)}
    scr_wo = nc.dram_tensor("scr_wo", [8, 128, KT * 128], BF16).ap()
    scr_wi = nc.dram_tensor("scr_wi", [11, 128, KT * 128], BF16).ap()

    ffn_order = [1]
    for s in range(NT_RUN - 1):
        ffn_order += [1, 2]
    ffn_order.append(2)
    ffn_w = {1: ("wg1", "wu1", "wd1"), 2: ("wg2", "wu2", "wd2")}
    wgu_loads = [(fid, f, kh) for fid in ffn_order for f in range(FT) for kh in range(2)]
    wd_loads = [(fid, half, jc) for fid in ffn_order for half in range(2) for jc in range(11)]
    wgu_ptr = [0]
    wd_ptr = [0]
    seen_gu = set()
    seen_d = set()
    multi = NT_RUN > 1

    NCV = 16
    cvi = [0]

    def conv(out_ap, in_ap, key):
        i = cvi[0] % NCV
        cvi[0] += 1
        dma("pool", out_ap, in_ap, [], [key, ("cvslot", i)], "cv%d" % i)

    def conv_gu(fid):
        gname, uname, _ = ffn_w[fid]
        for f in range(FT):
            for kh in range(2):
                scr = scr_gu[fid][2 * f + kh].rearrange("p (g k c) -> p g k c", g=2, k=4)
                for gi, nm in enumerate((gname, uname)):
                    src = dr[nm].rearrange("(k p) f -> p k f", p=128)[:, 4 * kh:4 * kh + 4, f * 128:(f + 1) * 128]
                    conv(scr[:, gi], src, ("scr_gu", fid, f, kh, gi))

    def conv_d(fid):
        for half in range(2):
            for jc in range(11):
                scr = scr_d[fid][half * 11 + jc].rearrange("p (f d) -> p f d", f=2)
                src = dr[ffn_w[fid][2]].rearrange("(f p) d -> p f d", p=128)[:, 2 * jc:2 * jc + 2, half * 512:(half + 1) * 512]
                conv(scr, src, ("scr_d", fid, half, jc))

    P.tag = 'prep'
    dma("pool", w_glu[:], dr["wglu"].rearrange("(k p) f -> p k f", p=128), [], ["w_glu"], "ld_w_glu")
    conv_gu(1)
    conv_d(1)
    for cg in range(11):
        conv(scr_wi[cg].rearrange("p (k c) -> p k c", k=KT), dr["win"].rearrange("(k p) f -> p k f", p=128)[:, :, cg * 128:(cg + 1) * 128], ("scr_wi", cg))
    for o in range(8):
        conv(scr_wo[o].rearrange("p (k c) -> p k c", k=KT), dr["wout"].rearrange("(k p) d -> p k d", p=128)[:, :, o * 128:(o + 1) * 128], ("scr_wo", o))
    conv_gu(2)
    conv_d(2)

    def wgu_ensure(upto):
        while wgu_ptr[0] <= upto and wgu_ptr[0] < len(wgu_loads):
            L = wgu_ptr[0]
            fid, f, kh = wgu_loads[L]
            slot = L % 4
            scr = scr_gu[fid][2 * f + kh].rearrange("p (g k c) -> p g k c", g=2, k=4)
            dma("sp", wgu[:, slot], scr, [("scr_gu", fid, f, kh, 0), ("scr_gu", fid, f, kh, 1)], [("wgu", slot, 0), ("wgu", slot, 1)], "wgu%d" % slot)
            wgu_ptr[0] += 1

    def wd_ensure(upto):
        while wd_ptr[0] <= upto and wd_ptr[0] < len(wd_loads):
            L = wd_ptr[0]
            fid, half, jc = wd_loads[L]
            slot = L % 3
            scr = scr_d[fid][half * 11 + jc].rearrange("p (f d) -> p f d", f=2)
            dma("sp", wd[:, slot], scr, [("scr_d", fid, half, jc)], [("wd", slot)], "wd%d" % slot)
            wd_ptr[0] += 1

    wgu_use = [0]
    wd_use = [0]

    wi_ptr = [0]
    wi_use = [0]
    wo_ptr = [0]
    wo_use = [0]

    def wi_ensure(upto):
        while wi_ptr[0] <= upto and wi_ptr[0] < 11 * NT_RUN:
            L = wi_ptr[0]
            cg = L % 11
            slot = L % 4
            dma("sp", wis[:, slot], scr_wi[cg].rearrange("p (k c) -> p k c", k=KT), [("scr_wi", cg)], [("wis", slot)], "wis%d" % slot)
            wi_ptr[0] += 1

    def wo_ensure(upto):
        while wo_ptr[0] <= upto and wo_ptr[0] < 8 * NT_RUN:
            L = wo_ptr[0]
            o = L % 8
            slot = L % 2
            dma("sp", wo[:, slot], scr_wo[o].rearrange("p (k c) -> p k c", k=KT), [("scr_wo", o)], [("wo", slot)], "wo%d" % slot)
            wo_ptr[0] += 1

    def ssm_precompute():
        P.tag = 'pre'
        are = sb("p_are", [128, 16], F32); aim = sb("p_aim", [128, 16], F32); ldt = sb("p_ldt", [128, 16], F32)
        bre = sb("p_bre", [128, 16, 16], F32); bim = sb("p_bim", [128, 16, 16], F32)
        cre = sb("p_cre", [128, 16, 16], F32); cim = sb("p_cim", [128, 16, 16], F32)
        sm = sb("p_sm", [128, 24, 16], F32)
        pw = sb("p_pw", [128, 2, 16, 9], F32)
        bb = sb("p_bb", [128, 2, 16, 16], F32)
        bbp = sb("p_bbp", [128, 2, 2, 240], BF16)
        wtt = sb("p_wt", [128, 2, 2, 8, 16], F32)
        big = sb("p_big", [128, 2, 16, 64], F32)
        for nm, t in (("are", are), ("aim", aim), ("ldt", ldt), ("bre", bre), ("bim", bim), ("cre", cre), ("cim", cim)):
            dma("sp", t[:], dr[nm], [], ["p_" + nm], "ld_p_" + nm)
        S = lambda i: sm[:, i, :]
        K = "p_sm"
        rk = ["p_are", "p_aim", "p_ldt", K, "cst"]
        DT, AR, TH, MAG, C0, S0, CC, SS, CS, LBR, LBI, DEN, NRE, T1, T2, ZR, ZI, C8, S8, T3 = range(20)
        actf(S(DT), ldt[:], AF.Exp, rk, [K])
        tt(S(AR), are[:], S(DT), ALU.mult, rk, [K])
        tt(S(TH), aim[:], S(DT), ALU.mult, rk, [K])
        actf(S(MAG), S(AR), AF.Exp, rk, [K])
        actf(r8[:], S(AR), AF.Exp, rk, ["r8"], scale=8.0)
        actf(S(S0), S(TH), AF.Sin, rk, [K], scale=1.0 / 16)
        actf(S(C0), S(TH), AF.Sin, rk, [K], scale=1.0 / 16, bias=cst[:, 1:2])
        yield

        def dbl():
            tt(S(CC), S(C0), S(C0), ALU.mult, rk, [K])
            tt(S(SS), S(S0), S(S0), ALU.mult, rk, [K])
            tt(S(CS), S(C0), S(S0), ALU.mult, rk, [K])
            tt(S(C0), S(CC), S(SS), ALU.subtract, rk, [K])
            ts1(S(S0), S(CS), 2.0, ALU.mult, rk, [K])
        for _ in range(4):
            dbl()
            yield
        tt(S(LBR), S(MAG), S(C0), ALU.mult, rk, [K])
        tt(S(LBI), S(MAG), S(S0), ALU.mult, rk, [K])
        for _ in range(3):
            dbl()
            yield
        cp(S(C8), S(C0), rk, [K]); cp(S(S8), S(S0), rk, [K])
        tt(S(T1), are[:], are[:], ALU.mult, rk, [K])
        tt(S(T2), aim[:], aim[:], ALU.mult, rk, [K])
        tt(S(DEN), S(T1), S(T2), ALU.add, rk, [K])
        recip(S(DEN), S(DEN), rk, [K])
        ts1(S(NRE), S(LBR), -1.0, ALU.add, rk, [K])
        tt(S(T1), S(NRE), are[:], ALU.mult, rk, [K])
        tt(S(T2), S(LBI), aim[:], ALU.mult, rk, [K])
        tt(S(T1), S(T1), S(T2), ALU.add, rk, [K])
        tt(S(ZR), S(T1), S(DEN), ALU.mult, rk, [K])
        tt(S(T1), S(LBI), are[:], ALU.mult, rk, [K])
        tt(S(T2), S(NRE), aim[:], ALU.mult, rk, [K])
        tt(S(T1), S(T1), S(T2), ALU.subtract, rk, [K])
        tt(S(ZI), S(T1), S(DEN), ALU.mult, rk, [K])
        yield
        kp = ["p_pw", K]
        memset(pw[:, 0, :, 0:1], 1.0, ["p_pw"]); memset(pw[:, 1, :, 0:1], 0.0, ["p_pw"])
        for k in range(1, 9):
            pr, pi_ = pw[:, 0, :, k - 1], pw[:, 1, :, k - 1]
            tt(S(T1), pr, S(LBR), ALU.mult, kp, [K]); tt(S(T2), pi_, S(LBI), ALU.mult, kp, [K])
            tt(pw[:, 0, :, k], S(T1), S(T2), ALU.subtract, kp, ["p_pw"])
            tt(S(T1), pr, S(LBI), ALU.mult, kp, [K]); tt(S(T2), pi_, S(LBR), ALU.mult, kp, [K])
            tt(pw[:, 1, :, k], S(T1), S(T2), ALU.add, kp, ["p_pw"])
            yield
        bc16 = lambda ap2: ap2.unsqueeze(2).to_broadcast([128, 16, 16])
        kb = ["p_bre", "p_bim", K, "p_bb", "p_big"]
        t1 = big[:, 0, :, 0:16]; t2 = big[:, 1, :, 0:16]
        tt(t1, bre[:], bc16(S(ZR)), ALU.mult, kb, ["p_big"]); tt(t2, bim[:], bc16(S(ZI)), ALU.mult, kb, ["p_big"])
        tt(bb[:, 0], t1, t2, ALU.subtract, kb, ["p_bb"])
        tt(t1, bim[:], bc16(S(ZR)), ALU.mult, kb, ["p_big"]); tt(t2, bre[:], bc16(S(ZI)), ALU.mult, kb, ["p_big"])
        tt(bb[:, 1], t1, t2, ALU.add, kb, ["p_bb"])
        memset(bbp[:], 0.0, [("p_bbp", 0), ("p_bbp", 1)])
        yield
        kc = ["p_cre", "p_cim", "p_pw", "p_big", "p_wt"]
        tabRe = sb("p_tabRe", [128, 16, 256], BF16); tabIm = sb("p_tabIm", [128, 16, 256], BF16)
        memset(tabRe[:], 0.0, ["tabRe"]); memset(tabIm[:], 0.0, ["tabIm"])
        for tau in range(9):
            pr = pw[:, 0, :, tau:tau + 1].to_broadcast([128, 16, 16])
            pi_ = pw[:, 1, :, tau:tau + 1].to_broadcast([128, 16, 16])
            o_re = tabRe[:, :, 112 + tau * 16:112 + (tau + 1) * 16]; o_im = tabIm[:, :, 112 + tau * 16:112 + (tau + 1) * 16]
            ta = wtt[:, 0].rearrange("p r i c -> p (r i) c")
            tb = wtt[:, 1].rearrange("p r i c -> p (r i) c")
            tt(ta, cre[:], pr, ALU.mult, kc, ["p_wt"]); tt(tb, cim[:], pi_, ALU.mult, kc, ["p_wt"])
            tt(o_re, ta, tb, ALU.subtract, kc, ["tabRe"])
            tt(ta, cre[:], pi_, ALU.mult, kc, ["p_wt"]); tt(tb, cim[:], pr, ALU.mult, kc, ["p_wt"])
            tt(tb, ta, tb, ALU.add, kc, ["p_wt"])
            ts1(o_im, tb, -1.0, ALU.mult, kc, ["tabIm"])
            yield
        yield "PE_PART"
        pwr = sb("p_pwr", [128, 3, 16, 8], F32)
        for i in range(8):
            cp(pwr[:, 0, :, i:i + 1], pw[:, 0, :, 7 - i:8 - i], ["p_pw"], ["p_pwr"])
            cp(pwr[:, 1, :, i:i + 1], pw[:, 1, :, 7 - i:8 - i], ["p_pw"], ["p_pwr"])
        ts1(pwr[:, 2], pwr[:, 1], -1.0, ALU.mult, ["p_pwr"], ["p_pwr"])
        kw_ = ["p_pwr", "p_bb", "p_wt", "p_big"]
        for t in range(16):
            buf = t % 2
            wre = wtt[:, buf, 0]; wim = wtt[:, buf, 1]
            pr = pwr[:, 0, t, :].unsqueeze(2).to_broadcast([128, 8, 16])
            pi_ = pwr[:, 1, t, :].unsqueeze(2).to_broadcast([128, 8, 16])
            br = bb[:, 0, t, :].unsqueeze(1).to_broadcast([128, 8, 16])
            bi = bb[:, 1, t, :].unsqueeze(1).to_broadcast([128, 8, 16])
            x1 = big[:, 0, 0:8, 0:16]; x2 = big[:, 1, 0:8, 0:16]
            tt(x1, pr, br, ALU.mult, kw_, ["p_big"]); tt(x2, pi_, bi, ALU.mult, kw_, ["p_big"])
            tt(wre, x1, x2, ALU.subtract, kw_, ["p_wt"])
            tt(x1, pr, bi, ALU.mult, kw_, ["p_big"]); tt(x2, pi_, br, ALU.mult, kw_, ["p_big"])
            tt(wim, x1, x2, ALU.add, kw_, ["p_wt"])
            b = bankA()
            tr(ps[b][:, 0:128], wre.rearrange("p i c -> p (i c)"), ["p_wt"], [("ps", b)])
            tr(ps[b][:, 128:256], wim.rearrange("p i c -> p (i c)"), ["p_wt"], [("ps", b)])
            src = ps[b][:, 0:256].rearrange("p (r g q) -> p g r q", r=2, g=2)
            dst = Wm[:, 2 * t:2 * t + 2, :].rearrange("p g (r q) -> p g r q", r=2)
            cp(dst, src, [("ps", b)], ["Wm"])
            yield
        for g in range(32):
            t, g2 = g // 2, g % 2
            buf = t % 2
            if g2 == 0:
                cp(bbp[:, buf, 0, 112:128], bb[:, 0, t, :], ["p_bb"], [("p_bbp", buf)])
                cp(bbp[:, buf, 1, 112:128], bb[:, 1, t, :], ["p_bb"], [("p_bbp", buf)])
            rows = slice(g2 * 64, (g2 + 1) * 64)
            b = bankA()
            n = 0
            for i in range(8):
                off = (7 - i) * 16
                for ri, tab in ((0, tabRe), (1, tabIm)):
                    mm(ps[b][:, 0:128], bbp[rows, buf, ri, off:off + 128], tab[rows, t, off:off + 128],
                       n == 0, n == 15, [("p_bbp", buf), "tabRe", "tabIm"], [("ps", b)])
                    n += 1
            cp(Mm[:, g, :], ps[b][:, 0:128], [("ps", b)], ["Mm"], eng="act")
            yield
        cp(tabR[:], tabRe[:, :, 128:256], ["tabRe"], ["tabR"])
        cp(tabI[:], tabIm[:, :, 128:256], ["tabIm"], ["tabI"])
        kt = ["cosT", "sinT", K, "p_wt", "p_big"]
        cp(cosT[:, :, 0:1], S(C8).unsqueeze(2), kt, ["cosT"]); cp(sinT[:, :, 0:1], S(S8).unsqueeze(2), kt, ["sinT"])
        m = 1
        while m < NB:
            cr = cosT[:, :, m - 1:m].to_broadcast([128, 16, m]); sr = sinT[:, :, m - 1:m].to_broadcast([128, 16, m])
            a1 = big[:, 0, :, 0:m]; a2 = big[:, 1, :, 0:m]; a3 = big[:, 0, :, 32:32 + m]; a4 = big[:, 1, :, 32:32 + m]
            tt(a1, cosT[:, :, 0:m], cr, ALU.mult, kt, ["p_big"]); tt(a2, sinT[:, :, 0:m], sr, ALU.mult, kt, ["p_big"])
            tt(a3, cosT[:, :, 0:m], sr, ALU.mult, kt, ["p_big"]); tt(a4, sinT[:, :, 0:m], cr, ALU.mult, kt, ["p_big"])
            tt(cosT[:, :, m:2 * m], a1, a2, ALU.subtract, kt, ["cosT"])
            tt(sinT[:, :, m:2 * m], a3, a4, ALU.add, kt, ["sinT"])
            m *= 2
            yield

    pstack = ExitStack()

    def sbp(name, shape, dt):
        return pstack.enter_context(nc.sbuf_tensor("s_" + name, list(shape), dt))

    xTa = sb("xTa", [128, KT, TS], F32)
    hTF = sb("hTF", [128, KT, TS], BF16)
    act = sb("act", [128, FT, TS], BF16)
    xsel = lambda par: xTa if par == 0 else xTb

    def xT(par, k):
        return xsel(par)[:, k, :]

    def xk(par, k):
        return ("xT", par, k)

    def rstd_from(b, n, dst, dkey, ec=0):
        actf(dst, ps[b][:, :], AF.Sqrt, [("ps", b), "cst"], [dkey], bias=cst[:, ec:ec + 1], scale=1.0 / n)
        recip(dst, dst, [dkey], [dkey])

    def norm_to(par, goff, hT, hkey, sqs, rdst, rkey, b):
        for k in range(KT):
            sap, skey = sqs[k % len(sqs)]
            actf(sap, xT(par, k), AF.Square, [xk(par, k)], [skey])
            mm(ps[b][:, :], onesb[:], sap, k == 0, k == KT - 1, [skey, "onesb"], [("ps", b)])
        rstd_from(b, D, rdst, rkey)
        for k in range(KT):
            stt(hT[:, k, :], xT(par, k), gains[:, goff + k:goff + k + 1], rdst, ALU.mult, ALU.mult,
                [xk(par, k), "gains", rkey], [(hkey, k)])

    def ffn_norm_gen(goff, par):
        P.tag = 'ffn.norm'
        norm_to(par, goff, hTF, "hTF", [(sqF[:, 0, :], ("sqF", 0)), (sqF[:, 1, :], ("sqF", 1))], rsF[:, 0, :], ("rsF", 0), 4)
        yield

    def ffn_gen(fid, goff, par, do_norm=True):
        if do_norm:
            P.tag = 'ffn.norm'
            norm_to(par, goff, hTF, "hTF", [(sqF[:, 0, :], ("sqF", 0)), (sqF[:, 1, :], ("sqF", 1))], rsF[:, 0, :], ("rsF", 0), 4)
            yield
        for f in range(FT):
            P.tag = 'ffn.gu'
            bg, bu = (4, 5) if f % 2 == 0 else (6, 7)
            for kh in range(2):
                L = wgu_use[0]
                wgu_use[0] += 1
                slot = L % 4
                wgu_ensure(L + 3)
                for k4 in range(4):
                    k = 4 * kh + k4
                    mm(ps[bg][:, :], wgu[:, slot, 0, k4, :], hTF[:, k, :], k == 0, k == KT - 1,
                       [("wgu", slot, 0), ("hTF", k)], [("ps", bg)])
                for k4 in range(4):
                    k = 4 * kh + k4
                    mm(ps[bu][:, :], wgu[:, slot, 1, k4, :], hTF[:, k, :], k == 0, k == KT - 1,
                       [("wgu", slot, 1), ("hTF", k)], [("ps", bu)])
            s_ = f % 2
            actf(sgs[:, s_, :], ps[bg][:, :], AF.Tanh, [("ps", bg)], [("sgs", s_)], scale=0.5)
            stt(sgs[:, s_, :], sgs[:, s_, :], 1.0, ps[bg][:, :], ALU.add, ALU.mult, [("sgs", s_), ("ps", bg)], [("sgs", s_)])
            tt(act[:, f, :], sgs[:, s_, :], ps[bu][:, :], ALU.mult, [("sgs", s_), ("ps", bu)], [("act", f)])
            yield
        if wd_ptr[0] == 0:
            wd_ensure(1)
        for half in range(2):
            bs = [4, 5, 6, 7]
            for jc in range(11):
                P.tag = 'ffn.down'
                L = wd_use[0]
                wd_use[0] += 1
                slot = L % 3
                wd_ensure(L + 2)
                for f2 in range(2):
                    f = 2 * jc + f2
                    for o in range(4):
                        mm(ps[bs[o]][:, :], wd[:, slot, f2, o * 128:(o + 1) * 128], act[:, f, :], f == 0, f == FT - 1,
                           [("wd", slot), ("act", f)], [("ps", bs[o])])
                if jc == 10:
                    for o in range(4):
                        k = half * 4 + o
                        stt(xT(par, k), ps[bs[o]][:, :], 0.25, xT(par, k), ALU.mult, ALU.add, [("ps", bs[o]), xk(par, k)], [xk(par, k)])
                yield

    def loadx_gen(s):
        par = s % 2
        P.tag = 'loadx'
        for k in range(KT):
            dma("sp", xsel(par)[:, k, :], x_d[k * 128:(k + 1) * 128, s * TS:(s + 1) * TS], [], [xk(par, k)], "xl%d" % k)
        yield

    def xn_tile(k):
        return act[:, 2 * k:2 * k + 2, :].rearrange("p a t -> p (a t)").bitcast(F32)

    def final_gen(s):
        P.tag = 'final'
        par = s % 2
        b = 4
        for k in range(KT):
            s_ = k % 2
            actf(sqF[:, s_, :], xT(par, k), AF.Square, [xk(par, k)], [("sqF", s_)])
            mm(ps[b][:, :], onesb[:], sqF[:, s_, :], k == 0, k == KT - 1, [("sqF", s_), "onesb"], [("ps", b)])
        rstd_from(b, D, rsF[:, 1, :], ("rsF", 1))
        yield
        for k in range(KT):
            P.tag = 'final'
            stt(xn_tile(k), xT(par, k), gains[:, 24 + k:25 + k], rsF[:, 1, :], ALU.mult, ALU.mult,
                [xk(par, k), "gains", ("rsF", 1)], [("act", 2 * k), ("act", 2 * k + 1)])
            dma("sp", out_d[k * 128:(k + 1) * 128, s * TS:(s + 1) * TS], xn_tile(k),
                [("act", 2 * k), ("act", 2 * k + 1)], [("out", s, k)], "out%d" % k)
        yield

    uTd = lambda m: am[:, m, :].rearrange("p (i n) -> p i n", i=8)
    qT = lambda m: am[:, 4 + m, :]
    Ugrp = lambda g: am[:, 8 + g // 8, (g % 8) * NB:(g % 8 + 1) * NB]
    Y2grp = lambda g: am[:, g // 8, (g % 8) * NB:(g % 8 + 1) * NB]
    y2T = lambda m: am[:, 12 + m, :]
    yg = lambda m: (am[:, 4 + m, :], ("am", 4 + m))
    yat = lambda m: (am[:, 16 + m, :], ("am", 16 + m))

    def rope_finish(qs, out_ap, out_key):
        qap, qkey = qs
        b2 = bankM()
        mm(ps[b2][:, :], permb[:], qap, True, True, ["permb", qkey], [("ps", b2)])
        tt(tmpf[:, 0, :], qap, ropec[:], ALU.mult, [qkey, "ropec"], [("tmpf", 0), ("tmpf", "0b")])
        tt(tmpf[:, 1, :], ps[b2][:, :], ropes[:], ALU.mult, [("ps", b2), "ropes"], [("tmpf", 1)])
        tt(out_ap, tmpf[:, 0, :], tmpf[:, 1, :], ALU.add, [("tmpf", 0), ("tmpf", 1)], [out_key])

    def mixer_gen(s):
        par = s % 2
        P.tag = 'mix.norm'
        if wi_ptr[0] == 0:
            wi_ensure(2)
            wo_ensure(0)
        dma("sp", ropec[:], dr["ropec"][:, s * TS:(s + 1) * TS], [], ["ropec"], "ropec")
        dma("sp", ropes[:], dr["ropes"][:, s * TS:(s + 1) * TS], [], ["ropes"], "ropes")
        norm_to(par, 8, hTM, "hTM", [(sqM[:, 0, :], ("sqM", 0)), (qpre[:, 0, :], ("qpre", 0))], rsM[:, 0, :], ("rsM", 0), bankM())
        yield

        def proj_group():
            L = wi_use[0]
            wi_use[0] += 1
            slot = L % 4
            wi_ensure(L + 3)
            b = bankM()
            for k in range(KT):
                mm(ps[b][:, :], wis[:, slot, k, :], hTM[:, k, :], k == 0, k == KT - 1, [("wis", slot), ("hTM", k)], [("ps", b)])
            return b

        for m in range(4):
            P.tag = 'mix.proj'
            b = proj_group()
            actf(uTd(m), ps[b][:, :].rearrange("p (n i) -> p i n", i=8), AF.Copy, [("ps", b)], [("am", m)])
            yield
        jobs = [(qT(m), ("am", 4 + m)) for m in range(4)]
        jobs += [(kTd[:, kk, 128:640], ("kTd", kk)) for kk in range(2)]
        QS = [(qpre[:, 0, :], ("qpre", 0)), (sqM[:, 0, :], ("sqM", 0))]
        pending = None
        for idx, (out_ap, out_key) in enumerate(jobs):
            P.tag = 'mix.proj'
            b = proj_group()
            qs = QS[idx % 2]
            actf(qs[0], ps[b][:, :], AF.Copy, [("ps", b)], [qs[1]])
            if pending is not None:
                rope_finish(*pending)
            pending = (qs, out_ap, out_key)
            yield
        P.tag = 'mix.proj'
        L = wi_use[0]
        wi_use[0] += 1
        slot = L % 4
        wi_ensure(L + 3)
        b = bankM()
        for blk in range(4):
            for k in range(KT):
                mm(ps[b][:, blk * 128:(blk + 1) * 128], hTM[:, k, blk * 128:(blk + 1) * 128], wis[:, slot, k, :],
                   k == 0, k == KT - 1, [("wis", slot), ("hTM", k)], [("ps", b)])
        rope_finish(*pending)
        actf(vsb[:, 1:5, :], ps[b][:, :].rearrange("p (a d) -> p a d", a=4), AF.Copy, [("ps", b)], ["vsb"])
        yield

        def attn_gen():
            ptc = 0
            units = [(m, pair) for m in range(4) for pair in range(2)]

            def a1(m, pair, hh, pi_):
                kv = m // 2
                rows = slice(hh * 64, (hh + 1) * 64)
                bs_ = bankM()
                first = (s == 0 and pair == 0)
                mk, mkk = (maskf, "maskf") if first else (mask, "mask")
                mm(ps[bs_][:, :], identb[:], mk[:], True, False, ["identb", mkk], [("ps", bs_)])
                n = 0
                for qb in range(2):
                    blkq = 2 * pair + qb
                    for piece in range(2):
                        slot = blkq + piece
                        mm(ps[bs_][:, (qb * 2 + piece) * 128:(qb * 2 + piece + 1) * 128],
                           kTd[rows, kv, slot * 128:(slot + 1) * 128], qT(m)[rows, blkq * 128:(blkq + 1) * 128],
                           False, n == 3, [("kTd", kv), ("am", 4 + m)], [("ps", bs_)])
                        n += 1
                actf(PT[:, pi_, :], ps[bs_][:, :], AF.Exp, [("ps", bs_)], [("PT", pi_)], scale=0.125)

            def pv(m, pair, hh, pi_):
                kv = m // 2
                rows = slice(hh * 64, (hh + 1) * 64)
                for qb in range(2):
                    blkq = 2 * pair + qb
                    ncol = slice(qb * 128, (qb + 1) * 128)
                    dcol = slice(256 + qb * 128, 256 + (qb + 1) * 128)
                    for piece in range(2):
                        slot = blkq + piece
                        mm(ps[BATT][rows, ncol], vsb[:, slot, kv * 64:(kv + 1) * 64], PT[:, pi_, (qb * 2 + piece) * 128:(qb * 2 + piece + 1) * 128],
                           piece == 0, piece == 1, ["vsb", ("PT", pi_)], [("ps", BATT)])
                    for piece in range(2):
                        mm(ps[BATT][rows, dcol], onesb[:, 0:64], PT[:, pi_, (qb * 2 + piece) * 128:(qb * 2 + piece + 1) * 128],
                           piece == 0, piece == 1, ["onesb", ("PT", pi_)], [("ps", BATT)])

            def norm_unit(m, pair):
                den = tmpf[:, 2, 0:256]
                ts1(den, ps[BATT][:, 256:512], esink[:, m:m + 1], ALU.add, [("ps", BATT), "esink"], [("tmpf", 2)])
                recip(den, den, [("tmpf", 2)], [("tmpf", 2)])
                ya, yak = yat(m)
                tt(ya[:, pair * 256:(pair + 1) * 256], ps[BATT][:, 0:256], den, ALU.mult, [("ps", BATT), ("tmpf", 2)], [yak])

            a1(0, 0, 0, 0); yield
            a1(0, 0, 1, 1); yield
            for ui, (m, pair) in enumerate(units):
                nxt = units[ui + 1] if ui + 1 < len(units) else None
                pv(m, pair, 0, 0); yield
                if nxt:
                    a1(nxt[0], nxt[1], 0, 0); yield
                pv(m, pair, 1, 1); yield
                norm_unit(m, pair)
                if nxt:
                    a1(nxt[0], nxt[1], 1, 1)
                yield
            b = bankM()
            for m in range(4):
                ya, yak = yat(m)
                actf(sqM[:, 0, :], ya, AF.Square, [yak], [("sqM", 0)])
                mm(ps[b][:, :], onesb[:], sqM[:, 0, :], m == 0, m == 3, [("sqM", 0), "onesb"], [("ps", b)])
            rstd_from(b, 512, rsM[:, 1, :], ("rsM", 1))
            yield

        def ssm_gen():
            cp(Hb[:, :, :, 0:1], Hc[:].unsqueeze(3), ["Hc"], ["Hb"])
            vbank = {}

            def U_unit(m):
                b = bankM()
                for gg in range(8):
                    for i in range(8):
                        mm(ps[b][:, gg * NB:(gg + 1) * NB], sel[:, gg, (7 - i) * 16:(7 - i) * 16 + 128], uTd(m)[:, i, :],
                           i == 0, i == 7, ["sel", ("am", m)], [("ps", b)])
                actf(am[:, 8 + m, :], ps[b][:, :], AF.Copy, [("ps", b)], [("am", 8 + m)])

            def V_unit(q4):
                b = bankM()
                vbank[q4] = b
                for tt_ in range(4):
                    t = 4 * q4 + tt_
                    for g2 in range(2):
                        g = 2 * t + g2
                        for ri in range(2):
                            c0 = (tt_ * 2 + ri) * NB
                            mm(ps[b][g2 * 64:(g2 + 1) * 64, c0:c0 + NB], Wm[:, g, ri * 64:(ri + 1) * 64], Ugrp(g), True, True,
                               ["Wm", ("am", 8 + g // 8)], [("ps", b)])

            def D_unit(q4):
                b = vbank[q4]
                V = ps[b][:, :].rearrange("p (t r n) -> p t r n", t=4, r=2)
                Vre, Vim = V[:, :, 0, :], V[:, :, 1, :]
                cs_ = cosT[:, 4 * q4:4 * q4 + 4, :]
                sn_ = sinT[:, 4 * q4:4 * q4 + 4, :]
                tA = tmpf[:, 0, 0:256].rearrange("p (t n) -> p t n", t=4)
                tB = tmpf[:, 0, 256:512].rearrange("p (t n) -> p t n", t=4)
                kA, kB = ("tmpf", 0), ("tmpf", "0b")
                gk = "Gin"
                tt(tA, Vre, cs_, ALU.mult, [("ps", b), "cosT"], [kA]); tt(tB, Vim, sn_, ALU.mult, [("ps", b), "sinT"], [kB])
                tt(Gin[:, 0], tA, tB, ALU.add, [kA, kB], [gk])
                tt(tA, Vim, cs_, ALU.mult, [("ps", b), "cosT"], [kA]); tt(tB, Vre, sn_, ALU.mult, [("ps", b), "sinT"], [kB])
                tt(Gin[:, 1], tA, tB, ALU.subtract, [kA, kB, gk], [gk])
                sk_ = "Gs"
                for tt_ in range(4):
                    t = 4 * q4 + tt_
                    for ri in range(2):
                        out_ap = Gs[:, ri, tt_, :]
                        d0 = r8[:, t:t + 1].to_broadcast([128, NB])
                        d1 = Gin[:, ri, tt_, :]
                        init = Hc[:, t, ri:ri + 1]
                        P.add("dve", (lambda o_=out_ap, a_=d0, b_=d1, i_=init: (lambda e: e.tensor_tensor_scan(
                            out=o_, data0=a_, data1=b_, initial=i_, op0=ALU.mult, op1=ALU.add)))(),
                            [gk, "r8", "Hc", sk_], [sk_])
                Gre, Gim = Gs[:, 0], Gs[:, 1]
                tt(tA, Gre, cs_, ALU.mult, [sk_, "cosT"], [kA]); tt(tB, Gim, sn_, ALU.mult, [sk_, "sinT"], [kB])
                tt(Gin[:, 0], tA, tB, ALU.subtract, [kA, kB, gk], [gk])
                tt(tA, Gre, sn_, ALU.mult, [sk_, "sinT"], [kA]); tt(tB, Gim, cs_, ALU.mult, [sk_, "cosT"], [kB])
                tt(Gin[:, 1], tA, tB, ALU.add, [kA, kB, gk], [gk])
                for ri in range(2):
                    actf(Hb[:, 4 * q4:4 * q4 + 4, ri, 1:NB + 1], Gin[:, ri], AF.Copy, [gk], [("Hb", q4)])
                    cp(Hc[:, 4 * q4:4 * q4 + 4, ri:ri + 1], Gin[:, ri, :, NB - 1:NB], [gk], ["Hc"])

            def Y_unit(m):
                b = bankM()
                for gg in range(8):
                    g = 8 * m + gg
                    t, g2 = g // 2, g % 2
                    rows = slice(g2 * 64, (g2 + 1) * 64)
                    o_ = ps[b][:, gg * NB:(gg + 1) * NB]
                    mm(o_, Mm[:, g, :], Ugrp(g), True, False, ["Mm", ("am", 8 + m)], [("ps", b)])
                    mm(o_, tabR[rows, t, :], Hb[rows, t, 0, 0:NB], False, False, ["tabR", "Hb", ("Hb", m)], [("ps", b)])
                    mm(o_, tabI[rows, t, :], Hb[rows, t, 1, 0:NB], False, True, ["tabI", "Hb", ("Hb", m)], [("ps", b)])
                U3 = am[:, 8 + m, :].rearrange("p (g n) -> p g n", g=8)
                Db = dblk[:, 8 * m:8 * m + 8].unsqueeze(2).to_broadcast([128, 8, NB])
                y1 = tmpf[:, 1, :]
                tt(y1.rearrange("p (g n) -> p g n", g=8), U3, Db, ALU.mult, [("am", 8 + m), "dblk"], [("tmpf", 1)])
                tt(y1, y1, ps[b][:, :], ALU.add, [("tmpf", 1), ("ps", b)], [("tmpf", 1)])
                actf(am[:, m, :], y1, AF.Gelu_apprx_tanh, [("tmpf", 1)], [("am", m)])

            def I_unit(m):
                b = bankM()
                for j in range(8):
                    for gg in range(8):
                        mm(ps[b][:, j * NB:(j + 1) * NB], sel[:, j, (7 - gg) * 16:(7 - gg) * 16 + 128], Y2grp(8 * m + gg),
                           gg == 0, gg == 7, ["sel", ("am", m)], [("ps", b)])
                actf(y2T(m).rearrange("p (n j) -> p j n", j=8), ps[b][:, :].rearrange("p (j n) -> p j n", j=8), AF.Copy,
                     [("ps", b)], [("am", 12 + m)])

            order = [(U_unit, 0), (U_unit, 1), (V_unit, 0), (D_unit, 0), (U_unit, 2), (V_unit, 1), (D_unit, 1), (U_unit, 3),
                     (V_unit, 2), (D_unit, 2), (Y_unit, 0), (V_unit, 3), (D_unit, 3), (Y_unit, 1), (I_unit, 0), (Y_unit, 2),
                     (I_unit, 1), (Y_unit, 3), (I_unit, 2), (I_unit, 3)]
            for fn_, arg in order:
                fn_(arg)
                yield
            yield "need_attn_done"
            for mo in range(4):
                b = bankM()
                for k in range(4):
                    mm(ps[b][:, :], w_glu[:, k, mo * 128:(mo + 1) * 128], y2T(k), k == 0, k == 3, ["w_glu", ("am", 12 + k)], [("ps", b)])
                sg = tmpf[:, 1, :]
                actf(sg, ps[b][:, :], AF.Tanh, [("ps", b), "hbg"], [("tmpf", 1)], bias=hbg[:, mo:mo + 1], scale=0.5)
                ygm, ygk = yg(mo)
                stt(ygm, sg, 1.0, y2T(mo), ALU.add, ALU.mult, [("am", 12 + mo), ("tmpf", 1)], [ygk])
                yield
            b = bankM()
            for mo in range(4):
                ygm, ygk = yg(mo)
                actf(sqM[:, 0, :], ygm, AF.Square, [ygk], [("sqM", 0)])
                mm(ps[b][:, :], onesb[:], sqM[:, 0, :], mo == 0, mo == 3, [("sqM", 0), "onesb"], [("ps", b)])
            rstd_from(b, 512, rsM[:, 0, :], ("rsM", 0), ec=2)
            yield

        ga, gs = attn_gen(), ssm_gen()
        a_done = s_done = s_wait = False
        while not (a_done and s_done):
            if not a_done:
                P.tag = 'mix.attn'
                try:
                    next(ga)
                except StopIteration:
                    a_done = True
                yield
            if not s_done and not (s_wait and not a_done):
                P.tag = 'mix.ssm'
                try:
                    if next(gs) == "need_attn_done":
                        s_wait = True
                except StopIteration:
                    s_done = True
                yield
        if s == 0:
            dump("yattn", am[:, 16:20, :], [("am", 16 + m) for m in range(4)], [128, 4, TS], BF16)
            dump("y2T", am[:, 12:16, :], [("am", 12 + m) for m in range(4)], [128, 4, TS], BF16)
        P.tag = 'mix.onorm'
        for k in range(4):
            ygm, ygk = yg(k)
            stt(hTM[:, k, :], ygm, gains[:, 32 + k:33 + k], rsM[:, 0, :], ALU.mult, ALU.mult, [ygk, "gains", ("rsM", 0)], [("hTM", k)])
        for k in range(4):
            ya, yak = yat(k)
            stt(hTM[:, 4 + k, :], ya, gains[:, 36 + k:37 + k], rsM[:, 1, :], ALU.mult, ALU.mult, [yak, "gains", ("rsM", 1)], [("hTM", 4 + k)])
        yield
        for o in range(8):
            P.tag = 'mix.wout'
            L = wo_use[0]
            wo_use[0] += 1
            slot = L % 2
            wo_ensure(L + 1)
            b = bankM()
            for k in range(KT):
                mm(ps[b][:, :], wo[:, slot, k, :], hTM[:, k, :], k == 0, k == KT - 1, [("wo", slot), ("hTM", k)], [("ps", b)])
            tt(xT(par, o), ps[b][:, :], xT(par, o), ALU.add, [("ps", b), xk(par, o)], [xk(par, o)])
            yield
        cp(kTd[:, :, 0:128], kTd[:, :, 512:640], [("kTd", 0), ("kTd", 1)], [("kTd", 0), ("kTd", 1)], eng="act")
        cp(vsb[:, 0, :], vsb[:, 4, :], ["vsb"], ["vsb"], eng="act")
        yield

    def drain(g):
        for _ in g:
            pass

    def chain(*gens):
        for g in gens:
            for _ in g:
                yield

    def interleave(ga, na, gb, nb):
        ca = cb = 0
        a_done = b_done = False
        while not (a_done and b_done):
            pick_a = (not a_done) and (b_done or ca * nb <= cb * na)
            if pick_a:
                try:
                    next(ga)
                    ca += 1
                except StopIteration:
                    a_done = True
            else:
                try:
                    next(gb)
                    cb += 1
                except StopIteration:
                    b_done = True

    P.tile = 0
    sb = sbp
    pg = ssm_precompute()
    next(pg)
    wgu_ensure(2)
    drain(chain(loadx_gen(0), ffn_norm_gen(0, 0)))
    fg = ffn_gen(1, 0, 0, do_norm=False)
    p_part1 = True
    f_live = True
    while p_part1 or f_live:
        if p_part1:
            P.tag = 'pre'
            if next(pg) == "PE_PART":
                p_part1 = False
        if f_live:
            try:
                next(fg)
            except StopIteration:
                f_live = False
    P.tag = 'pre'
    drain(pg)
    P.barrier()
    pstack.close()
    sb = sb_main
    xTb = sb("xTb", [128, KT, TS], F32)
    hTM = sb("hTM", [128, KT, TS], BF16)
    am = sb("am", [128, 20, TS], BF16)
    dump("x1", xTa[:], [xk(0, k) for k in range(KT)], [128, KT, TS])
    for s in range(NT_RUN):
        P.tile = s
        bparts = []
        nb = 0
        if s >= 1:
            bparts += [ffn_gen(2, 16, (s - 1) % 2), final_gen(s - 1)]
            nb += 50
        if s + 1 < NT_RUN:
            bparts += [loadx_gen(s + 1), ffn_norm_gen(0, (s + 1) % 2)]
            nb += 5
            if s == 0:
                bparts.append(ffn_gen(1, 0, 1, do_norm=False))
                nb += 44
        if bparts:
            interleave(mixer_gen(s), 95, chain(*bparts), nb)
        else:
            drain(mixer_gen(s))
        if s == 0:
            dump("x2", xTa[:], [xk(0, k) for k in range(KT)], [128, KT, TS])
        if s + 1 < NT_RUN and s > 0:
            drain(ffn_gen(1, 0, (s + 1) % 2, do_norm=False))
    drain(chain(ffn_gen(2, 16, (NT_RUN - 1) % 2), final_gen(NT_RUN - 1)))
    P.add("sp", None, [("out", s, k) for s in range(NT_RUN) for k in range(KT)] + [("dbg", n) for n in dbg_d], [])

    P.finalize(nc, stack)
    with nc.Block() as block:
        @block.sync
        def _(e):
            P.emit("sp", e)

        @block.tensor
        def _(e):
            P.emit("pe", e)

        @block.scalar
        def _(e):
            P.emit("act", e)

        @block.vector
        def _(e):
            P.emit("dve", e)

        @block.gpsimd
        def _(e):
            P.emit("pool", e)
    stack.close()
    nc._prog = P
    return nc, list(dbg_d.keys())


_CACHE = {}


def kernel(**inputs):
    x = np.ascontiguousarray(np.asarray(inputs["x"], dtype=np.float32))
    B = x.shape[0]
    if "nc" not in _CACHE:
        _CACHE["nc"] = build_program()
    nc, dbg = _CACHE["nc"]
    shared = host_layout(inputs)
    shared.update(host_consts())
    in_maps = []
    for b in range(B):
        m = dict(shared)
        m["x"] = np.ascontiguousarray(x[b].T)
        in_maps.append(m)
    res = run_bass_kernel_spmd(nc, in_maps, core_ids=list(range(B)))
    out = np.stack([np.ascontiguousarray(np.asarray(r["out"], dtype=np.float32).T) for r in res.results], axis=0)
    if DEBUG:
        _CACHE["dbg"] = {n: np.asarray(res.results[0]["dbg_" + n]) for n in dbg}
    return out
```

```python
import math
import os
from contextlib import ExitStack

import numpy as np
import ml_dtypes

import concourse.bass as bass
import concourse.mybir as mybir
from concourse.bass_utils import run_bass_kernel_spmd

F32 = mybir.dt.float32
BF16 = mybir.dt.bfloat16
AF = mybir.ActivationFunctionType
ALU = mybir.AluOpType

D = 1024
KT = 8
FF = 2816
FT = 22
SEQ = 4096
TS = 512
NTILE = SEQ // TS
NB = TS // 8
EPS = 1e-6
NEG = -30000.0
NWIN = 1408

DEBUG = bool(int(os.environ.get("KDBG", "0")))
NT_RUN = int(os.environ.get("KNT", str(NTILE)))


class Prog:
    ENGS = ("pe", "act", "dve", "pool", "sp")

    def __init__(self):
        self.ops = []
        self.lastw = {}
        self.readers = {}
        self.tag = ""
        self.tile = -1

    def add(self, eng, fn, reads=(), writes=(), dma=None):
        i = len(self.ops)
        deps = set()
        for k in reads:
            w = self.lastw.get(k)
            if w is not None:
                deps.add(w)
        for k in writes:
            w = self.lastw.get(k)
            if w is not None:
                deps.add(w)
            for r in self.readers.get(k, ()):
                deps.add(r)
        for k in reads:
            lst = self.readers.setdefault(k, [])
            if dma is None:
                lst[:] = [r for r in lst if not (self.ops[r]["eng"] == eng and self.ops[r]["dma"] is None)]
            lst.append(i)
        for k in writes:
            self.lastw[k] = i
            self.readers[k] = []
        deps.discard(i)
        self.ops.append(dict(eng=eng, fn=fn, deps=deps, dma=dma, signal=False, count=0, tag=self.tag, tile=self.tile))
        return i

    def barrier(self):
        last = {}
        for i, op in enumerate(self.ops):
            if op["dma"] is not None:
                if str(op["dma"]).startswith("cv"):
                    continue
                last[("d", op["dma"])] = i
            elif op["fn"] is not None:
                last[("e", op["eng"])] = i
        deps = set(last.values())
        for e in self.ENGS:
            self.ops.append(dict(eng=e, fn=None, deps=set(deps), dma=None, signal=False, count=0, tag='barrier', tile=-1))

    def finalize(self, nc, stack):
        ops = self.ops
        for op in ops:
            for d in op["deps"]:
                dop = ops[d]
                if dop["dma"] is None:
                    if dop["eng"] == "pe" and op["eng"] == "pe" and op["dma"] is None:
                        continue
                    dop["signal"] = True
        esem = {e: stack.enter_context(nc.semaphore("sem_" + e)) for e in self.ENGS}
        dsem = {}
        ecnt = {e: 0 for e in self.ENGS}
        dcnt = {}
        waited = {e: {} for e in self.ENGS}
        streams = {e: [] for e in self.ENGS}
        for op in ops:
            e = op["eng"]
            waits = {}
            for d in op["deps"]:
                dop = ops[d]
                if dop["dma"] is not None:
                    key = ("d", dop["dma"])
                    val = dop["count"]
                    sem = dsem[dop["dma"]]
                else:
                    if dop["eng"] == "pe" and e == "pe" and op["dma"] is None:
                        continue
                    key = ("e", dop["eng"])
                    val = dop["count"]
                    sem = esem[dop["eng"]]
                if waited[e].get(key, 0) >= val:
                    continue
                if key not in waits or waits[key][1] < val:
                    waits[key] = (sem, val)
            for key, (sem, val) in waits.items():
                waited[e][key] = val
            if op["dma"] is not None:
                if op["dma"] not in dsem:
                    dsem[op["dma"]] = stack.enter_context(nc.semaphore("dsem%d" % len(dsem)))
                    dcnt[op["dma"]] = 0
                dcnt[op["dma"]] += 16
                op["count"] = dcnt[op["dma"]]
                inc = (dsem[op["dma"]], 16)
            elif op["signal"]:
                ecnt[e] += 1
                op["count"] = ecnt[e]
                inc = (esem[e], 1)
            else:
                inc = None
            streams[e].append((list(waits.values()), op["fn"], inc))
        self.streams = streams
        self.nsem = len(dsem) + len(esem)

    def emit(self, eng_name, e):
        for waits, fn, inc in self.streams[eng_name]:
            for sem, val in waits:
                e.wait_ge(sem, val)
            if fn is None:
                continue
            ins = fn(e)
            if inc is not None:
                ins.then_inc(inc[0], inc[1])


def _bf(a):
    return np.ascontiguousarray(a.astype(ml_dtypes.bfloat16))


def host_consts():
    c = {}
    c["identf"] = np.eye(128, dtype=np.float32)
    c["identb"] = _bf(np.eye(128, dtype=np.float32))
    c["onesb"] = _bf(np.ones((128, 128), np.float32))
    perm = np.zeros((128, 128), np.float32)
    for h in range(2):
        for d in range(16):
            pd = d + 8 if d < 8 else d - 8
            perm[h * 64 + pd, h * 64 + d] = 1.0
    c["permb"] = _bf(perm)
    kj = np.arange(128)[:, None]
    qi = np.arange(128)[None, :]
    prev = np.where(kj > qi, 0.0, NEG).astype(np.float32)
    same = np.where(kj <= qi, 0.0, NEG).astype(np.float32)
    full = np.full((128, 128), NEG, np.float32)
    c["mask"] = _bf(np.concatenate([prev, same, prev, same], axis=1))
    c["maskf"] = _bf(np.concatenate([full, same, prev, same], axis=1))
    sel = np.zeros((128, 8, 240), np.float32)
    for g in range(8):
        for cc in range(16):
            sel[16 * g + cc, g, 112 + cc] = 1.0
    c["sel"] = _bf(sel)
    half = 8
    inv_freq = (500000.0 ** (-np.arange(half, dtype=np.float32) * 2.0 / 16)).astype(np.float32)
    ang = np.arange(SEQ, dtype=np.float32)[:, None] * inv_freq[None, :]
    cos = np.cos(ang).astype(np.float32).T
    sin = np.sin(ang).astype(np.float32).T
    C = np.ones((128, SEQ), np.float32)
    S = np.zeros((128, SEQ), np.float32)
    for h in range(2):
        C[h * 64 + 0:h * 64 + 8] = cos
        C[h * 64 + 8:h * 64 + 16] = cos
        S[h * 64 + 0:h * 64 + 8] = -sin
        S[h * 64 + 8:h * 64 + 16] = sin
    c["ropec"] = C
    c["ropes"] = S
    return c


def host_layout(inp):
    o = {}
    f = lambda a: np.ascontiguousarray(np.asarray(a, dtype=np.float32))
    o["wg1"] = f(inp["ffn1_w_gate"][0]); o["wu1"] = f(inp["ffn1_w_up"][0]); o["wd1"] = f(inp["ffn1_w_down"][0])
    o["wg2"] = f(inp["ffn2_w_gate"][0]); o["wu2"] = f(inp["ffn2_w_up"][0]); o["wd2"] = f(inp["ffn2_w_down"][0])
    w_in = f(inp["w_in"][0])
    u = w_in[:, 0:512]; q = w_in[:, 512:1024]; k = w_in[:, 1024:1152]; v = w_in[:, 1152:1280]
    o["win"] = np.ascontiguousarray(np.concatenate([u, q, k[:, 0:64], k[:, 0:64], k[:, 64:128], k[:, 64:128], v], axis=1))
    o["wout"] = f(inp["w_out"][0])
    o["wglu"] = f(inp["ssm_w_glu"][0])
    fm = lambda vec: np.ascontiguousarray(f(vec).reshape(-1, 128).T)
    gains = np.concatenate([fm(inp["ffn1_norm"][0]), fm(inp["mix_norm"][0]), fm(inp["ffn2_norm"][0]),
                            fm(inp["final_norm"]), fm(inp["ssm_out_norm"][0]), fm(inp["attn_out_norm"][0]),
                            fm(inp["ssm_b_glu"][0])], axis=1)
    o["gains"] = np.ascontiguousarray(gains)
    Dv = f(inp["ssm_D"][0]).reshape(32, 16)
    o["dblk"] = np.ascontiguousarray(np.tile(Dv.T, (8, 1)))
    sk = f(inp["attn_sinks"][0])
    o["sinkrow"] = np.ascontiguousarray(np.repeat(sk.reshape(4, 2), 64, axis=1).T)
    pl = lambda a: np.ascontiguousarray(f(a).reshape(16, 2, 64).transpose(1, 2, 0).reshape(128, 16))
    o["are"] = pl(inp["ssm_A_re"][0]); o["aim"] = pl(inp["ssm_A_im"][0])
    ldt = f(inp["ssm_log_dt"][0])
    o["ldt"] = pl(np.repeat(ldt[:, None], 64, axis=1))
    pb = lambda a: np.ascontiguousarray(f(a).reshape(16, 2, 64, 16).transpose(1, 2, 0, 3).reshape(128, 16, 16))
    o["bre"] = pb(inp["ssm_B_re"][0]); o["bim"] = pb(inp["ssm_B_im"][0])
    pc = lambda a: np.ascontiguousarray(f(a).transpose(0, 2, 1).reshape(16, 2, 64, 16).transpose(1, 2, 0, 3).reshape(128, 16, 16))
    o["cre"] = pc(inp["ssm_C_re"][0]); o["cim"] = pc(inp["ssm_C_im"][0])
    return o


def build_program():
    nc = bass.Bass("TRN2", target_bir_lowering=False)
    P = Prog()
    stack = ExitStack()
    dr = {}

    def din(name, shape, dt=F32):
        dr[name] = nc.dram_tensor(name, list(shape), dt, kind="ExternalInput").ap()
        return dr[name]

    x_d = din("x", [D, SEQ])
    for n in ("wg1", "wu1", "wg2", "wu2"):
        din(n, [D, FF])
    for n in ("wd1", "wd2"):
        din(n, [FF, D])
    din("win", [D, NWIN]); din("wout", [D, D]); din("wglu", [512, 512])
    din("gains", [128, 44]); din("dblk", [128, 32]); din("sinkrow", [128, 4])
    for n in ("are", "aim", "ldt"):
        din(n, [128, 16])
    for n in ("bre", "bim", "cre", "cim"):
        din(n, [128, 16, 16])
    din("identf", [128, 128]); din("identb", [128, 128], BF16); din("onesb", [128, 128], BF16)
    din("permb", [128, 128], BF16); din("mask", [128, 512], BF16); din("maskf", [128, 512], BF16)
    din("sel", [128, 8, 240], BF16); din("ropec", [128, SEQ]); din("ropes", [128, SEQ])
    out_d = nc.dram_tensor("out", [D, SEQ], F32, kind="ExternalOutput").ap()
    dbg_d = {}

    def sb_main(name, shape, dt):
        return stack.enter_context(nc.sbuf_tensor("s_" + name, list(shape), dt))

    sb = sb_main

    wgu = sb("wgu", [128, 4, 2, 4, 128], BF16)
    wd = sb("wd", [128, 3, 2, 512], BF16)
    wis = sb("wis", [128, 4, KT, 128], BF16)
    wo = sb("wo", [128, 2, KT, 128], BF16)
    w_glu = sb("w_glu", [128, 4, 512], BF16)
    sgs = sb("sgs", [128, 2, TS], F32)
    sqF = sb("sqF", [128, 2, TS], BF16)
    sqM = sb("sqM", [128, 1, TS], BF16)
    rsF = sb("rsF", [128, 2, TS], F32)
    rsM = sb("rsM", [128, 2, TS], F32)
    tabR = sb("tabR", [128, 16, 128], BF16)
    tabI = sb("tabI", [128, 16, 128], BF16)
    Wm = sb("Wm", [128, 32, 128], BF16)
    Mm = sb("Mm", [128, 32, 128], BF16)
    cosT = sb("cosT", [128, 16, NB], F32)
    sinT = sb("sinT", [128, 16, NB], F32)
    r8 = sb("r8", [128, 16], F32)
    sel = sb("sel", [128, 8, 240], BF16)
    dblk = sb("dblk", [128, 32], F32)
    gains = sb("gains", [128, 44], F32)
    esink = sb("esink", [128, 4], F32)
    cst = sb("cst", [128, 4], F32)
    identf = sb("identf", [128, 128], F32)
    identb = sb("identb", [128, 128], BF16)
    onesb = sb("onesb", [128, 128], BF16)
    permb = sb("permb", [128, 128], BF16)
    mask = sb("mask", [128, 512], BF16)
    maskf = sb("maskf", [128, 512], BF16)
    kTd = sb("kTd", [128, 2, 5 * 128], BF16)
    vsb = sb("vsb", [128, 5, 128], BF16)
    PT = sb("PT", [128, 2, 512], BF16)
    ropec = sb("ropec", [128, TS], F32)
    ropes = sb("ropes", [128, TS], F32)
    qpre = sb("qpre", [128, 1, TS], BF16)
    tmpf = sb("tmpf", [128, 3, TS], F32)
    Gin = sb("Gin", [128, 2, 4, NB], F32)
    Gs = sb("Gs", [128, 2, 4, NB], F32)
    Hb = sb("Hb", [128, 16, 2, NB + 1], BF16)
    Hc = sb("Hc", [128, 16, 2], F32)

    ps = [stack.enter_context(nc.psum_tensor("ps%d" % i, [128, 512], F32)) for i in range(8)]

    rrM = [0]
    rrF = [0]

    def bankM():
        b = rrM[0] % 3
        rrM[0] += 1
        return b

    def bankF():
        b = 4 + rrF[0] % 4
        rrF[0] += 1
        return b

    bankA = bankM
    BATT = 3

    def mm(out, lhsT, rhs, start, stop, reads, writes):
        P.add("pe", lambda e: e.matmul(out, lhsT=lhsT, rhs=rhs, start=start, stop=stop), reads, writes)

    def tr(out, in_, reads, writes):
        P.add("pe", lambda e: e.transpose(out, in_, identf[:]), reads + ["identf"], writes)

    def actf(out, in_, func, reads, writes, bias=None, scale=None):
        kw = {}
        if bias is not None:
            kw["bias"] = bias
        if scale is not None:
            kw["scale"] = scale
        P.add("act", lambda e: e.activation(out=out, in_=in_, func=func, **kw), reads, writes)

    def tt(out, in0, in1, op, reads, writes, eng="dve"):
        P.add(eng, lambda e: e.tensor_tensor(out=out, in0=in0, in1=in1, op=op), reads, writes)

    def ts1(out, in0, s1, op0, reads, writes, eng="dve"):
        P.add(eng, lambda e: e.tensor_single_scalar(out=out, in_=in0, scalar=s1, op=op0), reads, writes)

    def stt(out, in0, scalar, in1, op0, op1, reads, writes):
        P.add("dve", lambda e: e.scalar_tensor_tensor(out=out, in0=in0, scalar=scalar, in1=in1, op0=op0, op1=op1), reads, writes)

    def cp(out, in_, reads, writes, eng="dve"):
        if eng == "act":
            P.add("act", lambda e: e.activation(out=out, in_=in_, func=AF.Copy), reads, writes)
        else:
            P.add(eng, lambda e: e.tensor_copy(out=out, in_=in_), reads, writes)

    def recip(out, in_, reads, writes):
        P.add("dve", lambda e: e.reciprocal(out=out, in_=in_), reads, writes)

    def memset(ap, val, writes, eng="dve"):
        P.add(eng, lambda e: e.memset(ap, val), [], writes)

    def dma(q, out, in_, reads, writes, key):
        P.add(q, lambda e: e.dma_start(out=out, in_=in_), reads, writes, dma=key)

    def dump(name, ap, reads, shape, dt=F32):
        if not DEBUG:
            return
        if name not in dbg_d:
            dbg_d[name] = nc.dram_tensor("dbg_" + name, list(shape), dt, kind="ExternalOutput").ap()
        dma("sp", dbg_d[name], ap, reads, [("dbg", name)], "dbg_" + name)

    def ld(q, t, src, key):
        dma(q, t[:], src, [], [key], "ld_" + key)

    ld("sp", identf, dr["identf"], "identf"); ld("sp", identb, dr["identb"], "identb")
    ld("sp", onesb, dr["onesb"], "onesb"); ld("sp", permb, dr["permb"], "permb")
    ld("sp", mask, dr["mask"], "mask"); ld("sp", maskf, dr["maskf"], "maskf")
    ld("sp", sel, dr["sel"], "sel"); ld("sp", gains, dr["gains"], "gains")
    ld("sp", dblk, dr["dblk"], "dblk"); ld("sp", esink, dr["sinkrow"], "esink")
    memset(cst[:, 0:1], EPS, ["cst"])
    memset(cst[:, 1:2], math.pi / 2, ["cst"])
    memset(cst[:, 2:3], 4.0 * EPS, ["cst"])
    memset(kTd[:], 0.0, ["kTd"])
    memset(vsb[:], 0.0, ["vsb"])
    memset(Hc[:], 0.0, ["Hc"])
    memset(Hb[:], 0.0, ["Hb"])
    actf(esink[:], esink[:], AF.Exp, ["esink"], ["esink"])
    hbg = sb("hbg", [128, 4], F32)
    ts1(hbg[:], gains[:, 40:44], 0.5, ALU.mult, ["gains"], ["hbg"])

    scr_gu = {fid: nc.dram_tensor("scr_gu%d" % fid, [2 * FT, 128, 2 * 4 * 128], BF16).ap() for fid in (1, 2)}
    scr_d = {fid: nc.dram_tensor("scr_d%d" % fid, [22, 128, 2 * 512], BF16).ap() for fid in (1, 2)}
    scr_wo = nc.dram_tensor("scr_wo", [8, 128, KT * 128], BF16).ap()
    scr_wi = nc.dram_tensor("scr_wi", [11, 128, KT * 128], BF16).ap()

    ffn_order = [1]
    for s in range(NT_RUN - 1):
        ffn_order += [1, 2]
    ffn_order.append(2)
    ffn_w = {1: ("wg1", "wu1", "wd1"), 2: ("wg2", "wu2", "wd2")}
    wgu_loads = [(fid, f, kh) for fid in ffn_order for f in range(FT) for kh in range(2)]
    wd_loads = [(fid, half, jc) for fid in ffn_order for half in range(2) for jc in range(11)]
    wgu_ptr = [0]
    wd_ptr = [0]
    seen_gu = set()
    seen_d = set()
    multi = NT_RUN > 1

    NCV = 16
    cvi = [0]

    def conv(out_ap, in_ap, key):
        i = cvi[0] % NCV
        cvi[0] += 1
        dma("pool", out_ap, in_ap, [], [key, ("cvslot", i)], "cv%d" % i)

    def conv_gu(fid):
        gname, uname, _ = ffn_w[fid]
        for f in range(FT):
            for kh in range(2):
                scr = scr_gu[fid][2 * f + kh].rearrange("p (g k c) -> p g k c", g=2, k=4)
                for gi, nm in enumerate((gname, uname)):
                    src = dr[nm].rearrange("(k p) f -> p k f", p=128)[:, 4 * kh:4 * kh + 4, f * 128:(f + 1) * 128]
                    conv(scr[:, gi], src, ("scr_gu", fid, f, kh, gi))

    def conv_d(fid):
        for half in range(2):
            for jc in range(11):
                scr = scr_d[fid][half * 11 + jc].rearrange("p (f d) -> p f d", f=2)
                src = dr[ffn_w[fid][2]].rearrange("(f p) d -> p f d", p=128)[:, 2 * jc:2 * jc + 2, half * 512:(half + 1) * 512]
                conv(scr, src, ("scr_d", fid, half, jc))

    P.tag = 'prep'
    dma("pool", w_glu[:], dr["wglu"].rearrange("(k p) f -> p k f", p=128), [], ["w_glu"], "ld_w_glu")
    conv_gu(1)
    conv_d(1)
    for cg in range(11):
        conv(scr_wi[cg].rearrange("p (k c) -> p k c", k=KT), dr["win"].rearrange("(k p) f -> p k f", p=128)[:, :, cg * 128:(cg + 1) * 128], ("scr_wi", cg))
    for o in range(8):
        conv(scr_wo[o].rearrange("p (k c) -> p k c", k=KT), dr["wout"].rearrange("(k p) d -> p k d", p=128)[:, :, o * 128:(o + 1) * 128], ("scr_wo", o))
    conv_gu(2)
    conv_d(2)

    def wgu_ensure(upto):
        while wgu_ptr[0] <= upto and wgu_ptr[0] < len(wgu_loads):
            L = wgu_ptr[0]
            fid, f, kh = wgu_loads[L]
            slot = L % 4
            scr = scr_gu[fid][2 * f + kh].rearrange("p (g k c) -> p g k c", g=2, k=4)
            dma("sp", wgu[:, slot], scr, [("scr_gu", fid, f, kh, 0), ("scr_gu", fid, f, kh, 1)], [("wgu", slot, 0), ("wgu", slot, 1)], "wgu%d" % slot)
            wgu_ptr[0] += 1

    def wd_ensure(upto):
        while wd_ptr[0] <= upto and wd_ptr[0] < len(wd_loads):
            L = wd_ptr[0]
            fid, half, jc = wd_loads[L]
            slot = L % 3
            scr = scr_d[fid][half * 11 + jc].rearrange("p (f d) -> p f d", f=2)
            dma("sp", wd[:, slot], scr, [("scr_d", fid, half, jc)], [("wd", slot)], "wd%d" % slot)
            wd_ptr[0] += 1

    wgu_use = [0]
    wd_use = [0]

    wi_ptr = [0]
    wi_use = [0]
    wo_ptr = [0]
    wo_use = [0]

    def wi_ensure(upto):
        while wi_ptr[0] <= upto and wi_ptr[0] < 11 * NT_RUN:
            L = wi_ptr[0]
            cg = L % 11
            slot = L % 4
            dma("sp", wis[:, slot], scr_wi[cg].rearrange("p (k c) -> p k c", k=KT), [("scr_wi", cg)], [("wis", slot)], "wis%d" % slot)
            wi_ptr[0] += 1

    def wo_ensure(upto):
        while wo_ptr[0] <= upto and wo_ptr[0] < 8 * NT_RUN:
            L = wo_ptr[0]
            o = L % 8
            slot = L % 2
            dma("sp", wo[:, slot], scr_wo[o].rearrange("p (k c) -> p k c", k=KT), [("scr_wo", o)], [("wo", slot)], "wo%d" % slot)
            wo_ptr[0] += 1

    def ssm_precompute():
        P.tag = 'pre'
        are = sb("p_are", [128, 16], F32); aim = sb("p_aim", [128, 16], F32); ldt = sb("p_ldt", [128, 16], F32)
        bre = sb("p_bre", [128, 16, 16], F32); bim = sb("p_bim", [128, 16, 16], F32)
        cre = sb("p_cre", [128, 16, 16], F32); cim = sb("p_cim", [128, 16, 16], F32)
        sm = sb("p_sm", [128, 24, 16], F32)
        pw = sb("p_pw", [128, 2, 16, 9], F32)
        bb = sb("p_bb", [128, 2, 16, 16], F32)
        bbp = sb("p_bbp", [128, 2, 2, 240], BF16)
        wtt = sb("p_wt", [128, 2, 2, 8, 16], F32)
        big = sb("p_big", [128, 2, 16, 64], F32)
        for nm, t in (("are", are), ("aim", aim), ("ldt", ldt), ("bre", bre), ("bim", bim), ("cre", cre), ("cim", cim)):
            dma("sp", t[:], dr[nm], [], ["p_" + nm], "ld_p_" + nm)
        S = lambda i: sm[:, i, :]
        K = "p_sm"
        rk = ["p_are", "p_aim", "p_ldt", K, "cst"]
        DT, AR, TH, MAG, C0, S0, CC, SS, CS, LBR, LBI, DEN, NRE, T1, T2, ZR, ZI, C8, S8, T3 = range(20)
        actf(S(DT), ldt[:], AF.Exp, rk, [K])
        tt(S(AR), are[:], S(DT), ALU.mult, rk, [K])
        tt(S(TH), aim[:], S(DT), ALU.mult, rk, [K])
        actf(S(MAG), S(AR), AF.Exp, rk, [K])
        actf(r8[:], S(AR), AF.Exp, rk, ["r8"], scale=8.0)
        actf(S(S0), S(TH), AF.Sin, rk, [K], scale=1.0 / 16)
        actf(S(C0), S(TH), AF.Sin, rk, [K], scale=1.0 / 16, bias=cst[:, 1:2])
        yield

        def dbl():
            tt(S(CC), S(C0), S(C0), ALU.mult, rk, [K])
            tt(S(SS), S(S0), S(S0), ALU.mult, rk, [K])
            tt(S(CS), S(C0), S(S0), ALU.mult, rk, [K])
            tt(S(C0), S(CC), S(SS), ALU.subtract, rk, [K])
            ts1(S(S0), S(CS), 2.0, ALU.mult, rk, [K])
        for _ in range(4):
            dbl()
            yield
        tt(S(LBR), S(MAG), S(C0), ALU.mult, rk, [K])
        tt(S(LBI), S(MAG), S(S0), ALU.mult, rk, [K])
        for _ in range(3):
            dbl()
            yield
        cp(S(C8), S(C0), rk, [K]); cp(S(S8), S(S0), rk, [K])
        tt(S(T1), are[:], are[:], ALU.mult, rk, [K])
        tt(S(T2), aim[:], aim[:], ALU.mult, rk, [K])
        tt(S(DEN), S(T1), S(T2), ALU.add, rk, [K])
        recip(S(DEN), S(DEN), rk, [K])
        ts1(S(NRE), S(LBR), -1.0, ALU.add, rk, [K])
        tt(S(T1), S(NRE), are[:], ALU.mult, rk, [K])
        tt(S(T2), S(LBI), aim[:], ALU.mult, rk, [K])
        tt(S(T1), S(T1), S(T2), ALU.add, rk, [K])
        tt(S(ZR), S(T1), S(DEN), ALU.mult, rk, [K])
        tt(S(T1), S(LBI), are[:], ALU.mult, rk, [K])
        tt(S(T2), S(NRE), aim[:], ALU.mult, rk, [K])
        tt(S(T1), S(T1), S(T2), ALU.subtract, rk, [K])
        tt(S(ZI), S(T1), S(DEN), ALU.mult, rk, [K])
        yield
        kp = ["p_pw", K]
        memset(pw[:, 0, :, 0:1], 1.0, ["p_pw"]); memset(pw[:, 1, :, 0:1], 0.0, ["p_pw"])
        for k in range(1, 9):
            pr, pi_ = pw[:, 0, :, k - 1], pw[:, 1, :, k - 1]
            tt(S(T1), pr, S(LBR), ALU.mult, kp, [K]); tt(S(T2), pi_, S(LBI), ALU.mult, kp, [K])
            tt(pw[:, 0, :, k], S(T1), S(T2), ALU.subtract, kp, ["p_pw"])
            tt(S(T1), pr, S(LBI), ALU.mult, kp, [K]); tt(S(T2), pi_, S(LBR), ALU.mult, kp, [K])
            tt(pw[:, 1, :, k], S(T1), S(T2), ALU.add, kp, ["p_pw"])
            yield
        bc16 = lambda ap2: ap2.unsqueeze(2).to_broadcast([128, 16, 16])
        kb = ["p_bre", "p_bim", K, "p_bb", "p_big"]
        t1 = big[:, 0, :, 0:16]; t2 = big[:, 1, :, 0:16]
        tt(t1, bre[:], bc16(S(ZR)), ALU.mult, kb, ["p_big"]); tt(t2, bim[:], bc16(S(ZI)), ALU.mult, kb, ["p_big"])
        tt(bb[:, 0], t1, t2, ALU.subtract, kb, ["p_bb"])
        tt(t1, bim[:], bc16(S(ZR)), ALU.mult, kb, ["p_big"]); tt(t2, bre[:], bc16(S(ZI)), ALU.mult, kb, ["p_big"])
        tt(bb[:, 1], t1, t2, ALU.add, kb, ["p_bb"])
        memset(bbp[:], 0.0, [("p_bbp", 0), ("p_bbp", 1)])
        yield
        kc = ["p_cre", "p_cim", "p_pw", "p_big", "p_wt"]
        tabRe = sb("p_tabRe", [128, 16, 256], BF16); tabIm = sb("p_tabIm", [128, 16, 256], BF16)
        memset(tabRe[:], 0.0, ["tabRe"]); memset(tabIm[:], 0.0, ["tabIm"])
        for tau in range(9):
            pr = pw[:, 0, :, tau:tau + 1].to_broadcast([128, 16, 16])
            pi_ = pw[:, 1, :, tau:tau + 1].to_broadcast([128, 16, 16])
            o_re = tabRe[:, :, 112 + tau * 16:112 + (tau + 1) * 16]; o_im = tabIm[:, :, 112 + tau * 16:112 + (tau + 1) * 16]
            ta = wtt[:, 0].rearrange("p r i c -> p (r i) c")
            tb = wtt[:, 1].rearrange("p r i c -> p (r i) c")
            tt(ta, cre[:], pr, ALU.mult, kc, ["p_wt"]); tt(tb, cim[:], pi_, ALU.mult, kc, ["p_wt"])
            tt(o_re, ta, tb, ALU.subtract, kc, ["tabRe"])
            tt(ta, cre[:], pi_, ALU.mult, kc, ["p_wt"]); tt(tb, cim[:], pr, ALU.mult, kc, ["p_wt"])
            tt(tb, ta, tb, ALU.add, kc, ["p_wt"])
            ts1(o_im, tb, -1.0, ALU.mult, kc, ["tabIm"])
            yield
        kt = ["cosT", "sinT", K, "p_wt", "p_big"]
        cp(cosT[:, :, 0:1], S(C8).unsqueeze(2), kt, ["cosT"]); cp(sinT[:, :, 0:1], S(S8).unsqueeze(2), kt, ["sinT"])
        m = 1
        while m < NB:
            cr = cosT[:, :, m - 1:m].to_broadcast([128, 16, m]); sr = sinT[:, :, m - 1:m].to_broadcast([128, 16, m])
            a1 = big[:, 0, :, 0:m]; a2 = big[:, 1, :, 0:m]; a3 = big[:, 0, :, 32:32 + m]; a4 = big[:, 1, :, 32:32 + m]
            tt(a1, cosT[:, :, 0:m], cr, ALU.mult, kt, ["p_big"]); tt(a2, sinT[:, :, 0:m], sr, ALU.mult, kt, ["p_big"])
            tt(a3, cosT[:, :, 0:m], sr, ALU.mult, kt, ["p_big"]); tt(a4, sinT[:, :, 0:m], cr, ALU.mult, kt, ["p_big"])
            tt(cosT[:, :, m:2 * m], a1, a2, ALU.subtract, kt, ["cosT"])
            tt(sinT[:, :, m:2 * m], a3, a4, ALU.add, kt, ["sinT"])
            m *= 2
            yield
        yield "PE_PART"
        pwr = sb("p_pwr", [128, 3, 16, 8], F32)
        for i in range(8):
            cp(pwr[:, 0, :, i:i + 1], pw[:, 0, :, 7 - i:8 - i], ["p_pw"], ["p_pwr"])
            cp(pwr[:, 1, :, i:i + 1], pw[:, 1, :, 7 - i:8 - i], ["p_pw"], ["p_pwr"])
        ts1(pwr[:, 2], pwr[:, 1], -1.0, ALU.mult, ["p_pwr"], ["p_pwr"])
        kw_ = ["p_pwr", "p_bb", "p_wt", "p_big"]
        for t in range(16):
            buf = t % 2
            wre = wtt[:, buf, 0]; wim = wtt[:, buf, 1]
            pr = pwr[:, 0, t, :].unsqueeze(2).to_broadcast([128, 8, 16])
            pi_ = pwr[:, 1, t, :].unsqueeze(2).to_broadcast([128, 8, 16])
            br = bb[:, 0, t, :].unsqueeze(1).to_broadcast([128, 8, 16])
            bi = bb[:, 1, t, :].unsqueeze(1).to_broadcast([128, 8, 16])
            x1 = big[:, 0, 0:8, 0:16]; x2 = big[:, 1, 0:8, 0:16]
            tt(x1, pr, br, ALU.mult, kw_, ["p_big"]); tt(x2, pi_, bi, ALU.mult, kw_, ["p_big"])
            tt(wre, x1, x2, ALU.subtract, kw_, ["p_wt"])
            tt(x1, pr, bi, ALU.mult, kw_, ["p_big"]); tt(x2, pi_, br, ALU.mult, kw_, ["p_big"])
            tt(wim, x1, x2, ALU.add, kw_, ["p_wt"])
            b = bankA()
            tr(ps[b][:, 0:128], wre.rearrange("p i c -> p (i c)"), ["p_wt"], [("ps", b)])
            tr(ps[b][:, 128:256], wim.rearrange("p i c -> p (i c)"), ["p_wt"], [("ps", b)])
            src = ps[b][:, 0:256].rearrange("p (r g q) -> p g r q", r=2, g=2)
            dst = Wm[:, 2 * t:2 * t + 2, :].rearrange("p g (r q) -> p g r q", r=2)
            cp(dst, src, [("ps", b)], ["Wm"])
            yield
        for g in range(32):
            t, g2 = g // 2, g % 2
            buf = t % 2
            if g2 == 0:
                cp(bbp[:, buf, 0, 112:128], bb[:, 0, t, :], ["p_bb"], [("p_bbp", buf)])
                cp(bbp[:, buf, 1, 112:128], bb[:, 1, t, :], ["p_bb"], [("p_bbp", buf)])
            rows = slice(g2 * 64, (g2 + 1) * 64)
            b = bankA()
            n = 0
            for i in range(8):
                off = (7 - i) * 16
                for ri, tab in ((0, tabRe), (1, tabIm)):
                    mm(ps[b][:, 0:128], bbp[rows, buf, ri, off:off + 128], tab[rows, t, off:off + 128],
                       n == 0, n == 15, [("p_bbp", buf), "tabRe", "tabIm"], [("ps", b)])
                    n += 1
            cp(Mm[:, g, :], ps[b][:, 0:128], [("ps", b)], ["Mm"], eng="act")
            yield
        cp(tabR[:], tabRe[:, :, 128:256], ["tabRe"], ["tabR"])
        cp(tabI[:], tabIm[:, :, 128:256], ["tabIm"], ["tabI"])
    pstack = ExitStack()

    def sbp(name, shape, dt):
        return pstack.enter_context(nc.sbuf_tensor("s_" + name, list(shape), dt))

    xTa = sb("xTa", [128, KT, TS], F32)
    hTF = sb("hTF", [128, KT, TS], BF16)
    act = sb("act", [128, FT, TS], BF16)
    xsel = lambda par: xTa if par == 0 else xTb

    def xT(par, k):
        return xsel(par)[:, k, :]

    def xk(par, k):
        return ("xT", par, k)

    def rstd_from(b, n, dst, dkey, ec=0):
        actf(dst, ps[b][:, :], AF.Sqrt, [("ps", b), "cst"], [dkey], bias=cst[:, ec:ec + 1], scale=1.0 / n)
        recip(dst, dst, [dkey], [dkey])

    def norm_to(par, goff, hT, hkey, sqs, rdst, rkey, b):
        for k in range(KT):
            sap, skey = sqs[k % len(sqs)]
            actf(sap, xT(par, k), AF.Square, [xk(par, k)], [skey])
            mm(ps[b][:, :], onesb[:], sap, k == 0, k == KT - 1, [skey, "onesb"], [("ps", b)])
        rstd_from(b, D, rdst, rkey)
        for k in range(KT):
            stt(hT[:, k, :], xT(par, k), gains[:, goff + k:goff + k + 1], rdst, ALU.mult, ALU.mult,
                [xk(par, k), "gains", rkey], [(hkey, k)])

    def ffn_norm_gen(goff, par):
        P.tag = 'ffn.norm'
        norm_to(par, goff, hTF, "hTF", [(sqF[:, 0, :], ("sqF", 0)), (sqF[:, 1, :], ("sqF", 1))], rsF[:, 0, :], ("rsF", 0), 4)
        yield

    def ffn_gen(fid, goff, par, do_norm=True):
        if do_norm:
            P.tag = 'ffn.norm'
            norm_to(par, goff, hTF, "hTF", [(sqF[:, 0, :], ("sqF", 0)), (sqF[:, 1, :], ("sqF", 1))], rsF[:, 0, :], ("rsF", 0), 4)
            yield
        for f in range(FT):
            P.tag = 'ffn.gu'
            bg, bu = (4, 5) if f % 2 == 0 else (6, 7)
            for kh in range(2):
                L = wgu_use[0]
                wgu_use[0] += 1
                slot = L % 4
                wgu_ensure(L + 3)
                for k4 in range(4):
                    k = 4 * kh + k4
                    mm(ps[bg][:, :], wgu[:, slot, 0, k4, :], hTF[:, k, :], k == 0, k == KT - 1,
                       [("wgu", slot, 0), ("hTF", k)], [("ps", bg)])
                for k4 in range(4):
                    k = 4 * kh + k4
                    mm(ps[bu][:, :], wgu[:, slot, 1, k4, :], hTF[:, k, :], k == 0, k == KT - 1,
                       [("wgu", slot, 1), ("hTF", k)], [("ps", bu)])
            s_ = f % 2
            actf(sgs[:, s_, :], ps[bg][:, :], AF.Tanh, [("ps", bg)], [("sgs", s_)], scale=0.5)
            stt(sgs[:, s_, :], sgs[:, s_, :], 1.0, ps[bg][:, :], ALU.add, ALU.mult, [("sgs", s_), ("ps", bg)], [("sgs", s_)])
            tt(act[:, f, :], sgs[:, s_, :], ps[bu][:, :], ALU.mult, [("sgs", s_), ("ps", bu)], [("act", f)])
            yield
        if wd_ptr[0] == 0:
            wd_ensure(1)
        for half in range(2):
            bs = [4, 5, 6, 7]
            for jc in range(11):
                P.tag = 'ffn.down'
                L = wd_use[0]
                wd_use[0] += 1
                slot = L % 3
                wd_ensure(L + 2)
                for f2 in range(2):
                    f = 2 * jc + f2
                    for o in range(4):
                        mm(ps[bs[o]][:, :], wd[:, slot, f2, o * 128:(o + 1) * 128], act[:, f, :], f == 0, f == FT - 1,
                           [("wd", slot), ("act", f)], [("ps", bs[o])])
                if jc == 10:
                    for o in range(4):
                        k = half * 4 + o
                        stt(xT(par, k), ps[bs[o]][:, :], 0.25, xT(par, k), ALU.mult, ALU.add, [("ps", bs[o]), xk(par, k)], [xk(par, k)])
                yield

    def loadx_gen(s):
        par = s % 2
        P.tag = 'loadx'
        for k in range(KT):
            dma("sp", xsel(par)[:, k, :], x_d[k * 128:(k + 1) * 128, s * TS:(s + 1) * TS], [], [xk(par, k)], "xl%d" % k)
        yield

    def xn_tile(k):
        return act[:, 2 * k:2 * k + 2, :].rearrange("p a t -> p (a t)").bitcast(F32)

    def final_gen(s):
        P.tag = 'final'
        par = s % 2
        b = 4
        for k in range(KT):
            s_ = k % 2
            actf(sqF[:, s_, :], xT(par, k), AF.Square, [xk(par, k)], [("sqF", s_)])
            mm(ps[b][:, :], onesb[:], sqF[:, s_, :], k == 0, k == KT - 1, [("sqF", s_), "onesb"], [("ps", b)])
        rstd_from(b, D, rsF[:, 1, :], ("rsF", 1))
        yield
        for k in range(KT):
            P.tag = 'final'
            stt(xn_tile(k), xT(par, k), gains[:, 24 + k:25 + k], rsF[:, 1, :], ALU.mult, ALU.mult,
                [xk(par, k), "gains", ("rsF", 1)], [("act", 2 * k), ("act", 2 * k + 1)])
            dma("sp", out_d[k * 128:(k + 1) * 128, s * TS:(s + 1) * TS], xn_tile(k),
                [("act", 2 * k), ("act", 2 * k + 1)], [("out", s, k)], "out%d" % k)
        yield

    uTd = lambda m: am[:, m, :].rearrange("p (i n) -> p i n", i=8)
    qT = lambda m: am[:, 4 + m, :]
    Ugrp = lambda g: am[:, 8 + g // 8, (g % 8) * NB:(g % 8 + 1) * NB]
    Y2grp = lambda g: am[:, g // 8, (g % 8) * NB:(g % 8 + 1) * NB]
    y2T = lambda m: am[:, 12 + m, :]
    yg = lambda m: (am[:, 4 + m, :], ("am", 4 + m))
    yat = lambda m: (am[:, 16 + m, :], ("am", 16 + m))

    def rope_finish(qs, out_ap, out_key):
        qap, qkey = qs
        b2 = bankM()
        mm(ps[b2][:, :], permb[:], qap, True, True, ["permb", qkey], [("ps", b2)])
        tt(tmpf[:, 0, :], qap, ropec[:], ALU.mult, [qkey, "ropec"], [("tmpf", 0), ("tmpf", "0b")])
        tt(tmpf[:, 1, :], ps[b2][:, :], ropes[:], ALU.mult, [("ps", b2), "ropes"], [("tmpf", 1)])
        tt(out_ap, tmpf[:, 0, :], tmpf[:, 1, :], ALU.add, [("tmpf", 0), ("tmpf", 1)], [out_key])

    def mixer_gen(s):
        par = s % 2
        P.tag = 'mix.norm'
        if wi_ptr[0] == 0:
            wi_ensure(2)
            wo_ensure(0)
        dma("sp", ropec[:], dr["ropec"][:, s * TS:(s + 1) * TS], [], ["ropec"], "ropec")
        dma("sp", ropes[:], dr["ropes"][:, s * TS:(s + 1) * TS], [], ["ropes"], "ropes")
        norm_to(par, 8, hTM, "hTM", [(sqM[:, 0, :], ("sqM", 0)), (qpre[:, 0, :], ("qpre", 0))], rsM[:, 0, :], ("rsM", 0), bankM())
        yield

        def proj_group():
            L = wi_use[0]
            wi_use[0] += 1
            slot = L % 4
            wi_ensure(L + 3)
            b = bankM()
            for k in range(KT):
                mm(ps[b][:, :], wis[:, slot, k, :], hTM[:, k, :], k == 0, k == KT - 1, [("wis", slot), ("hTM", k)], [("ps", b)])
            return b

        for m in range(4):
            P.tag = 'mix.proj'
            b = proj_group()
            actf(uTd(m), ps[b][:, :].rearrange("p (n i) -> p i n", i=8), AF.Copy, [("ps", b)], [("am", m)])
            yield
        jobs = [(qT(m), ("am", 4 + m)) for m in range(4)]
        jobs += [(kTd[:, kk, 128:640], ("kTd", kk)) for kk in range(2)]
        QS = [(qpre[:, 0, :], ("qpre", 0)), (sqM[:, 0, :], ("sqM", 0))]
        pending = None
        for idx, (out_ap, out_key) in enumerate(jobs):
            P.tag = 'mix.proj'
            b = proj_group()
            qs = QS[idx % 2]
            actf(qs[0], ps[b][:, :], AF.Copy, [("ps", b)], [qs[1]])
            if pending is not None:
                rope_finish(*pending)
            pending = (qs, out_ap, out_key)
            yield
        P.tag = 'mix.proj'
        L = wi_use[0]
        wi_use[0] += 1
        slot = L % 4
        wi_ensure(L + 3)
        b = bankM()
        for blk in range(4):
            for k in range(KT):
                mm(ps[b][:, blk * 128:(blk + 1) * 128], hTM[:, k, blk * 128:(blk + 1) * 128], wis[:, slot, k, :],
                   k == 0, k == KT - 1, [("wis", slot), ("hTM", k)], [("ps", b)])
        rope_finish(*pending)
        actf(vsb[:, 1:5, :], ps[b][:, :].rearrange("p (a d) -> p a d", a=4), AF.Copy, [("ps", b)], ["vsb"])
        yield

        def attn_gen():
            ptc = 0
            units = [(m, pair) for m in range(4) for pair in range(2)]

            def a1(m, pair, hh, pi_):
                kv = m // 2
                rows = slice(hh * 64, (hh + 1) * 64)
                bs_ = bankM()
                first = (s == 0 and pair == 0)
                mk, mkk = (maskf, "maskf") if first else (mask, "mask")
                mm(ps[bs_][:, :], identb[:], mk[:], True, False, ["identb", mkk], [("ps", bs_)])
                n = 0
                for qb in range(2):
                    blkq = 2 * pair + qb
                    for piece in range(2):
                        slot = blkq + piece
                        mm(ps[bs_][:, (qb * 2 + piece) * 128:(qb * 2 + piece + 1) * 128],
                           kTd[rows, kv, slot * 128:(slot + 1) * 128], qT(m)[rows, blkq * 128:(blkq + 1) * 128],
                           False, n == 3, [("kTd", kv), ("am", 4 + m)], [("ps", bs_)])
                        n += 1
                actf(PT[:, pi_, :], ps[bs_][:, :], AF.Exp, [("ps", bs_)], [("PT", pi_)], scale=0.125)

            def pv(m, pair, hh, pi_):
                kv = m // 2
                rows = slice(hh * 64, (hh + 1) * 64)
                for qb in range(2):
                    blkq = 2 * pair + qb
                    ncol = slice(qb * 128, (qb + 1) * 128)
                    dcol = slice(256 + qb * 128, 256 + (qb + 1) * 128)
                    for piece in range(2):
                        slot = blkq + piece
                        mm(ps[BATT][rows, ncol], vsb[:, slot, kv * 64:(kv + 1) * 64], PT[:, pi_, (qb * 2 + piece) * 128:(qb * 2 + piece + 1) * 128],
                           piece == 0, piece == 1, ["vsb", ("PT", pi_)], [("ps", BATT)])
                    for piece in range(2):
                        mm(ps[BATT][rows, dcol], onesb[:, 0:64], PT[:, pi_, (qb * 2 + piece) * 128:(qb * 2 + piece + 1) * 128],
                           piece == 0, piece == 1, ["onesb", ("PT", pi_)], [("ps", BATT)])

            def norm_unit(m, pair):
                den = tmpf[:, 2, 0:256]
                ts1(den, ps[BATT][:, 256:512], esink[:, m:m + 1], ALU.add, [("ps", BATT), "esink"], [("tmpf", 2)])
                recip(den, den, [("tmpf", 2)], [("tmpf", 2)])
                ya, yak = yat(m)
                tt(ya[:, pair * 256:(pair + 1) * 256], ps[BATT][:, 0:256], den, ALU.mult, [("ps", BATT), ("tmpf", 2)], [yak])

            a1(0, 0, 0, 0); yield
            a1(0, 0, 1, 1); yield
            for ui, (m, pair) in enumerate(units):
                nxt = units[ui + 1] if ui + 1 < len(units) else None
                pv(m, pair, 0, 0); yield
                if nxt:
                    a1(nxt[0], nxt[1], 0, 0); yield
                pv(m, pair, 1, 1); yield
                norm_unit(m, pair)
                if nxt:
                    a1(nxt[0], nxt[1], 1, 1)
                yield
            b = bankM()
            for m in range(4):
                ya, yak = yat(m)
                actf(sqM[:, 0, :], ya, AF.Square, [yak], [("sqM", 0)])
                mm(ps[b][:, :], onesb[:], sqM[:, 0, :], m == 0, m == 3, [("sqM", 0), "onesb"], [("ps", b)])
            rstd_from(b, 512, rsM[:, 1, :], ("rsM", 1))
            yield

        def ssm_gen():
            cp(Hb[:, :, :, 0:1], Hc[:].unsqueeze(3), ["Hc"], ["Hb"])
            vbank = {}

            def U_unit(m):
                b = bankM()
                for gg in range(8):
                    for i in range(8):
                        mm(ps[b][:, gg * NB:(gg + 1) * NB], sel[:, gg, (7 - i) * 16:(7 - i) * 16 + 128], uTd(m)[:, i, :],
                           i == 0, i == 7, ["sel", ("am", m)], [("ps", b)])
                actf(am[:, 8 + m, :], ps[b][:, :], AF.Copy, [("ps", b)], [("am", 8 + m)])

            def V_unit(q4):
                b = bankM()
                vbank[q4] = b
                for tt_ in range(4):
                    t = 4 * q4 + tt_
                    for g2 in range(2):
                        g = 2 * t + g2
                        for ri in range(2):
                            c0 = (tt_ * 2 + ri) * NB
                            mm(ps[b][g2 * 64:(g2 + 1) * 64, c0:c0 + NB], Wm[:, g, ri * 64:(ri + 1) * 64], Ugrp(g), True, True,
                               ["Wm", ("am", 8 + g // 8)], [("ps", b)])

            def D_unit(q4):
                b = vbank[q4]
                V = ps[b][:, :].rearrange("p (t r n) -> p t r n", t=4, r=2)
                Vre, Vim = V[:, :, 0, :], V[:, :, 1, :]
                cs_ = cosT[:, 4 * q4:4 * q4 + 4, :]
                sn_ = sinT[:, 4 * q4:4 * q4 + 4, :]
                tA = tmpf[:, 0, 0:256].rearrange("p (t n) -> p t n", t=4)
                tB = tmpf[:, 0, 256:512].rearrange("p (t n) -> p t n", t=4)
                kA, kB = ("tmpf", 0), ("tmpf", "0b")
                gk = "Gin"
                tt(tA, Vre, cs_, ALU.mult, [("ps", b), "cosT"], [kA]); tt(tB, Vim, sn_, ALU.mult, [("ps", b), "sinT"], [kB])
                tt(Gin[:, 0], tA, tB, ALU.add, [kA, kB], [gk])
                tt(tA, Vim, cs_, ALU.mult, [("ps", b), "cosT"], [kA]); tt(tB, Vre, sn_, ALU.mult, [("ps", b), "sinT"], [kB])
                tt(Gin[:, 1], tA, tB, ALU.subtract, [kA, kB, gk], [gk])
                sk_ = "Gs"
                for tt_ in range(4):
                    t = 4 * q4 + tt_
                    for ri in range(2):
                        out_ap = Gs[:, ri, tt_, :]
                        d0 = r8[:, t:t + 1].to_broadcast([128, NB])
                        d1 = Gin[:, ri, tt_, :]
                        init = Hc[:, t, ri:ri + 1]
                        P.add("dve", (lambda o_=out_ap, a_=d0, b_=d1, i_=init: (lambda e: e.tensor_tensor_scan(
                            out=o_, data0=a_, data1=b_, initial=i_, op0=ALU.mult, op1=ALU.add)))(),
                            [gk, "r8", "Hc", sk_], [sk_])
                Gre, Gim = Gs[:, 0], Gs[:, 1]
                tt(tA, Gre, cs_, ALU.mult, [sk_, "cosT"], [kA]); tt(tB, Gim, sn_, ALU.mult, [sk_, "sinT"], [kB])
                tt(Gin[:, 0], tA, tB, ALU.subtract, [kA, kB, gk], [gk])
                tt(tA, Gre, sn_, ALU.mult, [sk_, "sinT"], [kA]); tt(tB, Gim, cs_, ALU.mult, [sk_, "cosT"], [kB])
                tt(Gin[:, 1], tA, tB, ALU.add, [kA, kB, gk], [gk])
                for ri in range(2):
                    actf(Hb[:, 4 * q4:4 * q4 + 4, ri, 1:NB + 1], Gin[:, ri], AF.Copy, [gk], [("Hb", q4)])
                    cp(Hc[:, 4 * q4:4 * q4 + 4, ri:ri + 1], Gin[:, ri, :, NB - 1:NB], [gk], ["Hc"])

            def Y_unit(m):
                b = bankM()
                for gg in range(8):
                    g = 8 * m + gg
                    t, g2 = g // 2, g % 2
                    rows = slice(g2 * 64, (g2 + 1) * 64)
                    o_ = ps[b][:, gg * NB:(gg + 1) * NB]
                    mm(o_, Mm[:, g, :], Ugrp(g), True, False, ["Mm", ("am", 8 + m)], [("ps", b)])
                    mm(o_, tabR[rows, t, :], Hb[rows, t, 0, 0:NB], False, False, ["tabR", "Hb", ("Hb", m)], [("ps", b)])
                    mm(o_, tabI[rows, t, :], Hb[rows, t, 1, 0:NB], False, True, ["tabI", "Hb", ("Hb", m)], [("ps", b)])
                U3 = am[:, 8 + m, :].rearrange("p (g n) -> p g n", g=8)
                Db = dblk[:, 8 * m:8 * m + 8].unsqueeze(2).to_broadcast([128, 8, NB])
                y1 = tmpf[:, 1, :]
                tt(y1.rearrange("p (g n) -> p g n", g=8), U3, Db, ALU.mult, [("am", 8 + m), "dblk"], [("tmpf", 1)])
                tt(y1, y1, ps[b][:, :], ALU.add, [("tmpf", 1), ("ps", b)], [("tmpf", 1)])
                actf(am[:, m, :], y1, AF.Gelu_apprx_tanh, [("tmpf", 1)], [("am", m)])

            def I_unit(m):
                b = bankM()
                for j in range(8):
                    for gg in range(8):
                        mm(ps[b][:, j * NB:(j + 1) * NB], sel[:, j, (7 - gg) * 16:(7 - gg) * 16 + 128], Y2grp(8 * m + gg),
                           gg == 0, gg == 7, ["sel", ("am", m)], [("ps", b)])
                actf(y2T(m).rearrange("p (n j) -> p j n", j=8), ps[b][:, :].rearrange("p (j n) -> p j n", j=8), AF.Copy,
                     [("ps", b)], [("am", 12 + m)])

            order = [(U_unit, 0), (U_unit, 1), (V_unit, 0), (D_unit, 0), (U_unit, 2), (V_unit, 1), (D_unit, 1), (U_unit, 3),
                     (V_unit, 2), (D_unit, 2), (Y_unit, 0), (V_unit, 3), (D_unit, 3), (Y_unit, 1), (I_unit, 0), (Y_unit, 2),
                     (I_unit, 1), (Y_unit, 3), (I_unit, 2), (I_unit, 3)]
            for fn_, arg in order:
                fn_(arg)
                yield
            yield "need_attn_done"
            for mo in range(4):
                b = bankM()
                for k in range(4):
                    mm(ps[b][:, :], w_glu[:, k, mo * 128:(mo + 1) * 128], y2T(k), k == 0, k == 3, ["w_glu", ("am", 12 + k)], [("ps", b)])
                sg = tmpf[:, 1, :]
                actf(sg, ps[b][:, :], AF.Tanh, [("ps", b), "hbg"], [("tmpf", 1)], bias=hbg[:, mo:mo + 1], scale=0.5)
                ygm, ygk = yg(mo)
                stt(ygm, sg, 1.0, y2T(mo), ALU.add, ALU.mult, [("am", 12 + mo), ("tmpf", 1)], [ygk])
                yield
            b = bankM()
            for mo in range(4):
                ygm, ygk = yg(mo)
                actf(sqM[:, 0, :], ygm, AF.Square, [ygk], [("sqM", 0)])
                mm(ps[b][:, :], onesb[:], sqM[:, 0, :], mo == 0, mo == 3, [("sqM", 0), "onesb"], [("ps", b)])
            rstd_from(b, 512, rsM[:, 0, :], ("rsM", 0), ec=2)
            yield

        ga, gs = attn_gen(), ssm_gen()
        a_done = s_done = s_wait = False
        while not (a_done and s_done):
            if not a_done:
                P.tag = 'mix.attn'
                try:
                    next(ga)
                except StopIteration:
                    a_done = True
                yield
            if not s_done and not (s_wait and not a_done):
                P.tag = 'mix.ssm'
                try:
                    if next(gs) == "need_attn_done":
                        s_wait = True
                except StopIteration:
                    s_done = True
                yield
        if s == 0:
            dump("yattn", am[:, 16:20, :], [("am", 16 + m) for m in range(4)], [128, 4, TS], BF16)
            dump("y2T", am[:, 12:16, :], [("am", 12 + m) for m in range(4)], [128, 4, TS], BF16)
        P.tag = 'mix.onorm'
        for k in range(4):
            ygm, ygk = yg(k)
            stt(hTM[:, k, :], ygm, gains[:, 32 + k:33 + k], rsM[:, 0, :], ALU.mult, ALU.mult, [ygk, "gains", ("rsM", 0)], [("hTM", k)])
        for k in range(4):
            ya, yak = yat(k)
            stt(hTM[:, 4 + k, :], ya, gains[:, 36 + k:37 + k], rsM[:, 1, :], ALU.mult, ALU.mult, [yak, "gains", ("rsM", 1)], [("hTM", 4 + k)])
        yield
        for o in range(8):
            P.tag = 'mix.wout'
            L = wo_use[0]
            wo_use[0] += 1
            slot = L % 2
            wo_ensure(L + 1)
            b = bankM()
            for k in range(KT):
                mm(ps[b][:, :], wo[:, slot, k, :], hTM[:, k, :], k == 0, k == KT - 1, [("wo", slot), ("hTM", k)], [("ps", b)])
            tt(xT(par, o), ps[b][:, :], xT(par, o), ALU.add, [("ps", b), xk(par, o)], [xk(par, o)])
            yield
        cp(kTd[:, :, 0:128], kTd[:, :, 512:640], [("kTd", 0), ("kTd", 1)], [("kTd", 0), ("kTd", 1)], eng="act")
        cp(vsb[:, 0, :], vsb[:, 4, :], ["vsb"], ["vsb"], eng="act")
        yield

    def drain(g):
        for _ in g:
            pass

    def chain(*gens):
        for g in gens:
            for _ in g:
                yield

    def interleave(ga, na, gb, nb):
        ca = cb = 0
        a_done = b_done = False
        while not (a_done and b_done):
            pick_a = (not a_done) and (b_done or ca * nb <= cb * na)
            if pick_a:
                try:
                    next(ga)
                    ca += 1
                except StopIteration:
                    a_done = True
            else:
                try:
                    next(gb)
                    cb += 1
                except StopIteration:
                    b_done = True

    P.tile = 0
    sb = sbp
    pg = ssm_precompute()
    next(pg)
    wgu_ensure(2)
    drain(chain(loadx_gen(0), ffn_norm_gen(0, 0)))
    fg = ffn_gen(1, 0, 0, do_norm=False)
    p_part1 = True
    f_live = True
    while p_part1 or f_live:
        if p_part1:
            P.tag = 'pre'
            if next(pg) == "PE_PART":
                p_part1 = False
        if f_live:
            try:
                next(fg)
            except StopIteration:
                f_live = False
    P.tag = 'pre'
    drain(pg)
    P.barrier()
    pstack.close()
    sb = sb_main
    xTb = sb("xTb", [128, KT, TS], F32)
    hTM = sb("hTM", [128, KT, TS], BF16)
    am = sb("am", [128, 20, TS], BF16)
    dump("x1", xTa[:], [xk(0, k) for k in range(KT)], [128, KT, TS])
    for s in range(NT_RUN):
        P.tile = s
        bparts = []
        nb = 0
        if s >= 1:
            bparts += [ffn_gen(2, 16, (s - 1) % 2), final_gen(s - 1)]
            nb += 50
        if s + 1 < NT_RUN:
            bparts += [loadx_gen(s + 1), ffn_norm_gen(0, (s + 1) % 2)]
            nb += 5
        if bparts:
            interleave(mixer_gen(s), 95, chain(*bparts), nb)
        else:
            drain(mixer_gen(s))
        if s == 0:
            dump("x2", xTa[:], [xk(0, k) for k in range(KT)], [128, KT, TS])
        if s + 1 < NT_RUN:
            drain(ffn_gen(1, 0, (s + 1) % 2, do_norm=False))
    drain(chain(ffn_gen(2, 16, (NT_RUN - 1) % 2), final_gen(NT_RUN - 1)))
    P.add("sp", None, [("out", s, k) for s in range(NT_RUN) for k in range(KT)] + [("dbg", n) for n in dbg_d], [])

    P.finalize(nc, stack)
    with nc.Block() as block:
        @block.sync
        def _(e):
            P.emit("sp", e)

        @block.tensor
        def _(e):
            P.emit("pe", e)

        @block.scalar
        def _(e):
            P.emit("act", e)

        @block.vector
        def _(e):
            P.emit("dve", e)

        @block.gpsimd
        def _(e):
            P.emit("pool", e)
    stack.close()
    nc._prog = P
    return nc, list(dbg_d.keys())


_CACHE = {}


def kernel(**inputs):
    x = np.ascontiguousarray(np.asarray(inputs["x"], dtype=np.float32))
    B = x.shape[0]
    if "nc" not in _CACHE:
        _CACHE["nc"] = build_program()
    nc, dbg = _CACHE["nc"]
    shared = host_layout(inputs)
    shared.update(host_consts())
    in_maps = []
    for b in range(B):
        m = dict(shared)
        m["x"] = np.ascontiguousarray(x[b].T)
        in_maps.append(m)
    res = run_bass_kernel_spmd(nc, in_maps, core_ids=list(range(B)))
    out = np.stack([np.ascontiguousarray(np.asarray(r["out"], dtype=np.float32).T) for r in res.results], axis=0)
    if DEBUG:
        _CACHE["dbg"] = {n: np.asarray(res.results[0]["dbg_" + n]) for n in dbg}
    return out
```

```python
import math
import os
from contextlib import ExitStack

import numpy as np
import ml_dtypes

import concourse.bass as bass
import concourse.mybir as mybir
from concourse.bass_utils import run_bass_kernel_spmd

F32 = mybir.dt.float32
BF16 = mybir.dt.bfloat16
AF = mybir.ActivationFunctionType
ALU = mybir.AluOpType

D = 1024
KT = 8
FF = 2816
FT = 22
SEQ = 4096
TS = 512
NTILE = SEQ // TS
NB = TS // 8
EPS = 1e-6
NEG = -30000.0
NWIN = 1408

DEBUG = bool(int(os.environ.get("KDBG", "0")))
NT_RUN = int(os.environ.get("KNT", str(NTILE)))


class Prog:
    ENGS = ("pe", "act", "dve", "pool", "sp")

    def __init__(self):
        self.ops = []
        self.lastw = {}
        self.readers = {}
        self.tag = ""
        self.tile = -1

    def add(self, eng, fn, reads=(), writes=(), dma=None):
        i = len(self.ops)
        deps = set()
        for k in reads:
            w = self.lastw.get(k)
            if w is not None:
                deps.add(w)
        for k in writes:
            w = self.lastw.get(k)
            if w is not None:
                deps.add(w)
            for r in self.readers.get(k, ()):
                deps.add(r)
        for k in reads:
            lst = self.readers.setdefault(k, [])
            if dma is None:
                lst[:] = [r for r in lst if not (self.ops[r]["eng"] == eng and self.ops[r]["dma"] is None)]
            lst.append(i)
        for k in writes:
            self.lastw[k] = i
            self.readers[k] = []
        deps.discard(i)
        self.ops.append(dict(eng=eng, fn=fn, deps=deps, dma=dma, signal=False, count=0, tag=self.tag, tile=self.tile))
        return i

    def barrier(self):
        last = {}
        for i, op in enumerate(self.ops):
            if op["dma"] is not None:
                if str(op["dma"]).startswith("cv"):
                    continue
                last[("d", op["dma"])] = i
            elif op["fn"] is not None:
                last[("e", op["eng"])] = i
        deps = set(last.values())
        for e in self.ENGS:
            self.ops.append(dict(eng=e, fn=None, deps=set(deps), dma=None, signal=False, count=0, tag='barrier', tile=-1))

    def finalize(self, nc, stack):
        ops = self.ops
        for op in ops:
            for d in op["deps"]:
                dop = ops[d]
                if dop["dma"] is None:
                    if dop["eng"] == "pe" and op["eng"] == "pe" and op["dma"] is None:
                        continue
                    dop["signal"] = True
        esem = {e: stack.enter_context(nc.semaphore("sem_" + e)) for e in self.ENGS}
        dsem = {}
        ecnt = {e: 0 for e in self.ENGS}
        dcnt = {}
        waited = {e: {} for e in self.ENGS}
        streams = {e: [] for e in self.ENGS}
        for op in ops:
            e = op["eng"]
            waits = {}
            for d in op["deps"]:
                dop = ops[d]
                if dop["dma"] is not None:
                    key = ("d", dop["dma"])
                    val = dop["count"]
                    sem = dsem[dop["dma"]]
                else:
                    if dop["eng"] == "pe" and e == "pe" and op["dma"] is None:
                        continue
                    key = ("e", dop["eng"])
                    val = dop["count"]
                    sem = esem[dop["eng"]]
                if waited[e].get(key, 0) >= val:
                    continue
                if key not in waits or waits[key][1] < val:
                    waits[key] = (sem, val)
            for key, (sem, val) in waits.items():
                waited[e][key] = val
            if op["dma"] is not None:
                if op["dma"] not in dsem:
                    dsem[op["dma"]] = stack.enter_context(nc.semaphore("dsem%d" % len(dsem)))
                    dcnt[op["dma"]] = 0
                dcnt[op["dma"]] += 16
                op["count"] = dcnt[op["dma"]]
                inc = (dsem[op["dma"]], 16)
            elif op["signal"]:
                ecnt[e] += 1
                op["count"] = ecnt[e]
                inc = (esem[e], 1)
            else:
                inc = None
            streams[e].append((list(waits.values()), op["fn"], inc))
        self.streams = streams
        self.nsem = len(dsem) + len(esem)

    def emit(self, eng_name, e):
        for waits, fn, inc in self.streams[eng_name]:
            for sem, val in waits:
                e.wait_ge(sem, val)
            if fn is None:
                continue
            ins = fn(e)
            if inc is not None:
                ins.then_inc(inc[0], inc[1])


def _bf(a):
    return np.ascontiguousarray(a.astype(ml_dtypes.bfloat16))


def host_consts():
    c = {}
    c["identf"] = np.eye(128, dtype=np.float32)
    c["identb"] = _bf(np.eye(128, dtype=np.float32))
    c["onesb"] = _bf(np.ones((128, 128), np.float32))
    perm = np.zeros((128, 128), np.float32)
    for h in range(2):
        for d in range(16):
            pd = d + 8 if d < 8 else d - 8
            perm[h * 64 + pd, h * 64 + d] = 1.0
    c["permb"] = _bf(perm)
    kj = np.arange(128)[:, None]
    qi = np.arange(128)[None, :]
    prev = np.where(kj > qi, 0.0, NEG).astype(np.float32)
    same = np.where(kj <= qi, 0.0, NEG).astype(np.float32)
    full = np.full((128, 128), NEG, np.float32)
    c["mask"] = _bf(np.concatenate([prev, same, prev, same], axis=1))
    c["maskf"] = _bf(np.concatenate([full, same, prev, same], axis=1))
    sel = np.zeros((128, 8, 240), np.float32)
    for g in range(8):
        for cc in range(16):
            sel[16 * g + cc, g, 112 + cc] = 1.0
    c["sel"] = _bf(sel)
    half = 8
    inv_freq = (500000.0 ** (-np.arange(half, dtype=np.float32) * 2.0 / 16)).astype(np.float32)
    ang = np.arange(SEQ, dtype=np.float32)[:, None] * inv_freq[None, :]
    cos = np.cos(ang).astype(np.float32).T
    sin = np.sin(ang).astype(np.float32).T
    C = np.ones((128, SEQ), np.float32)
    S = np.zeros((128, SEQ), np.float32)
    for h in range(2):
        C[h * 64 + 0:h * 64 + 8] = cos
        C[h * 64 + 8:h * 64 + 16] = cos
        S[h * 64 + 0:h * 64 + 8] = -sin
        S[h * 64 + 8:h * 64 + 16] = sin
    c["ropec"] = C
    c["ropes"] = S
    return c


def host_layout(inp):
    o = {}
    f = lambda a: np.ascontiguousarray(np.asarray(a, dtype=np.float32))
    o["wg1"] = f(inp["ffn1_w_gate"][0]); o["wu1"] = f(inp["ffn1_w_up"][0]); o["wd1"] = f(inp["ffn1_w_down"][0])
    o["wg2"] = f(inp["ffn2_w_gate"][0]); o["wu2"] = f(inp["ffn2_w_up"][0]); o["wd2"] = f(inp["ffn2_w_down"][0])
    w_in = f(inp["w_in"][0])
    u = w_in[:, 0:512]; q = w_in[:, 512:1024]; k = w_in[:, 1024:1152]; v = w_in[:, 1152:1280]
    o["win"] = np.ascontiguousarray(np.concatenate([u, q, k[:, 0:64], k[:, 0:64], k[:, 64:128], k[:, 64:128], v], axis=1))
    o["wout"] = f(inp["w_out"][0])
    o["wglu"] = f(inp["ssm_w_glu"][0])
    fm = lambda vec: np.ascontiguousarray(f(vec).reshape(-1, 128).T)
    gains = np.concatenate([fm(inp["ffn1_norm"][0]), fm(inp["mix_norm"][0]), fm(inp["ffn2_norm"][0]),
                            fm(inp["final_norm"]), fm(inp["ssm_out_norm"][0]), fm(inp["attn_out_norm"][0]),
                            fm(inp["ssm_b_glu"][0])], axis=1)
    o["gains"] = np.ascontiguousarray(gains)
    Dv = f(inp["ssm_D"][0]).reshape(32, 16)
    o["dblk"] = np.ascontiguousarray(np.tile(Dv.T, (8, 1)))
    sk = f(inp["attn_sinks"][0])
    o["sinkrow"] = np.ascontiguousarray(np.repeat(sk.reshape(4, 2), 64, axis=1).T)
    pl = lambda a: np.ascontiguousarray(f(a).reshape(16, 2, 64).transpose(1, 2, 0).reshape(128, 16))
    o["are"] = pl(inp["ssm_A_re"][0]); o["aim"] = pl(inp["ssm_A_im"][0])
    ldt = f(inp["ssm_log_dt"][0])
    o["ldt"] = pl(np.repeat(ldt[:, None], 64, axis=1))
    pb = lambda a: np.ascontiguousarray(f(a).reshape(16, 2, 64, 16).transpose(1, 2, 0, 3).reshape(128, 16, 16))
    o["bre"] = pb(inp["ssm_B_re"][0]); o["bim"] = pb(inp["ssm_B_im"][0])
    pc = lambda a: np.ascontiguousarray(f(a).transpose(0, 2, 1).reshape(16, 2, 64, 16).transpose(1, 2, 0, 3).reshape(128, 16, 16))
    o["cre"] = pc(inp["ssm_C_re"][0]); o["cim"] = pc(inp["ssm_C_im"][0])
    return o


def build_program():
    nc = bass.Bass("TRN2", target_bir_lowering=False)
    P = Prog()
    stack = ExitStack()
    dr = {}

    def din(name, shape, dt=F32):
        dr[name] = nc.dram_tensor(name, list(shape), dt, kind="ExternalInput").ap()
        return dr[name]

    x_d = din("x", [D, SEQ])
    for n in ("wg1", "wu1", "wg2", "wu2"):
        din(n, [D, FF])
    for n in ("wd1", "wd2"):
        din(n, [FF, D])
    din("win", [D, NWIN]); din("wout", [D, D]); din("wglu", [512, 512])
    din("gains", [128, 44]); din("dblk", [128, 32]); din("sinkrow", [128, 4])
    for n in ("are", "aim", "ldt"):
        din(n, [128, 16])
    for n in ("bre", "bim", "cre", "cim"):
        din(n, [128, 16, 16])
    din("identf", [128, 128]); din("identb", [128, 128], BF16); din("onesb", [128, 128], BF16)
    din("permb", [128, 128], BF16); din("mask", [128, 512], BF16); din("maskf", [128, 512], BF16)
    din("sel", [128, 8, 240], BF16); din("ropec", [128, SEQ]); din("ropes", [128, SEQ])
    out_d = nc.dram_tensor("out", [D, SEQ], F32, kind="ExternalOutput").ap()
    dbg_d = {}

    def sb_main(name, shape, dt):
        return stack.enter_context(nc.sbuf_tensor("s_" + name, list(shape), dt))

    sb = sb_main

    wgu = sb("wgu", [128, 4, 2, 4, 128], BF16)
    wd = sb("wd", [128, 3, 2, 512], BF16)
    wis = sb("wis", [128, 4, KT, 128], BF16)
    wo = sb("wo", [128, 2, KT, 128], BF16)
    w_glu = sb("w_glu", [128, 4, 512], BF16)
    sgs = sb("sgs", [128, 2, TS], F32)
    sqF = sb("sqF", [128, 2, TS], BF16)
    sqM = sb("sqM", [128, 1, TS], BF16)
    rsF = sb("rsF", [128, 2, TS], F32)
    rsM = sb("rsM", [128, 2, TS], F32)
    tabR = sb("tabR", [128, 16, 128], BF16)
    tabI = sb("tabI", [128, 16, 128], BF16)
    Wm = sb("Wm", [128, 32, 128], BF16)
    Mm = sb("Mm", [128, 32, 128], BF16)
    cosT = sb("cosT", [128, 16, NB], F32)
    sinT = sb("sinT", [128, 16, NB], F32)
    r8 = sb("r8", [128, 16], F32)
    sel = sb("sel", [128, 8, 240], BF16)
    dblk = sb("dblk", [128, 32], F32)
    gains = sb("gains", [128, 44], F32)
    esink = sb("esink", [128, 4], F32)
    cst = sb("cst", [128, 4], F32)
    identf = sb("identf", [128, 128], F32)
    identb = sb("identb", [128, 128], BF16)
    onesb = sb("onesb", [128, 128], BF16)
    permb = sb("permb", [128, 128], BF16)
    mask = sb("mask", [128, 512], BF16)
    maskf = sb("maskf", [128, 512], BF16)
    kTd = sb("kTd", [128, 2, 5 * 128], BF16)
    vsb = sb("vsb", [128, 5, 128], BF16)
    PT = sb("PT", [128, 2, 512], BF16)
    ropec = sb("ropec", [128, TS], F32)
    ropes = sb("ropes", [128, TS], F32)
    qpre = sb("qpre", [128, 1, TS], BF16)
    tmpf = sb("tmpf", [128, 3, TS], F32)
    Gin = sb("Gin", [128, 2, 4, NB], F32)
    Gs = sb("Gs", [128, 2, 4, NB], F32)
    Hb = sb("Hb", [128, 16, 2, NB + 1], BF16)
    Hc = sb("Hc", [128, 16, 2], F32)

    ps = [stack.enter_context(nc.psum_tensor("ps%d" % i, [128, 512], F32)) for i in range(8)]

    rrM = [0]
    rrF = [0]

    def bankM():
        b = rrM[0] % 3
        rrM[0] += 1
        return b

    def bankF():
        b = 4 + rrF[0] % 4
        rrF[0] += 1
        return b

    bankA = bankM
    BATT = 3

    def mm(out, lhsT, rhs, start, stop, reads, writes):
        P.add("pe", lambda e: e.matmul(out, lhsT=lhsT, rhs=rhs, start=start, stop=stop), reads, writes)

    def tr(out, in_, reads, writes):
        P.add("pe", lambda e: e.transpose(out, in_, identf[:]), reads + ["identf"], writes)

    def actf(out, in_, func, reads, writes, bias=None, scale=None):
        kw = {}
        if bias is not None:
            kw["bias"] = bias
        if scale is not None:
            kw["scale"] = scale
        P.add("act", lambda e: e.activation(out=out, in_=in_, func=func, **kw), reads, writes)

    def tt(out, in0, in1, op, reads, writes, eng="dve"):
        P.add(eng, lambda e: e.tensor_tensor(out=out, in0=in0, in1=in1, op=op), reads, writes)

    def ts1(out, in0, s1, op0, reads, writes, eng="dve"):
        P.add(eng, lambda e: e.tensor_single_scalar(out=out, in_=in0, scalar=s1, op=op0), reads, writes)

    def stt(out, in0, scalar, in1, op0, op1, reads, writes):
        P.add("dve", lambda e: e.scalar_tensor_tensor(out=out, in0=in0, scalar=scalar, in1=in1, op0=op0, op1=op1), reads, writes)

    def cp(out, in_, reads, writes, eng="dve"):
        if eng == "act":
            P.add("act", lambda e: e.activation(out=out, in_=in_, func=AF.Copy), reads, writes)
        else:
            P.add(eng, lambda e: e.tensor_copy(out=out, in_=in_), reads, writes)

    def recip(out, in_, reads, writes):
        P.add("dve", lambda e: e.reciprocal(out=out, in_=in_), reads, writes)

    def memset(ap, val, writes, eng="dve"):
        P.add(eng, lambda e: e.memset(ap, val), [], writes)

    def dma(q, out, in_, reads, writes, key):
        P.add(q, lambda e: e.dma_start(out=out, in_=in_), reads, writes, dma=key)

    def dump(name, ap, reads, shape, dt=F32):
        if not DEBUG:
            return
        if name not in dbg_d:
            dbg_d[name] = nc.dram_tensor("dbg_" + name, list(shape), dt, kind="ExternalOutput").ap()
        dma("sp", dbg_d[name], ap, reads, [("dbg", name)], "dbg_" + name)

    def ld(q, t, src, key):
        dma(q, t[:], src, [], [key], "ld_" + key)

    ld("sp", identf, dr["identf"], "identf"); ld("sp", identb, dr["identb"], "identb")
    ld("sp", onesb, dr["onesb"], "onesb"); ld("sp", permb, dr["permb"], "permb")
    ld("sp", mask, dr["mask"], "mask"); ld("sp", maskf, dr["maskf"], "maskf")
    ld("sp", sel, dr["sel"], "sel"); ld("sp", gains, dr["gains"], "gains")
    ld("sp", dblk, dr["dblk"], "dblk"); ld("sp", esink, dr["sinkrow"], "esink")
    memset(cst[:, 0:1], EPS, ["cst"])
    memset(cst[:, 1:2], math.pi / 2, ["cst"])
    memset(cst[:, 2:3], 4.0 * EPS, ["cst"])
    memset(kTd[:], 0.0, ["kTd"])
    memset(vsb[:], 0.0, ["vsb"])
    memset(Hc[:], 0.0, ["Hc"])
    memset(Hb[:], 0.0, ["Hb"])
    actf(esink[:], esink[:], AF.Exp, ["esink"], ["esink"])
    hbg = sb("hbg", [128, 4], F32)
    ts1(hbg[:], gains[:, 40:44], 0.5, ALU.mult, ["gains"], ["hbg"])

    scr_gu = {fid: nc.dram_tensor("scr_gu%d" % fid, [2 * FT, 128, 2 * 4 * 128], BF16).ap() for fid in (1, 2)}
    scr_d = {fid: nc.dram_tensor("scr_d%d" % fid, [22, 128, 2 * 512], BF16).ap() for fid in (1, 2)}
    scr_wo = nc.dram_tensor("scr_wo", [8, 128, KT * 128], BF16).ap()
    scr_wi = nc.dram_tensor("scr_wi", [11, 128, KT * 128], BF16).ap()

    ffn_order = [1]
    for s in range(NT_RUN - 1):
        ffn_order += [1, 2]
    ffn_order.append(2)
    ffn_w = {1: ("wg1", "wu1", "wd1"), 2: ("wg2", "wu2", "wd2")}
    wgu_loads = [(fid, f, kh) for fid in ffn_order for f in range(FT) for kh in range(2)]
    wd_loads = [(fid, half, jc) for fid in ffn_order for half in range(2) for jc in range(11)]
    wgu_ptr = [0]
    wd_ptr = [0]
    seen_gu = set()
    seen_d = set()
    multi = NT_RUN > 1

    NCV = 16
    cvi = [0]

    def conv(out_ap, in_ap, key):
        i = cvi[0] % NCV
        cvi[0] += 1
        dma("pool", out_ap, in_ap, [], [key, ("cvslot", i)], "cv%d" % i)

    def conv_gu(fid):
        gname, uname, _ = ffn_w[fid]
        for f in range(FT):
            for kh in range(2):
                scr = scr_gu[fid][2 * f + kh].rearrange("p (g k c) -> p g k c", g=2, k=4)
                for gi, nm in enumerate((gname, uname)):
                    src = dr[nm].rearrange("(k p) f -> p k f", p=128)[:, 4 * kh:4 * kh + 4, f * 128:(f + 1) * 128]
                    conv(scr[:, gi], src, ("scr_gu", fid, f, kh, gi))

    def conv_d(fid):
        for half in range(2):
            for jc in range(11):
                scr = scr_d[fid][half * 11 + jc].rearrange("p (f d) -> p f d", f=2)
                src = dr[ffn_w[fid][2]].rearrange("(f p) d -> p f d", p=128)[:, 2 * jc:2 * jc + 2, half * 512:(half + 1) * 512]
                conv(scr, src, ("scr_d", fid, half, jc))

    P.tag = 'prep'
    dma("pool", w_glu[:], dr["wglu"].rearrange("(k p) f -> p k f", p=128), [], ["w_glu"], "ld_w_glu")
    conv_gu(1)
    conv_d(1)
    for cg in range(11):
        conv(scr_wi[cg].rearrange("p (k c) -> p k c", k=KT), dr["win"].rearrange("(k p) f -> p k f", p=128)[:, :, cg * 128:(cg + 1) * 128], ("scr_wi", cg))
    for o in range(8):
        conv(scr_wo[o].rearrange("p (k c) -> p k c", k=KT), dr["wout"].rearrange("(k p) d -> p k d", p=128)[:, :, o * 128:(o + 1) * 128], ("scr_wo", o))
    conv_gu(2)
    conv_d(2)

    def wgu_ensure(upto):
        while wgu_ptr[0] <= upto and wgu_ptr[0] < len(wgu_loads):
            L = wgu_ptr[0]
            fid, f, kh = wgu_loads[L]
            slot = L % 4
            scr = scr_gu[fid][2 * f + kh].rearrange("p (g k c) -> p g k c", g=2, k=4)
            dma("sp", wgu[:, slot], scr, [("scr_gu", fid, f, kh, 0), ("scr_gu", fid, f, kh, 1)], [("wgu", slot, 0), ("wgu", slot, 1)], "wgu%d" % slot)
            wgu_ptr[0] += 1

    def wd_ensure(upto):
        while wd_ptr[0] <= upto and wd_ptr[0] < len(wd_loads):
            L = wd_ptr[0]
            fid, half, jc = wd_loads[L]
            slot = L % 3
            scr = scr_d[fid][half * 11 + jc].rearrange("p (f d) -> p f d", f=2)
            dma("sp", wd[:, slot], scr, [("scr_d", fid, half, jc)], [("wd", slot)], "wd%d" % slot)
            wd_ptr[0] += 1

    wgu_use = [0]
    wd_use = [0]

    wi_ptr = [0]
    wi_use = [0]
    wo_ptr = [0]
    wo_use = [0]

    def wi_ensure(upto):
        while wi_ptr[0] <= upto and wi_ptr[0] < 11 * NT_RUN:
            L = wi_ptr[0]
            cg = L % 11
            slot = L % 4
            dma("sp", wis[:, slot], scr_wi[cg].rearrange("p (k c) -> p k c", k=KT), [("scr_wi", cg)], [("wis", slot)], "wis%d" % slot)
            wi_ptr[0] += 1

    def wo_ensure(upto):
        while wo_ptr[0] <= upto and wo_ptr[0] < 8 * NT_RUN:
            L = wo_ptr[0]
            o = L % 8
            slot = L % 2
            dma("sp", wo[:, slot], scr_wo[o].rearrange("p (k c) -> p k c", k=KT), [("scr_wo", o)], [("wo", slot)], "wo%d" % slot)
            wo_ptr[0] += 1

    def ssm_precompute():
        P.tag = 'pre'
        are = sb("p_are", [128, 16], F32); aim = sb("p_aim", [128, 16], F32); ldt = sb("p_ldt", [128, 16], F32)
        bre = sb("p_bre", [128, 16, 16], F32); bim = sb("p_bim", [128, 16, 16], F32)
        cre = sb("p_cre", [128, 16, 16], F32); cim = sb("p_cim", [128, 16, 16], F32)
        sm = sb("p_sm", [128, 24, 16], F32)
        pw = sb("p_pw", [128, 2, 16, 9], F32)
        bb = sb("p_bb", [128, 2, 16, 16], F32)
        bbp = sb("p_bbp", [128, 2, 2, 240], BF16)
        wtt = sb("p_wt", [128, 2, 2, 8, 16], F32)
        big = sb("p_big", [128, 2, 16, 64], F32)
        for nm, t in (("are", are), ("aim", aim), ("ldt", ldt), ("bre", bre), ("bim", bim), ("cre", cre), ("cim", cim)):
            dma("sp", t[:], dr[nm], [], ["p_" + nm], "ld_p_" + nm)
        S = lambda i: sm[:, i, :]
        K = "p_sm"
        rk = ["p_are", "p_aim", "p_ldt", K, "cst"]
        DT, AR, TH, MAG, C0, S0, CC, SS, CS, LBR, LBI, DEN, NRE, T1, T2, ZR, ZI, C8, S8, T3 = range(20)
        actf(S(DT), ldt[:], AF.Exp, rk, [K])
        tt(S(AR), are[:], S(DT), ALU.mult, rk, [K])
        tt(S(TH), aim[:], S(DT), ALU.mult, rk, [K])
        actf(S(MAG), S(AR), AF.Exp, rk, [K])
        actf(r8[:], S(AR), AF.Exp, rk, ["r8"], scale=8.0)
        actf(S(S0), S(TH), AF.Sin, rk, [K], scale=1.0 / 16)
        actf(S(C0), S(TH), AF.Sin, rk, [K], scale=1.0 / 16, bias=cst[:, 1:2])
        yield

        def dbl():
            tt(S(CC), S(C0), S(C0), ALU.mult, rk, [K])
            tt(S(SS), S(S0), S(S0), ALU.mult, rk, [K])
            tt(S(CS), S(C0), S(S0), ALU.mult, rk, [K])
            tt(S(C0), S(CC), S(SS), ALU.subtract, rk, [K])
            ts1(S(S0), S(CS), 2.0, ALU.mult, rk, [K])
        for _ in range(4):
            dbl()
            yield
        tt(S(LBR), S(MAG), S(C0), ALU.mult, rk, [K])
        tt(S(LBI), S(MAG), S(S0), ALU.mult, rk, [K])
        for _ in range(3):
            dbl()
            yield
        cp(S(C8), S(C0), rk, [K]); cp(S(S8), S(S0), rk, [K])
        tt(S(T1), are[:], are[:], ALU.mult, rk, [K])
        tt(S(T2), aim[:], aim[:], ALU.mult, rk, [K])
        tt(S(DEN), S(T1), S(T2), ALU.add, rk, [K])
        recip(S(DEN), S(DEN), rk, [K])
        ts1(S(NRE), S(LBR), -1.0, ALU.add, rk, [K])
        tt(S(T1), S(NRE), are[:], ALU.mult, rk, [K])
        tt(S(T2), S(LBI), aim[:], ALU.mult, rk, [K])
        tt(S(T1), S(T1), S(T2), ALU.add, rk, [K])
        tt(S(ZR), S(T1), S(DEN), ALU.mult, rk, [K])
        tt(S(T1), S(LBI), are[:], ALU.mult, rk, [K])
        tt(S(T2), S(NRE), aim[:], ALU.mult, rk, [K])
        tt(S(T1), S(T1), S(T2), ALU.subtract, rk, [K])
        tt(S(ZI), S(T1), S(DEN), ALU.mult, rk, [K])
        yield
        kp = ["p_pw", K]
        memset(pw[:, 0, :, 0:1], 1.0, ["p_pw"]); memset(pw[:, 1, :, 0:1], 0.0, ["p_pw"])
        for k in range(1, 9):
            pr, pi_ = pw[:, 0, :, k - 1], pw[:, 1, :, k - 1]
            tt(S(T1), pr, S(LBR), ALU.mult, kp, [K]); tt(S(T2), pi_, S(LBI), ALU.mult, kp, [K])
            tt(pw[:, 0, :, k], S(T1), S(T2), ALU.subtract, kp, ["p_pw"])
            tt(S(T1), pr, S(LBI), ALU.mult, kp, [K]); tt(S(T2), pi_, S(LBR), ALU.mult, kp, [K])
            tt(pw[:, 1, :, k], S(T1), S(T2), ALU.add, kp, ["p_pw"])
            yield
        bc16 = lambda ap2: ap2.unsqueeze(2).to_broadcast([128, 16, 16])
        kb = ["p_bre", "p_bim", K, "p_bb", "p_big"]
        t1 = big[:, 0, :, 0:16]; t2 = big[:, 1, :, 0:16]
        tt(t1, bre[:], bc16(S(ZR)), ALU.mult, kb, ["p_big"]); tt(t2, bim[:], bc16(S(ZI)), ALU.mult, kb, ["p_big"])
        tt(bb[:, 0], t1, t2, ALU.subtract, kb, ["p_bb"])
        tt(t1, bim[:], bc16(S(ZR)), ALU.mult, kb, ["p_big"]); tt(t2, bre[:], bc16(S(ZI)), ALU.mult, kb, ["p_big"])
        tt(bb[:, 1], t1, t2, ALU.add, kb, ["p_bb"])
        memset(bbp[:], 0.0, [("p_bbp", 0), ("p_bbp", 1)])
        yield
        kc = ["p_cre", "p_cim", "p_pw", "p_big", "p_wt"]
        tabRe = sb("p_tabRe", [128, 16, 256], BF16); tabIm = sb("p_tabIm", [128, 16, 256], BF16)
        memset(tabRe[:], 0.0, ["tabRe"]); memset(tabIm[:], 0.0, ["tabIm"])
        for tau in range(9):
            pr = pw[:, 0, :, tau:tau + 1].to_broadcast([128, 16, 16])
            pi_ = pw[:, 1, :, tau:tau + 1].to_broadcast([128, 16, 16])
            o_re = tabRe[:, :, 112 + tau * 16:112 + (tau + 1) * 16]; o_im = tabIm[:, :, 112 + tau * 16:112 + (tau + 1) * 16]
            ta = wtt[:, 0].rearrange("p r i c -> p (r i) c")
            tb = wtt[:, 1].rearrange("p r i c -> p (r i) c")
            tt(ta, cre[:], pr, ALU.mult, kc, ["p_wt"]); tt(tb, cim[:], pi_, ALU.mult, kc, ["p_wt"])
            tt(o_re, ta, tb, ALU.subtract, kc, ["tabRe"])
            tt(ta, cre[:], pi_, ALU.mult, kc, ["p_wt"]); tt(tb, cim[:], pr, ALU.mult, kc, ["p_wt"])
            tt(tb, ta, tb, ALU.add, kc, ["p_wt"])
            ts1(o_im, tb, -1.0, ALU.mult, kc, ["tabIm"])
            yield
        yield "PE_PART"
        pwr = sb("p_pwr", [128, 3, 16, 8], F32)
        for i in range(8):
            cp(pwr[:, 0, :, i:i + 1], pw[:, 0, :, 7 - i:8 - i], ["p_pw"], ["p_pwr"])
            cp(pwr[:, 1, :, i:i + 1], pw[:, 1, :, 7 - i:8 - i], ["p_pw"], ["p_pwr"])
        ts1(pwr[:, 2], pwr[:, 1], -1.0, ALU.mult, ["p_pwr"], ["p_pwr"])
        kw_ = ["p_pwr", "p_bb", "p_wt", "p_big"]
        for t in range(16):
            buf = t % 2
            wre = wtt[:, buf, 0]; wim = wtt[:, buf, 1]
            pr = pwr[:, 0, t, :].unsqueeze(2).to_broadcast([128, 8, 16])
            pi_ = pwr[:, 1, t, :].unsqueeze(2).to_broadcast([128, 8, 16])
            br = bb[:, 0, t, :].unsqueeze(1).to_broadcast([128, 8, 16])
            bi = bb[:, 1, t, :].unsqueeze(1).to_broadcast([128, 8, 16])
            x1 = big[:, 0, 0:8, 0:16]; x2 = big[:, 1, 0:8, 0:16]
            tt(x1, pr, br, ALU.mult, kw_, ["p_big"]); tt(x2, pi_, bi, ALU.mult, kw_, ["p_big"])
            tt(wre, x1, x2, ALU.subtract, kw_, ["p_wt"])
            tt(x1, pr, bi, ALU.mult, kw_, ["p_big"]); tt(x2, pi_, br, ALU.mult, kw_, ["p_big"])
            tt(wim, x1, x2, ALU.add, kw_, ["p_wt"])
            b = bankA()
            tr(ps[b][:, 0:128], wre.rearrange("p i c -> p (i c)"), ["p_wt"], [("ps", b)])
            tr(ps[b][:, 128:256], wim.rearrange("p i c -> p (i c)"), ["p_wt"], [("ps", b)])
            src = ps[b][:, 0:256].rearrange("p (r g q) -> p g r q", r=2, g=2)
            dst = Wm[:, 2 * t:2 * t + 2, :].rearrange("p g (r q) -> p g r q", r=2)
            cp(dst, src, [("ps", b)], ["Wm"])
            yield
        for g in range(32):
            t, g2 = g // 2, g % 2
            buf = t % 2
            if g2 == 0:
                cp(bbp[:, buf, 0, 112:128], bb[:, 0, t, :], ["p_bb"], [("p_bbp", buf)])
                cp(bbp[:, buf, 1, 112:128], bb[:, 1, t, :], ["p_bb"], [("p_bbp", buf)])
            rows = slice(g2 * 64, (g2 + 1) * 64)
            b = bankA()
            n = 0
            for i in range(8):
                off = (7 - i) * 16
                for ri, tab in ((0, tabRe), (1, tabIm)):
                    mm(ps[b][:, 0:128], bbp[rows, buf, ri, off:off + 128], tab[rows, t, off:off + 128],
                       n == 0, n == 15, [("p_bbp", buf), "tabRe", "tabIm"], [("ps", b)])
                    n += 1
            cp(Mm[:, g, :], ps[b][:, 0:128], [("ps", b)], ["Mm"], eng="act")
            yield
        cp(tabR[:], tabRe[:, :, 128:256], ["tabRe"], ["tabR"])
        cp(tabI[:], tabIm[:, :, 128:256], ["tabIm"], ["tabI"])
        kt = ["cosT", "sinT", K, "p_wt", "p_big"]
        cp(cosT[:, :, 0:1], S(C8).unsqueeze(2), kt, ["cosT"]); cp(sinT[:, :, 0:1], S(S8).unsqueeze(2), kt, ["sinT"])
        m = 1
        while m < NB:
            cr = cosT[:, :, m - 1:m].to_broadcast([128, 16, m]); sr = sinT[:, :, m - 1:m].to_broadcast([128, 16, m])
            a1 = big[:, 0, :, 0:m]; a2 = big[:, 1, :, 0:m]; a3 = big[:, 0, :, 32:32 + m]; a4 = big[:, 1, :, 32:32 + m]
            tt(a1, cosT[:, :, 0:m], cr, ALU.mult, kt, ["p_big"]); tt(a2, sinT[:, :, 0:m], sr, ALU.mult, kt, ["p_big"])
            tt(a3, cosT[:, :, 0:m], sr, ALU.mult, kt, ["p_big"]); tt(a4, sinT[:, :, 0:m], cr, ALU.mult, kt, ["p_big"])
            tt(cosT[:, :, m:2 * m], a1, a2, ALU.subtract, kt, ["cosT"])
            tt(sinT[:, :, m:2 * m], a3, a4, ALU.add, kt, ["sinT"])
            m *= 2
            yield

    pstack = ExitStack()

    def sbp(name, shape, dt):
        return pstack.enter_context(nc.sbuf_tensor("s_" + name, list(shape), dt))

    xTa = sb("xTa", [128, KT, TS], F32)
    hTF = sb("hTF", [128, KT, TS], BF16)
    act = sb("act", [128, FT, TS], BF16)
    xsel = lambda par: xTa if par == 0 else xTb

    def xT(par, k):
        return xsel(par)[:, k, :]

    def xk(par, k):
        return ("xT", par, k)

    def rstd_from(b, n, dst, dkey, ec=0):
        actf(dst, ps[b][:, :], AF.Sqrt, [("ps", b), "cst"], [dkey], bias=cst[:, ec:ec + 1], scale=1.0 / n)
        recip(dst, dst, [dkey], [dkey])

    def norm_to(par, goff, hT, hkey, sqs, rdst, rkey, b):
        for k in range(KT):
            sap, skey = sqs[k % len(sqs)]
            actf(sap, xT(par, k), AF.Square, [xk(par, k)], [skey])
            mm(ps[b][:, :], onesb[:], sap, k == 0, k == KT - 1, [skey, "onesb"], [("ps", b)])
        rstd_from(b, D, rdst, rkey)
        for k in range(KT):
            stt(hT[:, k, :], xT(par, k), gains[:, goff + k:goff + k + 1], rdst, ALU.mult, ALU.mult,
                [xk(par, k), "gains", rkey], [(hkey, k)])

    def ffn_norm_gen(goff, par):
        P.tag = 'ffn.norm'
        norm_to(par, goff, hTF, "hTF", [(sqF[:, 0, :], ("sqF", 0)), (sqF[:, 1, :], ("sqF", 1))], rsF[:, 0, :], ("rsF", 0), 4)
        yield

    def ffn_gen(fid, goff, par, do_norm=True):
        if do_norm:
            P.tag = 'ffn.norm'
            norm_to(par, goff, hTF, "hTF", [(sqF[:, 0, :], ("sqF", 0)), (sqF[:, 1, :], ("sqF", 1))], rsF[:, 0, :], ("rsF", 0), 4)
            yield
        for f in range(FT):
            P.tag = 'ffn.gu'
            bg, bu = (4, 5) if f % 2 == 0 else (6, 7)
            for kh in range(2):
                L = wgu_use[0]
                wgu_use[0] += 1
                slot = L % 4
                wgu_ensure(L + 3)
                for k4 in range(4):
                    k = 4 * kh + k4
                    mm(ps[bg][:, :], wgu[:, slot, 0, k4, :], hTF[:, k, :], k == 0, k == KT - 1,
                       [("wgu", slot, 0), ("hTF", k)], [("ps", bg)])
                for k4 in range(4):
                    k = 4 * kh + k4
                    mm(ps[bu][:, :], wgu[:, slot, 1, k4, :], hTF[:, k, :], k == 0, k == KT - 1,
                       [("wgu", slot, 1), ("hTF", k)], [("ps", bu)])
            s_ = f % 2
            actf(sgs[:, s_, :], ps[bg][:, :], AF.Tanh, [("ps", bg)], [("sgs", s_)], scale=0.5)
            stt(sgs[:, s_, :], sgs[:, s_, :], 1.0, ps[bg][:, :], ALU.add, ALU.mult, [("sgs", s_), ("ps", bg)], [("sgs", s_)])
            tt(act[:, f, :], sgs[:, s_, :], ps[bu][:, :], ALU.mult, [("sgs", s_), ("ps", bu)], [("act", f)])
            yield
        if wd_ptr[0] == 0:
            wd_ensure(1)
        for half in range(2):
            bs = [4, 5, 6, 7]
            for jc in range(11):
                P.tag = 'ffn.down'
                L = wd_use[0]
                wd_use[0] += 1
                slot = L % 3
                wd_ensure(L + 2)
                for f2 in range(2):
                    f = 2 * jc + f2
                    for o in range(4):
                        mm(ps[bs[o]][:, :], wd[:, slot, f2, o * 128:(o + 1) * 128], act[:, f, :], f == 0, f == FT - 1,
                           [("wd", slot), ("act", f)], [("ps", bs[o])])
                if jc == 10:
                    for o in range(4):
                        k = half * 4 + o
                        stt(xT(par, k), ps[bs[o]][:, :], 0.25, xT(par, k), ALU.mult, ALU.add, [("ps", bs[o]), xk(par, k)], [xk(par, k)])
                yield

    def loadx_gen(s):
        par = s % 2
        P.tag = 'loadx'
        for k in range(KT):
            dma("sp", xsel(par)[:, k, :], x_d[k * 128:(k + 1) * 128, s * TS:(s + 1) * TS], [], [xk(par, k)], "xl%d" % k)
        yield

    def xn_tile(k):
        return act[:, 2 * k:2 * k + 2, :].rearrange("p a t -> p (a t)").bitcast(F32)

    def final_gen(s):
        P.tag = 'final'
        par = s % 2
        b = 4
        for k in range(KT):
            s_ = k % 2
            actf(sqF[:, s_, :], xT(par, k), AF.Square, [xk(par, k)], [("sqF", s_)])
            mm(ps[b][:, :], onesb[:], sqF[:, s_, :], k == 0, k == KT - 1, [("sqF", s_), "onesb"], [("ps", b)])
        rstd_from(b, D, rsF[:, 1, :], ("rsF", 1))
        yield
        for k in range(KT):
            P.tag = 'final'
            stt(xn_tile(k), xT(par, k), gains[:, 24 + k:25 + k], rsF[:, 1, :], ALU.mult, ALU.mult,
                [xk(par, k), "gains", ("rsF", 1)], [("act", 2 * k), ("act", 2 * k + 1)])
            dma("sp", out_d[k * 128:(k + 1) * 128, s * TS:(s + 1) * TS], xn_tile(k),
                [("act", 2 * k), ("act", 2 * k + 1)], [("out", s, k)], "out%d" % k)
        yield

    uTd = lambda m: am[:, m, :].rearrange("p (i n) -> p i n", i=8)
    qT = lambda m: am[:, 4 + m, :]
    Ugrp = lambda g: am[:, 8 + g // 8, (g % 8) * NB:(g % 8 + 1) * NB]
    Y2grp = lambda g: am[:, g // 8, (g % 8) * NB:(g % 8 + 1) * NB]
    y2T = lambda m: am[:, 12 + m, :]
    yg = lambda m: (am[:, 4 + m, :], ("am", 4 + m))
    yat = lambda m: (am[:, 16 + m, :], ("am", 16 + m))

    def rope_finish(qs, out_ap, out_key):
        qap, qkey = qs
        b2 = bankM()
        mm(ps[b2][:, :], permb[:], qap, True, True, ["permb", qkey], [("ps", b2)])
        tt(tmpf[:, 0, :], qap, ropec[:], ALU.mult, [qkey, "ropec"], [("tmpf", 0), ("tmpf", "0b")])
        tt(tmpf[:, 1, :], ps[b2][:, :], ropes[:], ALU.mult, [("ps", b2), "ropes"], [("tmpf", 1)])
        tt(out_ap, tmpf[:, 0, :], tmpf[:, 1, :], ALU.add, [("tmpf", 0), ("tmpf", 1)], [out_key])

    def mixer_gen(s):
        par = s % 2
        P.tag = 'mix.norm'
        if wi_ptr[0] == 0:
            wi_ensure(2)
            wo_ensure(0)
        dma("sp", ropec[:], dr["ropec"][:, s * TS:(s + 1) * TS], [], ["ropec"], "ropec")
        dma("sp", ropes[:], dr["ropes"][:, s * TS:(s + 1) * TS], [], ["ropes"], "ropes")
        norm_to(par, 8, hTM, "hTM", [(sqM[:, 0, :], ("sqM", 0)), (qpre[:, 0, :], ("qpre", 0))], rsM[:, 0, :], ("rsM", 0), bankM())
        yield

        def proj_group():
            L = wi_use[0]
            wi_use[0] += 1
            slot = L % 4
            wi_ensure(L + 3)
            b = bankM()
            for k in range(KT):
                mm(ps[b][:, :], wis[:, slot, k, :], hTM[:, k, :], k == 0, k == KT - 1, [("wis", slot), ("hTM", k)], [("ps", b)])
            return b

        for m in range(4):
            P.tag = 'mix.proj'
            b = proj_group()
            actf(uTd(m), ps[b][:, :].rearrange("p (n i) -> p i n", i=8), AF.Copy, [("ps", b)], [("am", m)])
            yield
        jobs = [(qT(m), ("am", 4 + m)) for m in range(4)]
        jobs += [(kTd[:, kk, 128:640], ("kTd", kk)) for kk in range(2)]
        QS = [(qpre[:, 0, :], ("qpre", 0)), (sqM[:, 0, :], ("sqM", 0))]
        pending = None
        for idx, (out_ap, out_key) in enumerate(jobs):
            P.tag = 'mix.proj'
            b = proj_group()
            qs = QS[idx % 2]
            actf(qs[0], ps[b][:, :], AF.Copy, [("ps", b)], [qs[1]])
            if pending is not None:
                rope_finish(*pending)
            pending = (qs, out_ap, out_key)
            yield
        P.tag = 'mix.proj'
        L = wi_use[0]
        wi_use[0] += 1
        slot = L % 4
        wi_ensure(L + 3)
        b = bankM()
        for blk in range(4):
            for k in range(KT):
                mm(ps[b][:, blk * 128:(blk + 1) * 128], hTM[:, k, blk * 128:(blk + 1) * 128], wis[:, slot, k, :],
                   k == 0, k == KT - 1, [("wis", slot), ("hTM", k)], [("ps", b)])
        rope_finish(*pending)
        actf(vsb[:, 1:5, :], ps[b][:, :].rearrange("p (a d) -> p a d", a=4), AF.Copy, [("ps", b)], ["vsb"])
        yield

        def attn_gen():
            ptc = 0
            units = [(m, pair) for m in range(4) for pair in range(2)]

            def a1(m, pair, hh, pi_):
                kv = m // 2
                rows = slice(hh * 64, (hh + 1) * 64)
                bs_ = bankM()
                first = (s == 0 and pair == 0)
                mk, mkk = (maskf, "maskf") if first else (mask, "mask")
                mm(ps[bs_][:, :], identb[:], mk[:], True, False, ["identb", mkk], [("ps", bs_)])
                n = 0
                for qb in range(2):
                    blkq = 2 * pair + qb
                    for piece in range(2):
                        slot = blkq + piece
                        mm(ps[bs_][:, (qb * 2 + piece) * 128:(qb * 2 + piece + 1) * 128],
                           kTd[rows, kv, slot * 128:(slot + 1) * 128], qT(m)[rows, blkq * 128:(blkq + 1) * 128],
                           False, n == 3, [("kTd", kv), ("am", 4 + m)], [("ps", bs_)])
                        n += 1
                actf(PT[:, pi_, :], ps[bs_][:, :], AF.Exp, [("ps", bs_)], [("PT", pi_)], scale=0.125)

            def pv(m, pair, hh, pi_):
                kv = m // 2
                rows = slice(hh * 64, (hh + 1) * 64)
                for qb in range(2):
                    blkq = 2 * pair + qb
                    ncol = slice(qb * 128, (qb + 1) * 128)
                    dcol = slice(256 + qb * 128, 256 + (qb + 1) * 128)
                    for piece in range(2):
                        slot = blkq + piece
                        mm(ps[BATT][rows, ncol], vsb[:, slot, kv * 64:(kv + 1) * 64], PT[:, pi_, (qb * 2 + piece) * 128:(qb * 2 + piece + 1) * 128],
                           piece == 0, piece == 1, ["vsb", ("PT", pi_)], [("ps", BATT)])
                    for piece in range(2):
                        mm(ps[BATT][rows, dcol], onesb[:, 0:64], PT[:, pi_, (qb * 2 + piece) * 128:(qb * 2 + piece + 1) * 128],
                           piece == 0, piece == 1, ["onesb", ("PT", pi_)], [("ps", BATT)])

            def norm_unit(m, pair):
                den = tmpf[:, 2, 0:256]
                ts1(den, ps[BATT][:, 256:512], esink[:, m:m + 1], ALU.add, [("ps", BATT), "esink"], [("tmpf", 2)])
                recip(den, den, [("tmpf", 2)], [("tmpf", 2)])
                ya, yak = yat(m)
                tt(ya[:, pair * 256:(pair + 1) * 256], ps[BATT][:, 0:256], den, ALU.mult, [("ps", BATT), ("tmpf", 2)], [yak])

            a1(0, 0, 0, 0); yield
            a1(0, 0, 1, 1); yield
            for ui, (m, pair) in enumerate(units):
                nxt = units[ui + 1] if ui + 1 < len(units) else None
                pv(m, pair, 0, 0); yield
                if nxt:
                    a1(nxt[0], nxt[1], 0, 0); yield
                pv(m, pair, 1, 1); yield
                norm_unit(m, pair)
                if nxt:
                    a1(nxt[0], nxt[1], 1, 1)
                yield
            b = bankM()
            for m in range(4):
                ya, yak = yat(m)
                actf(sqM[:, 0, :], ya, AF.Square, [yak], [("sqM", 0)])
                mm(ps[b][:, :], onesb[:], sqM[:, 0, :], m == 0, m == 3, [("sqM", 0), "onesb"], [("ps", b)])
            rstd_from(b, 512, rsM[:, 1, :], ("rsM", 1))
            yield

        def ssm_gen():
            cp(Hb[:, :, :, 0:1], Hc[:].unsqueeze(3), ["Hc"], ["Hb"])
            vbank = {}

            def U_unit(m):
                b = bankM()
                for gg in range(8):
                    for i in range(8):
                        mm(ps[b][:, gg * NB:(gg + 1) * NB], sel[:, gg, (7 - i) * 16:(7 - i) * 16 + 128], uTd(m)[:, i, :],
                           i == 0, i == 7, ["sel", ("am", m)], [("ps", b)])
                actf(am[:, 8 + m, :], ps[b][:, :], AF.Copy, [("ps", b)], [("am", 8 + m)])

            def V_unit(q4):
                b = bankM()
                vbank[q4] = b
                for tt_ in range(4):
                    t = 4 * q4 + tt_
                    for g2 in range(2):
                        g = 2 * t + g2
                        for ri in range(2):
                            c0 = (tt_ * 2 + ri) * NB
                            mm(ps[b][g2 * 64:(g2 + 1) * 64, c0:c0 + NB], Wm[:, g, ri * 64:(ri + 1) * 64], Ugrp(g), True, True,
                               ["Wm", ("am", 8 + g // 8)], [("ps", b)])

            def D_unit(q4):
                b = vbank[q4]
                V = ps[b][:, :].rearrange("p (t r n) -> p t r n", t=4, r=2)
                Vre, Vim = V[:, :, 0, :], V[:, :, 1, :]
                cs_ = cosT[:, 4 * q4:4 * q4 + 4, :]
                sn_ = sinT[:, 4 * q4:4 * q4 + 4, :]
                tA = tmpf[:, 0, 0:256].rearrange("p (t n) -> p t n", t=4)
                tB = tmpf[:, 0, 256:512].rearrange("p (t n) -> p t n", t=4)
                kA, kB = ("tmpf", 0), ("tmpf", "0b")
                gk = "Gin"
                tt(tA, Vre, cs_, ALU.mult, [("ps", b), "cosT"], [kA]); tt(tB, Vim, sn_, ALU.mult, [("ps", b), "sinT"], [kB])
                tt(Gin[:, 0], tA, tB, ALU.add, [kA, kB], [gk])
                tt(tA, Vim, cs_, ALU.mult, [("ps", b), "cosT"], [kA]); tt(tB, Vre, sn_, ALU.mult, [("ps", b), "sinT"], [kB])
                tt(Gin[:, 1], tA, tB, ALU.subtract, [kA, kB, gk], [gk])
                sk_ = "Gs"
                for tt_ in range(4):
                    t = 4 * q4 + tt_
                    for ri in range(2):
                        out_ap = Gs[:, ri, tt_, :]
                        d0 = r8[:, t:t + 1].to_broadcast([128, NB])
                        d1 = Gin[:, ri, tt_, :]
                        init = Hc[:, t, ri:ri + 1]
                        P.add("dve", (lambda o_=out_ap, a_=d0, b_=d1, i_=init: (lambda e: e.tensor_tensor_scan(
                            out=o_, data0=a_, data1=b_, initial=i_, op0=ALU.mult, op1=ALU.add)))(),
                            [gk, "r8", "Hc", sk_], [sk_])
                Gre, Gim = Gs[:, 0], Gs[:, 1]
                tt(tA, Gre, cs_, ALU.mult, [sk_, "cosT"], [kA]); tt(tB, Gim, sn_, ALU.mult, [sk_, "sinT"], [kB])
                tt(Gin[:, 0], tA, tB, ALU.subtract, [kA, kB, gk], [gk])
                tt(tA, Gre, sn_, ALU.mult, [sk_, "sinT"], [kA]); tt(tB, Gim, cs_, ALU.mult, [sk_, "cosT"], [kB])
                tt(Gin[:, 1], tA, tB, ALU.add, [kA, kB, gk], [gk])
                for ri in range(2):
                    actf(Hb[:, 4 * q4:4 * q4 + 4, ri, 1:NB + 1], Gin[:, ri], AF.Copy, [gk], [("Hb", q4)])
                    cp(Hc[:, 4 * q4:4 * q4 + 4, ri:ri + 1], Gin[:, ri, :, NB - 1:NB], [gk], ["Hc"])

            def Y_unit(m):
                b = bankM()
                for gg in range(8):
                    g = 8 * m + gg
                    t, g2 = g // 2, g % 2
                    rows = slice(g2 * 64, (g2 + 1) * 64)
                    o_ = ps[b][:, gg * NB:(gg + 1) * NB]
                    mm(o_, Mm[:, g, :], Ugrp(g), True, False, ["Mm", ("am", 8 + m)], [("ps", b)])
                    mm(o_, tabR[rows, t, :], Hb[rows, t, 0, 0:NB], False, False, ["tabR", "Hb", ("Hb", m)], [("ps", b)])
                    mm(o_, tabI[rows, t, :], Hb[rows, t, 1, 0:NB], False, True, ["tabI", "Hb", ("Hb", m)], [("ps", b)])
                U3 = am[:, 8 + m, :].rearrange("p (g n) -> p g n", g=8)
                Db = dblk[:, 8 * m:8 * m + 8].unsqueeze(2).to_broadcast([128, 8, NB])
                y1 = tmpf[:, 1, :]
                tt(y1.rearrange("p (g n) -> p g n", g=8), U3, Db, ALU.mult, [("am", 8 + m), "dblk"], [("tmpf", 1)])
                tt(y1, y1, ps[b][:, :], ALU.add, [("tmpf", 1), ("ps", b)], [("tmpf", 1)])
                actf(am[:, m, :], y1, AF.Gelu_apprx_tanh, [("tmpf", 1)], [("am", m)])

            def I_unit(m):
                b = bankM()
                for j in range(8):
                    for gg in range(8):
                        mm(ps[b][:, j * NB:(j + 1) * NB], sel[:, j, (7 - gg) * 16:(7 - gg) * 16 + 128], Y2grp(8 * m + gg),
                           gg == 0, gg == 7, ["sel", ("am", m)], [("ps", b)])
                actf(y2T(m).rearrange("p (n j) -> p j n", j=8), ps[b][:, :].rearrange("p (j n) -> p j n", j=8), AF.Copy,
                     [("ps", b)], [("am", 12 + m)])

            order = [(U_unit, 0), (U_unit, 1), (V_unit, 0), (D_unit, 0), (U_unit, 2), (V_unit, 1), (D_unit, 1), (U_unit, 3),
                     (V_unit, 2), (D_unit, 2), (Y_unit, 0), (V_unit, 3), (D_unit, 3), (Y_unit, 1), (I_unit, 0), (Y_unit, 2),
                     (I_unit, 1), (Y_unit, 3), (I_unit, 2), (I_unit, 3)]
            for fn_, arg in order:
                fn_(arg)
                yield
            yield "need_attn_done"
            for mo in range(4):
                b = bankM()
                for k in range(4):
                    mm(ps[b][:, :], w_glu[:, k, mo * 128:(mo + 1) * 128], y2T(k), k == 0, k == 3, ["w_glu", ("am", 12 + k)], [("ps", b)])
                sg = tmpf[:, 1, :]
                actf(sg, ps[b][:, :], AF.Tanh, [("ps", b), "hbg"], [("tmpf", 1)], bias=hbg[:, mo:mo + 1], scale=0.5)
                ygm, ygk = yg(mo)
                stt(ygm, sg, 1.0, y2T(mo), ALU.add, ALU.mult, [("am", 12 + mo), ("tmpf", 1)], [ygk])
                yield
            b = bankM()
            for mo in range(4):
                ygm, ygk = yg(mo)
                actf(sqM[:, 0, :], ygm, AF.Square, [ygk], [("sqM", 0)])
                mm(ps[b][:, :], onesb[:], sqM[:, 0, :], mo == 0, mo == 3, [("sqM", 0), "onesb"], [("ps", b)])
            rstd_from(b, 512, rsM[:, 0, :], ("rsM", 0), ec=2)
            yield

        ga, gs = attn_gen(), ssm_gen()
        a_done = s_done = s_wait = False
        while not (a_done and s_done):
            if not a_done:
                P.tag = 'mix.attn'
                try:
                    next(ga)
                except StopIteration:
                    a_done = True
                yield
            if not s_done and not (s_wait and not a_done):
                P.tag = 'mix.ssm'
                try:
                    if next(gs) == "need_attn_done":
                        s_wait = True
                except StopIteration:
                    s_done = True
                yield
        if s == 0:
            dump("yattn", am[:, 16:20, :], [("am", 16 + m) for m in range(4)], [128, 4, TS], BF16)
            dump("y2T", am[:, 12:16, :], [("am", 12 + m) for m in range(4)], [128, 4, TS], BF16)
        P.tag = 'mix.onorm'
        for k in range(4):
            ygm, ygk = yg(k)
            stt(hTM[:, k, :], ygm, gains[:, 32 + k:33 + k], rsM[:, 0, :], ALU.mult, ALU.mult, [ygk, "gains", ("rsM", 0)], [("hTM", k)])
        for k in range(4):
            ya, yak = yat(k)
            stt(hTM[:, 4 + k, :], ya, gains[:, 36 + k:37 + k], rsM[:, 1, :], ALU.mult, ALU.mult, [yak, "gains", ("rsM", 1)], [("hTM", 4 + k)])
        yield
        for o in range(8):
            P.tag = 'mix.wout'
            L = wo_use[0]
            wo_use[0] += 1
            slot = L % 2
            wo_ensure(L + 1)
            b = bankM()
            for k in range(KT):
                mm(ps[b][:, :], wo[:, slot, k, :], hTM[:, k, :], k == 0, k == KT - 1, [("wo", slot), ("hTM", k)], [("ps", b)])
            tt(xT(par, o), ps[b][:, :], xT(par, o), ALU.add, [("ps", b), xk(par, o)], [xk(par, o)])
            yield
        cp(kTd[:, :, 0:128], kTd[:, :, 512:640], [("kTd", 0), ("kTd", 1)], [("kTd", 0), ("kTd", 1)], eng="act")
        cp(vsb[:, 0, :], vsb[:, 4, :], ["vsb"], ["vsb"], eng="act")
        yield

    def drain(g):
        for _ in g:
            pass

    def chain(*gens):
        for g in gens:
            for _ in g:
                yield

    def interleave(ga, na, gb, nb):
        ca = cb = 0
        a_done = b_done = False
        while not (a_done and b_done):
            pick_a = (not a_done) and (b_done or ca * nb <= cb * na)
            if pick_a:
                try:
                    next(ga)
                    ca += 1
                except StopIteration:
                    a_done = True
            else:
                try:
                    next(gb)
                    cb += 1
                except StopIteration:
                    b_done = True

    P.tile = 0
    sb = sbp
    pg = ssm_precompute()
    next(pg)
    wgu_ensure(2)
    drain(chain(loadx_gen(0), ffn_norm_gen(0, 0)))
    fg = ffn_gen(1, 0, 0, do_norm=False)
    p_part1 = True
    f_live = True
    while p_part1 or f_live:
        if p_part1:
            P.tag = 'pre'
            if next(pg) == "PE_PART":
                p_part1 = False
        if f_live:
            try:
                next(fg)
            except StopIteration:
                f_live = False
    P.tag = 'pre'
    drain(pg)
    P.barrier()
    pstack.close()
    sb = sb_main
    xTb = sb("xTb", [128, KT, TS], F32)
    hTM = sb("hTM", [128, KT, TS], BF16)
    am = sb("am", [128, 20, TS], BF16)
    dump("x1", xTa[:], [xk(0, k) for k in range(KT)], [128, KT, TS])
    for s in range(NT_RUN):
        P.tile = s
        bparts = []
        nb = 0
        if s >= 1:
            bparts += [ffn_gen(2, 16, (s - 1) % 2), final_gen(s - 1)]
            nb += 50
        if s + 1 < NT_RUN:
            bparts += [loadx_gen(s + 1), ffn_norm_gen(0, (s + 1) % 2), ffn_gen(1, 0, (s + 1) % 2, do_norm=False)]
            nb += 49
        if bparts:
            interleave(mixer_gen(s), 95, chain(*bparts), nb)
        else:
            drain(mixer_gen(s))
        if s == 0:
            dump("x2", xTa[:], [xk(0, k) for k in range(KT)], [128, KT, TS])
    drain(chain(ffn_gen(2, 16, (NT_RUN - 1) % 2), final_gen(NT_RUN - 1)))
    P.add("sp", None, [("out", s, k) for s in range(NT_RUN) for k in range(KT)] + [("dbg", n) for n in dbg_d], [])

    P.finalize(nc, stack)
    with nc.Block() as block:
        @block.sync
        def _(e):
            P.emit("sp", e)

        @block.tensor
        def _(e):
            P.emit("pe", e)

        @block.scalar
        def _(e):
            P.emit("act", e)

        @block.vector
        def _(e):
            P.emit("dve", e)

        @block.gpsimd
        def _(e):
            P.emit("pool", e)
    stack.close()
    nc._prog = P
    return nc, list(dbg_d.keys())


_CACHE = {}


def kernel(**inputs):
    x = np.ascontiguousarray(np.asarray(inputs["x"], dtype=np.float32))
    B = x.shape[0]
    if "nc" not in _CACHE:
        _CACHE["nc"] = build_program()
    nc, dbg = _CACHE["nc"]
    shared = host_layout(inputs)
    shared.update(host_consts())
    in_maps = []
    for b in range(B):
        m = dict(shared)
        m["x"] = np.ascontiguousarray(x[b].T)
        in_maps.append(m)
    res = run_bass_kernel_spmd(nc, in_maps, core_ids=list(range(B)))
    out = np.stack([np.ascontiguousarray(np.asarray(r["out"], dtype=np.float32).T) for r in res.results], axis=0)
    if DEBUG:
        _CACHE["dbg"] = {n: np.asarray(res.results[0]["dbg_" + n]) for n in dbg}
    return out
```
